# Optimizing a Trainium2 kernel written in Bass

```python
import math
import jax, jax.numpy as jnp
from jax import lax
import numpy as np

D_MODEL = 1024
BATCH = 32
SEQ = 2048
DEPTH = 1

DA_HEADS = 8
DA_DK = 64
DA_DV = 2 * DA_DK
Q_BLOCK = 128
ML_HEADS = 4
ML_DK = 128
ML_DV = 256
ML_CHUNK = 128
CONV_K = 4
PEER_HEADS = 8
PEER_TOPK = 16
N_KEYS = 128
N_EXPERTS = N_KEYS * N_KEYS
PEER_DKEY = 128
PEER_CHUNK = 128

DA_QK_W = DA_HEADS * 2 * DA_DK
DA_V_W = DA_HEADS * DA_DV
ML_QK_W = ML_HEADS * ML_DK
ML_V_W = ML_HEADS * ML_DV
IN_SIZES = (DA_QK_W, DA_QK_W, DA_V_W, ML_QK_W, ML_QK_W, ML_V_W, ML_V_W,
            ML_HEADS, ML_HEADS, D_MODEL, D_MODEL)
IN_W = sum(IN_SIZES)

ALPHA = (2 * DEPTH) ** 0.25
BETA = (8 * DEPTH) ** -0.25
LN_EPS = 1e-5

kernel_name = "hybrid_diffattn_mlstm_peer_deepnorm_adaln"


def _split_cols(z, sizes):
    idx = np.cumsum(np.array(sizes))[:-1].tolist()
    return jnp.split(z, idx, axis=-1)


def _layer_norm(x):
    xf = x.astype(jnp.float32)
    mu = jnp.mean(xf, axis=-1, keepdims=True)
    var = jnp.mean(jnp.square(xf - mu), axis=-1, keepdims=True)
    return ((xf - mu) * lax.rsqrt(var + LN_EPS)).astype(x.dtype)


def _rms_norm(x, g):
    xf = x.astype(jnp.float32)
    y = xf * lax.rsqrt(jnp.mean(jnp.square(xf), axis=-1, keepdims=True) + LN_EPS)
    return y * g.astype(jnp.float32)


def _causal_depthwise_conv(x, w, b):
    C = x.shape[-1]
    y = lax.conv_general_dilated(x, w[:, None, :].astype(x.dtype), window_strides=(1,),
                                 padding=[(CONV_K - 1, 0)],
                                 dimension_numbers=('NWC', 'WIO', 'NWC'),
                                 feature_group_count=C)
    return y + b


def _diff_attention(q, k, v, lam, subln_g, lambda_init):
    dtype = v.dtype
    q = jnp.transpose(q, (0, 2, 3, 1, 4))
    k = jnp.transpose(k, (0, 2, 3, 1, 4))
    v = jnp.transpose(v, (0, 2, 1, 3)).astype(jnp.float32)
    S = q.shape[3]
    lamf = lam.astype(jnp.float32)
    lam_val = (jnp.exp(jnp.sum(lamf[0] * lamf[1])) - jnp.exp(jnp.sum(lamf[2] * lamf[3]))
               + lambda_init)
    scale = DA_DK ** -0.5
    outs = []
    for blk in range(S // Q_BLOCK):
        q0 = blk * Q_BLOCK
        kend = q0 + Q_BLOCK
        qb = q[:, :, :, q0:kend]
        kb = k[:, :, :, :kend]
        s = jnp.einsum('bhmqd,bhmkd->bhmqk', qb, kb).astype(jnp.float32) * scale
        mask = jnp.arange(kend)[None, :] <= (q0 + jnp.arange(Q_BLOCK))[:, None]
        p = jax.nn.softmax(jnp.where(mask, s, -jnp.inf), axis=-1)
        a = p[:, :, 0] - lam_val * p[:, :, 1]
        outs.append(jnp.einsum('bhqk,bhkd->bhqd', a, v[:, :, :kend]))
    o = jnp.concatenate(outs, axis=2)
    o = _rms_norm(o, subln_g) * (1.0 - lambda_init)
    B = o.shape[0]
    return jnp.transpose(o, (0, 2, 1, 3)).reshape(B, S, DA_HEADS * DA_DV).astype(dtype)


def _mlstm(q, k, v, i_pre, f_pre, norm_g):
    dtype = v.dtype
    B, S, H, _ = q.shape
    L = ML_CHUNK
    NC = S // L
    q = q.astype(jnp.float32)
    k = k.astype(jnp.float32) * (ML_DK ** -0.5)
    v = v.astype(jnp.float32)
    ig = i_pre.astype(jnp.float32)
    lf = jax.nn.log_sigmoid(f_pre.astype(jnp.float32))

    def to_chunks(t):
        return jnp.transpose(t.reshape(B, NC, L, H, t.shape[-1]), (1, 0, 3, 2, 4))

    def g_chunks(t):
        return jnp.transpose(t.reshape(B, NC, L, H), (1, 0, 3, 2))

    causal = jnp.tril(jnp.ones((L, L), dtype=bool))

    def step(carry, inp):
        C, n, m = carry
        qc, kc, vc, ic, fc = inp
        b = jnp.cumsum(fc, axis=-1)
        Dm = b[..., :, None] - b[..., None, :] + ic[..., None, :]
        Dm = jnp.where(causal, Dm, -jnp.inf)
        m_inter = b + m[..., None]
        m_t = jnp.maximum(m_inter, jnp.max(Dm, axis=-1))
        W = jnp.exp(Dm - m_t[..., None])
        P = W * jnp.einsum('bhtd,bhsd->bhts', qc, kc)
        inter = jnp.exp(m_inter - m_t)
        num = (jnp.einsum('bhts,bhsv->bhtv', P, vc)
               + inter[..., None] * jnp.einsum('bhvd,bhtd->bhtv', C, qc))
        nq = jnp.sum(P, axis=-1) + inter * jnp.einsum('bhd,bhtd->bht', n, qc)
        h = num / jnp.maximum(jnp.abs(nq), jnp.exp(-m_t))[..., None]
        m_new = m_t[..., -1]
        decay = jnp.exp(b[..., -1] + m - m_new)
        w_s = jnp.exp(b[..., -1:] - b + ic - m_new[..., None])
        C_new = decay[..., None, None] * C + jnp.einsum('bhsv,bhsd->bhvd', vc * w_s[..., None], kc)
        n_new = decay[..., None] * n + jnp.einsum('bhs,bhsd->bhd', w_s, kc)
        return (C_new, n_new, m_new), h

    init = (jnp.zeros((B, H, ML_DV, ML_DK), jnp.float32),
            jnp.zeros((B, H, ML_DK), jnp.float32),
            jnp.zeros((B, H), jnp.float32))
    _, hs = lax.scan(step, init, (to_chunks(q), to_chunks(k), to_chunks(v), g_chunks(ig), g_chunks(lf)))
    h = jnp.transpose(hs, (1, 0, 3, 2, 4)).reshape(B, S, H, ML_DV)
    h = _rms_norm(h, norm_g.reshape(H, ML_DV))
    return h.reshape(B, S, H * ML_DV).astype(dtype)


def _peer(h, w_q, sub_keys, u_tab, v_tab):
    B, S, D = h.shape
    T = B * S
    ht = h.reshape(T, D)
    q = (ht @ w_q).reshape(T, PEER_HEADS, 2, PEER_DKEY // 2)
    s = jnp.einsum('thpd,pnd->thpn', q, sub_keys).astype(jnp.float32)
    sv, si = lax.top_k(s, PEER_TOPK)
    cand = (sv[:, :, 0, :, None] + sv[:, :, 1, None, :]).reshape(T, PEER_HEADS, PEER_TOPK * PEER_TOPK)
    cv, ci = lax.top_k(cand, PEER_TOPK)
    i1 = jnp.take_along_axis(si[:, :, 0], ci // PEER_TOPK, axis=-1)
    i2 = jnp.take_along_axis(si[:, :, 1], ci % PEER_TOPK, axis=-1)
    eidx = i1 * N_KEYS + i2
    g = jax.nn.softmax(cv, axis=-1)
    nch = T // PEER_CHUNK

    def expert_block(args):
        hc, ec, gc = args
        uc = u_tab[ec]
        act = jax.nn.gelu(jnp.einsum('cd,chkd->chk', hc, uc), approximate=False)
        vc = v_tab[ec]
        return jnp.einsum('chk,chkd->cd', (gc * act).astype(hc.dtype), vc)

    out = lax.map(expert_block, (ht.reshape(nch, PEER_CHUNK, D),
                                 eidx.reshape(nch, PEER_CHUNK, PEER_HEADS, PEER_TOPK),
                                 g.reshape(nch, PEER_CHUNK, PEER_HEADS, PEER_TOPK)))
    return out.reshape(B, S, D)


def setup_inputs(seed: int = 0) -> dict:
    key = jax.random.key(seed)
    ks = jax.random.split(key, 26)
    f32 = jnp.float32
    nrm = lambda k, shape, s: jax.random.normal(k, shape, f32) * s
    gain = lambda k, shape: 1.0 + 0.02 * jax.random.normal(k, shape, f32)
    b_if = jnp.stack([0.1 * jax.random.normal(ks[4], (DEPTH, ML_HEADS), f32),
                      jnp.linspace(3.0, 6.0, ML_HEADS, dtype=f32)[None, :]
                      + 0.1 * jax.random.normal(ks[5], (DEPTH, ML_HEADS), f32)], axis=1)
    return {
        "x": nrm(ks[0], (BATCH, SEQ, D_MODEL), 1.0),
        "c": nrm(ks[1], (BATCH, D_MODEL), 1.0),
        "w_ada": nrm(ks[2], (DEPTH, D_MODEL, 6 * D_MODEL), 0.5 * D_MODEL ** -0.5),
        "b_ada": nrm(ks[3], (DEPTH, 6 * D_MODEL), 0.02),
        "w_in": nrm(ks[6], (DEPTH, D_MODEL, IN_W), D_MODEL ** -0.5),
        "b_if": b_if,
        "conv_w": nrm(ks[7], (DEPTH, CONV_K, 2 * ML_QK_W), CONV_K ** -0.5),
        "conv_b": nrm(ks[8], (DEPTH, 2 * ML_QK_W), 0.02),
        "da_lambda": nrm(ks[9], (DEPTH, 4, DA_DK), 0.1),
        "da_subln_g": gain(ks[10], (DEPTH, DA_DV)),
        "ml_norm_g": gain(ks[11], (DEPTH, ML_V_W)),
        "w_br_attn": nrm(ks[12], (DEPTH, DA_V_W, D_MODEL), BETA * DA_V_W ** -0.5),
        "w_br_mlstm": nrm(ks[13], (DEPTH, ML_V_W, D_MODEL), BETA * ML_V_W ** -0.5),
        "w_out": nrm(ks[14], (DEPTH, D_MODEL, D_MODEL), BETA * D_MODEL ** -0.5),
        "ln1_g": gain(ks[15], (DEPTH, D_MODEL)),
        "ln1_b": nrm(ks[16], (DEPTH, D_MODEL), 0.02),
        "peer_wq": nrm(ks[17], (DEPTH, D_MODEL, PEER_HEADS * PEER_DKEY), D_MODEL ** -0.5),
        "peer_keys": nrm(ks[18], (DEPTH, 2, N_KEYS, PEER_DKEY // 2), (PEER_DKEY // 2) ** -0.5),
        "peer_u": nrm(ks[19], (DEPTH, N_EXPERTS, D_MODEL), D_MODEL ** -0.5),
        "peer_v": nrm(ks[20], (DEPTH, N_EXPERTS, D_MODEL), BETA * PEER_HEADS ** -0.5),
        "ln2_g": gain(ks[21], (DEPTH, D_MODEL)),
        "ln2_b": nrm(ks[22], (DEPTH, D_MODEL), 0.02),
    }


def reference(x, c, w_ada, b_ada, w_in, b_if, conv_w, conv_b, da_lambda, da_subln_g,
              ml_norm_g, w_br_attn, w_br_mlstm, w_out, ln1_g, ln1_b, peer_wq, peer_keys,
              peer_u, peer_v, ln2_g, ln2_b):
    B, S, D = x.shape
    for l in range(DEPTH):
        mod = jax.nn.silu(c) @ w_ada[l] + b_ada[l]
        sh1, sc1, gt1, sh2, sc2, gt2 = jnp.split(mod[:, None, :], 6, axis=-1)

        h = _layer_norm(x) * (1.0 + sc1) + sh1
        z = h @ w_in[l]
        (da_q, da_k, da_v, ml_q, ml_k, ml_v, ml_o, ml_i, ml_f,
         g_attn, g_ml) = _split_cols(z, IN_SIZES)
        qk = jax.nn.silu(_causal_depthwise_conv(jnp.concatenate([ml_q, ml_k], axis=-1),
                                                conv_w[l], conv_b[l]))
        ml_q, ml_k = jnp.split(qk, 2, axis=-1)
        lambda_init = 0.8 - 0.6 * math.exp(-0.3 * l)
        ya = _diff_attention(da_q.reshape(B, S, DA_HEADS, 2, DA_DK),
                             da_k.reshape(B, S, DA_HEADS, 2, DA_DK),
                             da_v.reshape(B, S, DA_HEADS, DA_DV),
                             da_lambda[l], da_subln_g[l], lambda_init)
        ym = _mlstm(ml_q.reshape(B, S, ML_HEADS, ML_DK),
                    ml_k.reshape(B, S, ML_HEADS, ML_DK),
                    ml_v.reshape(B, S, ML_HEADS, ML_DV),
                    ml_i + b_if[l, 0], ml_f + b_if[l, 1], ml_norm_g[l])
        ym = ym * jax.nn.sigmoid(ml_o)
        y = (jax.nn.sigmoid(g_attn) * (ya @ w_br_attn[l])
             + jax.nn.sigmoid(g_ml) * (ym @ w_br_mlstm[l]))
        x = _layer_norm(ALPHA * x + gt1 * (y @ w_out[l])) * ln1_g[l] + ln1_b[l]

        h = _layer_norm(x) * (1.0 + sc2) + sh2
        yf = _peer(h, peer_wq[l], peer_keys[l], peer_u[l], peer_v[l])
        x = _layer_norm(ALPHA * x + gt2 * yf) * ln2_g[l] + ln2_b[l]
    return x
```

```python
import contextlib
import math
import numpy as np
import concourse.bass as bass
import concourse.mybir as mybir
from concourse.bass_utils import run_bass_kernel_spmd

F32 = mybir.dt.float32
BF16 = mybir.dt.bfloat16
AF = mybir.ActivationFunctionType
ALU = mybir.AluOpType

D = 1024
SEQ = 2048
NT = SEQ // 128
NCORES = 8
BATCH = 32
NSEQ_FULL = BATCH // NCORES
ALPHA = 2.0 ** 0.25
LN_EPS = 1e-5
LAMBDA_INIT = 0.8 - 0.6 * math.exp(0.0)
NEXP = 16384


class Buf:
    __slots__ = ("name", "last_w", "readers")

    def __init__(self, name):
        self.name = name
        self.last_w = None
        self.readers = []


class _Rec:
    def __init__(self):
        self.calls = []

    def __getattr__(self, name):
        def f(*a, **kw):
            self.calls.append((name, a, kw))
            return None
        return f


class Sched:
    ENGS = ("pe", "act", "dve", "pool", "sp")

    def __init__(self, nc):
        self.nc = nc
        self.ops = {e: [] for e in self.ENGS}
        self.cnt = {e: 0 for e in self.ENGS}
        self.seen = {e: {} for e in self.ENGS}
        self.sems = {}
        self.dma_cnt = {}
        self.nops = 0
        self.nwait = 0

    def op(self, eng, emit, reads=(), writes=(), dma=None, n=1):
        deps = {}
        for b in reads:
            if b.last_w is not None:
                k, v, e = b.last_w
                if not (e == eng and k[0] == "eng" and False):
                    deps[k] = max(deps.get(k, 0), v)
        for b in writes:
            if b.last_w is not None:
                k, v, e = b.last_w
                if not (e == eng and k[0] == "eng" and dma is None):
                    deps[k] = max(deps.get(k, 0), v)
            for (k, v, e) in b.readers:
                if e == eng and k[0] == "eng" and dma is None:
                    continue
                deps[k] = max(deps.get(k, 0), v)
        waits = []
        seen = self.seen[eng]
        for k, v in deps.items():
            if seen.get(k, 0) >= v:
                continue
            seen[k] = v
            waits.append((k, v))
        if dma is None:
            self.cnt[eng] += 1
            key = ("eng", eng)
            tok = (key, self.cnt[eng], eng)
        else:
            key = ("dma", dma)
            self.dma_cnt[key] = self.dma_cnt.get(key, 0) + 16 * n
            tok = (key, self.dma_cnt[key], eng)
        for b in reads:
            b.readers.append(tok)
        for b in writes:
            b.last_w = tok
            b.readers = []
        rec = _Rec()
        emit(rec)
        assert len(rec.calls) >= 1
        if dma is not None:
            assert len(rec.calls) == n, (len(rec.calls), n)
        self.ops[eng].append((waits, rec.calls, key, dma is not None, n))
        self.nops += 1
        self.nwait += len(waits)
        return tok

    def barrier(self):
        cur = {("eng", e): self.cnt[e] for e in self.ENGS if self.cnt[e] > 0}
        cur.update(self.dma_cnt)
        for eng in self.ENGS:
            waits = []
            seen = self.seen[eng]
            for k, v in cur.items():
                if k == ("eng", eng) or seen.get(k, 0) >= v:
                    continue
                seen[k] = v
                waits.append((k, v))
            if waits:
                self.cnt[eng] += 1
                self.ops[eng].append((waits, [("nop", (), {})], ("eng", eng), False, 1))

    def emit_all(self):
        nc = self.nc
        keys = [("eng", e) for e in self.ENGS]
        for e in self.ENGS:
            for rec in self.ops[e]:
                if rec[2] not in keys:
                    keys.append(rec[2])
        for k in keys:
            self.sems[k] = nc.alloc_semaphore("s_" + "_".join(str(x) for x in k))
        with nc.Block() as block:
            def mk(ename):
                def body(eng):
                    for waits, calls, key, is_dma, n in self.ops[ename]:
                        for (k, v) in waits:
                            eng.wait_ge(self.sems[k], v)
                        s = self.sems[key]
                        r = None
                        for (name, a, kw) in calls:
                            r = getattr(eng, name)(*a, **kw)
                            if is_dma:
                                r.then_inc(s, 16)
                        if not is_dma:
                            r.then_inc(s, 1)
                return body
            block.tensor(mk("pe"))
            block.scalar(mk("act"))
            block.vector(mk("dve"))
            block.gpsimd(mk("pool"))
            block.sync(mk("sp"))


class Arena:
    def __init__(self, t, nbytes):
        self.t = t
        self.nbytes = nbytes
        self.off = 0
        self.stack = []
        self.peak = 0

    def push(self):
        self.stack.append(self.off)

    def pop(self):
        self.off = self.stack.pop()

    def alloc(self, shape, dt):
        esz = 4 if dt == F32 else 2
        n = 1
        for s in shape[1:]:
            n *= s
        nb = (n * esz + 63) // 64 * 64
        assert self.off + nb <= self.nbytes, ("arena overflow", self.off, nb, self.nbytes)
        a = self.t[0:shape[0], self.off // 4:(self.off + nb) // 4]
        self.off += nb
        self.peak = max(self.peak, self.off)
        if dt != F32:
            a = a.bitcast(dt)
        a = a[:, 0:n]
        if len(shape) == 3:
            a = a.rearrange("p (a b) -> p a b", a=shape[1])
        elif len(shape) == 4:
            a = a.rearrange("p (a b c) -> p a b c", a=shape[1], b=shape[2])
        return a


def build(NSEQ=NSEQ_FULL, do_peer=True, dbg=False):
    nc = bass.Bass("TRN2", target_bir_lowering=False)
    SC = Sched(nc)
    op = SC.op
    NTOK = NSEQ * SEQ

    def din(name, shape, dt=F32):
        return nc.dram_tensor(name, list(shape), dt, kind="ExternalInput").ap()

    def dscr(name, shape, dt):
        return nc.dram_tensor(name, list(shape), dt, kind="Internal").ap()

    x_d = din("x", [NTOK, D])
    cT_d = din("cT", [128, 8 * NSEQ])
    wada_d = din("w_ada", [128, 8 * 6144])
    bada_d = din("b_ada", [1, 6144])
    wda_d = din("w_da", [8 * 128, 3072])
    wml_d = din("w_ml", [4 * 128, 6144])
    wif_d = din("w_if", [128, 64])
    wmg_d = din("w_mg", [8 * 128, 4096])
    wout_d = din("w_out", [128, 8192])
    wpq_d = din("w_pq", [8 * 128, 1024])
    ut_d = din("UT", [32 * 128, 4096])
    v_d = din("V", [NEXP, D])
    bif_d = din("b_ifT", [4, 2])
    cw_d = din("conv_wT", [128, 32])
    cb_d = din("conv_bT", [128, 8])
    lam_d = din("da_lambda", [1, 256])
    subg_d = din("subln_g", [1, 128])
    mlg_d = din("ml_norm_g", [1, D])
    ln1g_d = din("ln1_g", [1, D])
    ln1b_d = din("ln1_b", [1, D])
    ln2g_d = din("ln2_g", [1, D])
    ln2b_d = din("ln2_b", [1, D])
    keysT_d = din("keysT", [128, 256])
    ident_d = din("ident", [128, 128])
    cmask_d = din("cmask", [128, 128])
    sel4_d = din("sel4", [4, 512])
    out_d = nc.dram_tensor("out", [NTOK, D], F32, kind="ExternalOutput").ap()

    wda_b = dscr("w_da_b", [8 * 128, 3072], BF16)
    wml_b = dscr("w_ml_b", [4 * 128, 6144], BF16)
    wif_b = dscr("w_if_b", [128, 64], BF16)
    wmg_b = dscr("w_mg_b", [8 * 128, 4096], BF16)
    wout_b = dscr("w_out_b", [128, 8192], BF16)
    ut_b = dscr("UT_b", [32 * 128, 4096], BF16)
    v_b = dscr("V_b", [NEXP, D], BF16)
    x1_d = dscr("x1_s", [NTOK, D], F32)
    gt_d = dscr("gt_s", [NSEQ, 2048], F32)

    es = contextlib.ExitStack()
    with es:
        ARENA_BYTES = 206 * 1024
        arena_t = es.enter_context(nc.sbuf_tensor("arena", [128, ARENA_BYTES // 4], F32))
        A = Arena(arena_t, ARENA_BYTES)
        psum_t = es.enter_context(nc.psum_tensor("psum", [128, 8, 512], F32))
        PB = [psum_t[:, i, :] for i in range(8)]
        bPB = [Buf(f"pb{i}") for i in range(8)]
        bank_rr = [0]

        def nextbank(banks):
            i = banks[bank_rr[0] % len(banks)]
            bank_rr[0] += 1
            return i

        def dma(eng, out, in_, reads, writes, key):
            return op(eng, lambda e: e.dma_start(out=out, in_=in_), reads=reads, writes=writes, dma=key)

        dbg_outs = {}

        def dump(name, ap, bufs, shape, dt):
            if not dbg:
                return
            t = nc.dram_tensor("d_" + name, list(shape), dt, kind="ExternalOutput").ap()
            dbg_outs[name] = t
            dma("sp", t, ap, list(bufs), [Buf("dbg")], "dbg_" + name)

        bOUT = Buf("out")
        bDR = Buf("dram_scratch")
        bGT = Buf("gt_scratch")
        bXO = [Buf("xo0"), Buf("xo1")]

        identf = A.alloc([128, 128], F32)
        identb = A.alloc([128, 128], BF16)
        maskb = A.alloc([128, 128], BF16)
        sel4 = A.alloc([4, 512], F32)
        cm05 = A.alloc([128, 8], F32)
        ones4 = A.alloc([4, 512], F32)
        modT = A.alloc([128, 48 * NSEQ], F32)
        neglam = A.alloc([128, 1], F32)
        gda_bc = A.alloc([128, 128], F32)
        bifT = A.alloc([4, 2], F32)
        nbf = A.alloc([4, 1], F32)
        cwT = A.alloc([128, 32], F32)
        cbT = A.alloc([128, 8], F32)
        wif = A.alloc([128, 8, 8], BF16)
        junk = A.alloc([128, 256], F32)
        bC = Buf("consts")
        bjunk = Buf("junk")
        NST = 4
        stt = [A.alloc([128, 12], F32) for _ in range(NST)]
        mvt = [A.alloc([128, 2], F32) for _ in range(NST)]
        rst = [A.alloc([128, 1], F32) for _ in range(NST)]
        bst = [Buf(f"st{i}") for i in range(NST)]
        bmv = [Buf(f"mv{i}") for i in range(NST)]
        brs = [Buf(f"rs{i}") for i in range(NST)]
        stk = [0]

        def ln_stats(src, bsrc):
            k = stk[0] % NST
            stk[0] += 1
            st, mv, rs = stt[k], mvt[k], rst[k]
            op("dve", lambda e: e.bn_stats(out=st[:, 0:6], in_=src[:, 0:512]), reads=[bsrc], writes=[bst[k]])
            op("dve", lambda e: e.bn_stats(out=st[:, 6:12], in_=src[:, 512:1024]), reads=[bsrc], writes=[bst[k]])
            op("dve", lambda e: e.bn_aggr(out=mv[:], in_=st[:]), reads=[bst[k]], writes=[bmv[k]])
            op("dve", lambda e: e.tensor_scalar(out=rs[:], in0=mv[:, 1:2], scalar1=LN_EPS, scalar2=None, op0=ALU.add),
               reads=[bmv[k]], writes=[brs[k]])
            op("pool", lambda e: e.tensor_tensor(out=rs[:], in0=rs[:], in1=cm05[:, 0:1], op=ALU.pow),
               reads=[brs[k], bC], writes=[brs[k]])
            return mv, rs, bmv[k], brs[k]

        A.push()
        tmpf = A.alloc([128, 128], F32)
        tmpm = A.alloc([128, 128], F32)
        lamt = A.alloc([128, 256], F32)
        lamp = A.alloc([128, 128], F32)
        lams = A.alloc([128, 2], F32)
        wif_f = A.alloc([128, 64], F32)
        btmp = Buf("tmp0")
        dma("sp", identf[:], ident_d[:, :], [], [bC], "c0")
        dma("sp", tmpm[:], cmask_d[:, :], [], [btmp], "c1")
        dma("sp", sel4[:], sel4_d[:, :], [], [bC], "c0")
        dma("sp", bifT[:], bif_d[:, :], [], [bC], "c0")
        dma("sp", cwT[:], cw_d[:, :], [], [bC], "c0")
        dma("sp", cbT[:], cb_d[:, :], [], [bC], "c0")
        dma("sp", gda_bc[:], subg_d[0:1, :].partition_broadcast(128), [], [bC], "c0")
        dma("sp", lamt[:], lam_d[0:1, :].partition_broadcast(128), [], [btmp], "c1")
        dma("sp", wif_f[:], wif_d[:, :], [], [btmp], "c1")
        op("pool", lambda e: e.tensor_copy(out=identb[:], in_=identf[:]), reads=[bC], writes=[bC])
        op("pool", lambda e: e.tensor_copy(out=maskb[:], in_=tmpm[:]), reads=[btmp], writes=[bC])
        op("pool", lambda e: e.memset(cm05[:], -0.5), writes=[bC])
        op("pool", lambda e: e.memset(ones4[:], 1.0), writes=[bC])
        op("pool", lambda e: e.tensor_copy(out=wif[:].rearrange("p a b -> p (a b)"), in_=wif_f[:]), reads=[btmp], writes=[bC])
        op("dve", lambda e: e.tensor_scalar(out=gda_bc[:], in0=gda_bc[:], scalar1=1.0 - LAMBDA_INIT, scalar2=None, op0=ALU.mult),
           reads=[bC], writes=[bC])
        op("dve", lambda e: e.tensor_scalar(out=nbf[:], in0=bifT[:, 1:2], scalar1=-1.0, scalar2=None, op0=ALU.mult),
           reads=[bC], writes=[bC])
        lt3 = lamt[:].rearrange("p (a b) -> p a b", a=4)
        op("dve", lambda e: e.tensor_tensor(out=lamp[:].rearrange("p (a b) -> p a b", a=2),
                                            in0=lt3[:, 0:4:2, :], in1=lt3[:, 1:4:2, :], op=ALU.mult),
           reads=[btmp], writes=[btmp])
        op("dve", lambda e: e.tensor_reduce(out=lams[:], in_=lamp[:].rearrange("p (a b) -> p a b", a=2),
                                            axis=mybir.AxisListType.X, op=ALU.add),
           reads=[btmp], writes=[btmp])
        op("act", lambda e: e.activation(out=lams[:], in_=lams[:], func=AF.Exp), reads=[btmp], writes=[btmp])
        op("dve", lambda e: e.tensor_tensor(out=neglam[:], in0=lams[:, 1:2], in1=lams[:, 0:1], op=ALU.subtract),
           reads=[btmp], writes=[bC])
        op("dve", lambda e: e.tensor_scalar(out=neglam[:], in0=neglam[:], scalar1=-LAMBDA_INIT, scalar2=None, op0=ALU.add),
           reads=[bC], writes=[bC])
        SC.barrier()
        A.pop()

        A.push()
        NCV = 3
        CW = 4096
        cvf = [A.alloc([128, CW], F32) for _ in range(NCV)]
        cvb = [A.alloc([128, CW], BF16) for _ in range(NCV)]
        bcvf = [Buf(f"cvf{i}") for i in range(NCV)]
        bcvb = [Buf(f"cvb{i}") for i in range(NCV)]
        kcv = [0]

        def convert(src, dst, R, C):
            for r0 in range(0, R, 128):
                for c0 in range(0, C, CW):
                    cw = min(CW, C - c0)
                    k = kcv[0] % NCV
                    kk = kcv[0]
                    kcv[0] += 1
                    dma("sp", cvf[k][:, 0:cw], src[r0:r0 + 128, c0:c0 + cw], [], [bcvf[k]], f"cvf{k}")
                    if kk % 2 == 0:
                        op("dve", lambda e, k=k, cw=cw: e.tensor_copy(out=cvb[k][:, 0:cw], in_=cvf[k][:, 0:cw]),
                           reads=[bcvf[k]], writes=[bcvb[k]])
                    else:
                        op("act", lambda e, k=k, cw=cw: e.activation(out=cvb[k][:, 0:cw], in_=cvf[k][:, 0:cw], func=AF.Copy),
                           reads=[bcvf[k]], writes=[bcvb[k]])
                    dma("pool", dst[r0:r0 + 128, c0:c0 + cw], cvb[k][:, 0:cw], [bcvb[k]], [bDR], f"cvb{k}")

        convert(wda_d, wda_b, 1024, 3072)
        convert(wml_d, wml_b, 512, 6144)
        convert(wmg_d, wmg_b, 1024, 4096)
        convert(wout_d, wout_b, 128, 8192)
        import os as _os0
        if do_peer and not int(_os0.environ.get("NOCONV", 0)):
            convert(ut_d, ut_b, 4096, 4096)
            convert(v_d, v_b, NEXP, D)
        SC.barrier()
        A.pop()

        A.push()
        siluT = A.alloc([128, 8, NSEQ], F32)
        modall = A.alloc([NSEQ, 6144], F32)
        badab = A.alloc([NSEQ, 6144], F32)
        wad = [A.alloc([128, 8, 512], F32) for _ in range(2)]
        bwad = [Buf("wad0"), Buf("wad1")]
        bsil = Buf("silu")
        bmod = Buf("modall")
        bmodT = Buf("modT")
        dma("sp", siluT[:].rearrange("p a b -> p (a b)"), cT_d[:, :], [], [bsil], "c1")
        dma("sp", badab[:], bada_d[0:1, :].partition_broadcast(NSEQ), [], [bmod], "c0")
        op("act", lambda e: e.activation(out=siluT[:].rearrange("p a b -> p (a b)"),
                                         in_=siluT[:].rearrange("p a b -> p (a b)"), func=AF.Silu),
           reads=[bsil], writes=[bsil])
        wada3 = wada_d.rearrange("p (a b) -> p a b", a=8)
        for pc in range(12):
            k = pc % 2
            dma("sp", wad[k][:], wada3[:, :, pc * 512:(pc + 1) * 512], [], [bwad[k]], f"wad{k}")
            bk = nextbank([0, 1])

            def mmg(e, k=k, bk=bk):
                r = None
                for kc in range(8):
                    r = e.matmul(PB[bk][0:NSEQ, :], lhsT=siluT[:, kc, :], rhs=wad[k][:, kc, :],
                                 start=(kc == 0), stop=(kc == 7))
                return r
            op("pe", mmg, reads=[bsil, bwad[k]], writes=[bPB[bk]])
            op("dve", lambda e, bk=bk, pc=pc: e.tensor_tensor(out=modall[:, pc * 512:(pc + 1) * 512], in0=PB[bk][0:NSEQ, :],
                                                              in1=badab[:, pc * 512:(pc + 1) * 512], op=ALU.add),
               reads=[bPB[bk], bmod], writes=[bmod])
        dma("sp", gt_d[:, 0:1024], modall[:, 2048:3072], [bmod], [bGT], "gtd")
        dma("sp", gt_d[:, 1024:2048], modall[:, 5120:6144], [bmod], [bGT], "gtd")
        bk = nextbank([0, 1])

        def trg(e, bk=bk):
            r = None
            for c in range(48):
                r = e.transpose(out=PB[bk][:, c * NSEQ:(c + 1) * NSEQ], in_=modall[0:NSEQ, c * 128:(c + 1) * 128],
                                identity=identf[0:NSEQ, 0:NSEQ])
            return r
        op("pe", trg, reads=[bmod, bC], writes=[bPB[bk]])
        op("dve", lambda e, bk=bk: e.tensor_copy(out=modT[:], in_=PB[bk][:, 0:48 * NSEQ]), reads=[bPB[bk]], writes=[bmodT])
        for c0 in (8, 32):
            op("dve", lambda e, c0=c0: e.tensor_scalar(out=modT[:, c0 * NSEQ:(c0 + 8) * NSEQ], in0=modT[:, c0 * NSEQ:(c0 + 8) * NSEQ],
                                                       scalar1=1.0, scalar2=None, op0=ALU.add),
               reads=[bmodT], writes=[bmodT])
        SC.barrier()
        A.pop()

        dump("modT", modT[:], [bmodT], [128, 48 * NSEQ], F32)

        def modcol(which, c, b):
            j = (which * 8 + c) * NSEQ + b
            return modT[:, j:j + 1]

        A.push()
        hT_raw = A.alloc([128, 8192], F32)
        hT = hT_raw.bitcast(BF16).rearrange("p (a b) -> p a b", a=8)
        yaT = A.alloc([128, 8, SEQ], BF16)
        ymT = A.alloc([128, 8, SEQ], BF16)
        bxts = [Buf("xt0"), Buf("xt1")]
        bhT = [Buf(f"hT{i}") for i in range(NT)]
        byaT = [Buf(f"yaT{i}") for i in range(8)]
        bymT = [Buf(f"ymT{i}") for i in range(NT)]

        def hbufs(t0, t1):
            return bhT[t0:t1]

        for b in range(NSEQ):
            A.push()
            xts = [A.alloc([128, D], F32) for _ in range(2)]
            xns = [A.alloc([128, D], BF16) for _ in range(2)]
            bxns = [Buf("xn0"), Buf("xn1")]
            for i in range(NT):
                k = i % 2
                r0 = b * SEQ + i * 128
                dma("sp", xts[k][:], x_d[r0:r0 + 128, :], [], [bxts[k]], f"xt{k}")
                mv, rs, bm, br = ln_stats(xts[k], bxts[k])
                op("dve", lambda e, k=k, mv=mv, rs=rs: e.tensor_scalar(out=xns[k][:], in0=xts[k][:], scalar1=mv[:, 0:1], scalar2=rs[:],
                                                                      op0=ALU.subtract, op1=ALU.mult),
                   reads=[bxts[k], bm, br], writes=[bxns[k]])
                if b == 0 and i == 0:
                    dump("mv0", mv[:], [bm], [128, 2], F32)
                    dump("rs0", rs[:], [br], [128, 1], F32)
                    dump("xn0", xns[k][:], [bxns[k]], [128, D], BF16)
                    dump("xt0", xts[k][:], [bxts[k]], [128, D], F32)
                bk = nextbank([0, 1])
                ptb = PB[bk].bitcast(BF16)

                def trx(e, k=k, ptb=ptb):
                    r = None
                    for c in range(8):
                        r = e.transpose(out=ptb[:, c * 128:(c + 1) * 128], in_=xns[k][:, c * 128:(c + 1) * 128], identity=identb[:])
                    return r
                op("pe", trx, reads=[bxns[k], bC], writes=[bPB[bk]])
                for c in range(8):
                    if c % 2 == 0:
                        op("act", lambda e, c=c, ptb=ptb, i=i: e.activation(out=hT[:, c, i * 128:(i + 1) * 128], in_=ptb[:, c * 128:(c + 1) * 128],
                                                                           func=AF.Identity, scale=modcol(1, c, b), bias=modcol(0, c, b)),
                           reads=[bPB[bk], bmodT], writes=[bhT[i]])
                    else:
                        op("dve", lambda e, c=c, ptb=ptb, i=i: e.tensor_scalar(out=hT[:, c, i * 128:(i + 1) * 128], in0=ptb[:, c * 128:(c + 1) * 128],
                                                                              scalar1=modcol(1, c, b), scalar2=modcol(0, c, b),
                                                                              op0=ALU.mult, op1=ALU.add),
                           reads=[bPB[bk], bmodT], writes=[bhT[i]])
            SC.barrier()
            A.pop()
            if b == 0:
                dump("hT", hT.rearrange("p a b -> p (a b)"), bhT, [128, 8 * SEQ], BF16)

            A.push()
            wda = [A.alloc([128, 8, 384], BF16) for _ in range(2)]
            bwda = [Buf("wda0"), Buf("wda1")]
            qTs = [A.alloc([128, SEQ], BF16) for _ in range(2)]
            kTs = [A.alloc([128, SEQ], BF16) for _ in range(2)]
            vss = [A.alloc([128, NT, 129], BF16) for _ in range(2)]
            bq = [[Buf(f"q{s}{c}") for c in range(4)] for s in range(2)]
            bkk = [[Buf(f"k{s}{c}") for c in range(4)] for s in range(2)]
            bv = [[Buf(f"v{s}{c}") for c in range(4)] for s in range(2)]
            Es = [A.alloc([128, NT, 256], BF16) for _ in range(2)]
            bE = [[Buf(f"E{s}{j}") for j in range(NT)] for s in range(2)]
            osb = [A.alloc([128, 2, 128], F32) for _ in range(2)]
            yat = [A.alloc([128, 2, 128], BF16) for _ in range(2)]
            rzs = [A.alloc([128, 4], F32) for _ in range(2)]
            sss = [A.alloc([128, 2], F32) for _ in range(2)]
            bo = [Buf("o0"), Buf("o1")]
            byat = [Buf("yat0"), Buf("yat1")]
            brz = [Buf("rz0"), Buf("rz1")]
            bss = [Buf("ss0"), Buf("ss1")]
            for s in range(2):
                op("pool", lambda e, s=s: e.memset(vss[s][:, :, 128:129], 1.0), writes=[bv[s][c] for c in range(4)])
            ecnt = 0
            ccnt = 0
            for h in range(8):
                s = h % 2
                dma("sp", wda[s][:].rearrange("p a b -> p (a b)"), wda_b[h * 128:(h + 1) * 128, :], [bDR], [bwda[s]], f"wda{s}")
                for c in range(4):
                    for which in range(2):
                        bk = nextbank([0, 1])

                        def mmg(e, s=s, c=c, which=which, bk=bk):
                            r = None
                            for kc in range(8):
                                r = e.matmul(PB[bk][:, :], lhsT=wda[s][:, kc, which * 128:(which + 1) * 128],
                                             rhs=hT[:, kc, c * 512:(c + 1) * 512], start=(kc == 0), stop=(kc == 7))
                            return r
                        op("pe", mmg, reads=[bwda[s]] + hbufs(4 * c, 4 * c + 4), writes=[bPB[bk]])
                        if which == 0:
                            op("act", lambda e, s=s, c=c, bk=bk: e.activation(out=qTs[s][:, c * 512:(c + 1) * 512], in_=PB[bk][:, :],
                                                                             func=AF.Copy, scale=0.125),
                               reads=[bPB[bk]], writes=[bq[s][c]])
                        else:
                            op("dve", lambda e, s=s, c=c, bk=bk: e.tensor_copy(out=kTs[s][:, c * 512:(c + 1) * 512], in_=PB[bk][:, :]),
                               reads=[bPB[bk]], writes=[bkk[s][c]])
                    bk = nextbank([0, 1])

                    def mmv(e, s=s, c=c, bk=bk):
                        r = None
                        for t in range(4):
                            i = 4 * c + t
                            for kc in range(8):
                                r = e.matmul(PB[bk][:, t * 128:(t + 1) * 128], lhsT=hT[:, kc, i * 128:(i + 1) * 128],
                                             rhs=wda[s][:, kc, 256:384], start=(kc == 0), stop=(kc == 7))
                        return r
                    op("pe", mmv, reads=[bwda[s]] + hbufs(4 * c, 4 * c + 4), writes=[bPB[bk]])
                    op("act", lambda e, s=s, c=c, bk=bk: e.activation(out=vss[s][:, 4 * c:4 * c + 4, 0:128],
                                                                     in_=PB[bk][:, :].rearrange("p (a b) -> p a b", a=4), func=AF.Copy),
                       reads=[bPB[bk]], writes=[bv[s][c]])
                for c in range(8):
                    cs = ccnt % 2
                    ccnt += 1
                    pob = [4 + cs * 2, 5 + cs * 2]
                    for m in range(2):
                        esl = ecnt % 2
                        ecnt += 1
                        E = Es[esl]
                        for j in range(2 * c + 2):
                            q0 = 128 if j == 2 * c + 1 else 0
                            sb_ = nextbank([2, 3])
                            op("pe", lambda e, s=s, m=m, j=j, c=c, q0=q0, sb_=sb_: e.matmul(
                                PB[sb_][:, q0:256], lhsT=kTs[s][m * 64:(m + 1) * 64, j * 128:(j + 1) * 128],
                                rhs=qTs[s][m * 64:(m + 1) * 64, c * 256 + q0:(c + 1) * 256], start=True, stop=True),
                               reads=[bkk[s][j // 4], bq[s][c // 2]], writes=[bPB[sb_]])
                            op("act", lambda e, E=E, j=j, q0=q0, sb_=sb_: e.activation(out=E[:, j, q0:256], in_=PB[sb_][:, q0:256], func=AF.Exp),
                               reads=[bPB[sb_]], writes=[bE[esl][j]])
                            if j >= 2 * c:
                                d0 = (j - 2 * c) * 128
                                op("pool", lambda e, E=E, j=j, d0=d0: e.tensor_tensor(out=E[:, j, d0:d0 + 128], in0=E[:, j, d0:d0 + 128],
                                                                                     in1=maskb[:], op=ALU.mult),
                                   reads=[bE[esl][j], bC], writes=[bE[esl][j]])
                        pv = PB[pob[m]][:, 0:258].rearrange("p (a b) -> p a b", a=2)

                        def pvg(e, s=s, c=c, E=E, pv=pv):
                            r = None
                            for ii in range(2):
                                i = 2 * c + ii
                                for j in range(i + 1):
                                    r = e.matmul(pv[:, ii, :], lhsT=E[:, j, ii * 128:(ii + 1) * 128], rhs=vss[s][:, j, :],
                                                 start=(j == 0), stop=(j == i))
                            return r
                        op("pe", pvg, reads=[bE[esl][j] for j in range(2 * c + 2)] + [bv[s][j] for j in range(c // 2 + 1)],
                           writes=[bPB[pob[m]]])
                    k = cs
                    p0 = PB[pob[0]][:, 0:258].rearrange("p (a b) -> p a b", a=2)
                    p1 = PB[pob[1]][:, 0:258].rearrange("p (a b) -> p a b", a=2)
                    op("dve", lambda e, k=k, p0=p0: e.reciprocal(out=rzs[k][:, 0:2], in_=p0[:, :, 128]), reads=[bPB[pob[0]]], writes=[brz[k]])
                    op("dve", lambda e, k=k, p1=p1: e.reciprocal(out=rzs[k][:, 2:4], in_=p1[:, :, 128]), reads=[bPB[pob[1]]], writes=[brz[k]])
                    op("dve", lambda e, k=k: e.tensor_scalar(out=rzs[k][:, 2:4], in0=rzs[k][:, 2:4], scalar1=neglam[:, 0:1], scalar2=None, op0=ALU.mult),
                       reads=[brz[k], bC], writes=[brz[k]])
                    for ii in range(2):
                        op("dve", lambda e, k=k, ii=ii, p0=p0: e.tensor_scalar(out=osb[k][:, ii, :], in0=p0[:, ii, 0:128], scalar1=rzs[k][:, ii:ii + 1],
                                                                              scalar2=None, op0=ALU.mult),
                           reads=[bPB[pob[0]], brz[k]], writes=[bo[k]])
                        op("dve", lambda e, k=k, ii=ii, p1=p1: e.scalar_tensor_tensor(out=osb[k][:, ii, :], in0=p1[:, ii, 0:128],
                                                                                     scalar=rzs[k][:, 2 + ii:3 + ii], in1=osb[k][:, ii, :],
                                                                                     op0=ALU.mult, op1=ALU.add),
                           reads=[bPB[pob[1]], brz[k], bo[k]], writes=[bo[k]])
                        op("act", lambda e, k=k, ii=ii: e.activation(out=junk[:, 0:128], in_=osb[k][:, ii, :], func=AF.Square,
                                                                    accum_out=sss[k][:, ii:ii + 1]),
                           reads=[bo[k]], writes=[bss[k], bjunk])
                    op("dve", lambda e, k=k: e.tensor_scalar(out=sss[k][:], in0=sss[k][:], scalar1=1.0 / 128.0, scalar2=LN_EPS,
                                                             op0=ALU.mult, op1=ALU.add),
                       reads=[bss[k]], writes=[bss[k]])
                    op("pool", lambda e, k=k: e.tensor_tensor(out=sss[k][:], in0=sss[k][:], in1=cm05[:, 0:2], op=ALU.pow),
                       reads=[bss[k], bC], writes=[bss[k]])
                    for ii in range(2):
                        op("dve", lambda e, k=k, ii=ii: e.scalar_tensor_tensor(out=yat[k][:, ii, :], in0=osb[k][:, ii, :], scalar=sss[k][:, ii:ii + 1],
                                                                              in1=gda_bc[:], op0=ALU.mult, op1=ALU.mult),
                           reads=[bo[k], bss[k], bC], writes=[byat[k]])
                    bk = nextbank([0, 1])
                    ptb = PB[bk].bitcast(BF16)

                    def trya(e, k=k, ptb=ptb):
                        r = None
                        for ii in range(2):
                            r = e.transpose(out=ptb[:, ii * 128:(ii + 1) * 128], in_=yat[k][:, ii, :], identity=identb[:])
                        return r
                    op("pe", trya, reads=[byat[k], bC], writes=[bPB[bk]])
                    op("act", lambda e, h=h, c=c, ptb=ptb: e.activation(out=yaT[:, h, c * 256:(c + 1) * 256], in_=ptb[:, 0:256], func=AF.Copy),
                       reads=[bPB[bk]], writes=[byaT[c]])
            SC.barrier()
            A.pop()
            if b == 0:
                dump("yaT", yaT.rearrange("p a b -> p (a b)"), byaT, [128, 8 * SEQ], BF16)

            A.push()
            LNS = math.log(128.0 ** -0.5)
            gch = [A.alloc([4, 512], F32) for _ in range(2)]
            negM = A.alloc([4, SEQ], F32)
            ech = [A.alloc([4, 512], F32) for _ in range(2)]
            nfc = [A.alloc([4, 512], F32) for _ in range(2)]
            mch = [A.alloc([4, 512], F32) for _ in range(2)]
            tfc = A.alloc([4, 512], F32)
            gtok = A.alloc([128, NT * 4], F32)
            etok = A.alloc([128, NT * 4], F32)
            bgc = [Buf("gch0"), Buf("gch1")]
            bnegM = Buf("negM")
            bec = [Buf("ech0"), Buf("ech1")]
            bnf = [Buf("nf0"), Buf("nf1")]
            bmc = [Buf("mc0"), Buf("mc1")]
            btf = Buf("tf")
            bgtok = Buf("gtok")
            betok = Buf("etok")
            for c in range(4):
                k = c % 2
                sl = slice(c * 512, (c + 1) * 512)
                bki = nextbank([0, 1])
                bkf = nextbank([0, 1])
                for (bk_, c0) in ((bki, 0), (bkf, 4)):
                    def mmif(e, bk_=bk_, c0=c0, c=c):
                        r = None
                        for kc in range(8):
                            r = e.matmul(PB[bk_][0:4, :], lhsT=wif[:, kc, c0:c0 + 4], rhs=hT[:, kc, c * 512:(c + 1) * 512],
                                         start=(kc == 0), stop=(kc == 7))
                        return r
                    op("pe", mmif, reads=[bC] + hbufs(4 * c, 4 * c + 4), writes=[bPB[bk_]])
                op("act", lambda e, bkf=bkf: e.activation(out=tfc[:], in_=PB[bkf][0:4, :], func=AF.Exp, scale=-1.0, bias=nbf[:, 0:1]),
                   reads=[bPB[bkf], bC], writes=[btf])
                op("act", lambda e: e.activation(out=tfc[:], in_=tfc[:], func=AF.Ln, bias=1.0), reads=[btf], writes=[btf])
                init_nf = 0.0 if c == 0 else nfc[1 - k][:, 511:512]
                op("dve", lambda e, k=k, init_nf=init_nf: e.tensor_tensor_scan(out=nfc[k][:], data0=ones4[:], data1=tfc[:], initial=init_nf,
                                                                              op0=ALU.mult, op1=ALU.add),
                   reads=[btf, bC, bnf[1 - k]], writes=[bnf[k]])
                op("dve", lambda e, k=k, bki=bki, sl=sl: e.scalar_tensor_tensor(out=gch[k][:], in0=PB[bki][0:4, :], scalar=bifT[:, 0:1], in1=nfc[k][:],
                                                                               op0=ALU.add, op1=ALU.add),
                   reads=[bPB[bki], bC, bnf[k]], writes=[bgc[k]])
                init_m = 0.0 if c == 0 else mch[1 - k][:, 511:512]
                op("dve", lambda e, k=k, sl=sl, init_m=init_m: e.tensor_tensor_scan(out=mch[k][:], data0=gch[k][:], data1=gch[k][:], initial=init_m,
                                                                                   op0=ALU.max, op1=ALU.max),
                   reads=[bgc[k], bmc[1 - k]], writes=[bmc[k]])
                op("dve", lambda e, k=k, sl=sl: e.tensor_scalar(out=negM[:, sl], in0=mch[k][:], scalar1=-1.0, scalar2=None, op0=ALU.mult),
                   reads=[bmc[k]], writes=[bnegM])
                op("dve", lambda e, k=k: e.tensor_tensor(out=ech[k][:], in0=nfc[k][:], in1=mch[k][:], op=ALU.subtract),
                   reads=[bnf[k], bmc[k]], writes=[bec[k]])
                op("act", lambda e, k=k: e.activation(out=ech[k][:], in_=ech[k][:], func=AF.Exp), reads=[bec[k]], writes=[bec[k]])

                def trgt(e, k=k, c=c):
                    r = None
                    for t in range(4):
                        i = 4 * c + t
                        r = e.transpose(out=PB[7][:, i * 4:(i + 1) * 4], in_=gch[k][0:4, t * 128:(t + 1) * 128], identity=identf[0:4, 0:4])
                        r = e.transpose(out=PB[7][:, 64 + i * 4:64 + (i + 1) * 4], in_=ech[k][0:4, t * 128:(t + 1) * 128], identity=identf[0:4, 0:4])
                    return r
                op("pe", trgt, reads=[bgc[k], bec[k], bC], writes=[bPB[7]])
            bk = 7
            op("dve", lambda e, bk=bk: e.tensor_scalar(out=gtok[:], in0=PB[bk][:, 0:64], scalar1=LNS, scalar2=None, op0=ALU.add),
               reads=[bPB[bk]], writes=[bgtok])
            op("dve", lambda e, bk=bk: e.tensor_copy(out=etok[:], in_=PB[bk][:, 64:128]), reads=[bPB[bk]], writes=[betok])

            wml = A.alloc([128, 8, 768], BF16)
            bwml = Buf("wml")
            negMbc = [A.alloc([128, 256], F32) for _ in range(2)]
            bnb = [Buf("nb0"), Buf("nb1")]
            zp = A.alloc([128, SEQ + 3], F32)
            bzp = [Buf(f"zp{c}") for c in range(4)]
            cv = A.alloc([128, SEQ], F32)
            bcv = Buf("cv")
            qTm = A.alloc([128, SEQ], BF16)
            kTm = A.alloc([128, SEQ], BF16)
            bqm = Buf("qTm")
            bkm = Buf("kTm")
            vm = A.alloc([128, NT, 257], BF16)
            bvm = [Buf(f"vm{i}") for i in range(8)]
            Ps = [A.alloc([128, NT, 256], BF16) for _ in range(2)]
            bP = [[Buf(f"P{s}{j}") for j in range(NT)] for s in range(2)]
            Wt = [A.alloc([128, 256], F32) for _ in range(2)]
            bWt = [Buf("Wt0"), Buf("Wt1")]
            gml_bc = A.alloc([128, 256], F32)
            bgml = Buf("gml")
            hn = [A.alloc([128, 256], F32) for _ in range(2)]
            bhn = [Buf("hn0"), Buf("hn1")]
            og = [A.alloc([128, 256], F32) for _ in range(2)]
            bog = [Buf("og0"), Buf("og1")]
            ymt = [A.alloc([128, 256], BF16) for _ in range(2)]
            bymt = [Buf("ymt0"), Buf("ymt1")]
            dens = [A.alloc([128, 2], F32) for _ in range(2)]
            bden = [Buf("den0"), Buf("den1")]
            op("pool", lambda e: e.memset(zp[:, 0:3], 0.0), writes=bzp)
            op("pool", lambda e: e.memset(vm[:, :, 256:257], 1.0), writes=bvm)
            pcnt = 0
            tcnt = 0
            wcnt = 0
            for h in range(4):
                dma("sp", wml[:].rearrange("p a b -> p (a b)"), wml_b[h * 128:(h + 1) * 128, :], [bDR], [bwml], "wml")
                dma("sp", gml_bc[:], mlg_d[0:1, h * 256:(h + 1) * 256].partition_broadcast(128), [], [bgml], "gml")
                for which in range(2):
                    ch = which * 4 + h
                    for c in range(4):
                        bk = nextbank([0, 1])

                        def mmq(e, c=c, which=which, bk=bk):
                            r = None
                            for kc in range(8):
                                r = e.matmul(PB[bk][:, :], lhsT=wml[:, kc, which * 128:(which + 1) * 128],
                                             rhs=hT[:, kc, c * 512:(c + 1) * 512], start=(kc == 0), stop=(kc == 7))
                            return r
                        op("pe", mmq, reads=[bwml] + hbufs(4 * c, 4 * c + 4), writes=[bPB[bk]])
                        op("act", lambda e, c=c, bk=bk: e.activation(out=zp[:, 3 + c * 512:3 + (c + 1) * 512], in_=PB[bk][:, :], func=AF.Copy),
                           reads=[bPB[bk]], writes=[bzp[c]])
                    op("dve", lambda e, ch=ch: e.tensor_scalar(out=cv[:], in0=zp[:, 3:SEQ + 3], scalar1=cwT[:, ch * 4 + 3:ch * 4 + 4],
                                                               scalar2=cbT[:, ch:ch + 1], op0=ALU.mult, op1=ALU.add),
                       reads=bzp + [bC], writes=[bcv])
                    for j in range(3):
                        op("dve", lambda e, ch=ch, j=j: e.scalar_tensor_tensor(out=cv[:], in0=zp[:, j:j + SEQ], scalar=cwT[:, ch * 4 + j:ch * 4 + j + 1],
                                                                              in1=cv[:], op0=ALU.mult, op1=ALU.add),
                           reads=bzp + [bC, bcv], writes=[bcv])
                    dst, bdst = (qTm, bqm) if which == 0 else (kTm, bkm)
                    op("act", lambda e, dst=dst: e.activation(out=dst[:], in_=cv[:], func=AF.Silu), reads=[bcv], writes=[bdst])
                for g2 in range(8):
                    bk = nextbank([0, 1])

                    def mmv2(e, g2=g2, bk=bk):
                        r = None
                        for t in range(2):
                            i = 2 * g2 + t
                            for kc in range(8):
                                r = e.matmul(PB[bk][:, t * 256:(t + 1) * 256], lhsT=hT[:, kc, i * 128:(i + 1) * 128],
                                             rhs=wml[:, kc, 256:512], start=(kc == 0), stop=(kc == 7))
                        return r
                    op("pe", mmv2, reads=[bwml] + hbufs(2 * g2, 2 * g2 + 2), writes=[bPB[bk]])
                    op("dve", lambda e, g2=g2, bk=bk: e.tensor_copy(out=vm[:, 2 * g2:2 * g2 + 2, 0:256],
                                                                   in_=PB[bk][:, :].rearrange("p (a b) -> p a b", a=2)),
                       reads=[bPB[bk]], writes=[bvm[g2]])
                for c in range(8):
                    psl = pcnt % 2
                    pcnt += 1
                    P = Ps[psl]
                    bk = nextbank([0, 1])
                    op("pe", lambda e, h=h, c=c, bk=bk: e.matmul(PB[bk][:, 0:256], lhsT=sel4[0:4, h * 128:(h + 1) * 128],
                                                                rhs=negM[0:4, c * 256:(c + 1) * 256], start=True, stop=True),
                       reads=[bC, bnegM], writes=[bPB[bk]])
                    op("act", lambda e, psl=psl, bk=bk: e.activation(out=negMbc[psl][:, :], in_=PB[bk][:, 0:256], func=AF.Copy),
                       reads=[bPB[bk]], writes=[bnb[psl]])
                    for j in range(2 * c + 2):
                        q0 = 128 if j == 2 * c + 1 else 0
                        sb_ = nextbank([2, 3])
                        ws = wcnt % 2
                        wcnt += 1
                        op("pe", lambda e, j=j, c=c, q0=q0, sb_=sb_: e.matmul(
                            PB[sb_][:, q0:256], lhsT=kTm[:, j * 128:(j + 1) * 128], rhs=qTm[:, c * 256 + q0:(c + 1) * 256], start=True, stop=True),
                           reads=[bkm, bqm], writes=[bPB[sb_]])
                        op("act", lambda e, j=j, c=c, q0=q0, ws=ws, h=h, psl=psl: e.activation(out=Wt[ws][:, q0:256], in_=negMbc[psl][:, q0:256],
                                                                                     func=AF.Exp, bias=gtok[:, j * 4 + h:j * 4 + h + 1]),
                           reads=[bnb[psl], bgtok], writes=[bWt[ws]])
                        op("dve", lambda e, P=P, j=j, q0=q0, ws=ws, sb_=sb_: e.tensor_tensor(out=P[:, j, q0:256], in0=PB[sb_][:, q0:256],
                                                                                            in1=Wt[ws][:, q0:256], op=ALU.mult),
                           reads=[bPB[sb_], bWt[ws]], writes=[bP[psl][j]])
                        if j >= 2 * c:
                            d0 = (j - 2 * c) * 128
                            op("pool", lambda e, P=P, j=j, d0=d0: e.tensor_tensor(out=P[:, j, d0:d0 + 128], in0=P[:, j, d0:d0 + 128],
                                                                                 in1=maskb[:], op=ALU.mult),
                               reads=[bP[psl][j], bC], writes=[bP[psl][j]])
                    for ii in range(2):
                        i = 2 * c + ii
                        k = tcnt % 2
                        tcnt += 1
                        pb_ = nextbank([4, 5, 6, 7])

                        def pvm(e, P=P, ii=ii, i=i, pb_=pb_):
                            r = None
                            for j in range(i + 1):
                                r = e.matmul(PB[pb_][:, 0:257], lhsT=P[:, j, ii * 128:(ii + 1) * 128], rhs=vm[:, j, :],
                                             start=(j == 0), stop=(j == i))
                            return r
                        op("pe", pvm, reads=[bP[psl][j] for j in range(i + 1)] + [bvm[j] for j in range(i // 2 + 1)], writes=[bPB[pb_]])
                        bk = nextbank([0, 1])

                        def mmo(e, i=i, bk=bk):
                            r = None
                            for kc in range(8):
                                r = e.matmul(PB[bk][:, 0:256], lhsT=hT[:, kc, i * 128:(i + 1) * 128], rhs=wml[:, kc, 512:768],
                                             start=(kc == 0), stop=(kc == 7))
                            return r
                        op("pe", mmo, reads=[bwml, bhT[i]], writes=[bPB[bk]])
                        op("act", lambda e, k=k, bk=bk: e.activation(out=og[k][:], in_=PB[bk][:, 0:256], func=AF.Sigmoid),
                           reads=[bPB[bk]], writes=[bog[k]])
                        op("dve", lambda e, k=k, pb_=pb_: e.tensor_copy(out=dens[k][:, 0:1], in_=PB[pb_][:, 256:257]),
                           reads=[bPB[pb_]], writes=[bden[k]])
                        op("dve", lambda e, k=k: e.scalar_tensor_tensor(out=dens[k][:, 0:1], in0=dens[k][:, 0:1], scalar=-1.0, in1=dens[k][:, 0:1],
                                                                        op0=ALU.mult, op1=ALU.max),
                           reads=[bden[k]], writes=[bden[k]])
                        op("dve", lambda e, k=k, i=i, h=h: e.tensor_tensor(out=dens[k][:, 0:1], in0=dens[k][:, 0:1], in1=etok[:, i * 4 + h:i * 4 + h + 1],
                                                                          op=ALU.max),
                           reads=[bden[k], betok], writes=[bden[k]])
                        op("dve", lambda e, k=k: e.reciprocal(out=dens[k][:, 0:1], in_=dens[k][:, 0:1]), reads=[bden[k]], writes=[bden[k]])
                        op("dve", lambda e, k=k, pb_=pb_: e.tensor_scalar(out=hn[k][:], in0=PB[pb_][:, 0:256], scalar1=dens[k][:, 0:1], scalar2=None,
                                                                         op0=ALU.mult),
                           reads=[bPB[pb_], bden[k]], writes=[bhn[k]])
                        op("act", lambda e, k=k: e.activation(out=junk[:, 0:256], in_=hn[k][:], func=AF.Square, accum_out=dens[k][:, 1:2]),
                           reads=[bhn[k]], writes=[bden[k], bjunk])
                        op("dve", lambda e, k=k: e.tensor_scalar(out=dens[k][:, 1:2], in0=dens[k][:, 1:2], scalar1=1.0 / 256.0, scalar2=LN_EPS,
                                                                 op0=ALU.mult, op1=ALU.add),
                           reads=[bden[k]], writes=[bden[k]])
                        op("pool", lambda e, k=k: e.tensor_tensor(out=dens[k][:, 1:2], in0=dens[k][:, 1:2], in1=cm05[:, 0:1], op=ALU.pow),
                           reads=[bden[k], bC], writes=[bden[k]])
                        op("dve", lambda e, k=k, h=h: e.scalar_tensor_tensor(out=hn[k][:], in0=hn[k][:], scalar=dens[k][:, 1:2],
                                                                            in1=gml_bc[:], op0=ALU.mult, op1=ALU.mult),
                           reads=[bhn[k], bden[k], bgml], writes=[bhn[k]])
                        op("pool", lambda e, k=k: e.tensor_tensor(out=ymt[k][:], in0=hn[k][:], in1=og[k][:], op=ALU.mult),
                           reads=[bhn[k], bog[k]], writes=[bymt[k]])
                        bk = nextbank([0, 1])
                        ptb = PB[bk].bitcast(BF16)

                        def trym(e, k=k, ptb=ptb):
                            r = None
                            for ee in range(2):
                                r = e.transpose(out=ptb[:, ee * 128:(ee + 1) * 128], in_=ymt[k][:, ee * 128:(ee + 1) * 128], identity=identb[:])
                            return r
                        op("pe", trym, reads=[bymt[k], bC], writes=[bPB[bk]])
                        op("act", lambda e, h=h, i=i, ptb=ptb: e.activation(out=ymT[:, 2 * h:2 * h + 2, i * 128:(i + 1) * 128],
                                                                           in_=ptb[:, 0:256].rearrange("p (a b) -> p a b", a=2), func=AF.Copy),
                           reads=[bPB[bk]], writes=[bymT[i]])
            SC.barrier()
            A.pop()
            if b == 0:
                dump("ymT", ymT.rearrange("p a b -> p (a b)"), bymT, [128, 8 * SEQ], BF16)

            A.push()
            yT = A.alloc([128, 8, SEQ], BF16)
            byT = [Buf(f"yT{c}") for c in range(4)]
            wmg = [A.alloc([128, 8, 512], BF16) for _ in range(2)]
            bwmg = [Buf("wmg0"), Buf("wmg1")]
            sgA = [A.alloc([128, 512], F32) for _ in range(2)]
            sgB = [A.alloc([128, 512], F32) for _ in range(2)]
            t1 = [A.alloc([128, 512], F32) for _ in range(2)]
            t2 = [A.alloc([128, 512], F32) for _ in range(2)]
            bsgA = [Buf("sgA0"), Buf("sgA1")]
            bsgB = [Buf("sgB0"), Buf("sgB1")]
            bt1 = [Buf("t10"), Buf("t11")]
            bt2 = [Buf("t20"), Buf("t21")]
            dcnt = 0
            allb = [0, 1, 2, 3, 4, 5, 6, 7]
            for fc in range(8):
                s = fc % 2
                dma("sp", wmg[s][:].rearrange("p a b -> p (a b)"), wmg_b[fc * 128:(fc + 1) * 128, :], [bDR], [bwmg[s]], f"wmg{s}")
                for c in range(4):
                    k = dcnt % 2
                    dcnt += 1
                    banks = []
                    for which, src, srcb in ((0, hT, hbufs(4 * c, 4 * c + 4)), (1, hT, hbufs(4 * c, 4 * c + 4)),
                                             (2, yaT, byaT[2 * c:2 * c + 2]), (3, ymT, bymT[4 * c:4 * c + 4])):
                        bk = nextbank(allb)
                        banks.append(bk)

                        def mmd(e, s=s, c=c, which=which, src=src, bk=bk):
                            r = None
                            for kc in range(8):
                                r = e.matmul(PB[bk][:, :], lhsT=wmg[s][:, kc, which * 128:(which + 1) * 128],
                                             rhs=src[:, kc, c * 512:(c + 1) * 512], start=(kc == 0), stop=(kc == 7))
                            return r
                        op("pe", mmd, reads=[bwmg[s]] + list(srcb), writes=[bPB[bk]])
                    op("act", lambda e, k=k, bk=banks[0]: e.activation(out=sgA[k][:], in_=PB[bk][:, :], func=AF.Sigmoid),
                       reads=[bPB[banks[0]]], writes=[bsgA[k]])
                    op("act", lambda e, k=k, bk=banks[1]: e.activation(out=sgB[k][:], in_=PB[bk][:, :], func=AF.Sigmoid),
                       reads=[bPB[banks[1]]], writes=[bsgB[k]])
                    op("dve", lambda e, k=k, bk=banks[2]: e.tensor_tensor(out=t1[k][:], in0=PB[bk][:, :], in1=sgA[k][:], op=ALU.mult),
                       reads=[bPB[banks[2]], bsgA[k]], writes=[bt1[k]])
                    op("dve", lambda e, k=k, bk=banks[3]: e.tensor_tensor(out=t2[k][:], in0=PB[bk][:, :], in1=sgB[k][:], op=ALU.mult),
                       reads=[bPB[banks[3]], bsgB[k]], writes=[bt2[k]])
                    op("pool", lambda e, k=k, fc=fc, c=c: e.tensor_tensor(out=yT[:, fc, c * 512:(c + 1) * 512], in0=t1[k][:], in1=t2[k][:], op=ALU.add),
                       reads=[bt1[k], bt2[k]], writes=[byT[c]])
            SC.barrier()
            if b == 0:
                dump("yT", yT.rearrange("p a b -> p (a b)"), byT, [128, 8 * SEQ], BF16)
            wo = hT_raw[:, 0:4096].bitcast(BF16).rearrange("p (a b) -> p a b", a=8)
            ttile = hT_raw[:, 4096:5120]
            l1g = hT_raw[:, 5120:6144]
            l1b = hT_raw[:, 6144:7168]
            gt1 = hT_raw[:, 7168:8192]
            xts = [A.alloc([128, D], F32) for _ in range(2)]
            bwo = Buf("wo")
            btt = Buf("tt")
            bl1 = Buf("l1")
            dma("sp", wo.rearrange("p a b -> p (a b)"), wout_b[:, :], [bDR], [bwo], "wo")
            dma("sp", l1g, ln1g_d[0:1, :].partition_broadcast(128), [], [bl1], "l1")
            dma("sp", l1b, ln1b_d[0:1, :].partition_broadcast(128), [], [bl1], "l1")
            dma("sp", gt1, gt_d[b:b + 1, 0:1024].partition_broadcast(128), [bGT], [bl1], "l1")
            for i in range(NT):
                k = i % 2
                r0 = b * SEQ + i * 128
                dma("sp", xts[k][:], x_d[r0:r0 + 128, :], [], [bxts[k]], f"xt{k}")
                for half in range(2):
                    bk = nextbank(allb)

                    def mmo2(e, i=i, half=half, bk=bk):
                        r = None
                        for kc in range(8):
                            r = e.matmul(PB[bk][:, :], lhsT=yT[:, kc, i * 128:(i + 1) * 128], rhs=wo[:, kc, half * 512:(half + 1) * 512],
                                         start=(kc == 0), stop=(kc == 7))
                        return r
                    op("pe", mmo2, reads=[byT[i // 4], bwo], writes=[bPB[bk]])
                    op("dve", lambda e, half=half, bk=bk: e.tensor_tensor(out=ttile[:, half * 512:(half + 1) * 512], in0=PB[bk][:, :],
                                                                         in1=gt1[:, half * 512:(half + 1) * 512], op=ALU.mult),
                       reads=[bPB[bk], bl1], writes=[btt])
                op("dve", lambda e, k=k: e.scalar_tensor_tensor(out=xts[k][:], in0=xts[k][:], scalar=ALPHA, in1=ttile, op0=ALU.mult, op1=ALU.add),
                   reads=[bxts[k], btt], writes=[bxts[k]])
                mv, rs, bm, br = ln_stats(xts[k], bxts[k])
                op("dve", lambda e, k=k, mv=mv, rs=rs: e.tensor_scalar(out=xts[k][:], in0=xts[k][:], scalar1=mv[:, 0:1], scalar2=rs[:],
                                                                      op0=ALU.subtract, op1=ALU.mult),
                   reads=[bxts[k], bm, br], writes=[bxts[k]])
                op("pool", lambda e, k=k: e.tensor_tensor(out=xts[k][:], in0=xts[k][:], in1=l1g, op=ALU.mult), reads=[bxts[k], bl1], writes=[bxts[k]])
                op("pool", lambda e, k=k: e.tensor_tensor(out=xts[k][:], in0=xts[k][:], in1=l1b, op=ALU.add), reads=[bxts[k], bl1], writes=[bxts[k]])
                dst = x1_d if do_peer else out_d
                dma("sp", dst[r0:r0 + 128, :], xts[k][:], [bxts[k]], [bXO[k]], f"xo{k}")
            SC.barrier()
            A.pop()
        A.pop()
        SC.barrier()

        if do_peer:
            A.push()
            RB = [0, 1, 2, 3, 4, 5]
            keysT = A.alloc([128, 256], F32)
            l2g = A.alloc([128, D], F32)
            l2b = A.alloc([128, D], F32)
            gt2 = A.alloc([128, D], F32)
            bl2 = Buf("l2")
            bgt2 = Buf("gt2")
            xps = [A.alloc([128, D], F32) for _ in range(2)]
            bxps = [Buf("xp0"), Buf("xp1")]
            xn2 = A.alloc([128, D], F32)
            bxn2 = Buf("xn2")
            h2T = A.alloc([128, 8, 128], F32)
            h2Tb = A.alloc([128, 8, 128], BF16)
            bh2 = Buf("h2T")
            bh2b = Buf("h2Tb")
            wq = [A.alloc([128, 8, 128], F32) for _ in range(2)]
            bwq = [Buf("wq0"), Buf("wq1")]
            qTh = [A.alloc([128, 128], F32) for _ in range(2)]
            bqTh = [Buf("qTh0"), Buf("qTh1")]
            s_sb = A.alloc([128, 8, 2, 128], F32)
            bs = Buf("s_sb")
            m16 = A.alloc([128, 8, 2, 16], F32)
            bm16 = Buf("m16")
            wk1 = A.alloc([128, 128], F32)
            bwk1 = Buf("wk1")
            cand = A.alloc([128, 8, 16, 16], F32)
            bcand = Buf("cand")
            wk2 = A.alloc([128, 256], F32)
            bwk2 = Buf("wk2")
            c16 = A.alloc([128, 8, 16], F32)
            bc16 = Buf("c16")
            negthr = A.alloc([128, 8], F32)
            zz = A.alloc([128, 8], F32)
            biasE = A.alloc([128, 8], F32)
            bsm = Buf("small")
            Sp = [A.alloc([128, 2048], F32) for _ in range(2)]
            Ep = [A.alloc([128, 2048], F32) for _ in range(2)]
            Mk = [A.alloc([128, 2048], BF16) for _ in range(2)]
            bSp = [Buf("Sp0"), Buf("Sp1")]
            bEp = [Buf("Ep0"), Buf("Ep1")]
            bMk = [Buf("Mk0"), Buf("Mk1")]
            acc = A.alloc([128, NEXP], BF16)
            bacc = [Buf(f"acc{i}") for i in range(8)]
            ub = [A.alloc([128, 8, 512], BF16) for _ in range(2)]
            bub = [Buf("ub0"), Buf("ub1")]
            vb = [A.alloc([128, 4, D], BF16) for _ in range(2)]
            bvb = [Buf("vb0"), Buf("vb1")]
            gel = [A.alloc([128, 512], F32) for _ in range(2)]
            bgel = [Buf("gel0"), Buf("gel1")]
            Wc = [A.alloc([128, 512], BF16) for _ in range(2)]
            bWc = [Buf("Wc0"), Buf("Wc1")]
            WT = [A.alloc([128, 4, 128], BF16) for _ in range(2)]
            bWT = [Buf("WT0"), Buf("WT1")]
            tt2 = A.alloc([128, D], F32)
            btt2 = Buf("tt2")
            bPO = [Buf("po0"), Buf("po1")]
            dma("sp", keysT[:], keysT_d[:, :], [], [bl2], "l2")
            dma("sp", l2g[:], ln2g_d[0:1, :].partition_broadcast(128), [], [bl2], "l2")
            dma("sp", l2b[:], ln2b_d[0:1, :].partition_broadcast(128), [], [bl2], "l2")
            wpq3 = wpq_d.rearrange("r (a b) -> r a b", a=8)
            v_b3 = v_b.rearrange("(g p) n -> p g n", p=128)
            gcnt = 0
            ecnt2 = 0
            import os as _os
            _PT = int(_os.environ.get("PEER_TILES", NSEQ * NT))
            _PS = int(_os.environ.get("PEER_STAGE", 9))
            for ti in range(min(_PT, NSEQ * NT)):
                b = ti // NT
                k = ti % 2
                r0 = ti * 128
                if ti % NT == 0:
                    dma("sp", gt2[:], gt_d[b:b + 1, 1024:2048].partition_broadcast(128), [bGT], [bgt2], "gt2")
                dma("sp", xps[k][:], x1_d[r0:r0 + 128, :], [bXO[0], bXO[1]], [bxps[k]], f"xp{k}")
                mv, rs, bm, br = ln_stats(xps[k], bxps[k])
                op("dve", lambda e, k=k, mv=mv, rs=rs: e.tensor_scalar(out=xn2[:], in0=xps[k][:], scalar1=mv[:, 0:1], scalar2=rs[:],
                                                                      op0=ALU.subtract, op1=ALU.mult),
                   reads=[bxps[k], bm, br], writes=[bxn2])
                for hf in range(2):
                    bk = nextbank(RB)

                    def trp(e, hf=hf, bk=bk):
                        for cc in range(4):
                            c = hf * 4 + cc
                            e.transpose(out=PB[bk][:, cc * 128:(cc + 1) * 128], in_=xn2[:, c * 128:(c + 1) * 128], identity=identf[:])
                    op("pe", trp, reads=[bxn2, bC], writes=[bPB[bk]])
                    for cc in range(4):
                        c = hf * 4 + cc
                        if cc % 2 == 0:
                            op("act", lambda e, c=c, cc=cc, bk=bk, b=b: e.activation(out=h2T[:, c, :], in_=PB[bk][:, cc * 128:(cc + 1) * 128],
                                                                                    func=AF.Identity, scale=modcol(4, c, b), bias=modcol(3, c, b)),
                               reads=[bPB[bk], bmodT], writes=[bh2])
                        else:
                            op("dve", lambda e, c=c, cc=cc, bk=bk, b=b: e.tensor_scalar(out=h2T[:, c, :], in0=PB[bk][:, cc * 128:(cc + 1) * 128],
                                                                                       scalar1=modcol(4, c, b), scalar2=modcol(3, c, b),
                                                                                       op0=ALU.mult, op1=ALU.add),
                               reads=[bPB[bk], bmodT], writes=[bh2])
                op("pool", lambda e: e.tensor_copy(out=h2Tb[:].rearrange("p a b -> p (a b)"), in_=h2T[:].rearrange("p a b -> p (a b)")),
                   reads=[bh2], writes=[bh2b])
                sbk = None
                for h in range(8 if _PS >= 2 else 0):
                    ws = h % 2
                    dma("sp", wq[ws][:], wpq3[h * 128:(h + 1) * 128, :, :], [], [bwq[ws]], f"wq{ws}")
                    bk = nextbank(RB)

                    def mmq(e, ws=ws, bk=bk):
                        for kc in range(8):
                            e.matmul(PB[bk][:, 0:128], lhsT=wq[ws][:, kc, :], rhs=h2T[:, kc, :], start=(kc == 0), stop=(kc == 7))
                    op("pe", mmq, reads=[bwq[ws], bh2], writes=[bPB[bk]])
                    op("act", lambda e, ws=ws, bk=bk: e.activation(out=qTh[ws][:], in_=PB[bk][:, 0:128], func=AF.Copy),
                       reads=[bPB[bk]], writes=[bqTh[ws]])
                    if h % 2 == 0:
                        sbk = nextbank(RB)

                    def mms(e, ws=ws, sbk=sbk, h=h):
                        o0 = (h % 2) * 256
                        e.matmul(PB[sbk][:, o0:o0 + 256], lhsT=qTh[ws][:, :], rhs=keysT[:, :], start=True, stop=True)
                    op("pe", mms, reads=[bqTh[ws], bl2], writes=[bPB[sbk]])
                    if h % 2 == 1:
                        op("dve", lambda e, h=h, sbk=sbk: e.tensor_copy(out=s_sb[:, h - 1:h + 1, :, :].rearrange("p a b c -> p (a b c)"), in_=PB[sbk][:, :]),
                           reads=[bPB[sbk]], writes=[bs])
                if _PS < 3:
                    dma("sp", out_d[r0:r0 + 128, :], xps[k][:], [bxps[k], bs, bh2b], [bPO[k]], f"po{k}")
                    continue
                for h in range(8):
                    for half in range(2):
                        op("dve", lambda e, h=h, half=half: e.max(out=m16[:, h, half, 0:8], in_=s_sb[:, h, half, :]), reads=[bs], writes=[bm16])
                        op("dve", lambda e, h=h, half=half: e.match_replace(out=wk1[:], in_to_replace=m16[:, h, half, 0:8], in_values=s_sb[:, h, half, :],
                                                                           imm_value=-1e30),
                           reads=[bs, bm16], writes=[bwk1])
                        op("dve", lambda e, h=h, half=half: e.max(out=m16[:, h, half, 8:16], in_=wk1[:]), reads=[bwk1], writes=[bm16])
                op("dve", lambda e: e.tensor_tensor(out=cand[:], in0=m16[:, :, 0, :].unsqueeze(3).to_broadcast([128, 8, 16, 16]),
                                                    in1=m16[:, :, 1, :].unsqueeze(2).to_broadcast([128, 8, 16, 16]), op=ALU.add),
                   reads=[bm16], writes=[bcand])
                for h in range(8):
                    ch2 = cand[:, h, :, :].rearrange("p a b -> p (a b)")
                    op("dve", lambda e, h=h, ch2=ch2: e.max(out=c16[:, h, 0:8], in_=ch2), reads=[bcand], writes=[bc16])
                    op("dve", lambda e, h=h, ch2=ch2: e.match_replace(out=wk2[:], in_to_replace=c16[:, h, 0:8], in_values=ch2, imm_value=-1e30),
                       reads=[bcand, bc16], writes=[bwk2])
                    op("dve", lambda e, h=h: e.max(out=c16[:, h, 8:16], in_=wk2[:]), reads=[bwk2], writes=[bc16])
                op("dve", lambda e: e.tensor_scalar(out=negthr[:], in0=c16[:, :, 15], scalar1=-1.0, scalar2=None, op0=ALU.mult),
                   reads=[bc16], writes=[bsm])
                for h in range(8):
                    op("act", lambda e, h=h: e.activation(out=junk[:, 0:16], in_=c16[:, h, :], func=AF.Exp, bias=negthr[:, h:h + 1],
                                                          accum_out=zz[:, h:h + 1]),
                       reads=[bc16, bsm], writes=[bsm, bjunk])
                op("act", lambda e: e.activation(out=zz[:], in_=zz[:], func=AF.Ln), reads=[bsm], writes=[bsm])
                op("dve", lambda e: e.tensor_tensor(out=biasE[:], in0=negthr[:], in1=zz[:], op=ALU.subtract), reads=[bsm], writes=[bsm])
                if _PS < 4:
                    dma("sp", out_d[r0:r0 + 128, :], xps[k][:], [bxps[k], bsm], [bPO[k]], f"po{k}")
                    continue
                for h in range(8):
                    for pc in range(8):
                        g = gcnt % 2
                        gcnt += 1
                        sl = slice(pc * 2048, (pc + 1) * 2048)
                        op("dve", lambda e, g=g, h=h, pc=pc: e.tensor_tensor(
                            out=Sp[g][:].rearrange("p (a b) -> p a b", a=16),
                            in0=s_sb[:, h, 0, pc * 16:(pc + 1) * 16].unsqueeze(2).to_broadcast([128, 16, 128]),
                            in1=s_sb[:, h, 1, :].unsqueeze(1).to_broadcast([128, 16, 128]), op=ALU.add),
                           reads=[bs], writes=[bSp[g]])
                        op("act", lambda e, g=g, h=h: e.activation(out=Ep[g][:], in_=Sp[g][:], func=AF.Exp, bias=biasE[:, h:h + 1]),
                           reads=[bSp[g], bsm], writes=[bEp[g]])
                        if h == 0:
                            op("dve", lambda e, g=g, h=h, sl=sl: e.scalar_tensor_tensor(out=acc[:, sl], in0=Sp[g][:], scalar=c16[:, h, 15:16], in1=Ep[g][:],
                                                                                       op0=ALU.is_ge, op1=ALU.mult),
                               reads=[bSp[g], bEp[g], bc16], writes=[bacc[pc]])
                        else:
                            op("dve", lambda e, g=g, h=h: e.scalar_tensor_tensor(out=Mk[g][:], in0=Sp[g][:], scalar=c16[:, h, 15:16], in1=Ep[g][:],
                                                                                op0=ALU.is_ge, op1=ALU.mult),
                               reads=[bSp[g], bEp[g], bc16], writes=[bMk[g]])
                            op("pool", lambda e, g=g, sl=sl: e.tensor_tensor(out=acc[:, sl], in0=acc[:, sl], in1=Mk[g][:], op=ALU.add),
                               reads=[bMk[g], bacc[pc]], writes=[bacc[pc]])
                if _PS < 5:
                    dma("sp", out_d[r0:r0 + 128, :], xps[k][:], [bxps[k]] + bacc, [bPO[k]], f"po{k}")
                    continue
                for ec in range(32):
                    u = ecnt2 % 2
                    ecnt2 += 1
                    dma("sp", ub[u][:].rearrange("p a b -> p (a b)"), ut_b[ec * 128:(ec + 1) * 128, :], [bDR], [bub[u]], f"ub{u}")
                    dma("sp", vb[u][:], v_b3[:, ec * 4:(ec + 1) * 4, :], [bDR], [bvb[u]], f"vb{u}")
                    bk = nextbank(RB)

                    def mma(e, u=u, bk=bk):
                        for kc in range(8):
                            e.matmul(PB[bk][:, :], lhsT=h2Tb[:, kc, :], rhs=ub[u][:, kc, :], start=(kc == 0), stop=(kc == 7))
                    op("pe", mma, reads=[bh2b, bub[u]], writes=[bPB[bk]])
                    op("act", lambda e, u=u, bk=bk: e.activation(out=gel[u][:], in_=PB[bk][:, :], func=AF.Gelu), reads=[bPB[bk]], writes=[bgel[u]])
                    op("dve", lambda e, u=u, ec=ec: e.tensor_tensor(out=Wc[u][:], in0=gel[u][:], in1=acc[:, ec * 512:(ec + 1) * 512], op=ALU.mult),
                       reads=[bgel[u], bacc[ec // 4]], writes=[bWc[u]])
                    tb = nextbank(RB)
                    ptb = PB[tb].bitcast(BF16)

                    def trw(e, u=u, ptb=ptb):
                        for a in range(4):
                            e.transpose(out=ptb[:, a * 128:(a + 1) * 128], in_=Wc[u][:, a * 128:(a + 1) * 128], identity=identb[:])
                    op("pe", trw, reads=[bWc[u], bC], writes=[bPB[tb]])
                    op("act", lambda e, u=u, ptb=ptb: e.activation(out=WT[u][:].rearrange("p a b -> p (a b)"), in_=ptb[:, 0:512], func=AF.Copy),
                       reads=[bPB[tb]], writes=[bWT[u]])

                    def mmv3(e, u=u, ec=ec):
                        for a in range(4):
                            for half in range(2):
                                e.matmul(PB[6 + half][:, :], lhsT=WT[u][:, a, :], rhs=vb[u][:, a, half * 512:(half + 1) * 512],
                                         start=(ec == 0 and a == 0), stop=(ec == 31 and a == 3))
                    op("pe", mmv3, reads=[bWT[u], bvb[u]], writes=[bPB[6], bPB[7]])
                for half in range(2):
                    op("dve", lambda e, half=half: e.tensor_tensor(out=tt2[:, half * 512:(half + 1) * 512], in0=PB[6 + half][:, :],
                                                                  in1=gt2[:, half * 512:(half + 1) * 512], op=ALU.mult),
                       reads=[bPB[6 + half], bgt2], writes=[btt2])
                op("dve", lambda e, k=k: e.scalar_tensor_tensor(out=xps[k][:], in0=xps[k][:], scalar=ALPHA, in1=tt2[:], op0=ALU.mult, op1=ALU.add),
                   reads=[bxps[k], btt2], writes=[bxps[k]])
                mv, rs, bm, br = ln_stats(xps[k], bxps[k])
                op("dve", lambda e, k=k, mv=mv, rs=rs: e.tensor_scalar(out=xps[k][:], in0=xps[k][:], scalar1=mv[:, 0:1], scalar2=rs[:],
                                                                      op0=ALU.subtract, op1=ALU.mult),
                   reads=[bxps[k], bm, br], writes=[bxps[k]])
                op("pool", lambda e, k=k: e.tensor_tensor(out=xps[k][:], in0=xps[k][:], in1=l2g[:], op=ALU.mult), reads=[bxps[k], bl2], writes=[bxps[k]])
                op("pool", lambda e, k=k: e.tensor_tensor(out=xps[k][:], in0=xps[k][:], in1=l2b[:], op=ALU.add), reads=[bxps[k], bl2], writes=[bxps[k]])
                dma("sp", out_d[r0:r0 + 128, :], xps[k][:], [bxps[k]], [bPO[k]], f"po{k}")
            A.pop()
        SC.barrier()
        print("arena peak", A.peak, "ops", SC.nops, "waits", SC.nwait)
        SC.emit_all()
    return nc


def _kmaj(w):
    return np.ascontiguousarray(w.reshape(8, 128, -1).transpose(1, 0, 2))


def prep_shared(inp, do_peer=True):
    f = np.float32
    w_in = np.asarray(inp["w_in"][0], f)
    da_q, da_k, da_v = w_in[:, 0:1024], w_in[:, 1024:2048], w_in[:, 2048:3072]
    ml_q, ml_k = w_in[:, 3072:3584], w_in[:, 3584:4096]
    ml_v, ml_o = w_in[:, 4096:5120], w_in[:, 5120:6144]
    ml_if = w_in[:, 6144:6152]
    g_attn, g_ml = w_in[:, 6152:7176], w_in[:, 7176:8200]
    wba = np.asarray(inp["w_br_attn"][0], f)
    wbm = np.asarray(inp["w_br_mlstm"][0], f)
    sh = {}
    sh["w_ada"] = _kmaj(np.asarray(inp["w_ada"][0], f)).reshape(128, 8 * 6144)
    sh["b_ada"] = np.asarray(inp["b_ada"], f).reshape(1, 6144)
    sh["w_da"] = np.stack([np.concatenate([_kmaj(da_q[:, h * 128:(h + 1) * 128]), _kmaj(da_k[:, h * 128:(h + 1) * 128]),
                                           _kmaj(da_v[:, h * 128:(h + 1) * 128])], axis=2) for h in range(8)]).reshape(1024, 3072)
    sh["w_ml"] = np.stack([np.concatenate([_kmaj(ml_q[:, h * 128:(h + 1) * 128]), _kmaj(ml_k[:, h * 128:(h + 1) * 128]),
                                           _kmaj(ml_v[:, h * 256:(h + 1) * 256]), _kmaj(ml_o[:, h * 256:(h + 1) * 256])], axis=2)
                           for h in range(4)]).reshape(512, 6144)
    sh["w_if"] = _kmaj(ml_if).reshape(128, 64)
    sh["w_mg"] = np.stack([np.concatenate([_kmaj(g_attn[:, c * 128:(c + 1) * 128]), _kmaj(g_ml[:, c * 128:(c + 1) * 128]),
                                           _kmaj(wba[:, c * 128:(c + 1) * 128]), _kmaj(wbm[:, c * 128:(c + 1) * 128])], axis=2)
                           for c in range(8)]).reshape(1024, 4096)
    sh["w_out"] = _kmaj(np.asarray(inp["w_out"][0], f)).reshape(128, 8192)
    wq = np.asarray(inp["peer_wq"][0], f)
    sh["w_pq"] = np.stack([_kmaj(wq[:, h * 128:(h + 1) * 128]) for h in range(8)]).reshape(1024, 1024)
    if do_peer:
        u = np.asarray(inp["peer_u"][0], f)
        sh["UT"] = np.stack([_kmaj(np.ascontiguousarray(u[ec * 512:(ec + 1) * 512, :].T)) for ec in range(32)]).reshape(4096, 4096)
        sh["V"] = np.ascontiguousarray(np.asarray(inp["peer_v"][0], f))
    else:
        sh["UT"] = np.zeros((4096, 4096), f)
        sh["V"] = np.zeros((NEXP, D), f)
    sh["b_ifT"] = np.ascontiguousarray(np.asarray(inp["b_if"][0], f).T)
    sh["conv_wT"] = np.ascontiguousarray(np.asarray(inp["conv_w"][0], f).reshape(4, 8, 128).transpose(2, 1, 0)).reshape(128, 32)
    sh["conv_bT"] = np.ascontiguousarray(np.asarray(inp["conv_b"][0], f).reshape(8, 128).T)
    sh["da_lambda"] = np.asarray(inp["da_lambda"][0], f).reshape(1, 256)
    sh["subln_g"] = np.asarray(inp["da_subln_g"][0], f).reshape(1, 128)
    sh["ml_norm_g"] = np.asarray(inp["ml_norm_g"][0], f).reshape(1, D)
    for nm in ("ln1_g", "ln1_b", "ln2_g", "ln2_b"):
        sh[nm] = np.asarray(inp[nm][0], f).reshape(1, D)
    kt = np.ascontiguousarray(np.asarray(inp["peer_keys"][0], f).transpose(0, 2, 1))
    kz = np.zeros((128, 256), f)
    kz[0:64, 0:128] = kt[0]
    kz[64:128, 128:256] = kt[1]
    sh["keysT"] = kz
    sh["ident"] = np.eye(128, dtype=f)
    kk = np.arange(128)
    sh["cmask"] = (kk[None, :] >= kk[:, None]).astype(f)
    sel = np.zeros((4, 4, 128), f)
    for h in range(4):
        sel[h, h, :] = 1.0
    sh["sel4"] = sel.reshape(4, 512)
    return sh


def core_inputs(inp, sh, b0, nseq):
    f = np.float32
    m = dict(sh)
    m["x"] = np.ascontiguousarray(np.asarray(inp["x"][b0:b0 + nseq], f).reshape(nseq * SEQ, D))
    c = np.asarray(inp["c"][b0:b0 + nseq], f)
    m["cT"] = _kmaj(np.ascontiguousarray(c.T)).reshape(128, 8 * nseq)
    return m


_NC_CACHE = {}


def kernel(**inputs):
    if "full" not in _NC_CACHE:
        _NC_CACHE["full"] = build(NSEQ_FULL, True)
    nc = _NC_CACHE["full"]
    sh = prep_shared(inputs, True)
    in_maps = [core_inputs(inputs, sh, i * NSEQ_FULL, NSEQ_FULL) for i in range(NCORES)]
    res = run_bass_kernel_spmd(nc, in_maps, core_ids=list(range(NCORES)))
    out = np.concatenate([np.asarray(r["out"]).reshape(NSEQ_FULL, SEQ, D) for r in res.results], axis=0)
    return out.astype(np.float32)
```

```python
import contextlib
import math
import numpy as np
import concourse.bass as bass
import concourse.mybir as mybir
from concourse.bass_utils import run_bass_kernel_spmd

F32 = mybir.dt.float32
BF16 = mybir.dt.bfloat16
AF = mybir.ActivationFunctionType
ALU = mybir.AluOpType

D = 1024
SEQ = 2048
NT = SEQ // 128
NCORES = 8
BATCH = 32
NSEQ_FULL = BATCH // NCORES
ALPHA = 2.0 ** 0.25
LN_EPS = 1e-5
LAMBDA_INIT = 0.8 - 0.6 * math.exp(0.0)
NEXP = 16384


class Buf:
    __slots__ = ("name", "last_w", "readers")

    def __init__(self, name):
        self.name = name
        self.last_w = None
        self.readers = []


class _Rec:
    def __init__(self):
        self.calls = []

    def __getattr__(self, name):
        def f(*a, **kw):
            self.calls.append((name, a, kw))
            return None
        return f


class Sched:
    ENGS = ("pe", "act", "dve", "pool", "sp")

    def __init__(self, nc):
        self.nc = nc
        self.ops = {e: [] for e in self.ENGS}
        self.cnt = {e: 0 for e in self.ENGS}
        self.seen = {e: {} for e in self.ENGS}
        self.sems = {}
        self.dma_cnt = {}
        self.nops = 0
        self.nwait = 0

    def op(self, eng, emit, reads=(), writes=(), dma=None, n=1):
        deps = {}
        for b in reads:
            if b.last_w is not None:
                k, v, e = b.last_w
                if not (e == eng and k[0] == "eng" and False):
                    deps[k] = max(deps.get(k, 0), v)
        for b in writes:
            if b.last_w is not None:
                k, v, e = b.last_w
                if not (e == eng and k[0] == "eng" and dma is None):
                    deps[k] = max(deps.get(k, 0), v)
            for (k, v, e) in b.readers:
                if e == eng and k[0] == "eng" and dma is None:
                    continue
                deps[k] = max(deps.get(k, 0), v)
        waits = []
        seen = self.seen[eng]
        for k, v in deps.items():
            if seen.get(k, 0) >= v:
                continue
            seen[k] = v
            waits.append((k, v))
        if dma is None:
            self.cnt[eng] += 1
            key = ("eng", eng)
            tok = (key, self.cnt[eng], eng)
        else:
            key = ("dma", dma)
            self.dma_cnt[key] = self.dma_cnt.get(key, 0) + 16 * n
            tok = (key, self.dma_cnt[key], eng)
        for b in reads:
            b.readers.append(tok)
        for b in writes:
            b.last_w = tok
            b.readers = []
        rec = _Rec()
        emit(rec)
        assert len(rec.calls) >= 1
        if dma is not None:
            assert len(rec.calls) == n, (len(rec.calls), n)
        self.ops[eng].append((waits, rec.calls, key, dma is not None, n))
        self.nops += 1
        self.nwait += len(waits)
        return tok

    def barrier(self):
        cur = {("eng", e): self.cnt[e] for e in self.ENGS if self.cnt[e] > 0}
        cur.update(self.dma_cnt)
        for eng in self.ENGS:
            waits = []
            seen = self.seen[eng]
            for k, v in cur.items():
                if k == ("eng", eng) or seen.get(k, 0) >= v:
                    continue
                seen[k] = v
                waits.append((k, v))
            if waits:
                self.cnt[eng] += 1
                self.ops[eng].append((waits, [("nop", (), {})], ("eng", eng), False, 1))

    def emit_all(self):
        nc = self.nc
        keys = [("eng", e) for e in self.ENGS]
        for e in self.ENGS:
            for rec in self.ops[e]:
                if rec[2] not in keys:
                    keys.append(rec[2])
        for k in keys:
            self.sems[k] = nc.alloc_semaphore("s_" + "_".join(str(x) for x in k))
        with nc.Block() as block:
            def mk(ename):
                def body(eng):
                    for waits, calls, key, is_dma, n in self.ops[ename]:
                        for (k, v) in waits:
                            eng.wait_ge(self.sems[k], v)
                        s = self.sems[key]
                        r = None
                        for (name, a, kw) in calls:
                            r = getattr(eng, name)(*a, **kw)
                            if is_dma:
                                r.then_inc(s, 16)
                        if not is_dma:
                            r.then_inc(s, 1)
                return body
            block.tensor(mk("pe"))
            block.scalar(mk("act"))
            block.vector(mk("dve"))
            block.gpsimd(mk("pool"))
            block.sync(mk("sp"))


class Arena:
    def __init__(self, t, nbytes):
        self.t = t
        self.nbytes = nbytes
        self.off = 0
        self.stack = []
        self.peak = 0

    def push(self):
        self.stack.append(self.off)

    def pop(self):
        self.off = self.stack.pop()

    def alloc(self, shape, dt):
        esz = 4 if dt == F32 else 2
        n = 1
        for s in shape[1:]:
            n *= s
        nb = (n * esz + 63) // 64 * 64
        assert self.off + nb <= self.nbytes, ("arena overflow", self.off, nb, self.nbytes)
        a = self.t[0:shape[0], self.off // 4:(self.off + nb) // 4]
        self.off += nb
        self.peak = max(self.peak, self.off)
        if dt != F32:
            a = a.bitcast(dt)
        a = a[:, 0:n]
        if len(shape) == 3:
            a = a.rearrange("p (a b) -> p a b", a=shape[1])
        elif len(shape) == 4:
            a = a.rearrange("p (a b c) -> p a b c", a=shape[1], b=shape[2])
        return a


def build(NSEQ=NSEQ_FULL, do_peer=True, dbg=False):
    nc = bass.Bass("TRN2", target_bir_lowering=False)
    SC = Sched(nc)
    op = SC.op
    NTOK = NSEQ * SEQ

    def din(name, shape, dt=F32):
        return nc.dram_tensor(name, list(shape), dt, kind="ExternalInput").ap()

    def dscr(name, shape, dt):
        return nc.dram_tensor(name, list(shape), dt, kind="Internal").ap()

    x_d = din("x", [NTOK, D])
    cT_d = din("cT", [128, 8 * NSEQ])
    wada_d = din("w_ada", [128, 8 * 6144])
    bada_d = din("b_ada", [1, 6144])
    wda_d = din("w_da", [8 * 128, 3072])
    wml_d = din("w_ml", [4 * 128, 6144])
    wif_d = din("w_if", [128, 64])
    wmg_d = din("w_mg", [8 * 128, 4096])
    wout_d = din("w_out", [128, 8192])
    wpq_d = din("w_pq", [8 * 128, 1024])
    ut_d = din("UT", [32 * 128, 4096])
    v_d = din("V", [NEXP, D])
    bif_d = din("b_ifT", [4, 2])
    cw_d = din("conv_wT", [128, 32])
    cb_d = din("conv_bT", [128, 8])
    lam_d = din("da_lambda", [1, 256])
    subg_d = din("subln_g", [1, 128])
    mlg_d = din("ml_norm_g", [1, D])
    ln1g_d = din("ln1_g", [1, D])
    ln1b_d = din("ln1_b", [1, D])
    ln2g_d = din("ln2_g", [1, D])
    ln2b_d = din("ln2_b", [1, D])
    keysT_d = din("keysT", [128, 256])
    ident_d = din("ident", [128, 128])
    cmask_d = din("cmask", [128, 128])
    sel4_d = din("sel4", [4, 512])
    out_d = nc.dram_tensor("out", [NTOK, D], F32, kind="ExternalOutput").ap()

    wda_b = dscr("w_da_b", [8 * 128, 3072], BF16)
    wml_b = dscr("w_ml_b", [4 * 128, 6144], BF16)
    wif_b = dscr("w_if_b", [128, 64], BF16)
    wmg_b = dscr("w_mg_b", [8 * 128, 4096], BF16)
    wout_b = dscr("w_out_b", [128, 8192], BF16)
    ut_b = dscr("UT_b", [32 * 128, 4096], BF16)
    v_b = dscr("V_b", [NEXP, D], BF16)
    x1_d = dscr("x1_s", [NTOK, D], F32)
    gt_d = dscr("gt_s", [NSEQ, 2048], F32)

    es = contextlib.ExitStack()
    with es:
        ARENA_BYTES = 206 * 1024
        arena_t = es.enter_context(nc.sbuf_tensor("arena", [128, ARENA_BYTES // 4], F32))
        A = Arena(arena_t, ARENA_BYTES)
        psum_t = es.enter_context(nc.psum_tensor("psum", [128, 8, 512], F32))
        PB = [psum_t[:, i, :] for i in range(8)]
        bPB = [Buf(f"pb{i}") for i in range(8)]
        bank_rr = [0]

        def nextbank(banks):
            i = banks[bank_rr[0] % len(banks)]
            bank_rr[0] += 1
            return i

        def dma(eng, out, in_, reads, writes, key):
            return op(eng, lambda e: e.dma_start(out=out, in_=in_), reads=reads, writes=writes, dma=key)

        dbg_outs = {}

        def dump(name, ap, bufs, shape, dt):
            if not dbg:
                return
            t = nc.dram_tensor("d_" + name, list(shape), dt, kind="ExternalOutput").ap()
            dbg_outs[name] = t
            dma("sp", t, ap, list(bufs), [Buf("dbg")], "dbg_" + name)

        bOUT = Buf("out")
        bDR = Buf("dram_scratch")
        bGT = Buf("gt_scratch")
        bXO = [Buf("xo0"), Buf("xo1")]

        identf = A.alloc([128, 128], F32)
        identb = A.alloc([128, 128], BF16)
        maskb = A.alloc([128, 128], BF16)
        sel4 = A.alloc([4, 512], F32)
        cm05 = A.alloc([128, 8], F32)
        ones4 = A.alloc([4, 512], F32)
        modT = A.alloc([128, 48 * NSEQ], F32)
        neglam = A.alloc([128, 1], F32)
        gda_bc = A.alloc([128, 128], F32)
        bifT = A.alloc([4, 2], F32)
        nbf = A.alloc([4, 1], F32)
        cwT = A.alloc([128, 32], F32)
        cbT = A.alloc([128, 8], F32)
        wif = A.alloc([128, 8, 8], BF16)
        junk = A.alloc([128, 256], F32)
        bC = Buf("consts")
        bjunk = Buf("junk")
        NST = 4
        stt = [A.alloc([128, 12], F32) for _ in range(NST)]
        mvt = [A.alloc([128, 2], F32) for _ in range(NST)]
        rst = [A.alloc([128, 1], F32) for _ in range(NST)]
        bst = [Buf(f"st{i}") for i in range(NST)]
        bmv = [Buf(f"mv{i}") for i in range(NST)]
        brs = [Buf(f"rs{i}") for i in range(NST)]
        stk = [0]

        def ln_stats(src, bsrc):
            k = stk[0] % NST
            stk[0] += 1
            st, mv, rs = stt[k], mvt[k], rst[k]
            op("dve", lambda e: e.bn_stats(out=st[:, 0:6], in_=src[:, 0:512]), reads=[bsrc], writes=[bst[k]])
            op("dve", lambda e: e.bn_stats(out=st[:, 6:12], in_=src[:, 512:1024]), reads=[bsrc], writes=[bst[k]])
            op("dve", lambda e: e.bn_aggr(out=mv[:], in_=st[:]), reads=[bst[k]], writes=[bmv[k]])
            op("dve", lambda e: e.tensor_scalar(out=rs[:], in0=mv[:, 1:2], scalar1=LN_EPS, scalar2=None, op0=ALU.add),
               reads=[bmv[k]], writes=[brs[k]])
            op("pool", lambda e: e.tensor_tensor(out=rs[:], in0=rs[:], in1=cm05[:, 0:1], op=ALU.pow),
               reads=[brs[k], bC], writes=[brs[k]])
            return mv, rs, bmv[k], brs[k]

        A.push()
        tmpf = A.alloc([128, 128], F32)
        tmpm = A.alloc([128, 128], F32)
        lamt = A.alloc([128, 256], F32)
        lamp = A.alloc([128, 128], F32)
        lams = A.alloc([128, 2], F32)
        wif_f = A.alloc([128, 64], F32)
        btmp = Buf("tmp0")
        dma("sp", identf[:], ident_d[:, :], [], [bC], "c0")
        dma("sp", tmpm[:], cmask_d[:, :], [], [btmp], "c1")
        dma("sp", sel4[:], sel4_d[:, :], [], [bC], "c0")
        dma("sp", bifT[:], bif_d[:, :], [], [bC], "c0")
        dma("sp", cwT[:], cw_d[:, :], [], [bC], "c0")
        dma("sp", cbT[:], cb_d[:, :], [], [bC], "c0")
        dma("sp", gda_bc[:], subg_d[0:1, :].partition_broadcast(128), [], [bC], "c0")
        dma("sp", lamt[:], lam_d[0:1, :].partition_broadcast(128), [], [btmp], "c1")
        dma("sp", wif_f[:], wif_d[:, :], [], [btmp], "c1")
        op("pool", lambda e: e.tensor_copy(out=identb[:], in_=identf[:]), reads=[bC], writes=[bC])
        op("pool", lambda e: e.tensor_copy(out=maskb[:], in_=tmpm[:]), reads=[btmp], writes=[bC])
        op("pool", lambda e: e.memset(cm05[:], -0.5), writes=[bC])
        op("pool", lambda e: e.memset(ones4[:], 1.0), writes=[bC])
        op("pool", lambda e: e.tensor_copy(out=wif[:].rearrange("p a b -> p (a b)"), in_=wif_f[:]), reads=[btmp], writes=[bC])
        op("dve", lambda e: e.tensor_scalar(out=gda_bc[:], in0=gda_bc[:], scalar1=1.0 - LAMBDA_INIT, scalar2=None, op0=ALU.mult),
           reads=[bC], writes=[bC])
        op("dve", lambda e: e.tensor_scalar(out=nbf[:], in0=bifT[:, 1:2], scalar1=-1.0, scalar2=None, op0=ALU.mult),
           reads=[bC], writes=[bC])
        lt3 = lamt[:].rearrange("p (a b) -> p a b", a=4)
        op("dve", lambda e: e.tensor_tensor(out=lamp[:].rearrange("p (a b) -> p a b", a=2),
                                            in0=lt3[:, 0:4:2, :], in1=lt3[:, 1:4:2, :], op=ALU.mult),
           reads=[btmp], writes=[btmp])
        op("dve", lambda e: e.tensor_reduce(out=lams[:], in_=lamp[:].rearrange("p (a b) -> p a b", a=2),
                                            axis=mybir.AxisListType.X, op=ALU.add),
           reads=[btmp], writes=[btmp])
        op("act", lambda e: e.activation(out=lams[:], in_=lams[:], func=AF.Exp), reads=[btmp], writes=[btmp])
        op("dve", lambda e: e.tensor_tensor(out=neglam[:], in0=lams[:, 1:2], in1=lams[:, 0:1], op=ALU.subtract),
           reads=[btmp], writes=[bC])
        op("dve", lambda e: e.tensor_scalar(out=neglam[:], in0=neglam[:], scalar1=-LAMBDA_INIT, scalar2=None, op0=ALU.add),
           reads=[bC], writes=[bC])
        SC.barrier()
        A.pop()

        A.push()
        NCV = 3
        CW = 4096
        cvf = [A.alloc([128, CW], F32) for _ in range(NCV)]
        cvb = [A.alloc([128, CW], BF16) for _ in range(NCV)]
        bcvf = [Buf(f"cvf{i}") for i in range(NCV)]
        bcvb = [Buf(f"cvb{i}") for i in range(NCV)]
        kcv = [0]

        def convert(src, dst, R, C):
            for r0 in range(0, R, 128):
                for c0 in range(0, C, CW):
                    cw = min(CW, C - c0)
                    k = kcv[0] % NCV
                    kk = kcv[0]
                    kcv[0] += 1
                    dma("sp", cvf[k][:, 0:cw], src[r0:r0 + 128, c0:c0 + cw], [], [bcvf[k]], f"cvf{k}")
                    if kk % 2 == 0:
                        op("dve", lambda e, k=k, cw=cw: e.tensor_copy(out=cvb[k][:, 0:cw], in_=cvf[k][:, 0:cw]),
                           reads=[bcvf[k]], writes=[bcvb[k]])
                    else:
                        op("act", lambda e, k=k, cw=cw: e.activation(out=cvb[k][:, 0:cw], in_=cvf[k][:, 0:cw], func=AF.Copy),
                           reads=[bcvf[k]], writes=[bcvb[k]])
                    dma("pool", dst[r0:r0 + 128, c0:c0 + cw], cvb[k][:, 0:cw], [bcvb[k]], [bDR], f"cvb{k}")

        convert(wda_d, wda_b, 1024, 3072)
        convert(wml_d, wml_b, 512, 6144)
        convert(wmg_d, wmg_b, 1024, 4096)
        convert(wout_d, wout_b, 128, 8192)
        import os as _os0
        if do_peer and not int(_os0.environ.get("NOCONV", 0)):
            convert(ut_d, ut_b, 4096, 4096)
            convert(v_d, v_b, NEXP, D)
        SC.barrier()
        A.pop()

        A.push()
        siluT = A.alloc([128, 8, NSEQ], F32)
        modall = A.alloc([NSEQ, 6144], F32)
        badab = A.alloc([NSEQ, 6144], F32)
        wad = [A.alloc([128, 8, 512], F32) for _ in range(2)]
        bwad = [Buf("wad0"), Buf("wad1")]
        bsil = Buf("silu")
        bmod = Buf("modall")
        bmodT = Buf("modT")
        dma("sp", siluT[:].rearrange("p a b -> p (a b)"), cT_d[:, :], [], [bsil], "c1")
        dma("sp", badab[:], bada_d[0:1, :].partition_broadcast(NSEQ), [], [bmod], "c0")
        op("act", lambda e: e.activation(out=siluT[:].rearrange("p a b -> p (a b)"),
                                         in_=siluT[:].rearrange("p a b -> p (a b)"), func=AF.Silu),
           reads=[bsil], writes=[bsil])
        wada3 = wada_d.rearrange("p (a b) -> p a b", a=8)
        for pc in range(12):
            k = pc % 2
            dma("sp", wad[k][:], wada3[:, :, pc * 512:(pc + 1) * 512], [], [bwad[k]], f"wad{k}")
            bk = nextbank([0, 1])

            def mmg(e, k=k, bk=bk):
                r = None
                for kc in range(8):
                    r = e.matmul(PB[bk][0:NSEQ, :], lhsT=siluT[:, kc, :], rhs=wad[k][:, kc, :],
                                 start=(kc == 0), stop=(kc == 7))
                return r
            op("pe", mmg, reads=[bsil, bwad[k]], writes=[bPB[bk]])
            op("dve", lambda e, bk=bk, pc=pc: e.tensor_tensor(out=modall[:, pc * 512:(pc + 1) * 512], in0=PB[bk][0:NSEQ, :],
                                                              in1=badab[:, pc * 512:(pc + 1) * 512], op=ALU.add),
               reads=[bPB[bk], bmod], writes=[bmod])
        dma("sp", gt_d[:, 0:1024], modall[:, 2048:3072], [bmod], [bGT], "gtd")
        dma("sp", gt_d[:, 1024:2048], modall[:, 5120:6144], [bmod], [bGT], "gtd")
        bk = nextbank([0, 1])

        def trg(e, bk=bk):
            r = None
            for c in range(48):
                r = e.transpose(out=PB[bk][:, c * NSEQ:(c + 1) * NSEQ], in_=modall[0:NSEQ, c * 128:(c + 1) * 128],
                                identity=identf[0:NSEQ, 0:NSEQ])
            return r
        op("pe", trg, reads=[bmod, bC], writes=[bPB[bk]])
        op("dve", lambda e, bk=bk: e.tensor_copy(out=modT[:], in_=PB[bk][:, 0:48 * NSEQ]), reads=[bPB[bk]], writes=[bmodT])
        for c0 in (8, 32):
            op("dve", lambda e, c0=c0: e.tensor_scalar(out=modT[:, c0 * NSEQ:(c0 + 8) * NSEQ], in0=modT[:, c0 * NSEQ:(c0 + 8) * NSEQ],
                                                       scalar1=1.0, scalar2=None, op0=ALU.add),
               reads=[bmodT], writes=[bmodT])
        SC.barrier()
        A.pop()

        dump("modT", modT[:], [bmodT], [128, 48 * NSEQ], F32)

        def modcol(which, c, b):
            j = (which * 8 + c) * NSEQ + b
            return modT[:, j:j + 1]

        A.push()
        hT_raw = A.alloc([128, 8192], F32)
        hT = hT_raw.bitcast(BF16).rearrange("p (a b) -> p a b", a=8)
        yaT = A.alloc([128, 8, SEQ], BF16)
        ymT = A.alloc([128, 8, SEQ], BF16)
        bxts = [Buf("xt0"), Buf("xt1")]
        bhT = [Buf(f"hT{i}") for i in range(NT)]
        byaT = [Buf(f"yaT{i}") for i in range(8)]
        bymT = [Buf(f"ymT{i}") for i in range(NT)]

        def hbufs(t0, t1):
            return bhT[t0:t1]

        for b in range(NSEQ):
            A.push()
            xts = [A.alloc([128, D], F32) for _ in range(2)]
            xns = [A.alloc([128, D], BF16) for _ in range(2)]
            bxns = [Buf("xn0"), Buf("xn1")]
            for i in range(NT):
                k = i % 2
                r0 = b * SEQ + i * 128
                dma("sp", xts[k][:], x_d[r0:r0 + 128, :], [], [bxts[k]], f"xt{k}")
                mv, rs, bm, br = ln_stats(xts[k], bxts[k])
                op("dve", lambda e, k=k, mv=mv, rs=rs: e.tensor_scalar(out=xns[k][:], in0=xts[k][:], scalar1=mv[:, 0:1], scalar2=rs[:],
                                                                      op0=ALU.subtract, op1=ALU.mult),
                   reads=[bxts[k], bm, br], writes=[bxns[k]])
                if b == 0 and i == 0:
                    dump("mv0", mv[:], [bm], [128, 2], F32)
                    dump("rs0", rs[:], [br], [128, 1], F32)
                    dump("xn0", xns[k][:], [bxns[k]], [128, D], BF16)
                    dump("xt0", xts[k][:], [bxts[k]], [128, D], F32)
                bk = nextbank([0, 1])
                ptb = PB[bk].bitcast(BF16)

                def trx(e, k=k, ptb=ptb):
                    r = None
                    for c in range(8):
                        r = e.transpose(out=ptb[:, c * 128:(c + 1) * 128], in_=xns[k][:, c * 128:(c + 1) * 128], identity=identb[:])
                    return r
                op("pe", trx, reads=[bxns[k], bC], writes=[bPB[bk]])
                for c in range(8):
                    if c % 2 == 0:
                        op("act", lambda e, c=c, ptb=ptb, i=i: e.activation(out=hT[:, c, i * 128:(i + 1) * 128], in_=ptb[:, c * 128:(c + 1) * 128],
                                                                           func=AF.Identity, scale=modcol(1, c, b), bias=modcol(0, c, b)),
                           reads=[bPB[bk], bmodT], writes=[bhT[i]])
                    else:
                        op("dve", lambda e, c=c, ptb=ptb, i=i: e.tensor_scalar(out=hT[:, c, i * 128:(i + 1) * 128], in0=ptb[:, c * 128:(c + 1) * 128],
                                                                              scalar1=modcol(1, c, b), scalar2=modcol(0, c, b),
                                                                              op0=ALU.mult, op1=ALU.add),
                           reads=[bPB[bk], bmodT], writes=[bhT[i]])
            SC.barrier()
            A.pop()
            if b == 0:
                dump("hT", hT.rearrange("p a b -> p (a b)"), bhT, [128, 8 * SEQ], BF16)

            A.push()
            wda = [A.alloc([128, 8, 384], BF16) for _ in range(2)]
            bwda = [Buf("wda0"), Buf("wda1")]
            qTs = [A.alloc([128, SEQ], BF16) for _ in range(2)]
            kTs = [A.alloc([128, SEQ], BF16) for _ in range(2)]
            vss = [A.alloc([128, NT, 129], BF16) for _ in range(2)]
            bq = [[Buf(f"q{s}{c}") for c in range(4)] for s in range(2)]
            bkk = [[Buf(f"k{s}{c}") for c in range(4)] for s in range(2)]
            bv = [[Buf(f"v{s}{c}") for c in range(4)] for s in range(2)]
            Es = [A.alloc([128, NT, 256], BF16) for _ in range(2)]
            bE = [[Buf(f"E{s}{j}") for j in range(NT)] for s in range(2)]
            osb = [A.alloc([128, 2, 128], F32) for _ in range(2)]
            yat = [A.alloc([128, 2, 128], BF16) for _ in range(2)]
            rzs = [A.alloc([128, 4], F32) for _ in range(2)]
            sss = [A.alloc([128, 2], F32) for _ in range(2)]
            bo = [Buf("o0"), Buf("o1")]
            byat = [Buf("yat0"), Buf("yat1")]
            brz = [Buf("rz0"), Buf("rz1")]
            bss = [Buf("ss0"), Buf("ss1")]
            for s in range(2):
                op("pool", lambda e, s=s: e.memset(vss[s][:, :, 128:129], 1.0), writes=[bv[s][c] for c in range(4)])
            ecnt = 0
            ccnt = 0
            for h in range(8):
                s = h % 2
                dma("sp", wda[s][:].rearrange("p a b -> p (a b)"), wda_b[h * 128:(h + 1) * 128, :], [bDR], [bwda[s]], f"wda{s}")
                for c in range(4):
                    for which in range(2):
                        bk = nextbank([0, 1])

                        def mmg(e, s=s, c=c, which=which, bk=bk):
                            r = None
                            for kc in range(8):
                                r = e.matmul(PB[bk][:, :], lhsT=wda[s][:, kc, which * 128:(which + 1) * 128],
                                             rhs=hT[:, kc, c * 512:(c + 1) * 512], start=(kc == 0), stop=(kc == 7))
                            return r
                        op("pe", mmg, reads=[bwda[s]] + hbufs(4 * c, 4 * c + 4), writes=[bPB[bk]])
                        if which == 0:
                            op("act", lambda e, s=s, c=c, bk=bk: e.activation(out=qTs[s][:, c * 512:(c + 1) * 512], in_=PB[bk][:, :],
                                                                             func=AF.Copy, scale=0.125),
                               reads=[bPB[bk]], writes=[bq[s][c]])
                        else:
                            op("dve", lambda e, s=s, c=c, bk=bk: e.tensor_copy(out=kTs[s][:, c * 512:(c + 1) * 512], in_=PB[bk][:, :]),
                               reads=[bPB[bk]], writes=[bkk[s][c]])
                    bk = nextbank([0, 1])

                    def mmv(e, s=s, c=c, bk=bk):
                        r = None
                        for t in range(4):
                            i = 4 * c + t
                            for kc in range(8):
                                r = e.matmul(PB[bk][:, t * 128:(t + 1) * 128], lhsT=hT[:, kc, i * 128:(i + 1) * 128],
                                             rhs=wda[s][:, kc, 256:384], start=(kc == 0), stop=(kc == 7))
                        return r
                    op("pe", mmv, reads=[bwda[s]] + hbufs(4 * c, 4 * c + 4), writes=[bPB[bk]])
                    op("act", lambda e, s=s, c=c, bk=bk: e.activation(out=vss[s][:, 4 * c:4 * c + 4, 0:128],
                                                                     in_=PB[bk][:, :].rearrange("p (a b) -> p a b", a=4), func=AF.Copy),
                       reads=[bPB[bk]], writes=[bv[s][c]])
                for c in range(8):
                    cs = ccnt % 2
                    ccnt += 1
                    pob = [4 + cs * 2, 5 + cs * 2]
                    for m in range(2):
                        esl = ecnt % 2
                        ecnt += 1
                        E = Es[esl]
                        for j in range(2 * c + 2):
                            q0 = 128 if j == 2 * c + 1 else 0
                            sb_ = nextbank([2, 3])
                            op("pe", lambda e, s=s, m=m, j=j, c=c, q0=q0, sb_=sb_: e.matmul(
                                PB[sb_][:, q0:256], lhsT=kTs[s][m * 64:(m + 1) * 64, j * 128:(j + 1) * 128],
                                rhs=qTs[s][m * 64:(m + 1) * 64, c * 256 + q0:(c + 1) * 256], start=True, stop=True),
                               reads=[bkk[s][j // 4], bq[s][c // 2]], writes=[bPB[sb_]])
                            op("act", lambda e, E=E, j=j, q0=q0, sb_=sb_: e.activation(out=E[:, j, q0:256], in_=PB[sb_][:, q0:256], func=AF.Exp),
                               reads=[bPB[sb_]], writes=[bE[esl][j]])
                            if j >= 2 * c:
                                d0 = (j - 2 * c) * 128
                                op("pool", lambda e, E=E, j=j, d0=d0: e.tensor_tensor(out=E[:, j, d0:d0 + 128], in0=E[:, j, d0:d0 + 128],
                                                                                     in1=maskb[:], op=ALU.mult),
                                   reads=[bE[esl][j], bC], writes=[bE[esl][j]])
                        pv = PB[pob[m]][:, 0:258].rearrange("p (a b) -> p a b", a=2)

                        def pvg(e, s=s, c=c, E=E, pv=pv):
                            r = None
                            for ii in range(2):
                                i = 2 * c + ii
                                for j in range(i + 1):
                                    r = e.matmul(pv[:, ii, :], lhsT=E[:, j, ii * 128:(ii + 1) * 128], rhs=vss[s][:, j, :],
                                                 start=(j == 0), stop=(j == i))
                            return r
                        op("pe", pvg, reads=[bE[esl][j] for j in range(2 * c + 2)] + [bv[s][j] for j in range(c // 2 + 1)],
                           writes=[bPB[pob[m]]])
                    k = cs
                    p0 = PB[pob[0]][:, 0:258].rearrange("p (a b) -> p a b", a=2)
                    p1 = PB[pob[1]][:, 0:258].rearrange("p (a b) -> p a b", a=2)
                    op("dve", lambda e, k=k, p0=p0: e.reciprocal(out=rzs[k][:, 0:2], in_=p0[:, :, 128]), reads=[bPB[pob[0]]], writes=[brz[k]])
                    op("dve", lambda e, k=k, p1=p1: e.reciprocal(out=rzs[k][:, 2:4], in_=p1[:, :, 128]), reads=[bPB[pob[1]]], writes=[brz[k]])
                    op("dve", lambda e, k=k: e.tensor_scalar(out=rzs[k][:, 2:4], in0=rzs[k][:, 2:4], scalar1=neglam[:, 0:1], scalar2=None, op0=ALU.mult),
                       reads=[brz[k], bC], writes=[brz[k]])
                    for ii in range(2):
                        op("dve", lambda e, k=k, ii=ii, p0=p0: e.tensor_scalar(out=osb[k][:, ii, :], in0=p0[:, ii, 0:128], scalar1=rzs[k][:, ii:ii + 1],
                                                                              scalar2=None, op0=ALU.mult),
                           reads=[bPB[pob[0]], brz[k]], writes=[bo[k]])
                        op("dve", lambda e, k=k, ii=ii, p1=p1: e.scalar_tensor_tensor(out=osb[k][:, ii, :], in0=p1[:, ii, 0:128],
                                                                                     scalar=rzs[k][:, 2 + ii:3 + ii], in1=osb[k][:, ii, :],
                                                                                     op0=ALU.mult, op1=ALU.add),
                           reads=[bPB[pob[1]], brz[k], bo[k]], writes=[bo[k]])
                        op("act", lambda e, k=k, ii=ii: e.activation(out=junk[:, 0:128], in_=osb[k][:, ii, :], func=AF.Square,
                                                                    accum_out=sss[k][:, ii:ii + 1]),
                           reads=[bo[k]], writes=[bss[k], bjunk])
                    op("dve", lambda e, k=k: e.tensor_scalar(out=sss[k][:], in0=sss[k][:], scalar1=1.0 / 128.0, scalar2=LN_EPS,
                                                             op0=ALU.mult, op1=ALU.add),
                       reads=[bss[k]], writes=[bss[k]])
                    op("pool", lambda e, k=k: e.tensor_tensor(out=sss[k][:], in0=sss[k][:], in1=cm05[:, 0:2], op=ALU.pow),
                       reads=[bss[k], bC], writes=[bss[k]])
                    for ii in range(2):
                        op("dve", lambda e, k=k, ii=ii: e.scalar_tensor_tensor(out=yat[k][:, ii, :], in0=osb[k][:, ii, :], scalar=sss[k][:, ii:ii + 1],
                                                                              in1=gda_bc[:], op0=ALU.mult, op1=ALU.mult),
                           reads=[bo[k], bss[k], bC], writes=[byat[k]])
                    bk = nextbank([0, 1])
                    ptb = PB[bk].bitcast(BF16)

                    def trya(e, k=k, ptb=ptb):
                        r = None
                        for ii in range(2):
                            r = e.transpose(out=ptb[:, ii * 128:(ii + 1) * 128], in_=yat[k][:, ii, :], identity=identb[:])
                        return r
                    op("pe", trya, reads=[byat[k], bC], writes=[bPB[bk]])
                    op("act", lambda e, h=h, c=c, ptb=ptb: e.activation(out=yaT[:, h, c * 256:(c + 1) * 256], in_=ptb[:, 0:256], func=AF.Copy),
                       reads=[bPB[bk]], writes=[byaT[c]])
            SC.barrier()
            A.pop()
            if b == 0:
                dump("yaT", yaT.rearrange("p a b -> p (a b)"), byaT, [128, 8 * SEQ], BF16)

            A.push()
            LNS = math.log(128.0 ** -0.5)
            gch = [A.alloc([4, 512], F32) for _ in range(2)]
            negM = A.alloc([4, SEQ], F32)
            ech = [A.alloc([4, 512], F32) for _ in range(2)]
            nfc = [A.alloc([4, 512], F32) for _ in range(2)]
            mch = [A.alloc([4, 512], F32) for _ in range(2)]
            tfc = A.alloc([4, 512], F32)
            gtok = A.alloc([128, NT * 4], F32)
            etok = A.alloc([128, NT * 4], F32)
            bgc = [Buf("gch0"), Buf("gch1")]
            bnegM = Buf("negM")
            bec = [Buf("ech0"), Buf("ech1")]
            bnf = [Buf("nf0"), Buf("nf1")]
            bmc = [Buf("mc0"), Buf("mc1")]
            btf = Buf("tf")
            bgtok = Buf("gtok")
            betok = Buf("etok")
            for c in range(4):
                k = c % 2
                sl = slice(c * 512, (c + 1) * 512)
                bki = nextbank([0, 1])
                bkf = nextbank([0, 1])
                for (bk_, c0) in ((bki, 0), (bkf, 4)):
                    def mmif(e, bk_=bk_, c0=c0, c=c):
                        r = None
                        for kc in range(8):
                            r = e.matmul(PB[bk_][0:4, :], lhsT=wif[:, kc, c0:c0 + 4], rhs=hT[:, kc, c * 512:(c + 1) * 512],
                                         start=(kc == 0), stop=(kc == 7))
                        return r
                    op("pe", mmif, reads=[bC] + hbufs(4 * c, 4 * c + 4), writes=[bPB[bk_]])
                op("act", lambda e, bkf=bkf: e.activation(out=tfc[:], in_=PB[bkf][0:4, :], func=AF.Exp, scale=-1.0, bias=nbf[:, 0:1]),
                   reads=[bPB[bkf], bC], writes=[btf])
                op("act", lambda e: e.activation(out=tfc[:], in_=tfc[:], func=AF.Ln, bias=1.0), reads=[btf], writes=[btf])
                init_nf = 0.0 if c == 0 else nfc[1 - k][:, 511:512]
                op("dve", lambda e, k=k, init_nf=init_nf: e.tensor_tensor_scan(out=nfc[k][:], data0=ones4[:], data1=tfc[:], initial=init_nf,
                                                                              op0=ALU.mult, op1=ALU.add),
                   reads=[btf, bC, bnf[1 - k]], writes=[bnf[k]])
                op("dve", lambda e, k=k, bki=bki, sl=sl: e.scalar_tensor_tensor(out=gch[k][:], in0=PB[bki][0:4, :], scalar=bifT[:, 0:1], in1=nfc[k][:],
                                                                               op0=ALU.add, op1=ALU.add),
                   reads=[bPB[bki], bC, bnf[k]], writes=[bgc[k]])
                init_m = 0.0 if c == 0 else mch[1 - k][:, 511:512]
                op("dve", lambda e, k=k, sl=sl, init_m=init_m: e.tensor_tensor_scan(out=mch[k][:], data0=gch[k][:], data1=gch[k][:], initial=init_m,
                                                                                   op0=ALU.max, op1=ALU.max),
                   reads=[bgc[k], bmc[1 - k]], writes=[bmc[k]])
                op("dve", lambda e, k=k, sl=sl: e.tensor_scalar(out=negM[:, sl], in0=mch[k][:], scalar1=-1.0, scalar2=None, op0=ALU.mult),
                   reads=[bmc[k]], writes=[bnegM])
                op("dve", lambda e, k=k: e.tensor_tensor(out=ech[k][:], in0=nfc[k][:], in1=mch[k][:], op=ALU.subtract),
                   reads=[bnf[k], bmc[k]], writes=[bec[k]])
                op("act", lambda e, k=k: e.activation(out=ech[k][:], in_=ech[k][:], func=AF.Exp), reads=[bec[k]], writes=[bec[k]])

                def trgt(e, k=k, c=c):
                    r = None
                    for t in range(4):
                        i = 4 * c + t
                        r = e.transpose(out=PB[7][:, i * 4:(i + 1) * 4], in_=gch[k][0:4, t * 128:(t + 1) * 128], identity=identf[0:4, 0:4])
                        r = e.transpose(out=PB[7][:, 64 + i * 4:64 + (i + 1) * 4], in_=ech[k][0:4, t * 128:(t + 1) * 128], identity=identf[0:4, 0:4])
                    return r
                op("pe", trgt, reads=[bgc[k], bec[k], bC], writes=[bPB[7]])
            bk = 7
            op("dve", lambda e, bk=bk: e.tensor_scalar(out=gtok[:], in0=PB[bk][:, 0:64], scalar1=LNS, scalar2=None, op0=ALU.add),
               reads=[bPB[bk]], writes=[bgtok])
            op("dve", lambda e, bk=bk: e.tensor_copy(out=etok[:], in_=PB[bk][:, 64:128]), reads=[bPB[bk]], writes=[betok])

            wml = A.alloc([128, 8, 768], BF16)
            bwml = Buf("wml")
            negMbc = [A.alloc([128, 256], F32) for _ in range(2)]
            bnb = [Buf("nb0"), Buf("nb1")]
            zp = A.alloc([128, SEQ + 3], F32)
            bzp = [Buf(f"zp{c}") for c in range(4)]
            cv = A.alloc([128, SEQ], F32)
            bcv = Buf("cv")
            qTm = A.alloc([128, SEQ], BF16)
            kTm = A.alloc([128, SEQ], BF16)
            bqm = Buf("qTm")
            bkm = Buf("kTm")
            vm = A.alloc([128, NT, 257], BF16)
            bvm = [Buf(f"vm{i}") for i in range(8)]
            Ps = [A.alloc([128, NT, 256], BF16) for _ in range(2)]
            bP = [[Buf(f"P{s}{j}") for j in range(NT)] for s in range(2)]
            Wt = [A.alloc([128, 256], F32) for _ in range(2)]
            bWt = [Buf("Wt0"), Buf("Wt1")]
            gml_bc = A.alloc([128, 256], F32)
            bgml = Buf("gml")
            hn = [A.alloc([128, 256], F32) for _ in range(2)]
            bhn = [Buf("hn0"), Buf("hn1")]
            og = [A.alloc([128, 256], F32) for _ in range(2)]
            bog = [Buf("og0"), Buf("og1")]
            ymt = [A.alloc([128, 256], BF16) for _ in range(2)]
            bymt = [Buf("ymt0"), Buf("ymt1")]
            dens = [A.alloc([128, 2], F32) for _ in range(2)]
            bden = [Buf("den0"), Buf("den1")]
            op("pool", lambda e: e.memset(zp[:, 0:3], 0.0), writes=bzp)
            op("pool", lambda e: e.memset(vm[:, :, 256:257], 1.0), writes=bvm)
            pcnt = 0
            tcnt = 0
            wcnt = 0
            for h in range(4):
                dma("sp", wml[:].rearrange("p a b -> p (a b)"), wml_b[h * 128:(h + 1) * 128, :], [bDR], [bwml], "wml")
                dma("sp", gml_bc[:], mlg_d[0:1, h * 256:(h + 1) * 256].partition_broadcast(128), [], [bgml], "gml")
                for which in range(2):
                    ch = which * 4 + h
                    for c in range(4):
                        bk = nextbank([0, 1])

                        def mmq(e, c=c, which=which, bk=bk):
                            r = None
                            for kc in range(8):
                                r = e.matmul(PB[bk][:, :], lhsT=wml[:, kc, which * 128:(which + 1) * 128],
                                             rhs=hT[:, kc, c * 512:(c + 1) * 512], start=(kc == 0), stop=(kc == 7))
                            return r
                        op("pe", mmq, reads=[bwml] + hbufs(4 * c, 4 * c + 4), writes=[bPB[bk]])
                        op("act", lambda e, c=c, bk=bk: e.activation(out=zp[:, 3 + c * 512:3 + (c + 1) * 512], in_=PB[bk][:, :], func=AF.Copy),
                           reads=[bPB[bk]], writes=[bzp[c]])
                    op("dve", lambda e, ch=ch: e.tensor_scalar(out=cv[:], in0=zp[:, 3:SEQ + 3], scalar1=cwT[:, ch * 4 + 3:ch * 4 + 4],
                                                               scalar2=cbT[:, ch:ch + 1], op0=ALU.mult, op1=ALU.add),
                       reads=bzp + [bC], writes=[bcv])
                    for j in range(3):
                        op("dve", lambda e, ch=ch, j=j: e.scalar_tensor_tensor(out=cv[:], in0=zp[:, j:j + SEQ], scalar=cwT[:, ch * 4 + j:ch * 4 + j + 1],
                                                                              in1=cv[:], op0=ALU.mult, op1=ALU.add),
                           reads=bzp + [bC, bcv], writes=[bcv])
                    dst, bdst = (qTm, bqm) if which == 0 else (kTm, bkm)
                    op("act", lambda e, dst=dst: e.activation(out=dst[:], in_=cv[:], func=AF.Silu), reads=[bcv], writes=[bdst])
                for g2 in range(8):
                    bk = nextbank([0, 1])

                    def mmv2(e, g2=g2, bk=bk):
                        r = None
                        for t in range(2):
                            i = 2 * g2 + t
                            for kc in range(8):
                                r = e.matmul(PB[bk][:, t * 256:(t + 1) * 256], lhsT=hT[:, kc, i * 128:(i + 1) * 128],
                                             rhs=wml[:, kc, 256:512], start=(kc == 0), stop=(kc == 7))
                        return r
                    op("pe", mmv2, reads=[bwml] + hbufs(2 * g2, 2 * g2 + 2), writes=[bPB[bk]])
                    op("dve", lambda e, g2=g2, bk=bk: e.tensor_copy(out=vm[:, 2 * g2:2 * g2 + 2, 0:256],
                                                                   in_=PB[bk][:, :].rearrange("p (a b) -> p a b", a=2)),
                       reads=[bPB[bk]], writes=[bvm[g2]])
                for c in range(8):
                    psl = pcnt % 2
                    pcnt += 1
                    P = Ps[psl]
                    bk = nextbank([0, 1])
                    op("pe", lambda e, h=h, c=c, bk=bk: e.matmul(PB[bk][:, 0:256], lhsT=sel4[0:4, h * 128:(h + 1) * 128],
                                                                rhs=negM[0:4, c * 256:(c + 1) * 256], start=True, stop=True),
                       reads=[bC, bnegM], writes=[bPB[bk]])
                    op("act", lambda e, psl=psl, bk=bk: e.activation(out=negMbc[psl][:, :], in_=PB[bk][:, 0:256], func=AF.Copy),
                       reads=[bPB[bk]], writes=[bnb[psl]])
                    for j in range(2 * c + 2):
                        q0 = 128 if j == 2 * c + 1 else 0
                        sb_ = nextbank([2, 3])
                        ws = wcnt % 2
                        wcnt += 1
                        op("pe", lambda e, j=j, c=c, q0=q0, sb_=sb_: e.matmul(
                            PB[sb_][:, q0:256], lhsT=kTm[:, j * 128:(j + 1) * 128], rhs=qTm[:, c * 256 + q0:(c + 1) * 256], start=True, stop=True),
                           reads=[bkm, bqm], writes=[bPB[sb_]])
                        op("act", lambda e, j=j, c=c, q0=q0, ws=ws, h=h, psl=psl: e.activation(out=Wt[ws][:, q0:256], in_=negMbc[psl][:, q0:256],
                                                                                     func=AF.Exp, bias=gtok[:, j * 4 + h:j * 4 + h + 1]),
                           reads=[bnb[psl], bgtok], writes=[bWt[ws]])
                        op("dve", lambda e, P=P, j=j, q0=q0, ws=ws, sb_=sb_: e.tensor_tensor(out=P[:, j, q0:256], in0=PB[sb_][:, q0:256],
                                                                                            in1=Wt[ws][:, q0:256], op=ALU.mult),
                           reads=[bPB[sb_], bWt[ws]], writes=[bP[psl][j]])
                        if j >= 2 * c:
                            d0 = (j - 2 * c) * 128
                            op("pool", lambda e, P=P, j=j, d0=d0: e.tensor_tensor(out=P[:, j, d0:d0 + 128], in0=P[:, j, d0:d0 + 128],
                                                                                 in1=maskb[:], op=ALU.mult),
                               reads=[bP[psl][j], bC], writes=[bP[psl][j]])
                    for ii in range(2):
                        i = 2 * c + ii
                        k = tcnt % 2
                        tcnt += 1
                        pb_ = nextbank([4, 5, 6, 7])

                        def pvm(e, P=P, ii=ii, i=i, pb_=pb_):
                            r = None
                            for j in range(i + 1):
                                r = e.matmul(PB[pb_][:, 0:257], lhsT=P[:, j, ii * 128:(ii + 1) * 128], rhs=vm[:, j, :],
                                             start=(j == 0), stop=(j == i))
                            return r
                        op("pe", pvm, reads=[bP[psl][j] for j in range(i + 1)] + [bvm[j] for j in range(i // 2 + 1)], writes=[bPB[pb_]])
                        bk = nextbank([0, 1])

                        def mmo(e, i=i, bk=bk):
                            r = None
                            for kc in range(8):
                                r = e.matmul(PB[bk][:, 0:256], lhsT=hT[:, kc, i * 128:(i + 1) * 128], rhs=wml[:, kc, 512:768],
                                             start=(kc == 0), stop=(kc == 7))
                            return r
                        op("pe", mmo, reads=[bwml, bhT[i]], writes=[bPB[bk]])
                        op("act", lambda e, k=k, bk=bk: e.activation(out=og[k][:], in_=PB[bk][:, 0:256], func=AF.Sigmoid),
                           reads=[bPB[bk]], writes=[bog[k]])
                        op("dve", lambda e, k=k, pb_=pb_: e.tensor_copy(out=dens[k][:, 0:1], in_=PB[pb_][:, 256:257]),
                           reads=[bPB[pb_]], writes=[bden[k]])
                        op("dve", lambda e, k=k: e.scalar_tensor_tensor(out=dens[k][:, 0:1], in0=dens[k][:, 0:1], scalar=-1.0, in1=dens[k][:, 0:1],
                                                                        op0=ALU.mult, op1=ALU.max),
                           reads=[bden[k]], writes=[bden[k]])
                        op("dve", lambda e, k=k, i=i, h=h: e.tensor_tensor(out=dens[k][:, 0:1], in0=dens[k][:, 0:1], in1=etok[:, i * 4 + h:i * 4 + h + 1],
                                                                          op=ALU.max),
                           reads=[bden[k], betok], writes=[bden[k]])
                        op("dve", lambda e, k=k: e.reciprocal(out=dens[k][:, 0:1], in_=dens[k][:, 0:1]), reads=[bden[k]], writes=[bden[k]])
                        op("dve", lambda e, k=k, pb_=pb_: e.tensor_scalar(out=hn[k][:], in0=PB[pb_][:, 0:256], scalar1=dens[k][:, 0:1], scalar2=None,
                                                                         op0=ALU.mult),
                           reads=[bPB[pb_], bden[k]], writes=[bhn[k]])
                        op("act", lambda e, k=k: e.activation(out=junk[:, 0:256], in_=hn[k][:], func=AF.Square, accum_out=dens[k][:, 1:2]),
                           reads=[bhn[k]], writes=[bden[k], bjunk])
                        op("dve", lambda e, k=k: e.tensor_scalar(out=dens[k][:, 1:2], in0=dens[k][:, 1:2], scalar1=1.0 / 256.0, scalar2=LN_EPS,
                                                                 op0=ALU.mult, op1=ALU.add),
                           reads=[bden[k]], writes=[bden[k]])
                        op("pool", lambda e, k=k: e.tensor_tensor(out=dens[k][:, 1:2], in0=dens[k][:, 1:2], in1=cm05[:, 0:1], op=ALU.pow),
                           reads=[bden[k], bC], writes=[bden[k]])
                        op("dve", lambda e, k=k, h=h: e.scalar_tensor_tensor(out=hn[k][:], in0=hn[k][:], scalar=dens[k][:, 1:2],
                                                                            in1=gml_bc[:], op0=ALU.mult, op1=ALU.mult),
                           reads=[bhn[k], bden[k], bgml], writes=[bhn[k]])
                        op("pool", lambda e, k=k: e.tensor_tensor(out=ymt[k][:], in0=hn[k][:], in1=og[k][:], op=ALU.mult),
                           reads=[bhn[k], bog[k]], writes=[bymt[k]])
                        bk = nextbank([0, 1])
                        ptb = PB[bk].bitcast(BF16)

                        def trym(e, k=k, ptb=ptb):
                            r = None
                            for ee in range(2):
                                r = e.transpose(out=ptb[:, ee * 128:(ee + 1) * 128], in_=ymt[k][:, ee * 128:(ee + 1) * 128], identity=identb[:])
                            return r
                        op("pe", trym, reads=[bymt[k], bC], writes=[bPB[bk]])
                        op("act", lambda e, h=h, i=i, ptb=ptb: e.activation(out=ymT[:, 2 * h:2 * h + 2, i * 128:(i + 1) * 128],
                                                                           in_=ptb[:, 0:256].rearrange("p (a b) -> p a b", a=2), func=AF.Copy),
                           reads=[bPB[bk]], writes=[bymT[i]])
            SC.barrier()
            A.pop()
            if b == 0:
                dump("ymT", ymT.rearrange("p a b -> p (a b)"), bymT, [128, 8 * SEQ], BF16)

            A.push()
            yT = A.alloc([128, 8, SEQ], BF16)
            byT = [Buf(f"yT{c}") for c in range(4)]
            wmg = [A.alloc([128, 8, 512], BF16) for _ in range(2)]
            bwmg = [Buf("wmg0"), Buf("wmg1")]
            sgA = [A.alloc([128, 512], F32) for _ in range(2)]
            sgB = [A.alloc([128, 512], F32) for _ in range(2)]
            t1 = [A.alloc([128, 512], F32) for _ in range(2)]
            t2 = [A.alloc([128, 512], F32) for _ in range(2)]
            bsgA = [Buf("sgA0"), Buf("sgA1")]
            bsgB = [Buf("sgB0"), Buf("sgB1")]
            bt1 = [Buf("t10"), Buf("t11")]
            bt2 = [Buf("t20"), Buf("t21")]
            dcnt = 0
            allb = [0, 1, 2, 3, 4, 5, 6, 7]
            for fc in range(8):
                s = fc % 2
                dma("sp", wmg[s][:].rearrange("p a b -> p (a b)"), wmg_b[fc * 128:(fc + 1) * 128, :], [bDR], [bwmg[s]], f"wmg{s}")
                for c in range(4):
                    k = dcnt % 2
                    dcnt += 1
                    banks = []
                    for which, src, srcb in ((0, hT, hbufs(4 * c, 4 * c + 4)), (1, hT, hbufs(4 * c, 4 * c + 4)),
                                             (2, yaT, byaT[2 * c:2 * c + 2]), (3, ymT, bymT[4 * c:4 * c + 4])):
                        bk = nextbank(allb)
                        banks.append(bk)

                        def mmd(e, s=s, c=c, which=which, src=src, bk=bk):
                            r = None
                            for kc in range(8):
                                r = e.matmul(PB[bk][:, :], lhsT=wmg[s][:, kc, which * 128:(which + 1) * 128],
                                             rhs=src[:, kc, c * 512:(c + 1) * 512], start=(kc == 0), stop=(kc == 7))
                            return r
                        op("pe", mmd, reads=[bwmg[s]] + list(srcb), writes=[bPB[bk]])
                    op("act", lambda e, k=k, bk=banks[0]: e.activation(out=sgA[k][:], in_=PB[bk][:, :], func=AF.Sigmoid),
                       reads=[bPB[banks[0]]], writes=[bsgA[k]])
                    op("act", lambda e, k=k, bk=banks[1]: e.activation(out=sgB[k][:], in_=PB[bk][:, :], func=AF.Sigmoid),
                       reads=[bPB[banks[1]]], writes=[bsgB[k]])
                    op("dve", lambda e, k=k, bk=banks[2]: e.tensor_tensor(out=t1[k][:], in0=PB[bk][:, :], in1=sgA[k][:], op=ALU.mult),
                       reads=[bPB[banks[2]], bsgA[k]], writes=[bt1[k]])
                    op("dve", lambda e, k=k, bk=banks[3]: e.tensor_tensor(out=t2[k][:], in0=PB[bk][:, :], in1=sgB[k][:], op=ALU.mult),
                       reads=[bPB[banks[3]], bsgB[k]], writes=[bt2[k]])
                    op("pool", lambda e, k=k, fc=fc, c=c: e.tensor_tensor(out=yT[:, fc, c * 512:(c + 1) * 512], in0=t1[k][:], in1=t2[k][:], op=ALU.add),
                       reads=[bt1[k], bt2[k]], writes=[byT[c]])
            SC.barrier()
            if b == 0:
                dump("yT", yT.rearrange("p a b -> p (a b)"), byT, [128, 8 * SEQ], BF16)
            wo = hT_raw[:, 0:4096].bitcast(BF16).rearrange("p (a b) -> p a b", a=8)
            ttile = hT_raw[:, 4096:5120]
            l1g = hT_raw[:, 5120:6144]
            l1b = hT_raw[:, 6144:7168]
            gt1 = hT_raw[:, 7168:8192]
            xts = [A.alloc([128, D], F32) for _ in range(2)]
            bwo = Buf("wo")
            btt = Buf("tt")
            bl1 = Buf("l1")
            dma("sp", wo.rearrange("p a b -> p (a b)"), wout_b[:, :], [bDR], [bwo], "wo")
            dma("sp", l1g, ln1g_d[0:1, :].partition_broadcast(128), [], [bl1], "l1")
            dma("sp", l1b, ln1b_d[0:1, :].partition_broadcast(128), [], [bl1], "l1")
            dma("sp", gt1, gt_d[b:b + 1, 0:1024].partition_broadcast(128), [bGT], [bl1], "l1")
            for i in range(NT):
                k = i % 2
                r0 = b * SEQ + i * 128
                dma("sp", xts[k][:], x_d[r0:r0 + 128, :], [], [bxts[k]], f"xt{k}")
                for half in range(2):
                    bk = nextbank(allb)

                    def mmo2(e, i=i, half=half, bk=bk):
                        r = None
                        for kc in range(8):
                            r = e.matmul(PB[bk][:, :], lhsT=yT[:, kc, i * 128:(i + 1) * 128], rhs=wo[:, kc, half * 512:(half + 1) * 512],
                                         start=(kc == 0), stop=(kc == 7))
                        return r
                    op("pe", mmo2, reads=[byT[i // 4], bwo], writes=[bPB[bk]])
                    op("dve", lambda e, half=half, bk=bk: e.tensor_tensor(out=ttile[:, half * 512:(half + 1) * 512], in0=PB[bk][:, :],
                                                                         in1=gt1[:, half * 512:(half + 1) * 512], op=ALU.mult),
                       reads=[bPB[bk], bl1], writes=[btt])
                op("dve", lambda e, k=k: e.scalar_tensor_tensor(out=xts[k][:], in0=xts[k][:], scalar=ALPHA, in1=ttile, op0=ALU.mult, op1=ALU.add),
                   reads=[bxts[k], btt], writes=[bxts[k]])
                mv, rs, bm, br = ln_stats(xts[k], bxts[k])
                op("dve", lambda e, k=k, mv=mv, rs=rs: e.tensor_scalar(out=xts[k][:], in0=xts[k][:], scalar1=mv[:, 0:1], scalar2=rs[:],
                                                                      op0=ALU.subtract, op1=ALU.mult),
                   reads=[bxts[k], bm, br], writes=[bxts[k]])
                op("pool", lambda e, k=k: e.tensor_tensor(out=xts[k][:], in0=xts[k][:], in1=l1g, op=ALU.mult), reads=[bxts[k], bl1], writes=[bxts[k]])
                op("pool", lambda e, k=k: e.tensor_tensor(out=xts[k][:], in0=xts[k][:], in1=l1b, op=ALU.add), reads=[bxts[k], bl1], writes=[bxts[k]])
                dst = x1_d if do_peer else out_d
                dma("sp", dst[r0:r0 + 128, :], xts[k][:], [bxts[k]], [bXO[k]], f"xo{k}")
            SC.barrier()
            A.pop()
        A.pop()
        SC.barrier()

        if do_peer:
            A.push()
            RB = [0, 1, 2, 3, 4, 5]
            keysT = A.alloc([128, 256], F32)
            l2g = A.alloc([128, D], F32)
            l2b = A.alloc([128, D], F32)
            gt2 = A.alloc([128, D], F32)
            bl2 = Buf("l2")
            bgt2 = Buf("gt2")
            xps = [A.alloc([128, D], F32) for _ in range(2)]
            bxps = [Buf("xp0"), Buf("xp1")]
            xn2 = A.alloc([128, D], F32)
            bxn2 = Buf("xn2")
            h2T = A.alloc([128, 8, 128], F32)
            h2Tb = A.alloc([128, 8, 128], BF16)
            bh2 = Buf("h2T")
            bh2b = Buf("h2Tb")
            wq = [A.alloc([128, 8, 128], F32) for _ in range(2)]
            bwq = [Buf("wq0"), Buf("wq1")]
            qTh = [A.alloc([128, 128], F32) for _ in range(2)]
            bqTh = [Buf("qTh0"), Buf("qTh1")]
            s_sb = A.alloc([128, 8, 2, 128], F32)
            bs = Buf("s_sb")
            m16 = A.alloc([128, 8, 2, 16], F32)
            bm16 = Buf("m16")
            wk1 = A.alloc([128, 128], F32)
            bwk1 = Buf("wk1")
            cand = A.alloc([128, 8, 16, 16], F32)
            bcand = Buf("cand")
            wk2 = A.alloc([128, 256], F32)
            bwk2 = Buf("wk2")
            c16 = A.alloc([128, 8, 16], F32)
            bc16 = Buf("c16")
            negthr = A.alloc([128, 8], F32)
            zz = A.alloc([128, 8], F32)
            biasE = A.alloc([128, 8], F32)
            bsm = Buf("small")
            Sp = [A.alloc([128, 2048], F32) for _ in range(3)]
            Ep = [A.alloc([128, 2048], BF16) for _ in range(3)]
            Mk = [A.alloc([128, 2048], BF16) for _ in range(2)]
            bSp = [Buf("Sp0"), Buf("Sp1"), Buf("Sp2")]
            bEp = [Buf("Ep0"), Buf("Ep1"), Buf("Ep2")]
            bMk = [Buf("Mk0"), Buf("Mk1"), Buf("Mk2")]
            gstate = [0, 0]
            acc = A.alloc([128, NEXP], BF16)
            bacc = [Buf(f"acc{i}") for i in range(8)]
            ub = [A.alloc([128, 8, 512], BF16) for _ in range(3)]
            bub = [Buf("ub0"), Buf("ub1"), Buf("ub2")]
            vb = [A.alloc([128, 4, D], BF16) for _ in range(3)]
            bvb = [Buf("vb0"), Buf("vb1"), Buf("vb2")]
            gel = [A.alloc([128, 512], F32) for _ in range(2)]
            bgel = [Buf("gel0"), Buf("gel1"), Buf("gel2")]
            Wc = [A.alloc([128, 512], BF16) for _ in range(2)]
            bWc = [Buf("Wc0"), Buf("Wc1"), Buf("Wc2")]
            WT = [A.alloc([128, 4, 128], BF16) for _ in range(2)]
            bWT = [Buf("WT0"), Buf("WT1"), Buf("WT2")]
            tt2 = A.alloc([128, D], F32)
            btt2 = Buf("tt2")
            bPO = [Buf("po0"), Buf("po1")]
            dma("sp", keysT[:], keysT_d[:, :], [], [bl2], "l2")
            dma("sp", l2g[:], ln2g_d[0:1, :].partition_broadcast(128), [], [bl2], "l2")
            dma("sp", l2b[:], ln2b_d[0:1, :].partition_broadcast(128), [], [bl2], "l2")
            wpq3 = wpq_d.rearrange("r (a b) -> r a b", a=8)
            v_b3 = v_b.rearrange("(g p) n -> p g n", p=128)
            gcnt = 0
            ecnt2 = 0
            import os as _os
            _PT = int(_os.environ.get("PEER_TILES", NSEQ * NT))
            _PS = int(_os.environ.get("PEER_STAGE", 9))
            for ti in range(min(_PT, NSEQ * NT)):
                b = ti // NT
                k = ti % 2
                r0 = ti * 128
                if ti % NT == 0:
                    dma("sp", gt2[:], gt_d[b:b + 1, 1024:2048].partition_broadcast(128), [bGT], [bgt2], "gt2")
                dma("sp", xps[k][:], x1_d[r0:r0 + 128, :], [bXO[0], bXO[1]], [bxps[k]], f"xp{k}")
                mv, rs, bm, br = ln_stats(xps[k], bxps[k])
                op("dve", lambda e, k=k, mv=mv, rs=rs: e.tensor_scalar(out=xn2[:], in0=xps[k][:], scalar1=mv[:, 0:1], scalar2=rs[:],
                                                                      op0=ALU.subtract, op1=ALU.mult),
                   reads=[bxps[k], bm, br], writes=[bxn2])
                for hf in range(2):
                    bk = nextbank(RB)

                    def trp(e, hf=hf, bk=bk):
                        for cc in range(4):
                            c = hf * 4 + cc
                            e.transpose(out=PB[bk][:, cc * 128:(cc + 1) * 128], in_=xn2[:, c * 128:(c + 1) * 128], identity=identf[:])
                    op("pe", trp, reads=[bxn2, bC], writes=[bPB[bk]])
                    for cc in range(4):
                        c = hf * 4 + cc
                        if cc % 2 == 0:
                            op("act", lambda e, c=c, cc=cc, bk=bk, b=b: e.activation(out=h2T[:, c, :], in_=PB[bk][:, cc * 128:(cc + 1) * 128],
                                                                                    func=AF.Identity, scale=modcol(4, c, b), bias=modcol(3, c, b)),
                               reads=[bPB[bk], bmodT], writes=[bh2])
                        else:
                            op("dve", lambda e, c=c, cc=cc, bk=bk, b=b: e.tensor_scalar(out=h2T[:, c, :], in0=PB[bk][:, cc * 128:(cc + 1) * 128],
                                                                                       scalar1=modcol(4, c, b), scalar2=modcol(3, c, b),
                                                                                       op0=ALU.mult, op1=ALU.add),
                               reads=[bPB[bk], bmodT], writes=[bh2])
                op("pool", lambda e: e.tensor_copy(out=h2Tb[:].rearrange("p a b -> p (a b)"), in_=h2T[:].rearrange("p a b -> p (a b)")),
                   reads=[bh2], writes=[bh2b])
                sbk = None
                for h in range(8 if _PS >= 2 else 0):
                    ws = h % 2
                    dma("sp", wq[ws][:], wpq3[h * 128:(h + 1) * 128, :, :], [], [bwq[ws]], f"wq{ws}")
                    bk = nextbank(RB)

                    def mmq(e, ws=ws, bk=bk):
                        for kc in range(8):
                            e.matmul(PB[bk][:, 0:128], lhsT=wq[ws][:, kc, :], rhs=h2T[:, kc, :], start=(kc == 0), stop=(kc == 7))
                    op("pe", mmq, reads=[bwq[ws], bh2], writes=[bPB[bk]])
                    op("act", lambda e, ws=ws, bk=bk: e.activation(out=qTh[ws][:], in_=PB[bk][:, 0:128], func=AF.Copy),
                       reads=[bPB[bk]], writes=[bqTh[ws]])
                    if h % 2 == 0:
                        sbk = nextbank(RB)

                    def mms(e, ws=ws, sbk=sbk, h=h):
                        o0 = (h % 2) * 256
                        e.matmul(PB[sbk][:, o0:o0 + 256], lhsT=qTh[ws][:, :], rhs=keysT[:, :], start=True, stop=True)
                    op("pe", mms, reads=[bqTh[ws], bl2], writes=[bPB[sbk]])
                    if h % 2 == 1:
                        op("dve", lambda e, h=h, sbk=sbk: e.tensor_copy(out=s_sb[:, h - 1:h + 1, :, :].rearrange("p a b c -> p (a b c)"), in_=PB[sbk][:, :]),
                           reads=[bPB[sbk]], writes=[bs])
                if _PS < 3:
                    dma("sp", out_d[r0:r0 + 128, :], xps[k][:], [bxps[k], bs, bh2b], [bPO[k]], f"po{k}")
                    continue
                for h in range(8):
                    for half in range(2):
                        op("dve", lambda e, h=h, half=half: e.max(out=m16[:, h, half, 0:8], in_=s_sb[:, h, half, :]), reads=[bs], writes=[bm16])
                        op("dve", lambda e, h=h, half=half: e.match_replace(out=wk1[:], in_to_replace=m16[:, h, half, 0:8], in_values=s_sb[:, h, half, :],
                                                                           imm_value=-1e30),
                           reads=[bs, bm16], writes=[bwk1])
                        op("dve", lambda e, h=h, half=half: e.max(out=m16[:, h, half, 8:16], in_=wk1[:]), reads=[bwk1], writes=[bm16])
                op("dve", lambda e: e.tensor_tensor(out=cand[:], in0=m16[:, :, 0, :].unsqueeze(3).to_broadcast([128, 8, 16, 16]),
                                                    in1=m16[:, :, 1, :].unsqueeze(2).to_broadcast([128, 8, 16, 16]), op=ALU.add),
                   reads=[bm16], writes=[bcand])
                for h in range(8):
                    ch2 = cand[:, h, :, :].rearrange("p a b -> p (a b)")
                    op("dve", lambda e, h=h, ch2=ch2: e.max(out=c16[:, h, 0:8], in_=ch2), reads=[bcand], writes=[bc16])
                    op("dve", lambda e, h=h, ch2=ch2: e.match_replace(out=wk2[:], in_to_replace=c16[:, h, 0:8], in_values=ch2, imm_value=-1e30),
                       reads=[bcand, bc16], writes=[bwk2])
                    op("dve", lambda e, h=h: e.max(out=c16[:, h, 8:16], in_=wk2[:]), reads=[bwk2], writes=[bc16])
                op("dve", lambda e: e.tensor_scalar(out=negthr[:], in0=c16[:, :, 15], scalar1=-1.0, scalar2=None, op0=ALU.mult),
                   reads=[bc16], writes=[bsm])
                for h in range(8):
                    op("act", lambda e, h=h: e.activation(out=junk[:, 0:16], in_=c16[:, h, :], func=AF.Exp, bias=negthr[:, h:h + 1],
                                                          accum_out=zz[:, h:h + 1]),
                       reads=[bc16, bsm], writes=[bsm, bjunk])
                op("act", lambda e: e.activation(out=zz[:], in_=zz[:], func=AF.Ln), reads=[bsm], writes=[bsm])
                op("dve", lambda e: e.tensor_tensor(out=biasE[:], in0=negthr[:], in1=zz[:], op=ALU.subtract), reads=[bsm], writes=[bsm])
                if _PS < 4:
                    dma("sp", out_d[r0:r0 + 128, :], xps[k][:], [bxps[k], bsm], [bPO[k]], f"po{k}")
                    continue
                def G_build(pc):
                    nonlocal_g = gstate
                    sl = slice(pc * 2048, (pc + 1) * 2048)
                    pend = None

                    def finish(h, g):
                        gm = g % 2
                        if h == 0:
                            op("dve", lambda e: e.scalar_tensor_tensor(out=acc[:, sl], in0=Sp[g][:], scalar=c16[:, h, 15:16], in1=Ep[g][:],
                                                                       op0=ALU.is_ge, op1=ALU.mult),
                               reads=[bSp[g], bEp[g], bc16], writes=[bacc[pc]])
                        else:
                            op("dve", lambda e: e.scalar_tensor_tensor(out=Mk[gm][:], in0=Sp[g][:], scalar=c16[:, h, 15:16], in1=Ep[g][:],
                                                                       op0=ALU.is_ge, op1=ALU.mult),
                               reads=[bSp[g], bEp[g], bc16], writes=[bMk[gm]])
                            op("pool", lambda e: e.tensor_tensor(out=acc[:, sl], in0=acc[:, sl], in1=Mk[gm][:], op=ALU.add),
                               reads=[bMk[gm], bacc[pc]], writes=[bacc[pc]])
                    for h in range(8):
                        g = nonlocal_g[0] % 3
                        nonlocal_g[0] += 1
                        op("dve", lambda e, g=g, h=h: e.tensor_tensor(
                            out=Sp[g][:].rearrange("p (a b) -> p a b", a=16),
                            in0=s_sb[:, h, 0, pc * 16:(pc + 1) * 16].unsqueeze(2).to_broadcast([128, 16, 128]),
                            in1=s_sb[:, h, 1, :].unsqueeze(1).to_broadcast([128, 16, 128]), op=ALU.add),
                           reads=[bs], writes=[bSp[g]])
                        op("act", lambda e, g=g, h=h: e.activation(out=Ep[g][:], in_=Sp[g][:], func=AF.Exp, bias=biasE[:, h:h + 1]),
                           reads=[bSp[g], bsm], writes=[bEp[g]])
                        if pend is not None:
                            finish(*pend)
                        pend = (h, g)
                    finish(*pend)

                def X_experts(pc):
                    for ec in range(4 * pc, 4 * pc + 4):
                        u = gstate[1] % 3
                        w = gstate[1] % 2
                        gstate[1] += 1
                        dma("sp", ub[u][:].rearrange("p a b -> p (a b)"), ut_b[ec * 128:(ec + 1) * 128, :], [bDR], [bub[u]], f"ub{u}")
                        dma("sp", vb[u][:], v_b3[:, ec * 4:(ec + 1) * 4, :], [bDR], [bvb[u]], f"vb{u}")
                        bk = nextbank(RB)

                        def mma(e, u=u, bk=bk):
                            for kc in range(8):
                                e.matmul(PB[bk][:, :], lhsT=h2Tb[:, kc, :], rhs=ub[u][:, kc, :], start=(kc == 0), stop=(kc == 7))
                        op("pe", mma, reads=[bh2b, bub[u]], writes=[bPB[bk]])
                        op("act", lambda e, w=w, bk=bk: e.activation(out=gel[w][:], in_=PB[bk][:, :], func=AF.Gelu), reads=[bPB[bk]], writes=[bgel[w]])
                        op("dve", lambda e, w=w, ec=ec: e.tensor_tensor(out=Wc[w][:], in0=gel[w][:], in1=acc[:, ec * 512:(ec + 1) * 512], op=ALU.mult),
                           reads=[bgel[w], bacc[ec // 4]], writes=[bWc[w]])
                        tb = nextbank(RB)
                        ptb = PB[tb].bitcast(BF16)

                        def trw(e, w=w, ptb=ptb):
                            for a in range(4):
                                e.transpose(out=ptb[:, a * 128:(a + 1) * 128], in_=Wc[w][:, a * 128:(a + 1) * 128], identity=identb[:])
                        op("pe", trw, reads=[bWc[w], bC], writes=[bPB[tb]])
                        op("act", lambda e, w=w, ptb=ptb: e.activation(out=WT[w][:].rearrange("p a b -> p (a b)"), in_=ptb[:, 0:512], func=AF.Copy),
                           reads=[bPB[tb]], writes=[bWT[w]])

                        def mmv3(e, u=u, w=w, ec=ec):
                            for a in range(4):
                                for half in range(2):
                                    e.matmul(PB[6 + half][:, :], lhsT=WT[w][:, a, :], rhs=vb[u][:, a, half * 512:(half + 1) * 512],
                                             start=(ec == 0 and a == 0), stop=(ec == 31 and a == 3))
                        op("pe", mmv3, reads=[bWT[w], bvb[u]], writes=[bPB[6], bPB[7]])

                if _PS < 5:
                    for pc in range(8):
                        G_build(pc)
                    dma("sp", out_d[r0:r0 + 128, :], xps[k][:], [bxps[k]] + bacc, [bPO[k]], f"po{k}")
                    continue
                G_build(0)
                for pc in range(1, 8):
                    G_build(pc)
                    X_experts(pc - 1)
                X_experts(7)
                for half in range(2):
                    op("dve", lambda e, half=half: e.tensor_tensor(out=tt2[:, half * 512:(half + 1) * 512], in0=PB[6 + half][:, :],
                                                                  in1=gt2[:, half * 512:(half + 1) * 512], op=ALU.mult),
                       reads=[bPB[6 + half], bgt2], writes=[btt2])
                op("dve", lambda e, k=k: e.scalar_tensor_tensor(out=xps[k][:], in0=xps[k][:], scalar=ALPHA, in1=tt2[:], op0=ALU.mult, op1=ALU.add),
                   reads=[bxps[k], btt2], writes=[bxps[k]])
                mv, rs, bm, br = ln_stats(xps[k], bxps[k])
                op("dve", lambda e, k=k, mv=mv, rs=rs: e.tensor_scalar(out=xps[k][:], in0=xps[k][:], scalar1=mv[:, 0:1], scalar2=rs[:],
                                                                      op0=ALU.subtract, op1=ALU.mult),
                   reads=[bxps[k], bm, br], writes=[bxps[k]])
                op("pool", lambda e, k=k: e.tensor_tensor(out=xps[k][:], in0=xps[k][:], in1=l2g[:], op=ALU.mult), reads=[bxps[k], bl2], writes=[bxps[k]])
                op("pool", lambda e, k=k: e.tensor_tensor(out=xps[k][:], in0=xps[k][:], in1=l2b[:], op=ALU.add), reads=[bxps[k], bl2], writes=[bxps[k]])
                dma("sp", out_d[r0:r0 + 128, :], xps[k][:], [bxps[k]], [bPO[k]], f"po{k}")
            A.pop()
        SC.barrier()
        print("arena peak", A.peak, "ops", SC.nops, "waits", SC.nwait)
        SC.emit_all()
    return nc


def _kmaj(w):
    return np.ascontiguousarray(w.reshape(8, 128, -1).transpose(1, 0, 2))


def prep_shared(inp, do_peer=True):
    f = np.float32
    w_in = np.asarray(inp["w_in"][0], f)
    da_q, da_k, da_v = w_in[:, 0:1024], w_in[:, 1024:2048], w_in[:, 2048:3072]
    ml_q, ml_k = w_in[:, 3072:3584], w_in[:, 3584:4096]
    ml_v, ml_o = w_in[:, 4096:5120], w_in[:, 5120:6144]
    ml_if = w_in[:, 6144:6152]
    g_attn, g_ml = w_in[:, 6152:7176], w_in[:, 7176:8200]
    wba = np.asarray(inp["w_br_attn"][0], f)
    wbm = np.asarray(inp["w_br_mlstm"][0], f)
    sh = {}
    sh["w_ada"] = _kmaj(np.asarray(inp["w_ada"][0], f)).reshape(128, 8 * 6144)
    sh["b_ada"] = np.asarray(inp["b_ada"], f).reshape(1, 6144)
    sh["w_da"] = np.stack([np.concatenate([_kmaj(da_q[:, h * 128:(h + 1) * 128]), _kmaj(da_k[:, h * 128:(h + 1) * 128]),
                                           _kmaj(da_v[:, h * 128:(h + 1) * 128])], axis=2) for h in range(8)]).reshape(1024, 3072)
    sh["w_ml"] = np.stack([np.concatenate([_kmaj(ml_q[:, h * 128:(h + 1) * 128]), _kmaj(ml_k[:, h * 128:(h + 1) * 128]),
                                           _kmaj(ml_v[:, h * 256:(h + 1) * 256]), _kmaj(ml_o[:, h * 256:(h + 1) * 256])], axis=2)
                           for h in range(4)]).reshape(512, 6144)
    sh["w_if"] = _kmaj(ml_if).reshape(128, 64)
    sh["w_mg"] = np.stack([np.concatenate([_kmaj(g_attn[:, c * 128:(c + 1) * 128]), _kmaj(g_ml[:, c * 128:(c + 1) * 128]),
                                           _kmaj(wba[:, c * 128:(c + 1) * 128]), _kmaj(wbm[:, c * 128:(c + 1) * 128])], axis=2)
                           for c in range(8)]).reshape(1024, 4096)
    sh["w_out"] = _kmaj(np.asarray(inp["w_out"][0], f)).reshape(128, 8192)
    wq = np.asarray(inp["peer_wq"][0], f)
    sh["w_pq"] = np.stack([_kmaj(wq[:, h * 128:(h + 1) * 128]) for h in range(8)]).reshape(1024, 1024)
    if do_peer:
        u = np.asarray(inp["peer_u"][0], f)
        sh["UT"] = np.stack([_kmaj(np.ascontiguousarray(u[ec * 512:(ec + 1) * 512, :].T)) for ec in range(32)]).reshape(4096, 4096)
        sh["V"] = np.ascontiguousarray(np.asarray(inp["peer_v"][0], f))
    else:
        sh["UT"] = np.zeros((4096, 4096), f)
        sh["V"] = np.zeros((NEXP, D), f)
    sh["b_ifT"] = np.ascontiguousarray(np.asarray(inp["b_if"][0], f).T)
    sh["conv_wT"] = np.ascontiguousarray(np.asarray(inp["conv_w"][0], f).reshape(4, 8, 128).transpose(2, 1, 0)).reshape(128, 32)
    sh["conv_bT"] = np.ascontiguousarray(np.asarray(inp["conv_b"][0], f).reshape(8, 128).T)
    sh["da_lambda"] = np.asarray(inp["da_lambda"][0], f).reshape(1, 256)
    sh["subln_g"] = np.asarray(inp["da_subln_g"][0], f).reshape(1, 128)
    sh["ml_norm_g"] = np.asarray(inp["ml_norm_g"][0], f).reshape(1, D)
    for nm in ("ln1_g", "ln1_b", "ln2_g", "ln2_b"):
        sh[nm] = np.asarray(inp[nm][0], f).reshape(1, D)
    kt = np.ascontiguousarray(np.asarray(inp["peer_keys"][0], f).transpose(0, 2, 1))
    kz = np.zeros((128, 256), f)
    kz[0:64, 0:128] = kt[0]
    kz[64:128, 128:256] = kt[1]
    sh["keysT"] = kz
    sh["ident"] = np.eye(128, dtype=f)
    kk = np.arange(128)
    sh["cmask"] = (kk[None, :] >= kk[:, None]).astype(f)
    sel = np.zeros((4, 4, 128), f)
    for h in range(4):
        sel[h, h, :] = 1.0
    sh["sel4"] = sel.reshape(4, 512)
    return sh


def core_inputs(inp, sh, b0, nseq):
    f = np.float32
    m = dict(sh)
    m["x"] = np.ascontiguousarray(np.asarray(inp["x"][b0:b0 + nseq], f).reshape(nseq * SEQ, D))
    c = np.asarray(inp["c"][b0:b0 + nseq], f)
    m["cT"] = _kmaj(np.ascontiguousarray(c.T)).reshape(128, 8 * nseq)
    return m


_NC_CACHE = {}


def kernel(**inputs):
    if "full" not in _NC_CACHE:
        _NC_CACHE["full"] = build(NSEQ_FULL, True)
    nc = _NC_CACHE["full"]
    sh = prep_shared(inputs, True)
    in_maps = [core_inputs(inputs, sh, i * NSEQ_FULL, NSEQ_FULL) for i in range(NCORES)]
    res = run_bass_kernel_spmd(nc, in_maps, core_ids=list(range(NCORES)))
    out = np.concatenate([np.asarray(r["out"]).reshape(NSEQ_FULL, SEQ, D) for r in res.results], axis=0)
    return out.astype(np.float32)
```

```python
import contextlib
import math
import numpy as np
import concourse.bass as bass
import concourse.mybir as mybir
from concourse.bass_utils import run_bass_kernel_spmd

F32 = mybir.dt.float32
BF16 = mybir.dt.bfloat16
AF = mybir.ActivationFunctionType
ALU = mybir.AluOpType

D = 1024
SEQ = 2048
NT = SEQ // 128
NCORES = 8
BATCH = 32
NSEQ_FULL = BATCH // NCORES
ALPHA = 2.0 ** 0.25
LN_EPS = 1e-5
LAMBDA_INIT = 0.8 - 0.6 * math.exp(0.0)
NEXP = 16384


class Buf:
    __slots__ = ("name", "last_w", "readers")

    def __init__(self, name):
        self.name = name
        self.last_w = None
        self.readers = []


class _Rec:
    def __init__(self):
        self.calls = []

    def __getattr__(self, name):
        def f(*a, **kw):
            self.calls.append((name, a, kw))
            return None
        return f


class Sched:
    ENGS = ("pe", "act", "dve", "pool", "sp")

    def __init__(self, nc):
        self.nc = nc
        self.ops = {e: [] for e in self.ENGS}
        self.cnt = {e: 0 for e in self.ENGS}
        self.seen = {e: {} for e in self.ENGS}
        self.sems = {}
        self.dma_cnt = {}
        self.nops = 0
        self.nwait = 0

    def op(self, eng, emit, reads=(), writes=(), dma=None, n=1):
        deps = {}
        for b in reads:
            if b.last_w is not None:
                k, v, e = b.last_w
                if not (e == eng and k[0] == "eng" and False):
                    deps[k] = max(deps.get(k, 0), v)
        for b in writes:
            if b.last_w is not None:
                k, v, e = b.last_w
                if not (e == eng and k[0] == "eng" and dma is None):
                    deps[k] = max(deps.get(k, 0), v)
            for (k, v, e) in b.readers:
                if e == eng and k[0] == "eng" and dma is None:
                    continue
                deps[k] = max(deps.get(k, 0), v)
        waits = []
        seen = self.seen[eng]
        for k, v in deps.items():
            if seen.get(k, 0) >= v:
                continue
            seen[k] = v
            waits.append((k, v))
        if dma is None:
            self.cnt[eng] += 1
            key = ("eng", eng)
            tok = (key, self.cnt[eng], eng)
        else:
            key = ("dma", dma)
            self.dma_cnt[key] = self.dma_cnt.get(key, 0) + 16 * n
            tok = (key, self.dma_cnt[key], eng)
        for b in reads:
            b.readers.append(tok)
        for b in writes:
            b.last_w = tok
            b.readers = []
        rec = _Rec()
        emit(rec)
        assert len(rec.calls) >= 1
        if dma is not None:
            assert len(rec.calls) == n, (len(rec.calls), n)
        self.ops[eng].append((waits, rec.calls, key, dma is not None, n))
        self.nops += 1
        self.nwait += len(waits)
        return tok

    def barrier(self):
        cur = {("eng", e): self.cnt[e] for e in self.ENGS if self.cnt[e] > 0}
        cur.update(self.dma_cnt)
        for eng in self.ENGS:
            waits = []
            seen = self.seen[eng]
            for k, v in cur.items():
                if k == ("eng", eng) or seen.get(k, 0) >= v:
                    continue
                seen[k] = v
                waits.append((k, v))
            if waits:
                self.cnt[eng] += 1
                self.ops[eng].append((waits, [("nop", (), {})], ("eng", eng), False, 1))

    def emit_all(self):
        nc = self.nc
        keys = [("eng", e) for e in self.ENGS]
        for e in self.ENGS:
            for rec in self.ops[e]:
                if rec[2] not in keys:
                    keys.append(rec[2])
        for k in keys:
            self.sems[k] = nc.alloc_semaphore("s_" + "_".join(str(x) for x in k))
        with nc.Block() as block:
            def mk(ename):
                def body(eng):
                    for waits, calls, key, is_dma, n in self.ops[ename]:
                        for (k, v) in waits:
                            eng.wait_ge(self.sems[k], v)
                        s = self.sems[key]
                        r = None
                        for (name, a, kw) in calls:
                            r = getattr(eng, name)(*a, **kw)
                            if is_dma:
                                r.then_inc(s, 16)
                        if not is_dma:
                            r.then_inc(s, 1)
                return body
            block.tensor(mk("pe"))
            block.scalar(mk("act"))
            block.vector(mk("dve"))
            block.gpsimd(mk("pool"))
            block.sync(mk("sp"))


class Arena:
    def __init__(self, t, nbytes):
        self.t = t
        self.nbytes = nbytes
        self.off = 0
        self.stack = []
        self.peak = 0

    def push(self):
        self.stack.append(self.off)

    def pop(self):
        self.off = self.stack.pop()

    def alloc(self, shape, dt):
        esz = 4 if dt == F32 else 2
        n = 1
        for s in shape[1:]:
            n *= s
        nb = (n * esz + 63) // 64 * 64
        assert self.off + nb <= self.nbytes, ("arena overflow", self.off, nb, self.nbytes)
        a = self.t[0:shape[0], self.off // 4:(self.off + nb) // 4]
        self.off += nb
        self.peak = max(self.peak, self.off)
        if dt != F32:
            a = a.bitcast(dt)
        a = a[:, 0:n]
        if len(shape) == 3:
            a = a.rearrange("p (a b) -> p a b", a=shape[1])
        elif len(shape) == 4:
            a = a.rearrange("p (a b c) -> p a b c", a=shape[1], b=shape[2])
        return a


def build(NSEQ=NSEQ_FULL, do_peer=True, dbg=False):
    nc = bass.Bass("TRN2", target_bir_lowering=False)
    SC = Sched(nc)
    op = SC.op
    NTOK = NSEQ * SEQ

    def din(name, shape, dt=F32):
        return nc.dram_tensor(name, list(shape), dt, kind="ExternalInput").ap()

    def dscr(name, shape, dt):
        return nc.dram_tensor(name, list(shape), dt, kind="Internal").ap()

    x_d = din("x", [NTOK, D])
    cT_d = din("cT", [128, 8 * NSEQ])
    wada_d = din("w_ada", [128, 8 * 6144])
    bada_d = din("b_ada", [1, 6144])
    wda_d = din("w_da", [8 * 128, 3072])
    wml_d = din("w_ml", [4 * 128, 6144])
    wif_d = din("w_if", [128, 64])
    wmg_d = din("w_mg", [8 * 128, 4096])
    wout_d = din("w_out", [128, 8192])
    wpq_d = din("w_pq", [8 * 128, 1024])
    ut_d = din("UT", [32 * 128, 4096])
    v_d = din("V", [NEXP, D])
    bif_d = din("b_ifT", [4, 2])
    cw_d = din("conv_wT", [128, 32])
    cb_d = din("conv_bT", [128, 8])
    lam_d = din("da_lambda", [1, 256])
    subg_d = din("subln_g", [1, 128])
    mlg_d = din("ml_norm_g", [1, D])
    ln1g_d = din("ln1_g", [1, D])
    ln1b_d = din("ln1_b", [1, D])
    ln2g_d = din("ln2_g", [1, D])
    ln2b_d = din("ln2_b", [1, D])
    keysT_d = din("keysT", [128, 256])
    ident_d = din("ident", [128, 128])
    cmask_d = din("cmask", [128, 128])
    sel4_d = din("sel4", [4, 512])
    out_d = nc.dram_tensor("out", [NTOK, D], F32, kind="ExternalOutput").ap()

    wda_b = dscr("w_da_b", [8 * 128, 3072], BF16)
    wml_b = dscr("w_ml_b", [4 * 128, 6144], BF16)
    wif_b = dscr("w_if_b", [128, 64], BF16)
    wmg_b = dscr("w_mg_b", [8 * 128, 4096], BF16)
    wout_b = dscr("w_out_b", [128, 8192], BF16)
    ut_b = dscr("UT_b", [32 * 128, 4096], BF16)
    v_b = dscr("V_b", [NEXP, D], BF16)
    x1_d = dscr("x1_s", [NTOK, D], F32)
    gt_d = dscr("gt_s", [NSEQ, 2048], F32)

    es = contextlib.ExitStack()
    with es:
        ARENA_BYTES = 206 * 1024
        arena_t = es.enter_context(nc.sbuf_tensor("arena", [128, ARENA_BYTES // 4], F32))
        A = Arena(arena_t, ARENA_BYTES)
        psum_t = es.enter_context(nc.psum_tensor("psum", [128, 8, 512], F32))
        PB = [psum_t[:, i, :] for i in range(8)]
        bPB = [Buf(f"pb{i}") for i in range(8)]
        bank_rr = [0]

        def nextbank(banks):
            i = banks[bank_rr[0] % len(banks)]
            bank_rr[0] += 1
            return i

        def dma(eng, out, in_, reads, writes, key):
            return op(eng, lambda e: e.dma_start(out=out, in_=in_), reads=reads, writes=writes, dma=key)

        dbg_outs = {}

        def dump(name, ap, bufs, shape, dt):
            if not dbg:
                return
            t = nc.dram_tensor("d_" + name, list(shape), dt, kind="ExternalOutput").ap()
            dbg_outs[name] = t
            dma("sp", t, ap, list(bufs), [Buf("dbg")], "dbg_" + name)

        bOUT = Buf("out")
        bDR = Buf("dram_scratch")
        bGT = Buf("gt_scratch")
        bXO = [Buf("xo0"), Buf("xo1")]

        identf = A.alloc([128, 128], F32)
        identb = A.alloc([128, 128], BF16)
        maskb = A.alloc([128, 128], BF16)
        sel4 = A.alloc([4, 512], F32)
        cm05 = A.alloc([128, 8], F32)
        ones4 = A.alloc([4, 512], F32)
        modT = A.alloc([128, 48 * NSEQ], F32)
        neglam = A.alloc([128, 1], F32)
        gda_bc = A.alloc([128, 128], F32)
        bifT = A.alloc([4, 2], F32)
        nbf = A.alloc([4, 1], F32)
        cwT = A.alloc([128, 32], F32)
        cbT = A.alloc([128, 8], F32)
        wif = A.alloc([128, 8, 8], BF16)
        junk = A.alloc([128, 256], F32)
        bC = Buf("consts")
        bjunk = Buf("junk")
        NST = 4
        stt = [A.alloc([128, 12], F32) for _ in range(NST)]
        mvt = [A.alloc([128, 2], F32) for _ in range(NST)]
        rst = [A.alloc([128, 1], F32) for _ in range(NST)]
        bst = [Buf(f"st{i}") for i in range(NST)]
        bmv = [Buf(f"mv{i}") for i in range(NST)]
        brs = [Buf(f"rs{i}") for i in range(NST)]
        stk = [0]

        def ln_stats(src, bsrc):
            k = stk[0] % NST
            stk[0] += 1
            st, mv, rs = stt[k], mvt[k], rst[k]
            op("dve", lambda e: e.bn_stats(out=st[:, 0:6], in_=src[:, 0:512]), reads=[bsrc], writes=[bst[k]])
            op("dve", lambda e: e.bn_stats(out=st[:, 6:12], in_=src[:, 512:1024]), reads=[bsrc], writes=[bst[k]])
            op("dve", lambda e: e.bn_aggr(out=mv[:], in_=st[:]), reads=[bst[k]], writes=[bmv[k]])
            op("dve", lambda e: e.tensor_scalar(out=rs[:], in0=mv[:, 1:2], scalar1=LN_EPS, scalar2=None, op0=ALU.add),
               reads=[bmv[k]], writes=[brs[k]])
            op("pool", lambda e: e.tensor_tensor(out=rs[:], in0=rs[:], in1=cm05[:, 0:1], op=ALU.pow),
               reads=[brs[k], bC], writes=[brs[k]])
            return mv, rs, bmv[k], brs[k]

        A.push()
        tmpf = A.alloc([128, 128], F32)
        tmpm = A.alloc([128, 128], F32)
        lamt = A.alloc([128, 256], F32)
        lamp = A.alloc([128, 128], F32)
        lams = A.alloc([128, 2], F32)
        wif_f = A.alloc([128, 64], F32)
        btmp = Buf("tmp0")
        dma("sp", identf[:], ident_d[:, :], [], [bC], "c0")
        dma("sp", tmpm[:], cmask_d[:, :], [], [btmp], "c1")
        dma("sp", sel4[:], sel4_d[:, :], [], [bC], "c0")
        dma("sp", bifT[:], bif_d[:, :], [], [bC], "c0")
        dma("sp", cwT[:], cw_d[:, :], [], [bC], "c0")
        dma("sp", cbT[:], cb_d[:, :], [], [bC], "c0")
        dma("sp", gda_bc[:], subg_d[0:1, :].partition_broadcast(128), [], [bC], "c0")
        dma("sp", lamt[:], lam_d[0:1, :].partition_broadcast(128), [], [btmp], "c1")
        dma("sp", wif_f[:], wif_d[:, :], [], [btmp], "c1")
        op("pool", lambda e: e.tensor_copy(out=identb[:], in_=identf[:]), reads=[bC], writes=[bC])
        op("pool", lambda e: e.tensor_copy(out=maskb[:], in_=tmpm[:]), reads=[btmp], writes=[bC])
        op("pool", lambda e: e.memset(cm05[:], -0.5), writes=[bC])
        op("pool", lambda e: e.memset(ones4[:], 1.0), writes=[bC])
        op("pool", lambda e: e.tensor_copy(out=wif[:].rearrange("p a b -> p (a b)"), in_=wif_f[:]), reads=[btmp], writes=[bC])
        op("dve", lambda e: e.tensor_scalar(out=gda_bc[:], in0=gda_bc[:], scalar1=1.0 - LAMBDA_INIT, scalar2=None, op0=ALU.mult),
           reads=[bC], writes=[bC])
        op("dve", lambda e: e.tensor_scalar(out=nbf[:], in0=bifT[:, 1:2], scalar1=-1.0, scalar2=None, op0=ALU.mult),
           reads=[bC], writes=[bC])
        lt3 = lamt[:].rearrange("p (a b) -> p a b", a=4)
        op("dve", lambda e: e.tensor_tensor(out=lamp[:].rearrange("p (a b) -> p a b", a=2),
                                            in0=lt3[:, 0:4:2, :], in1=lt3[:, 1:4:2, :], op=ALU.mult),
           reads=[btmp], writes=[btmp])
        op("dve", lambda e: e.tensor_reduce(out=lams[:], in_=lamp[:].rearrange("p (a b) -> p a b", a=2),
                                            axis=mybir.AxisListType.X, op=ALU.add),
           reads=[btmp], writes=[btmp])
        op("act", lambda e: e.activation(out=lams[:], in_=lams[:], func=AF.Exp), reads=[btmp], writes=[btmp])
        op("dve", lambda e: e.tensor_tensor(out=neglam[:], in0=lams[:, 1:2], in1=lams[:, 0:1], op=ALU.subtract),
           reads=[btmp], writes=[bC])
        op("dve", lambda e: e.tensor_scalar(out=neglam[:], in0=neglam[:], scalar1=-LAMBDA_INIT, scalar2=None, op0=ALU.add),
           reads=[bC], writes=[bC])
        SC.barrier()
        A.pop()

        A.push()
        NCV = 3
        CW = 4096
        cvf = [A.alloc([128, CW], F32) for _ in range(NCV)]
        cvb = [A.alloc([128, CW], BF16) for _ in range(NCV)]
        bcvf = [Buf(f"cvf{i}") for i in range(NCV)]
        bcvb = [Buf(f"cvb{i}") for i in range(NCV)]
        kcv = [0]

        def convert(src, dst, R, C):
            for r0 in range(0, R, 128):
                for c0 in range(0, C, CW):
                    cw = min(CW, C - c0)
                    k = kcv[0] % NCV
                    kk = kcv[0]
                    kcv[0] += 1
                    dma("sp", cvf[k][:, 0:cw], src[r0:r0 + 128, c0:c0 + cw], [], [bcvf[k]], f"cvf{k}")
                    if kk % 2 == 0:
                        op("dve", lambda e, k=k, cw=cw: e.tensor_copy(out=cvb[k][:, 0:cw], in_=cvf[k][:, 0:cw]),
                           reads=[bcvf[k]], writes=[bcvb[k]])
                    else:
                        op("act", lambda e, k=k, cw=cw: e.activation(out=cvb[k][:, 0:cw], in_=cvf[k][:, 0:cw], func=AF.Copy),
                           reads=[bcvf[k]], writes=[bcvb[k]])
                    dma("pool", dst[r0:r0 + 128, c0:c0 + cw], cvb[k][:, 0:cw], [bcvb[k]], [bDR], f"cvb{k}")

        convert(wda_d, wda_b, 1024, 3072)
        convert(wml_d, wml_b, 512, 6144)
        convert(wmg_d, wmg_b, 1024, 4096)
        convert(wout_d, wout_b, 128, 8192)
        import os as _os0
        if do_peer and not int(_os0.environ.get("NOCONV", 0)):
            convert(ut_d, ut_b, 4096, 4096)
            convert(v_d, v_b, NEXP, D)
        SC.barrier()
        A.pop()

        A.push()
        siluT = A.alloc([128, 8, NSEQ], F32)
        modall = A.alloc([NSEQ, 6144], F32)
        badab = A.alloc([NSEQ, 6144], F32)
        wad = [A.alloc([128, 8, 512], F32) for _ in range(2)]
        bwad = [Buf("wad0"), Buf("wad1")]
        bsil = Buf("silu")
        bmod = Buf("modall")
        bmodT = Buf("modT")
        dma("sp", siluT[:].rearrange("p a b -> p (a b)"), cT_d[:, :], [], [bsil], "c1")
        dma("sp", badab[:], bada_d[0:1, :].partition_broadcast(NSEQ), [], [bmod], "c0")
        op("act", lambda e: e.activation(out=siluT[:].rearrange("p a b -> p (a b)"),
                                         in_=siluT[:].rearrange("p a b -> p (a b)"), func=AF.Silu),
           reads=[bsil], writes=[bsil])
        wada3 = wada_d.rearrange("p (a b) -> p a b", a=8)
        for pc in range(12):
            k = pc % 2
            dma("sp", wad[k][:], wada3[:, :, pc * 512:(pc + 1) * 512], [], [bwad[k]], f"wad{k}")
            bk = nextbank([0, 1])

            def mmg(e, k=k, bk=bk):
                r = None
                for kc in range(8):
                    r = e.matmul(PB[bk][0:NSEQ, :], lhsT=siluT[:, kc, :], rhs=wad[k][:, kc, :],
                                 start=(kc == 0), stop=(kc == 7))
                return r
            op("pe", mmg, reads=[bsil, bwad[k]], writes=[bPB[bk]])
            op("dve", lambda e, bk=bk, pc=pc: e.tensor_tensor(out=modall[:, pc * 512:(pc + 1) * 512], in0=PB[bk][0:NSEQ, :],
                                                              in1=badab[:, pc * 512:(pc + 1) * 512], op=ALU.add),
               reads=[bPB[bk], bmod], writes=[bmod])
        dma("sp", gt_d[:, 0:1024], modall[:, 2048:3072], [bmod], [bGT], "gtd")
        dma("sp", gt_d[:, 1024:2048], modall[:, 5120:6144], [bmod], [bGT], "gtd")
        bk = nextbank([0, 1])

        def trg(e, bk=bk):
            r = None
            for c in range(48):
                r = e.transpose(out=PB[bk][:, c * NSEQ:(c + 1) * NSEQ], in_=modall[0:NSEQ, c * 128:(c + 1) * 128],
                                identity=identf[0:NSEQ, 0:NSEQ])
            return r
        op("pe", trg, reads=[bmod, bC], writes=[bPB[bk]])
        op("dve", lambda e, bk=bk: e.tensor_copy(out=modT[:], in_=PB[bk][:, 0:48 * NSEQ]), reads=[bPB[bk]], writes=[bmodT])
        for c0 in (8, 32):
            op("dve", lambda e, c0=c0: e.tensor_scalar(out=modT[:, c0 * NSEQ:(c0 + 8) * NSEQ], in0=modT[:, c0 * NSEQ:(c0 + 8) * NSEQ],
                                                       scalar1=1.0, scalar2=None, op0=ALU.add),
               reads=[bmodT], writes=[bmodT])
        SC.barrier()
        A.pop()

        dump("modT", modT[:], [bmodT], [128, 48 * NSEQ], F32)

        def modcol(which, c, b):
            j = (which * 8 + c) * NSEQ + b
            return modT[:, j:j + 1]

        A.push()
        hT_raw = A.alloc([128, 8192], F32)
        hT = hT_raw.bitcast(BF16).rearrange("p (a b) -> p a b", a=8)
        yaT = A.alloc([128, 8, SEQ], BF16)
        ymT = A.alloc([128, 8, SEQ], BF16)
        bxts = [Buf("xt0"), Buf("xt1")]
        bhT = [Buf(f"hT{i}") for i in range(NT)]
        byaT = [Buf(f"yaT{i}") for i in range(8)]
        bymT = [Buf(f"ymT{i}") for i in range(NT)]

        def hbufs(t0, t1):
            return bhT[t0:t1]

        for b in range(NSEQ):
            A.push()
            xts = [A.alloc([128, D], F32) for _ in range(2)]
            xns = [A.alloc([128, D], BF16) for _ in range(2)]
            bxns = [Buf("xn0"), Buf("xn1")]
            for i in range(NT):
                k = i % 2
                r0 = b * SEQ + i * 128
                dma("sp", xts[k][:], x_d[r0:r0 + 128, :], [], [bxts[k]], f"xt{k}")
                mv, rs, bm, br = ln_stats(xts[k], bxts[k])
                op("dve", lambda e, k=k, mv=mv, rs=rs: e.tensor_scalar(out=xns[k][:], in0=xts[k][:], scalar1=mv[:, 0:1], scalar2=rs[:],
                                                                      op0=ALU.subtract, op1=ALU.mult),
                   reads=[bxts[k], bm, br], writes=[bxns[k]])
                if b == 0 and i == 0:
                    dump("mv0", mv[:], [bm], [128, 2], F32)
                    dump("rs0", rs[:], [br], [128, 1], F32)
                    dump("xn0", xns[k][:], [bxns[k]], [128, D], BF16)
                    dump("xt0", xts[k][:], [bxts[k]], [128, D], F32)
                bk = nextbank([0, 1])
                ptb = PB[bk].bitcast(BF16)

                def trx(e, k=k, ptb=ptb):
                    r = None
                    for c in range(8):
                        r = e.transpose(out=ptb[:, c * 128:(c + 1) * 128], in_=xns[k][:, c * 128:(c + 1) * 128], identity=identb[:])
                    return r
                op("pe", trx, reads=[bxns[k], bC], writes=[bPB[bk]])
                for c in range(8):
                    if c % 2 == 0:
                        op("act", lambda e, c=c, ptb=ptb, i=i: e.activation(out=hT[:, c, i * 128:(i + 1) * 128], in_=ptb[:, c * 128:(c + 1) * 128],
                                                                           func=AF.Identity, scale=modcol(1, c, b), bias=modcol(0, c, b)),
                           reads=[bPB[bk], bmodT], writes=[bhT[i]])
                    else:
                        op("dve", lambda e, c=c, ptb=ptb, i=i: e.tensor_scalar(out=hT[:, c, i * 128:(i + 1) * 128], in0=ptb[:, c * 128:(c + 1) * 128],
                                                                              scalar1=modcol(1, c, b), scalar2=modcol(0, c, b),
                                                                              op0=ALU.mult, op1=ALU.add),
                           reads=[bPB[bk], bmodT], writes=[bhT[i]])
            SC.barrier()
            A.pop()
            if b == 0:
                dump("hT", hT.rearrange("p a b -> p (a b)"), bhT, [128, 8 * SEQ], BF16)

            A.push()
            wda = [A.alloc([128, 8, 384], BF16) for _ in range(2)]
            bwda = [Buf("wda0"), Buf("wda1")]
            qTs = [A.alloc([128, SEQ], BF16) for _ in range(2)]
            kTs = [A.alloc([128, SEQ], BF16) for _ in range(2)]
            vss = [A.alloc([128, NT, 129], BF16) for _ in range(2)]
            bq = [[Buf(f"q{s}{c}") for c in range(4)] for s in range(2)]
            bkk = [[Buf(f"k{s}{c}") for c in range(4)] for s in range(2)]
            bv = [[Buf(f"v{s}{c}") for c in range(4)] for s in range(2)]
            Es = [A.alloc([128, NT, 256], BF16) for _ in range(2)]
            bE = [[Buf(f"E{s}{j}") for j in range(NT)] for s in range(2)]
            osb = [A.alloc([128, 2, 128], F32) for _ in range(2)]
            yat = [A.alloc([128, 2, 128], BF16) for _ in range(2)]
            rzs = [A.alloc([128, 4], F32) for _ in range(2)]
            sss = [A.alloc([128, 2], F32) for _ in range(2)]
            bo = [Buf("o0"), Buf("o1")]
            byat = [Buf("yat0"), Buf("yat1")]
            brz = [Buf("rz0"), Buf("rz1")]
            bss = [Buf("ss0"), Buf("ss1")]
            for s in range(2):
                op("pool", lambda e, s=s: e.memset(vss[s][:, :, 128:129], 1.0), writes=[bv[s][c] for c in range(4)])
            ecnt = 0
            ccnt = 0
            for h in range(8):
                s = h % 2
                dma("sp", wda[s][:].rearrange("p a b -> p (a b)"), wda_b[h * 128:(h + 1) * 128, :], [bDR], [bwda[s]], f"wda{s}")
                for c in range(4):
                    for which in range(2):
                        bk = nextbank([0, 1])

                        def mmg(e, s=s, c=c, which=which, bk=bk):
                            r = None
                            for kc in range(8):
                                r = e.matmul(PB[bk][:, :], lhsT=wda[s][:, kc, which * 128:(which + 1) * 128],
                                             rhs=hT[:, kc, c * 512:(c + 1) * 512], start=(kc == 0), stop=(kc == 7))
                            return r
                        op("pe", mmg, reads=[bwda[s]] + hbufs(4 * c, 4 * c + 4), writes=[bPB[bk]])
                        if which == 0:
                            op("act", lambda e, s=s, c=c, bk=bk: e.activation(out=qTs[s][:, c * 512:(c + 1) * 512], in_=PB[bk][:, :],
                                                                             func=AF.Copy, scale=0.125),
                               reads=[bPB[bk]], writes=[bq[s][c]])
                        else:
                            op("dve", lambda e, s=s, c=c, bk=bk: e.tensor_copy(out=kTs[s][:, c * 512:(c + 1) * 512], in_=PB[bk][:, :]),
                               reads=[bPB[bk]], writes=[bkk[s][c]])
                    bk = nextbank([0, 1])

                    def mmv(e, s=s, c=c, bk=bk):
                        r = None
                        for t in range(4):
                            i = 4 * c + t
                            for kc in range(8):
                                r = e.matmul(PB[bk][:, t * 128:(t + 1) * 128], lhsT=hT[:, kc, i * 128:(i + 1) * 128],
                                             rhs=wda[s][:, kc, 256:384], start=(kc == 0), stop=(kc == 7))
                        return r
                    op("pe", mmv, reads=[bwda[s]] + hbufs(4 * c, 4 * c + 4), writes=[bPB[bk]])
                    op("act", lambda e, s=s, c=c, bk=bk: e.activation(out=vss[s][:, 4 * c:4 * c + 4, 0:128],
                                                                     in_=PB[bk][:, :].rearrange("p (a b) -> p a b", a=4), func=AF.Copy),
                       reads=[bPB[bk]], writes=[bv[s][c]])
                for c in range(8):
                    cs = ccnt % 2
                    ccnt += 1
                    pob = [4 + cs * 2, 5 + cs * 2]
                    for m in range(2):
                        esl = ecnt % 2
                        ecnt += 1
                        E = Es[esl]
                        for j in range(2 * c + 2):
                            q0 = 128 if j == 2 * c + 1 else 0
                            sb_ = nextbank([2, 3])
                            op("pe", lambda e, s=s, m=m, j=j, c=c, q0=q0, sb_=sb_: e.matmul(
                                PB[sb_][:, q0:256], lhsT=kTs[s][m * 64:(m + 1) * 64, j * 128:(j + 1) * 128],
                                rhs=qTs[s][m * 64:(m + 1) * 64, c * 256 + q0:(c + 1) * 256], start=True, stop=True),
                               reads=[bkk[s][j // 4], bq[s][c // 2]], writes=[bPB[sb_]])
                            op("act", lambda e, E=E, j=j, q0=q0, sb_=sb_: e.activation(out=E[:, j, q0:256], in_=PB[sb_][:, q0:256], func=AF.Exp),
                               reads=[bPB[sb_]], writes=[bE[esl][j]])
                            if j >= 2 * c:
                                d0 = (j - 2 * c) * 128
                                op("pool", lambda e, E=E, j=j, d0=d0: e.tensor_tensor(out=E[:, j, d0:d0 + 128], in0=E[:, j, d0:d0 + 128],
                                                                                     in1=maskb[:], op=ALU.mult),
                                   reads=[bE[esl][j], bC], writes=[bE[esl][j]])
                        pv = PB[pob[m]][:, 0:258].rearrange("p (a b) -> p a b", a=2)

                        def pvg(e, s=s, c=c, E=E, pv=pv):
                            r = None
                            for ii in range(2):
                                i = 2 * c + ii
                                for j in range(i + 1):
                                    r = e.matmul(pv[:, ii, :], lhsT=E[:, j, ii * 128:(ii + 1) * 128], rhs=vss[s][:, j, :],
                                                 start=(j == 0), stop=(j == i))
                            return r
                        op("pe", pvg, reads=[bE[esl][j] for j in range(2 * c + 2)] + [bv[s][j] for j in range(c // 2 + 1)],
                           writes=[bPB[pob[m]]])
                    k = cs
                    p0 = PB[pob[0]][:, 0:258].rearrange("p (a b) -> p a b", a=2)
                    p1 = PB[pob[1]][:, 0:258].rearrange("p (a b) -> p a b", a=2)
                    op("dve", lambda e, k=k, p0=p0: e.reciprocal(out=rzs[k][:, 0:2], in_=p0[:, :, 128]), reads=[bPB[pob[0]]], writes=[brz[k]])
                    op("dve", lambda e, k=k, p1=p1: e.reciprocal(out=rzs[k][:, 2:4], in_=p1[:, :, 128]), reads=[bPB[pob[1]]], writes=[brz[k]])
                    op("dve", lambda e, k=k: e.tensor_scalar(out=rzs[k][:, 2:4], in0=rzs[k][:, 2:4], scalar1=neglam[:, 0:1], scalar2=None, op0=ALU.mult),
                       reads=[brz[k], bC], writes=[brz[k]])
                    for ii in range(2):
                        op("dve", lambda e, k=k, ii=ii, p0=p0: e.tensor_scalar(out=osb[k][:, ii, :], in0=p0[:, ii, 0:128], scalar1=rzs[k][:, ii:ii + 1],
                                                                              scalar2=None, op0=ALU.mult),
                           reads=[bPB[pob[0]], brz[k]], writes=[bo[k]])
                        op("dve", lambda e, k=k, ii=ii, p1=p1: e.scalar_tensor_tensor(out=osb[k][:, ii, :], in0=p1[:, ii, 0:128],
                                                                                     scalar=rzs[k][:, 2 + ii:3 + ii], in1=osb[k][:, ii, :],
                                                                                     op0=ALU.mult, op1=ALU.add),
                           reads=[bPB[pob[1]], brz[k], bo[k]], writes=[bo[k]])
                        op("act", lambda e, k=k, ii=ii: e.activation(out=junk[:, 0:128], in_=osb[k][:, ii, :], func=AF.Square,
                                                                    accum_out=sss[k][:, ii:ii + 1]),
                           reads=[bo[k]], writes=[bss[k], bjunk])
                    op("dve", lambda e, k=k: e.tensor_scalar(out=sss[k][:], in0=sss[k][:], scalar1=1.0 / 128.0, scalar2=LN_EPS,
                                                             op0=ALU.mult, op1=ALU.add),
                       reads=[bss[k]], writes=[bss[k]])
                    op("pool", lambda e, k=k: e.tensor_tensor(out=sss[k][:], in0=sss[k][:], in1=cm05[:, 0:2], op=ALU.pow),
                       reads=[bss[k], bC], writes=[bss[k]])
                    for ii in range(2):
                        op("dve", lambda e, k=k, ii=ii: e.scalar_tensor_tensor(out=yat[k][:, ii, :], in0=osb[k][:, ii, :], scalar=sss[k][:, ii:ii + 1],
                                                                              in1=gda_bc[:], op0=ALU.mult, op1=ALU.mult),
                           reads=[bo[k], bss[k], bC], writes=[byat[k]])
                    bk = nextbank([0, 1])
                    ptb = PB[bk].bitcast(BF16)

                    def trya(e, k=k, ptb=ptb):
                        r = None
                        for ii in range(2):
                            r = e.transpose(out=ptb[:, ii * 128:(ii + 1) * 128], in_=yat[k][:, ii, :], identity=identb[:])
                        return r
                    op("pe", trya, reads=[byat[k], bC], writes=[bPB[bk]])
                    op("act", lambda e, h=h, c=c, ptb=ptb: e.activation(out=yaT[:, h, c * 256:(c + 1) * 256], in_=ptb[:, 0:256], func=AF.Copy),
                       reads=[bPB[bk]], writes=[byaT[c]])
            SC.barrier()
            A.pop()
            if b == 0:
                dump("yaT", yaT.rearrange("p a b -> p (a b)"), byaT, [128, 8 * SEQ], BF16)

            A.push()
            LNS = math.log(128.0 ** -0.5)
            gch = [A.alloc([4, 512], F32) for _ in range(2)]
            negM = A.alloc([4, SEQ], F32)
            ech = [A.alloc([4, 512], F32) for _ in range(2)]
            nfc = [A.alloc([4, 512], F32) for _ in range(2)]
            mch = [A.alloc([4, 512], F32) for _ in range(2)]
            tfc = A.alloc([4, 512], F32)
            gtok = A.alloc([128, NT * 4], F32)
            etok = A.alloc([128, NT * 4], F32)
            bgc = [Buf("gch0"), Buf("gch1")]
            bnegM = Buf("negM")
            bec = [Buf("ech0"), Buf("ech1")]
            bnf = [Buf("nf0"), Buf("nf1")]
            bmc = [Buf("mc0"), Buf("mc1")]
            btf = Buf("tf")
            bgtok = Buf("gtok")
            betok = Buf("etok")
            for c in range(4):
                k = c % 2
                sl = slice(c * 512, (c + 1) * 512)
                bki = nextbank([0, 1])
                bkf = nextbank([0, 1])
                for (bk_, c0) in ((bki, 0), (bkf, 4)):
                    def mmif(e, bk_=bk_, c0=c0, c=c):
                        r = None
                        for kc in range(8):
                            r = e.matmul(PB[bk_][0:4, :], lhsT=wif[:, kc, c0:c0 + 4], rhs=hT[:, kc, c * 512:(c + 1) * 512],
                                         start=(kc == 0), stop=(kc == 7))
                        return r
                    op("pe", mmif, reads=[bC] + hbufs(4 * c, 4 * c + 4), writes=[bPB[bk_]])
                op("act", lambda e, bkf=bkf: e.activation(out=tfc[:], in_=PB[bkf][0:4, :], func=AF.Exp, scale=-1.0, bias=nbf[:, 0:1]),
                   reads=[bPB[bkf], bC], writes=[btf])
                op("act", lambda e: e.activation(out=tfc[:], in_=tfc[:], func=AF.Ln, bias=1.0), reads=[btf], writes=[btf])
                init_nf = 0.0 if c == 0 else nfc[1 - k][:, 511:512]
                op("dve", lambda e, k=k, init_nf=init_nf: e.tensor_tensor_scan(out=nfc[k][:], data0=ones4[:], data1=tfc[:], initial=init_nf,
                                                                              op0=ALU.mult, op1=ALU.add),
                   reads=[btf, bC, bnf[1 - k]], writes=[bnf[k]])
                op("dve", lambda e, k=k, bki=bki, sl=sl: e.scalar_tensor_tensor(out=gch[k][:], in0=PB[bki][0:4, :], scalar=bifT[:, 0:1], in1=nfc[k][:],
                                                                               op0=ALU.add, op1=ALU.add),
                   reads=[bPB[bki], bC, bnf[k]], writes=[bgc[k]])
                init_m = 0.0 if c == 0 else mch[1 - k][:, 511:512]
                op("dve", lambda e, k=k, sl=sl, init_m=init_m: e.tensor_tensor_scan(out=mch[k][:], data0=gch[k][:], data1=gch[k][:], initial=init_m,
                                                                                   op0=ALU.max, op1=ALU.max),
                   reads=[bgc[k], bmc[1 - k]], writes=[bmc[k]])
                op("dve", lambda e, k=k, sl=sl: e.tensor_scalar(out=negM[:, sl], in0=mch[k][:], scalar1=-1.0, scalar2=None, op0=ALU.mult),
                   reads=[bmc[k]], writes=[bnegM])
                op("dve", lambda e, k=k: e.tensor_tensor(out=ech[k][:], in0=nfc[k][:], in1=mch[k][:], op=ALU.subtract),
                   reads=[bnf[k], bmc[k]], writes=[bec[k]])
                op("act", lambda e, k=k: e.activation(out=ech[k][:], in_=ech[k][:], func=AF.Exp), reads=[bec[k]], writes=[bec[k]])

                def trgt(e, k=k, c=c):
                    r = None
                    for t in range(4):
                        i = 4 * c + t
                        r = e.transpose(out=PB[7][:, i * 4:(i + 1) * 4], in_=gch[k][0:4, t * 128:(t + 1) * 128], identity=identf[0:4, 0:4])
                        r = e.transpose(out=PB[7][:, 64 + i * 4:64 + (i + 1) * 4], in_=ech[k][0:4, t * 128:(t + 1) * 128], identity=identf[0:4, 0:4])
                    return r
                op("pe", trgt, reads=[bgc[k], bec[k], bC], writes=[bPB[7]])
            bk = 7
            op("dve", lambda e, bk=bk: e.tensor_scalar(out=gtok[:], in0=PB[bk][:, 0:64], scalar1=LNS, scalar2=None, op0=ALU.add),
               reads=[bPB[bk]], writes=[bgtok])
            op("dve", lambda e, bk=bk: e.tensor_copy(out=etok[:], in_=PB[bk][:, 64:128]), reads=[bPB[bk]], writes=[betok])

            wml = A.alloc([128, 8, 768], BF16)
            bwml = Buf("wml")
            negMbc = [A.alloc([128, 256], F32) for _ in range(2)]
            bnb = [Buf("nb0"), Buf("nb1")]
            zp = A.alloc([128, SEQ + 3], F32)
            bzp = [Buf(f"zp{c}") for c in range(4)]
            cv = A.alloc([128, SEQ], F32)
            bcv = Buf("cv")
            qTm = A.alloc([128, SEQ], BF16)
            kTm = A.alloc([128, SEQ], BF16)
            bqm = Buf("qTm")
            bkm = Buf("kTm")
            vm = A.alloc([128, NT, 257], BF16)
            bvm = [Buf(f"vm{i}") for i in range(8)]
            Ps = [A.alloc([128, NT, 256], BF16) for _ in range(2)]
            bP = [[Buf(f"P{s}{j}") for j in range(NT)] for s in range(2)]
            Wt = [A.alloc([128, 256], F32) for _ in range(2)]
            bWt = [Buf("Wt0"), Buf("Wt1")]
            gml_bc = A.alloc([128, 256], F32)
            bgml = Buf("gml")
            hn = [A.alloc([128, 256], F32) for _ in range(2)]
            bhn = [Buf("hn0"), Buf("hn1")]
            og = [A.alloc([128, 256], F32) for _ in range(2)]
            bog = [Buf("og0"), Buf("og1")]
            ymt = [A.alloc([128, 256], BF16) for _ in range(2)]
            bymt = [Buf("ymt0"), Buf("ymt1")]
            dens = [A.alloc([128, 2], F32) for _ in range(2)]
            bden = [Buf("den0"), Buf("den1")]
            op("pool", lambda e: e.memset(zp[:, 0:3], 0.0), writes=bzp)
            op("pool", lambda e: e.memset(vm[:, :, 256:257], 1.0), writes=bvm)
            pcnt = 0
            tcnt = 0
            wcnt = 0
            for h in range(4):
                dma("sp", wml[:].rearrange("p a b -> p (a b)"), wml_b[h * 128:(h + 1) * 128, :], [bDR], [bwml], "wml")
                dma("sp", gml_bc[:], mlg_d[0:1, h * 256:(h + 1) * 256].partition_broadcast(128), [], [bgml], "gml")
                for which in range(2):
                    ch = which * 4 + h
                    for c in range(4):
                        bk = nextbank([0, 1])

                        def mmq(e, c=c, which=which, bk=bk):
                            r = None
                            for kc in range(8):
                                r = e.matmul(PB[bk][:, :], lhsT=wml[:, kc, which * 128:(which + 1) * 128],
                                             rhs=hT[:, kc, c * 512:(c + 1) * 512], start=(kc == 0), stop=(kc == 7))
                            return r
                        op("pe", mmq, reads=[bwml] + hbufs(4 * c, 4 * c + 4), writes=[bPB[bk]])
                        op("act", lambda e, c=c, bk=bk: e.activation(out=zp[:, 3 + c * 512:3 + (c + 1) * 512], in_=PB[bk][:, :], func=AF.Copy),
                           reads=[bPB[bk]], writes=[bzp[c]])
                    op("dve", lambda e, ch=ch: e.tensor_scalar(out=cv[:], in0=zp[:, 3:SEQ + 3], scalar1=cwT[:, ch * 4 + 3:ch * 4 + 4],
                                                               scalar2=cbT[:, ch:ch + 1], op0=ALU.mult, op1=ALU.add),
                       reads=bzp + [bC], writes=[bcv])
                    for j in range(3):
                        op("dve", lambda e, ch=ch, j=j: e.scalar_tensor_tensor(out=cv[:], in0=zp[:, j:j + SEQ], scalar=cwT[:, ch * 4 + j:ch * 4 + j + 1],
                                                                              in1=cv[:], op0=ALU.mult, op1=ALU.add),
                           reads=bzp + [bC, bcv], writes=[bcv])
                    dst, bdst = (qTm, bqm) if which == 0 else (kTm, bkm)
                    op("act", lambda e, dst=dst: e.activation(out=dst[:], in_=cv[:], func=AF.Silu), reads=[bcv], writes=[bdst])
                for g2 in range(8):
                    bk = nextbank([0, 1])

                    def mmv2(e, g2=g2, bk=bk):
                        r = None
                        for t in range(2):
                            i = 2 * g2 + t
                            for kc in range(8):
                                r = e.matmul(PB[bk][:, t * 256:(t + 1) * 256], lhsT=hT[:, kc, i * 128:(i + 1) * 128],
                                             rhs=wml[:, kc, 256:512], start=(kc == 0), stop=(kc == 7))
                        return r
                    op("pe", mmv2, reads=[bwml] + hbufs(2 * g2, 2 * g2 + 2), writes=[bPB[bk]])
                    op("dve", lambda e, g2=g2, bk=bk: e.tensor_copy(out=vm[:, 2 * g2:2 * g2 + 2, 0:256],
                                                                   in_=PB[bk][:, :].rearrange("p (a b) -> p a b", a=2)),
                       reads=[bPB[bk]], writes=[bvm[g2]])
                for c in range(8):
                    psl = pcnt % 2
                    pcnt += 1
                    P = Ps[psl]
                    bk = nextbank([0, 1])
                    op("pe", lambda e, h=h, c=c, bk=bk: e.matmul(PB[bk][:, 0:256], lhsT=sel4[0:4, h * 128:(h + 1) * 128],
                                                                rhs=negM[0:4, c * 256:(c + 1) * 256], start=True, stop=True),
                       reads=[bC, bnegM], writes=[bPB[bk]])
                    op("act", lambda e, psl=psl, bk=bk: e.activation(out=negMbc[psl][:, :], in_=PB[bk][:, 0:256], func=AF.Copy),
                       reads=[bPB[bk]], writes=[bnb[psl]])
                    for j in range(2 * c + 2):
                        q0 = 128 if j == 2 * c + 1 else 0
                        sb_ = nextbank([2, 3])
                        ws = wcnt % 2
                        wcnt += 1
                        op("pe", lambda e, j=j, c=c, q0=q0, sb_=sb_: e.matmul(
                            PB[sb_][:, q0:256], lhsT=kTm[:, j * 128:(j + 1) * 128], rhs=qTm[:, c * 256 + q0:(c + 1) * 256], start=True, stop=True),
                           reads=[bkm, bqm], writes=[bPB[sb_]])
                        op("act", lambda e, j=j, c=c, q0=q0, ws=ws, h=h, psl=psl: e.activation(out=Wt[ws][:, q0:256], in_=negMbc[psl][:, q0:256],
                                                                                     func=AF.Exp, bias=gtok[:, j * 4 + h:j * 4 + h + 1]),
                           reads=[bnb[psl], bgtok], writes=[bWt[ws]])
                        op("dve", lambda e, P=P, j=j, q0=q0, ws=ws, sb_=sb_: e.tensor_tensor(out=P[:, j, q0:256], in0=PB[sb_][:, q0:256],
                                                                                            in1=Wt[ws][:, q0:256], op=ALU.mult),
                           reads=[bPB[sb_], bWt[ws]], writes=[bP[psl][j]])
                        if j >= 2 * c:
                            d0 = (j - 2 * c) * 128
                            op("pool", lambda e, P=P, j=j, d0=d0: e.tensor_tensor(out=P[:, j, d0:d0 + 128], in0=P[:, j, d0:d0 + 128],
                                                                                 in1=maskb[:], op=ALU.mult),
                               reads=[bP[psl][j], bC], writes=[bP[psl][j]])
                    for ii in range(2):
                        i = 2 * c + ii
                        k = tcnt % 2
                        tcnt += 1
                        pb_ = nextbank([4, 5, 6, 7])

                        def pvm(e, P=P, ii=ii, i=i, pb_=pb_):
                            r = None
                            for j in range(i + 1):
                                r = e.matmul(PB[pb_][:, 0:257], lhsT=P[:, j, ii * 128:(ii + 1) * 128], rhs=vm[:, j, :],
                                             start=(j == 0), stop=(j == i))
                            return r
                        op("pe", pvm, reads=[bP[psl][j] for j in range(i + 1)] + [bvm[j] for j in range(i // 2 + 1)], writes=[bPB[pb_]])
                        bk = nextbank([0, 1])

                        def mmo(e, i=i, bk=bk):
                            r = None
                            for kc in range(8):
                                r = e.matmul(PB[bk][:, 0:256], lhsT=hT[:, kc, i * 128:(i + 1) * 128], rhs=wml[:, kc, 512:768],
                                             start=(kc == 0), stop=(kc == 7))
                            return r
                        op("pe", mmo, reads=[bwml, bhT[i]], writes=[bPB[bk]])
                        op("act", lambda e, k=k, bk=bk: e.activation(out=og[k][:], in_=PB[bk][:, 0:256], func=AF.Sigmoid),
                           reads=[bPB[bk]], writes=[bog[k]])
                        op("dve", lambda e, k=k, pb_=pb_: e.tensor_copy(out=dens[k][:, 0:1], in_=PB[pb_][:, 256:257]),
                           reads=[bPB[pb_]], writes=[bden[k]])
                        op("dve", lambda e, k=k: e.scalar_tensor_tensor(out=dens[k][:, 0:1], in0=dens[k][:, 0:1], scalar=-1.0, in1=dens[k][:, 0:1],
                                                                        op0=ALU.mult, op1=ALU.max),
                           reads=[bden[k]], writes=[bden[k]])
                        op("dve", lambda e, k=k, i=i, h=h: e.tensor_tensor(out=dens[k][:, 0:1], in0=dens[k][:, 0:1], in1=etok[:, i * 4 + h:i * 4 + h + 1],
                                                                          op=ALU.max),
                           reads=[bden[k], betok], writes=[bden[k]])
                        op("dve", lambda e, k=k: e.reciprocal(out=dens[k][:, 0:1], in_=dens[k][:, 0:1]), reads=[bden[k]], writes=[bden[k]])
                        op("dve", lambda e, k=k, pb_=pb_: e.tensor_scalar(out=hn[k][:], in0=PB[pb_][:, 0:256], scalar1=dens[k][:, 0:1], scalar2=None,
                                                                         op0=ALU.mult),
                           reads=[bPB[pb_], bden[k]], writes=[bhn[k]])
                        op("act", lambda e, k=k: e.activation(out=junk[:, 0:256], in_=hn[k][:], func=AF.Square, accum_out=dens[k][:, 1:2]),
                           reads=[bhn[k]], writes=[bden[k], bjunk])
                        op("dve", lambda e, k=k: e.tensor_scalar(out=dens[k][:, 1:2], in0=dens[k][:, 1:2], scalar1=1.0 / 256.0, scalar2=LN_EPS,
                                                                 op0=ALU.mult, op1=ALU.add),
                           reads=[bden[k]], writes=[bden[k]])
                        op("pool", lambda e, k=k: e.tensor_tensor(out=dens[k][:, 1:2], in0=dens[k][:, 1:2], in1=cm05[:, 0:1], op=ALU.pow),
                           reads=[bden[k], bC], writes=[bden[k]])
                        op("dve", lambda e, k=k, h=h: e.scalar_tensor_tensor(out=hn[k][:], in0=hn[k][:], scalar=dens[k][:, 1:2],
                                                                            in1=gml_bc[:], op0=ALU.mult, op1=ALU.mult),
                           reads=[bhn[k], bden[k], bgml], writes=[bhn[k]])
                        op("pool", lambda e, k=k: e.tensor_tensor(out=ymt[k][:], in0=hn[k][:], in1=og[k][:], op=ALU.mult),
                           reads=[bhn[k], bog[k]], writes=[bymt[k]])
                        bk = nextbank([0, 1])
                        ptb = PB[bk].bitcast(BF16)

                        def trym(e, k=k, ptb=ptb):
                            r = None
                            for ee in range(2):
                                r = e.transpose(out=ptb[:, ee * 128:(ee + 1) * 128], in_=ymt[k][:, ee * 128:(ee + 1) * 128], identity=identb[:])
                            return r
                        op("pe", trym, reads=[bymt[k], bC], writes=[bPB[bk]])
                        op("act", lambda e, h=h, i=i, ptb=ptb: e.activation(out=ymT[:, 2 * h:2 * h + 2, i * 128:(i + 1) * 128],
                                                                           in_=ptb[:, 0:256].rearrange("p (a b) -> p a b", a=2), func=AF.Copy),
                           reads=[bPB[bk]], writes=[bymT[i]])
            SC.barrier()
            A.pop()
            if b == 0:
                dump("ymT", ymT.rearrange("p a b -> p (a b)"), bymT, [128, 8 * SEQ], BF16)

            A.push()
            yT = A.alloc([128, 8, SEQ], BF16)
            byT = [Buf(f"yT{c}") for c in range(4)]
            wmg = [A.alloc([128, 8, 512], BF16) for _ in range(2)]
            bwmg = [Buf("wmg0"), Buf("wmg1")]
            sgA = [A.alloc([128, 512], F32) for _ in range(2)]
            sgB = [A.alloc([128, 512], F32) for _ in range(2)]
            t1 = [A.alloc([128, 512], F32) for _ in range(2)]
            t2 = [A.alloc([128, 512], F32) for _ in range(2)]
            bsgA = [Buf("sgA0"), Buf("sgA1")]
            bsgB = [Buf("sgB0"), Buf("sgB1")]
            bt1 = [Buf("t10"), Buf("t11")]
            bt2 = [Buf("t20"), Buf("t21")]
            dcnt = 0
            allb = [0, 1, 2, 3, 4, 5, 6, 7]
            for fc in range(8):
                s = fc % 2
                dma("sp", wmg[s][:].rearrange("p a b -> p (a b)"), wmg_b[fc * 128:(fc + 1) * 128, :], [bDR], [bwmg[s]], f"wmg{s}")
                for c in range(4):
                    k = dcnt % 2
                    dcnt += 1
                    banks = []
                    for which, src, srcb in ((0, hT, hbufs(4 * c, 4 * c + 4)), (1, hT, hbufs(4 * c, 4 * c + 4)),
                                             (2, yaT, byaT[2 * c:2 * c + 2]), (3, ymT, bymT[4 * c:4 * c + 4])):
                        bk = nextbank(allb)
                        banks.append(bk)

                        def mmd(e, s=s, c=c, which=which, src=src, bk=bk):
                            r = None
                            for kc in range(8):
                                r = e.matmul(PB[bk][:, :], lhsT=wmg[s][:, kc, which * 128:(which + 1) * 128],
                                             rhs=src[:, kc, c * 512:(c + 1) * 512], start=(kc == 0), stop=(kc == 7))
                            return r
                        op("pe", mmd, reads=[bwmg[s]] + list(srcb), writes=[bPB[bk]])
                    op("act", lambda e, k=k, bk=banks[0]: e.activation(out=sgA[k][:], in_=PB[bk][:, :], func=AF.Sigmoid),
                       reads=[bPB[banks[0]]], writes=[bsgA[k]])
                    op("act", lambda e, k=k, bk=banks[1]: e.activation(out=sgB[k][:], in_=PB[bk][:, :], func=AF.Sigmoid),
                       reads=[bPB[banks[1]]], writes=[bsgB[k]])
                    op("dve", lambda e, k=k, bk=banks[2]: e.tensor_tensor(out=t1[k][:], in0=PB[bk][:, :], in1=sgA[k][:], op=ALU.mult),
                       reads=[bPB[banks[2]], bsgA[k]], writes=[bt1[k]])
                    op("dve", lambda e, k=k, bk=banks[3]: e.tensor_tensor(out=t2[k][:], in0=PB[bk][:, :], in1=sgB[k][:], op=ALU.mult),
                       reads=[bPB[banks[3]], bsgB[k]], writes=[bt2[k]])
                    op("pool", lambda e, k=k, fc=fc, c=c: e.tensor_tensor(out=yT[:, fc, c * 512:(c + 1) * 512], in0=t1[k][:], in1=t2[k][:], op=ALU.add),
                       reads=[bt1[k], bt2[k]], writes=[byT[c]])
            SC.barrier()
            if b == 0:
                dump("yT", yT.rearrange("p a b -> p (a b)"), byT, [128, 8 * SEQ], BF16)
            wo = hT_raw[:, 0:4096].bitcast(BF16).rearrange("p (a b) -> p a b", a=8)
            ttile = hT_raw[:, 4096:5120]
            l1g = hT_raw[:, 5120:6144]
            l1b = hT_raw[:, 6144:7168]
            gt1 = hT_raw[:, 7168:8192]
            xts = [A.alloc([128, D], F32) for _ in range(2)]
            bwo = Buf("wo")
            btt = Buf("tt")
            bl1 = Buf("l1")
            dma("sp", wo.rearrange("p a b -> p (a b)"), wout_b[:, :], [bDR], [bwo], "wo")
            dma("sp", l1g, ln1g_d[0:1, :].partition_broadcast(128), [], [bl1], "l1")
            dma("sp", l1b, ln1b_d[0:1, :].partition_broadcast(128), [], [bl1], "l1")
            dma("sp", gt1, gt_d[b:b + 1, 0:1024].partition_broadcast(128), [bGT], [bl1], "l1")
            for i in range(NT):
                k = i % 2
                r0 = b * SEQ + i * 128
                dma("sp", xts[k][:], x_d[r0:r0 + 128, :], [], [bxts[k]], f"xt{k}")
                for half in range(2):
                    bk = nextbank(allb)

                    def mmo2(e, i=i, half=half, bk=bk):
                        r = None
                        for kc in range(8):
                            r = e.matmul(PB[bk][:, :], lhsT=yT[:, kc, i * 128:(i + 1) * 128], rhs=wo[:, kc, half * 512:(half + 1) * 512],
                                         start=(kc == 0), stop=(kc == 7))
                        return r
                    op("pe", mmo2, reads=[byT[i // 4], bwo], writes=[bPB[bk]])
                    op("dve", lambda e, half=half, bk=bk: e.tensor_tensor(out=ttile[:, half * 512:(half + 1) * 512], in0=PB[bk][:, :],
                                                                         in1=gt1[:, half * 512:(half + 1) * 512], op=ALU.mult),
                       reads=[bPB[bk], bl1], writes=[btt])
                op("dve", lambda e, k=k: e.scalar_tensor_tensor(out=xts[k][:], in0=xts[k][:], scalar=ALPHA, in1=ttile, op0=ALU.mult, op1=ALU.add),
                   reads=[bxts[k], btt], writes=[bxts[k]])
                mv, rs, bm, br = ln_stats(xts[k], bxts[k])
                op("dve", lambda e, k=k, mv=mv, rs=rs: e.tensor_scalar(out=xts[k][:], in0=xts[k][:], scalar1=mv[:, 0:1], scalar2=rs[:],
                                                                      op0=ALU.subtract, op1=ALU.mult),
                   reads=[bxts[k], bm, br], writes=[bxts[k]])
                op("pool", lambda e, k=k: e.tensor_tensor(out=xts[k][:], in0=xts[k][:], in1=l1g, op=ALU.mult), reads=[bxts[k], bl1], writes=[bxts[k]])
                op("pool", lambda e, k=k: e.tensor_tensor(out=xts[k][:], in0=xts[k][:], in1=l1b, op=ALU.add), reads=[bxts[k], bl1], writes=[bxts[k]])
                dst = x1_d if do_peer else out_d
                dma("sp", dst[r0:r0 + 128, :], xts[k][:], [bxts[k]], [bXO[k]], f"xo{k}")
            SC.barrier()
            A.pop()
        A.pop()
        SC.barrier()

        if do_peer:
            A.push()
            RB = [0, 1, 2, 3, 4, 5]
            RBX = [4, 5]
            keysT = A.alloc([128, 256], F32)
            l2g = A.alloc([128, D], F32)
            l2b = A.alloc([128, D], F32)
            gt2 = A.alloc([128, D], F32)
            bl2 = Buf("l2")
            bgt2 = Buf("gt2")
            xps = [A.alloc([128, D], F32) for _ in range(2)]
            bxps = [Buf("xp0"), Buf("xp1")]
            xn2 = A.alloc([128, D], F32)
            bxn2 = Buf("xn2")
            h2T = A.alloc([128, 8, 128], F32)
            h2Tb = A.alloc([128, 8, 128], BF16)
            bh2 = Buf("h2T")
            bh2b = Buf("h2Tb")
            wq = [A.alloc([128, 8, 128], F32) for _ in range(2)]
            bwq = [Buf("wq0"), Buf("wq1")]
            qTh = [A.alloc([128, 128], F32) for _ in range(2)]
            bqTh = [Buf("qTh0"), Buf("qTh1")]
            s_sb = A.alloc([128, 8, 2, 128], F32)
            bs = Buf("s_sb")
            m16 = A.alloc([128, 8, 2, 16], F32)
            bm16 = Buf("m16")
            wk1 = A.alloc([128, 128], F32)
            bwk1 = Buf("wk1")
            cand = A.alloc([128, 8, 16, 16], F32)
            bcand = Buf("cand")
            wk2 = A.alloc([128, 256], F32)
            bwk2 = Buf("wk2")
            c16 = A.alloc([128, 8, 16], F32)
            bc16 = Buf("c16")
            negthr = A.alloc([128, 8], F32)
            zz = A.alloc([128, 8], F32)
            biasE = A.alloc([128, 8], F32)
            bsm = Buf("small")
            Sp = [A.alloc([128, 2048], F32) for _ in range(3)]
            Ep = [A.alloc([128, 2048], BF16) for _ in range(3)]
            Mk = [A.alloc([128, 2048], BF16) for _ in range(2)]
            bSp = [Buf("Sp0"), Buf("Sp1"), Buf("Sp2")]
            bEp = [Buf("Ep0"), Buf("Ep1"), Buf("Ep2")]
            bMk = [Buf("Mk0"), Buf("Mk1"), Buf("Mk2")]
            gstate = [0, 0, 0]
            acc = A.alloc([128, NEXP], BF16)
            bacc = [Buf(f"acc{i}") for i in range(8)]
            ub = [A.alloc([128, 8, 512], BF16) for _ in range(3)]
            bub = [Buf("ub0"), Buf("ub1"), Buf("ub2")]
            vb = [A.alloc([128, 4, D], BF16) for _ in range(3)]
            bvb = [Buf("vb0"), Buf("vb1"), Buf("vb2")]
            gel = [A.alloc([128, 512], F32) for _ in range(2)]
            bgel = [Buf("gel0"), Buf("gel1"), Buf("gel2")]
            Wc = [A.alloc([128, 512], BF16) for _ in range(2)]
            bWc = [Buf("Wc0"), Buf("Wc1"), Buf("Wc2")]
            WT = [A.alloc([128, 4, 128], BF16) for _ in range(2)]
            bWT = [Buf("WT0"), Buf("WT1"), Buf("WT2")]
            tt2 = A.alloc([128, D], F32)
            btt2 = Buf("tt2")
            bPO = [Buf("po0"), Buf("po1")]
            dma("sp", keysT[:], keysT_d[:, :], [], [bl2], "l2")
            dma("sp", l2g[:], ln2g_d[0:1, :].partition_broadcast(128), [], [bl2], "l2")
            dma("sp", l2b[:], ln2b_d[0:1, :].partition_broadcast(128), [], [bl2], "l2")
            wpq3 = wpq_d.rearrange("r (a b) -> r a b", a=8)
            v_b3 = v_b.rearrange("(g p) n -> p g n", p=128)
            gcnt = 0
            ecnt2 = 0
            import os as _os
            _PT = int(_os.environ.get("PEER_TILES", NSEQ * NT))
            _PS = int(_os.environ.get("PEER_STAGE", 9))
            for ti in range(min(_PT, NSEQ * NT)):
                b = ti // NT
                k = ti % 2
                r0 = ti * 128
                if ti % NT == 0:
                    dma("sp", gt2[:], gt_d[b:b + 1, 1024:2048].partition_broadcast(128), [bGT], [bgt2], "gt2")
                dma("sp", xps[k][:], x1_d[r0:r0 + 128, :], [bXO[0], bXO[1]], [bxps[k]], f"xp{k}")
                mv, rs, bm, br = ln_stats(xps[k], bxps[k])
                op("dve", lambda e, k=k, mv=mv, rs=rs: e.tensor_scalar(out=xn2[:], in0=xps[k][:], scalar1=mv[:, 0:1], scalar2=rs[:],
                                                                      op0=ALU.subtract, op1=ALU.mult),
                   reads=[bxps[k], bm, br], writes=[bxn2])
                for hf in range(2):
                    bk = nextbank(RB)

                    def trp(e, hf=hf, bk=bk):
                        for cc in range(4):
                            c = hf * 4 + cc
                            e.transpose(out=PB[bk][:, cc * 128:(cc + 1) * 128], in_=xn2[:, c * 128:(c + 1) * 128], identity=identf[:])
                    op("pe", trp, reads=[bxn2, bC], writes=[bPB[bk]])
                    for cc in range(4):
                        c = hf * 4 + cc
                        if cc % 2 == 0:
                            op("act", lambda e, c=c, cc=cc, bk=bk, b=b: e.activation(out=h2T[:, c, :], in_=PB[bk][:, cc * 128:(cc + 1) * 128],
                                                                                    func=AF.Identity, scale=modcol(4, c, b), bias=modcol(3, c, b)),
                               reads=[bPB[bk], bmodT], writes=[bh2])
                        else:
                            op("dve", lambda e, c=c, cc=cc, bk=bk, b=b: e.tensor_scalar(out=h2T[:, c, :], in0=PB[bk][:, cc * 128:(cc + 1) * 128],
                                                                                       scalar1=modcol(4, c, b), scalar2=modcol(3, c, b),
                                                                                       op0=ALU.mult, op1=ALU.add),
                               reads=[bPB[bk], bmodT], writes=[bh2])
                op("pool", lambda e: e.tensor_copy(out=h2Tb[:].rearrange("p a b -> p (a b)"), in_=h2T[:].rearrange("p a b -> p (a b)")),
                   reads=[bh2], writes=[bh2b])
                sbk = None
                for h in range(8 if _PS >= 2 else 0):
                    ws = h % 2
                    dma("sp", wq[ws][:], wpq3[h * 128:(h + 1) * 128, :, :], [], [bwq[ws]], f"wq{ws}")
                    bk = nextbank(RB)

                    def mmq(e, ws=ws, bk=bk):
                        for kc in range(8):
                            e.matmul(PB[bk][:, 0:128], lhsT=wq[ws][:, kc, :], rhs=h2T[:, kc, :], start=(kc == 0), stop=(kc == 7))
                    op("pe", mmq, reads=[bwq[ws], bh2], writes=[bPB[bk]])
                    op("act", lambda e, ws=ws, bk=bk: e.activation(out=qTh[ws][:], in_=PB[bk][:, 0:128], func=AF.Copy),
                       reads=[bPB[bk]], writes=[bqTh[ws]])
                    if h % 2 == 0:
                        sbk = nextbank(RB)

                    def mms(e, ws=ws, sbk=sbk, h=h):
                        o0 = (h % 2) * 256
                        e.matmul(PB[sbk][:, o0:o0 + 256], lhsT=qTh[ws][:, :], rhs=keysT[:, :], start=True, stop=True)
                    op("pe", mms, reads=[bqTh[ws], bl2], writes=[bPB[sbk]])
                    if h % 2 == 1:
                        op("dve", lambda e, h=h, sbk=sbk: e.tensor_copy(out=s_sb[:, h - 1:h + 1, :, :].rearrange("p a b c -> p (a b c)"), in_=PB[sbk][:, :]),
                           reads=[bPB[sbk]], writes=[bs])
                if _PS < 3:
                    dma("sp", out_d[r0:r0 + 128, :], xps[k][:], [bxps[k], bs, bh2b], [bPO[k]], f"po{k}")
                    continue
                for h in range(8):
                    for half in range(2):
                        op("dve", lambda e, h=h, half=half: e.max(out=m16[:, h, half, 0:8], in_=s_sb[:, h, half, :]), reads=[bs], writes=[bm16])
                        op("dve", lambda e, h=h, half=half: e.match_replace(out=wk1[:], in_to_replace=m16[:, h, half, 0:8], in_values=s_sb[:, h, half, :],
                                                                           imm_value=-1e30),
                           reads=[bs, bm16], writes=[bwk1])
                        op("dve", lambda e, h=h, half=half: e.max(out=m16[:, h, half, 8:16], in_=wk1[:]), reads=[bwk1], writes=[bm16])
                op("dve", lambda e: e.tensor_tensor(out=cand[:], in0=m16[:, :, 0, :].unsqueeze(3).to_broadcast([128, 8, 16, 16]),
                                                    in1=m16[:, :, 1, :].unsqueeze(2).to_broadcast([128, 8, 16, 16]), op=ALU.add),
                   reads=[bm16], writes=[bcand])
                for h in range(8):
                    ch2 = cand[:, h, :, :].rearrange("p a b -> p (a b)")
                    op("dve", lambda e, h=h, ch2=ch2: e.max(out=c16[:, h, 0:8], in_=ch2), reads=[bcand], writes=[bc16])
                    op("dve", lambda e, h=h, ch2=ch2: e.match_replace(out=wk2[:], in_to_replace=c16[:, h, 0:8], in_values=ch2, imm_value=-1e30),
                       reads=[bcand, bc16], writes=[bwk2])
                    op("dve", lambda e, h=h: e.max(out=c16[:, h, 8:16], in_=wk2[:]), reads=[bwk2], writes=[bc16])
                op("dve", lambda e: e.tensor_scalar(out=negthr[:], in0=c16[:, :, 15], scalar1=-1.0, scalar2=None, op0=ALU.mult),
                   reads=[bc16], writes=[bsm])
                for h in range(8):
                    op("act", lambda e, h=h: e.activation(out=junk[:, 0:16], in_=c16[:, h, :], func=AF.Exp, bias=negthr[:, h:h + 1],
                                                          accum_out=zz[:, h:h + 1]),
                       reads=[bc16, bsm], writes=[bsm, bjunk])
                op("act", lambda e: e.activation(out=zz[:], in_=zz[:], func=AF.Ln), reads=[bsm], writes=[bsm])
                op("dve", lambda e: e.tensor_tensor(out=biasE[:], in0=negthr[:], in1=zz[:], op=ALU.subtract), reads=[bsm], writes=[bsm])
                if _PS < 4:
                    dma("sp", out_d[r0:r0 + 128, :], xps[k][:], [bxps[k], bsm], [bPO[k]], f"po{k}")
                    continue
                def g_finish(pc, h, g):
                    gm = g % 2
                    op("dve", lambda e: e.scalar_tensor_tensor(out=Mk[gm][:], in0=Sp[g][:], scalar=c16[:, h, 15:16], in1=Ep[g][:],
                                                               op0=ALU.is_ge, op1=ALU.mult),
                       reads=[bSp[g], bEp[g], bc16], writes=[bMk[gm]])

                    def accmm(e):
                        for q in range(4):
                            e.matmul(PB[q][:, :], lhsT=identb[:, :], rhs=Mk[gm][:, q * 512:(q + 1) * 512],
                                     start=(h == 0), stop=(h == 7))
                    op("pe", accmm, reads=[bMk[gm], bC], writes=[bPB[0], bPB[1], bPB[2], bPB[3]])
                    if h == 7:
                        for q in range(4):
                            op("act", lambda e, q=q: e.activation(out=acc[:, pc * 2048 + q * 512:pc * 2048 + (q + 1) * 512], in_=PB[q][:, :], func=AF.Copy),
                               reads=[bPB[q]], writes=[bacc[pc]])

                def g_start(pc, h):
                    g = gstate[0] % 3
                    gstate[0] += 1
                    op("dve", lambda e: e.tensor_tensor(
                        out=Sp[g][:].rearrange("p (a b) -> p a b", a=16),
                        in0=s_sb[:, h, 0, pc * 16:(pc + 1) * 16].unsqueeze(2).to_broadcast([128, 16, 128]),
                        in1=s_sb[:, h, 1, :].unsqueeze(1).to_broadcast([128, 16, 128]), op=ALU.add),
                       reads=[bs], writes=[bSp[g]])
                    op("act", lambda e: e.activation(out=Ep[g][:], in_=Sp[g][:], func=AF.Exp, bias=biasE[:, h:h + 1]),
                       reads=[bSp[g], bsm], writes=[bEp[g]])
                    return g

                def x_s0(ec):
                    u = gstate[1] % 3
                    gstate[1] += 1
                    dma("sp", ub[u][:].rearrange("p a b -> p (a b)"), ut_b[ec * 128:(ec + 1) * 128, :], [bDR], [bub[u]], f"ub{u}")
                    return {"ec": ec, "u": u}

                def x_s1a(st):
                    ec, u = st["ec"], st["u"]
                    v = gstate[2] % 3
                    w = gstate[2] % 2
                    gstate[2] += 1
                    st["v"], st["w"] = v, w
                    dma("sp", vb[v][:], v_b3[:, ec * 4:(ec + 1) * 4, :], [bDR], [bvb[v]], f"vb{v}")

                    def mma(e):
                        for kc in range(8):
                            e.matmul(PB[4][:, :], lhsT=h2Tb[:, kc, :], rhs=ub[u][:, kc, :], start=(kc == 0), stop=(kc == 7))
                    op("pe", mma, reads=[bh2b, bub[u]], writes=[bPB[4]])

                def x_s1b(st):
                    w = st["w"]
                    op("act", lambda e: e.activation(out=gel[w][:], in_=PB[4][:, :], func=AF.Gelu), reads=[bPB[4]], writes=[bgel[w]])

                def x_s2(st):
                    ec, w = st["ec"], st["w"]
                    op("dve", lambda e: e.tensor_tensor(out=Wc[w][:], in0=gel[w][:], in1=acc[:, ec * 512:(ec + 1) * 512], op=ALU.mult),
                       reads=[bgel[w], bacc[ec // 4]], writes=[bWc[w]])
                    ptb = PB[5].bitcast(BF16)

                    def trw(e):
                        for a in range(4):
                            e.transpose(out=ptb[:, a * 128:(a + 1) * 128], in_=Wc[w][:, a * 128:(a + 1) * 128], identity=identb[:])
                    op("pe", trw, reads=[bWc[w], bC], writes=[bPB[5]])
                    op("act", lambda e: e.activation(out=WT[w][:].rearrange("p a b -> p (a b)"), in_=ptb[:, 0:512], func=AF.Copy),
                       reads=[bPB[5]], writes=[bWT[w]])

                def x_s3(st):
                    ec, v, w = st["ec"], st["v"], st["w"]

                    def mmv3(e):
                        for a in range(4):
                            for half in range(2):
                                e.matmul(PB[6 + half][:, :], lhsT=WT[w][:, a, :], rhs=vb[v][:, a, half * 512:(half + 1) * 512],
                                         start=(ec == 0 and a == 0), stop=(ec == 31 and a == 3))
                    op("pe", mmv3, reads=[bWT[w], bvb[v]], writes=[bPB[6], bPB[7]])

                xq = []
                R = {"P0": None, "A": None, "P1g": None, "P2new": None, "P2": None}

                def phase_a():
                    if R["P0"] is not None:
                        x_s1a(R["P0"])
                        R["A"] = R["P0"]
                        R["P0"] = None
                    if xq:
                        R["P0"] = x_s0(xq.pop(0))
                    if R["P1g"] is not None:
                        x_s2(R["P1g"])
                        R["P2new"] = R["P1g"]
                        R["P1g"] = None

                def phase_b():
                    if R["P2"] is not None:
                        x_s3(R["P2"])
                        R["P2"] = None
                    R["P2"] = R["P2new"]
                    R["P2new"] = None

                def phase_c():
                    if R["A"] is not None:
                        x_s1b(R["A"])
                        R["P1g"] = R["A"]
                        R["A"] = None

                pend = None
                for pc in range(8):
                    for hp in range(4):
                        phase_a()
                        for h in (2 * hp, 2 * hp + 1):
                            g = g_start(pc, h)
                            if pend is not None:
                                g_finish(*pend)
                            pend = (pc, h, g)
                            if h % 2 == 0:
                                phase_b()
                        phase_c()
                    g_finish(*pend)
                    pend = None
                    xq.extend(range(4 * pc, 4 * pc + 4))
                while xq or any(v_ is not None for v_ in R.values()):
                    phase_a()
                    phase_b()
                    phase_c()
                for half in range(2):
                    op("dve", lambda e, half=half: e.tensor_tensor(out=tt2[:, half * 512:(half + 1) * 512], in0=PB[6 + half][:, :],
                                                                  in1=gt2[:, half * 512:(half + 1) * 512], op=ALU.mult),
                       reads=[bPB[6 + half], bgt2], writes=[btt2])
                op("dve", lambda e, k=k: e.scalar_tensor_tensor(out=xps[k][:], in0=xps[k][:], scalar=ALPHA, in1=tt2[:], op0=ALU.mult, op1=ALU.add),
                   reads=[bxps[k], btt2], writes=[bxps[k]])
                mv, rs, bm, br = ln_stats(xps[k], bxps[k])
                op("dve", lambda e, k=k, mv=mv, rs=rs: e.tensor_scalar(out=xps[k][:], in0=xps[k][:], scalar1=mv[:, 0:1], scalar2=rs[:],
                                                                      op0=ALU.subtract, op1=ALU.mult),
                   reads=[bxps[k], bm, br], writes=[bxps[k]])
                op("pool", lambda e, k=k: e.tensor_tensor(out=xps[k][:], in0=xps[k][:], in1=l2g[:], op=ALU.mult), reads=[bxps[k], bl2], writes=[bxps[k]])
                op("pool", lambda e, k=k: e.tensor_tensor(out=xps[k][:], in0=xps[k][:], in1=l2b[:], op=ALU.add), reads=[bxps[k], bl2], writes=[bxps[k]])
                dma("sp", out_d[r0:r0 + 128, :], xps[k][:], [bxps[k]], [bPO[k]], f"po{k}")
            A.pop()
        SC.barrier()
        print("arena peak", A.peak, "ops", SC.nops, "waits", SC.nwait)
        SC.emit_all()
    return nc


def _kmaj(w):
    return np.ascontiguousarray(w.reshape(8, 128, -1).transpose(1, 0, 2))


def prep_shared(inp, do_peer=True):
    f = np.float32
    w_in = np.asarray(inp["w_in"][0], f)
    da_q, da_k, da_v = w_in[:, 0:1024], w_in[:, 1024:2048], w_in[:, 2048:3072]
    ml_q, ml_k = w_in[:, 3072:3584], w_in[:, 3584:4096]
    ml_v, ml_o = w_in[:, 4096:5120], w_in[:, 5120:6144]
    ml_if = w_in[:, 6144:6152]
    g_attn, g_ml = w_in[:, 6152:7176], w_in[:, 7176:8200]
    wba = np.asarray(inp["w_br_attn"][0], f)
    wbm = np.asarray(inp["w_br_mlstm"][0], f)
    sh = {}
    sh["w_ada"] = _kmaj(np.asarray(inp["w_ada"][0], f)).reshape(128, 8 * 6144)
    sh["b_ada"] = np.asarray(inp["b_ada"], f).reshape(1, 6144)
    sh["w_da"] = np.stack([np.concatenate([_kmaj(da_q[:, h * 128:(h + 1) * 128]), _kmaj(da_k[:, h * 128:(h + 1) * 128]),
                                           _kmaj(da_v[:, h * 128:(h + 1) * 128])], axis=2) for h in range(8)]).reshape(1024, 3072)
    sh["w_ml"] = np.stack([np.concatenate([_kmaj(ml_q[:, h * 128:(h + 1) * 128]), _kmaj(ml_k[:, h * 128:(h + 1) * 128]),
                                           _kmaj(ml_v[:, h * 256:(h + 1) * 256]), _kmaj(ml_o[:, h * 256:(h + 1) * 256])], axis=2)
                           for h in range(4)]).reshape(512, 6144)
    sh["w_if"] = _kmaj(ml_if).reshape(128, 64)
    sh["w_mg"] = np.stack([np.concatenate([_kmaj(g_attn[:, c * 128:(c + 1) * 128]), _kmaj(g_ml[:, c * 128:(c + 1) * 128]),
                                           _kmaj(wba[:, c * 128:(c + 1) * 128]), _kmaj(wbm[:, c * 128:(c + 1) * 128])], axis=2)
                           for c in range(8)]).reshape(1024, 4096)
    sh["w_out"] = _kmaj(np.asarray(inp["w_out"][0], f)).reshape(128, 8192)
    wq = np.asarray(inp["peer_wq"][0], f)
    sh["w_pq"] = np.stack([_kmaj(wq[:, h * 128:(h + 1) * 128]) for h in range(8)]).reshape(1024, 1024)
    if do_peer:
        u = np.asarray(inp["peer_u"][0], f)
        sh["UT"] = np.stack([_kmaj(np.ascontiguousarray(u[ec * 512:(ec + 1) * 512, :].T)) for ec in range(32)]).reshape(4096, 4096)
        sh["V"] = np.ascontiguousarray(np.asarray(inp["peer_v"][0], f))
    else:
        sh["UT"] = np.zeros((4096, 4096), f)
        sh["V"] = np.zeros((NEXP, D), f)
    sh["b_ifT"] = np.ascontiguousarray(np.asarray(inp["b_if"][0], f).T)
    sh["conv_wT"] = np.ascontiguousarray(np.asarray(inp["conv_w"][0], f).reshape(4, 8, 128).transpose(2, 1, 0)).reshape(128, 32)
    sh["conv_bT"] = np.ascontiguousarray(np.asarray(inp["conv_b"][0], f).reshape(8, 128).T)
    sh["da_lambda"] = np.asarray(inp["da_lambda"][0], f).reshape(1, 256)
    sh["subln_g"] = np.asarray(inp["da_subln_g"][0], f).reshape(1, 128)
    sh["ml_norm_g"] = np.asarray(inp["ml_norm_g"][0], f).reshape(1, D)
    for nm in ("ln1_g", "ln1_b", "ln2_g", "ln2_b"):
        sh[nm] = np.asarray(inp[nm][0], f).reshape(1, D)
    kt = np.ascontiguousarray(np.asarray(inp["peer_keys"][0], f).transpose(0, 2, 1))
    kz = np.zeros((128, 256), f)
    kz[0:64, 0:128] = kt[0]
    kz[64:128, 128:256] = kt[1]
    sh["keysT"] = kz
    sh["ident"] = np.eye(128, dtype=f)
    kk = np.arange(128)
    sh["cmask"] = (kk[None, :] >= kk[:, None]).astype(f)
    sel = np.zeros((4, 4, 128), f)
    for h in range(4):
        sel[h, h, :] = 1.0
    sh["sel4"] = sel.reshape(4, 512)
    return sh


def core_inputs(inp, sh, b0, nseq):
    f = np.float32
    m = dict(sh)
    m["x"] = np.ascontiguousarray(np.asarray(inp["x"][b0:b0 + nseq], f).reshape(nseq * SEQ, D))
    c = np.asarray(inp["c"][b0:b0 + nseq], f)
    m["cT"] = _kmaj(np.ascontiguousarray(c.T)).reshape(128, 8 * nseq)
    return m


_NC_CACHE = {}


def kernel(**inputs):
    if "full" not in _NC_CACHE:
        _NC_CACHE["full"] = build(NSEQ_FULL, True)
    nc = _NC_CACHE["full"]
    sh = prep_shared(inputs, True)
    in_maps = [core_inputs(inputs, sh, i * NSEQ_FULL, NSEQ_FULL) for i in range(NCORES)]
    res = run_bass_kernel_spmd(nc, in_maps, core_ids=list(range(NCORES)))
    out = np.concatenate([np.asarray(r["out"]).reshape(NSEQ_FULL, SEQ, D) for r in res.results], axis=0)
    return out.astype(np.float32)
```

```python
import contextlib
import math
import numpy as np
import concourse.bass as bass
import concourse.mybir as mybir
from concourse.bass_utils import run_bass_kernel_spmd

F32 = mybir.dt.float32
BF16 = mybir.dt.bfloat16
AF = mybir.ActivationFunctionType
ALU = mybir.AluOpType

D = 1024
SEQ = 2048
NT = SEQ // 128
NCORES = 8
BATCH = 32
NSEQ_FULL = BATCH // NCORES
ALPHA = 2.0 ** 0.25
LN_EPS = 1e-5
LAMBDA_INIT = 0.8 - 0.6 * math.exp(0.0)
NEXP = 16384


class Buf:
    __slots__ = ("name", "last_w", "readers")

    def __init__(self, name):
        self.name = name
        self.last_w = None
        self.readers = []


class _Rec:
    def __init__(self):
        self.calls = []

    def __getattr__(self, name):
        def f(*a, **kw):
            self.calls.append((name, a, kw))
            return None
        return f


class Sched:
    ENGS = ("pe", "act", "dve", "pool", "sp")

    def __init__(self, nc):
        self.nc = nc
        self.ops = {e: [] for e in self.ENGS}
        self.cnt = {e: 0 for e in self.ENGS}
        self.seen = {e: {} for e in self.ENGS}
        self.sems = {}
        self.dma_cnt = {}
        self.nops = 0
        self.nwait = 0

    def op(self, eng, emit, reads=(), writes=(), dma=None, n=1):
        deps = {}
        for b in reads:
            if b.last_w is not None:
                k, v, e = b.last_w
                if not (e == eng and k[0] == "eng" and False):
                    deps[k] = max(deps.get(k, 0), v)
        for b in writes:
            if b.last_w is not None:
                k, v, e = b.last_w
                if not (e == eng and k[0] == "eng" and dma is None):
                    deps[k] = max(deps.get(k, 0), v)
            for (k, v, e) in b.readers:
                if e == eng and k[0] == "eng" and dma is None:
                    continue
                deps[k] = max(deps.get(k, 0), v)
        waits = []
        seen = self.seen[eng]
        for k, v in deps.items():
            if seen.get(k, 0) >= v:
                continue
            seen[k] = v
            waits.append((k, v))
        if dma is None:
            self.cnt[eng] += 1
            key = ("eng", eng)
            tok = (key, self.cnt[eng], eng)
        else:
            key = ("dma", dma)
            self.dma_cnt[key] = self.dma_cnt.get(key, 0) + 16 * n
            tok = (key, self.dma_cnt[key], eng)
        for b in reads:
            b.readers.append(tok)
        for b in writes:
            b.last_w = tok
            b.readers = []
        rec = _Rec()
        emit(rec)
        assert len(rec.calls) >= 1
        if dma is not None:
            assert len(rec.calls) == n, (len(rec.calls), n)
        self.ops[eng].append((waits, rec.calls, key, dma is not None, n))
        self.nops += 1
        self.nwait += len(waits)
        return tok

    def barrier(self):
        cur = {("eng", e): self.cnt[e] for e in self.ENGS if self.cnt[e] > 0}
        cur.update(self.dma_cnt)
        for eng in self.ENGS:
            waits = []
            seen = self.seen[eng]
            for k, v in cur.items():
                if k == ("eng", eng) or seen.get(k, 0) >= v:
                    continue
                seen[k] = v
                waits.append((k, v))
            if waits:
                self.cnt[eng] += 1
                self.ops[eng].append((waits, [("nop", (), {})], ("eng", eng), False, 1))

    def emit_all(self):
        nc = self.nc
        keys = [("eng", e) for e in self.ENGS]
        for e in self.ENGS:
            for rec in self.ops[e]:
                if rec[2] not in keys:
                    keys.append(rec[2])
        for k in keys:
            self.sems[k] = nc.alloc_semaphore("s_" + "_".join(str(x) for x in k))
        with nc.Block() as block:
            def mk(ename):
                def body(eng):
                    for waits, calls, key, is_dma, n in self.ops[ename]:
                        for (k, v) in waits:
                            eng.wait_ge(self.sems[k], v)
                        s = self.sems[key]
                        r = None
                        for (name, a, kw) in calls:
                            r = getattr(eng, name)(*a, **kw)
                            if is_dma:
                                r.then_inc(s, 16)
                        if not is_dma:
                            r.then_inc(s, 1)
                return body
            block.tensor(mk("pe"))
            block.scalar(mk("act"))
            block.vector(mk("dve"))
            block.gpsimd(mk("pool"))
            block.sync(mk("sp"))


class Arena:
    def __init__(self, t, nbytes):
        self.t = t
        self.nbytes = nbytes
        self.off = 0
        self.stack = []
        self.peak = 0

    def push(self):
        self.stack.append(self.off)

    def pop(self):
        self.off = self.stack.pop()

    def alloc(self, shape, dt):
        esz = 4 if dt == F32 else 2
        n = 1
        for s in shape[1:]:
            n *= s
        nb = (n * esz + 63) // 64 * 64
        assert self.off + nb <= self.nbytes, ("arena overflow", self.off, nb, self.nbytes)
        a = self.t[0:shape[0], self.off // 4:(self.off + nb) // 4]
        self.off += nb
        self.peak = max(self.peak, self.off)
        if dt != F32:
            a = a.bitcast(dt)
        a = a[:, 0:n]
        if len(shape) == 3:
            a = a.rearrange("p (a b) -> p a b", a=shape[1])
        elif len(shape) == 4:
            a = a.rearrange("p (a b c) -> p a b c", a=shape[1], b=shape[2])
        return a


def build(NSEQ=NSEQ_FULL, do_peer=True, dbg=False):
    nc = bass.Bass("TRN2", target_bir_lowering=False)
    SC = Sched(nc)
    op = SC.op
    NTOK = NSEQ * SEQ

    def din(name, shape, dt=F32):
        return nc.dram_tensor(name, list(shape), dt, kind="ExternalInput").ap()

    def dscr(name, shape, dt):
        return nc.dram_tensor(name, list(shape), dt, kind="Internal").ap()

    x_d = din("x", [NTOK, D])
    cT_d = din("cT", [128, 8 * NSEQ])
    wada_d = din("w_ada", [128, 8 * 6144])
    bada_d = din("b_ada", [1, 6144])
    wda_d = din("w_da", [8 * 128, 3072])
    wml_d = din("w_ml", [4 * 128, 6144])
    wif_d = din("w_if", [128, 64])
    wmg_d = din("w_mg", [8 * 128, 4096])
    wout_d = din("w_out", [128, 8192])
    wpq_d = din("w_pq", [8 * 128, 1024])
    ut_d = din("UT", [32 * 128, 4096])
    v_d = din("V", [NEXP, D])
    bif_d = din("b_ifT", [4, 2])
    cw_d = din("conv_wT", [128, 32])
    cb_d = din("conv_bT", [128, 8])
    lam_d = din("da_lambda", [1, 256])
    subg_d = din("subln_g", [1, 128])
    mlg_d = din("ml_norm_g", [1, D])
    ln1g_d = din("ln1_g", [1, D])
    ln1b_d = din("ln1_b", [1, D])
    ln2g_d = din("ln2_g", [1, D])
    ln2b_d = din("ln2_b", [1, D])
    keysT_d = din("keysT", [128, 256])
    ident_d = din("ident", [128, 128])
    cmask_d = din("cmask", [128, 128])
    sel4_d = din("sel4", [4, 512])
    out_d = nc.dram_tensor("out", [NTOK, D], F32, kind="ExternalOutput").ap()

    wda_b = dscr("w_da_b", [8 * 128, 3072], BF16)
    wml_b = dscr("w_ml_b", [4 * 128, 6144], BF16)
    wif_b = dscr("w_if_b", [128, 64], BF16)
    wmg_b = dscr("w_mg_b", [8 * 128, 4096], BF16)
    wout_b = dscr("w_out_b", [128, 8192], BF16)
    ut_b = dscr("UT_b", [32 * 128, 4096], BF16)
    v_b = dscr("V_b", [NEXP, D], BF16)
    x1_d = dscr("x1_s", [NTOK, D], F32)
    gt_d = dscr("gt_s", [NSEQ, 2048], F32)

    es = contextlib.ExitStack()
    with es:
        ARENA_BYTES = 206 * 1024
        arena_t = es.enter_context(nc.sbuf_tensor("arena", [128, ARENA_BYTES // 4], F32))
        A = Arena(arena_t, ARENA_BYTES)
        psum_t = es.enter_context(nc.psum_tensor("psum", [128, 8, 512], F32))
        PB = [psum_t[:, i, :] for i in range(8)]
        bPB = [Buf(f"pb{i}") for i in range(8)]
        bank_rr = [0]

        def nextbank(banks):
            i = banks[bank_rr[0] % len(banks)]
            bank_rr[0] += 1
            return i

        def dma(eng, out, in_, reads, writes, key):
            return op(eng, lambda e: e.dma_start(out=out, in_=in_), reads=reads, writes=writes, dma=key)

        dbg_outs = {}

        def dump(name, ap, bufs, shape, dt):
            if not dbg:
                return
            t = nc.dram_tensor("d_" + name, list(shape), dt, kind="ExternalOutput").ap()
            dbg_outs[name] = t
            dma("sp", t, ap, list(bufs), [Buf("dbg")], "dbg_" + name)

        bOUT = Buf("out")
        bDR = Buf("dram_scratch")
        bGT = Buf("gt_scratch")
        bXO = [Buf("xo0"), Buf("xo1")]

        identf = A.alloc([128, 128], F32)
        identb = A.alloc([128, 128], BF16)
        maskb = A.alloc([128, 128], BF16)
        sel4 = A.alloc([4, 512], F32)
        cm05 = A.alloc([128, 8], F32)
        ones4 = A.alloc([4, 512], F32)
        modT = A.alloc([128, 48 * NSEQ], F32)
        neglam = A.alloc([128, 1], F32)
        gda_bc = A.alloc([128, 128], F32)
        bifT = A.alloc([4, 2], F32)
        nbf = A.alloc([4, 1], F32)
        cwT = A.alloc([128, 32], F32)
        cbT = A.alloc([128, 8], F32)
        wif = A.alloc([128, 8, 8], BF16)
        junk = A.alloc([128, 256], F32)
        bC = Buf("consts")
        bjunk = Buf("junk")
        NST = 4
        stt = [A.alloc([128, 12], F32) for _ in range(NST)]
        mvt = [A.alloc([128, 2], F32) for _ in range(NST)]
        rst = [A.alloc([128, 1], F32) for _ in range(NST)]
        bst = [Buf(f"st{i}") for i in range(NST)]
        bmv = [Buf(f"mv{i}") for i in range(NST)]
        brs = [Buf(f"rs{i}") for i in range(NST)]
        stk = [0]

        def ln_stats(src, bsrc):
            k = stk[0] % NST
            stk[0] += 1
            st, mv, rs = stt[k], mvt[k], rst[k]
            op("dve", lambda e: e.bn_stats(out=st[:, 0:6], in_=src[:, 0:512]), reads=[bsrc], writes=[bst[k]])
            op("dve", lambda e: e.bn_stats(out=st[:, 6:12], in_=src[:, 512:1024]), reads=[bsrc], writes=[bst[k]])
            op("dve", lambda e: e.bn_aggr(out=mv[:], in_=st[:]), reads=[bst[k]], writes=[bmv[k]])
            op("dve", lambda e: e.tensor_scalar(out=rs[:], in0=mv[:, 1:2], scalar1=LN_EPS, scalar2=None, op0=ALU.add),
               reads=[bmv[k]], writes=[brs[k]])
            op("pool", lambda e: e.tensor_tensor(out=rs[:], in0=rs[:], in1=cm05[:, 0:1], op=ALU.pow),
               reads=[brs[k], bC], writes=[brs[k]])
            return mv, rs, bmv[k], brs[k]

        A.push()
        tmpf = A.alloc([128, 128], F32)
        tmpm = A.alloc([128, 128], F32)
        lamt = A.alloc([128, 256], F32)
        lamp = A.alloc([128, 128], F32)
        lams = A.alloc([128, 2], F32)
        wif_f = A.alloc([128, 64], F32)
        btmp = Buf("tmp0")
        dma("sp", identf[:], ident_d[:, :], [], [bC], "c0")
        dma("sp", tmpm[:], cmask_d[:, :], [], [btmp], "c1")
        dma("sp", sel4[:], sel4_d[:, :], [], [bC], "c0")
        dma("sp", bifT[:], bif_d[:, :], [], [bC], "c0")
        dma("sp", cwT[:], cw_d[:, :], [], [bC], "c0")
        dma("sp", cbT[:], cb_d[:, :], [], [bC], "c0")
        dma("sp", gda_bc[:], subg_d[0:1, :].partition_broadcast(128), [], [bC], "c0")
        dma("sp", lamt[:], lam_d[0:1, :].partition_broadcast(128), [], [btmp], "c1")
        dma("sp", wif_f[:], wif_d[:, :], [], [btmp], "c1")
        op("pool", lambda e: e.tensor_copy(out=identb[:], in_=identf[:]), reads=[bC], writes=[bC])
        op("pool", lambda e: e.tensor_copy(out=maskb[:], in_=tmpm[:]), reads=[btmp], writes=[bC])
        op("pool", lambda e: e.memset(cm05[:], -0.5), writes=[bC])
        op("pool", lambda e: e.memset(ones4[:], 1.0), writes=[bC])
        op("pool", lambda e: e.tensor_copy(out=wif[:].rearrange("p a b -> p (a b)"), in_=wif_f[:]), reads=[btmp], writes=[bC])
        op("dve", lambda e: e.tensor_scalar(out=gda_bc[:], in0=gda_bc[:], scalar1=1.0 - LAMBDA_INIT, scalar2=None, op0=ALU.mult),
           reads=[bC], writes=[bC])
        op("dve", lambda e: e.tensor_scalar(out=nbf[:], in0=bifT[:, 1:2], scalar1=-1.0, scalar2=None, op0=ALU.mult),
           reads=[bC], writes=[bC])
        lt3 = lamt[:].rearrange("p (a b) -> p a b", a=4)
        op("dve", lambda e: e.tensor_tensor(out=lamp[:].rearrange("p (a b) -> p a b", a=2),
                                            in0=lt3[:, 0:4:2, :], in1=lt3[:, 1:4:2, :], op=ALU.mult),
           reads=[btmp], writes=[btmp])
        op("dve", lambda e: e.tensor_reduce(out=lams[:], in_=lamp[:].rearrange("p (a b) -> p a b", a=2),
                                            axis=mybir.AxisListType.X, op=ALU.add),
           reads=[btmp], writes=[btmp])
        op("act", lambda e: e.activation(out=lams[:], in_=lams[:], func=AF.Exp), reads=[btmp], writes=[btmp])
        op("dve", lambda e: e.tensor_tensor(out=neglam[:], in0=lams[:, 1:2], in1=lams[:, 0:1], op=ALU.subtract),
           reads=[btmp], writes=[bC])
        op("dve", lambda e: e.tensor_scalar(out=neglam[:], in0=neglam[:], scalar1=-LAMBDA_INIT, scalar2=None, op0=ALU.add),
           reads=[bC], writes=[bC])
        SC.barrier()
        A.pop()

        A.push()
        NCV = 3
        CW = 4096
        cvf = [A.alloc([128, CW], F32) for _ in range(NCV)]
        cvb = [A.alloc([128, CW], BF16) for _ in range(NCV)]
        bcvf = [Buf(f"cvf{i}") for i in range(NCV)]
        bcvb = [Buf(f"cvb{i}") for i in range(NCV)]
        kcv = [0]

        def convert(src, dst, R, C):
            for r0 in range(0, R, 128):
                for c0 in range(0, C, CW):
                    cw = min(CW, C - c0)
                    k = kcv[0] % NCV
                    kk = kcv[0]
                    kcv[0] += 1
                    dma("sp", cvf[k][:, 0:cw], src[r0:r0 + 128, c0:c0 + cw], [], [bcvf[k]], f"cvf{k}")
                    if kk % 2 == 0:
                        op("dve", lambda e, k=k, cw=cw: e.tensor_copy(out=cvb[k][:, 0:cw], in_=cvf[k][:, 0:cw]),
                           reads=[bcvf[k]], writes=[bcvb[k]])
                    else:
                        op("act", lambda e, k=k, cw=cw: e.activation(out=cvb[k][:, 0:cw], in_=cvf[k][:, 0:cw], func=AF.Copy),
                           reads=[bcvf[k]], writes=[bcvb[k]])
                    dma("pool", dst[r0:r0 + 128, c0:c0 + cw], cvb[k][:, 0:cw], [bcvb[k]], [bDR], f"cvb{k}")

        convert(wda_d, wda_b, 1024, 3072)
        convert(wml_d, wml_b, 512, 6144)
        convert(wmg_d, wmg_b, 1024, 4096)
        convert(wout_d, wout_b, 128, 8192)
        import os as _os0
        if do_peer and not int(_os0.environ.get("NOCONV", 0)):
            convert(ut_d, ut_b, 4096, 4096)
            convert(v_d, v_b, NEXP, D)
        SC.barrier()
        A.pop()

        A.push()
        siluT = A.alloc([128, 8, NSEQ], F32)
        modall = A.alloc([NSEQ, 6144], F32)
        badab = A.alloc([NSEQ, 6144], F32)
        wad = [A.alloc([128, 8, 512], F32) for _ in range(2)]
        bwad = [Buf("wad0"), Buf("wad1")]
        bsil = Buf("silu")
        bmod = Buf("modall")
        bmodT = Buf("modT")
        dma("sp", siluT[:].rearrange("p a b -> p (a b)"), cT_d[:, :], [], [bsil], "c1")
        dma("sp", badab[:], bada_d[0:1, :].partition_broadcast(NSEQ), [], [bmod], "c0")
        op("act", lambda e: e.activation(out=siluT[:].rearrange("p a b -> p (a b)"),
                                         in_=siluT[:].rearrange("p a b -> p (a b)"), func=AF.Silu),
           reads=[bsil], writes=[bsil])
        wada3 = wada_d.rearrange("p (a b) -> p a b", a=8)
        for pc in range(12):
            k = pc % 2
            dma("sp", wad[k][:], wada3[:, :, pc * 512:(pc + 1) * 512], [], [bwad[k]], f"wad{k}")
            bk = nextbank([0, 1])

            def mmg(e, k=k, bk=bk):
                r = None
                for kc in range(8):
                    r = e.matmul(PB[bk][0:NSEQ, :], lhsT=siluT[:, kc, :], rhs=wad[k][:, kc, :],
                                 start=(kc == 0), stop=(kc == 7))
                return r
            op("pe", mmg, reads=[bsil, bwad[k]], writes=[bPB[bk]])
            op("dve", lambda e, bk=bk, pc=pc: e.tensor_tensor(out=modall[:, pc * 512:(pc + 1) * 512], in0=PB[bk][0:NSEQ, :],
                                                              in1=badab[:, pc * 512:(pc + 1) * 512], op=ALU.add),
               reads=[bPB[bk], bmod], writes=[bmod])
        dma("sp", gt_d[:, 0:1024], modall[:, 2048:3072], [bmod], [bGT], "gtd")
        dma("sp", gt_d[:, 1024:2048], modall[:, 5120:6144], [bmod], [bGT], "gtd")
        bk = nextbank([0, 1])

        def trg(e, bk=bk):
            r = None
            for c in range(48):
                r = e.transpose(out=PB[bk][:, c * NSEQ:(c + 1) * NSEQ], in_=modall[0:NSEQ, c * 128:(c + 1) * 128],
                                identity=identf[0:NSEQ, 0:NSEQ])
            return r
        op("pe", trg, reads=[bmod, bC], writes=[bPB[bk]])
        op("dve", lambda e, bk=bk: e.tensor_copy(out=modT[:], in_=PB[bk][:, 0:48 * NSEQ]), reads=[bPB[bk]], writes=[bmodT])
        for c0 in (8, 32):
            op("dve", lambda e, c0=c0: e.tensor_scalar(out=modT[:, c0 * NSEQ:(c0 + 8) * NSEQ], in0=modT[:, c0 * NSEQ:(c0 + 8) * NSEQ],
                                                       scalar1=1.0, scalar2=None, op0=ALU.add),
               reads=[bmodT], writes=[bmodT])
        SC.barrier()
        A.pop()

        dump("modT", modT[:], [bmodT], [128, 48 * NSEQ], F32)

        def modcol(which, c, b):
            j = (which * 8 + c) * NSEQ + b
            return modT[:, j:j + 1]

        A.push()
        hT_raw = A.alloc([128, 8192], F32)
        hT = hT_raw.bitcast(BF16).rearrange("p (a b) -> p a b", a=8)
        yaT = A.alloc([128, 8, SEQ], BF16)
        ymT = A.alloc([128, 8, SEQ], BF16)
        bxts = [Buf("xt0"), Buf("xt1")]
        bhT = [Buf(f"hT{i}") for i in range(NT)]
        byaT = [Buf(f"yaT{i}") for i in range(8)]
        bymT = [Buf(f"ymT{i}") for i in range(NT)]

        def hbufs(t0, t1):
            return bhT[t0:t1]

        for b in range(NSEQ):
            A.push()
            xts = [A.alloc([128, D], F32) for _ in range(2)]
            xns = [A.alloc([128, D], BF16) for _ in range(2)]
            bxns = [Buf("xn0"), Buf("xn1")]
            for i in range(NT):
                k = i % 2
                r0 = b * SEQ + i * 128
                dma("sp", xts[k][:], x_d[r0:r0 + 128, :], [], [bxts[k]], f"xt{k}")
                mv, rs, bm, br = ln_stats(xts[k], bxts[k])
                op("dve", lambda e, k=k, mv=mv, rs=rs: e.tensor_scalar(out=xns[k][:], in0=xts[k][:], scalar1=mv[:, 0:1], scalar2=rs[:],
                                                                      op0=ALU.subtract, op1=ALU.mult),
                   reads=[bxts[k], bm, br], writes=[bxns[k]])
                if b == 0 and i == 0:
                    dump("mv0", mv[:], [bm], [128, 2], F32)
                    dump("rs0", rs[:], [br], [128, 1], F32)
                    dump("xn0", xns[k][:], [bxns[k]], [128, D], BF16)
                    dump("xt0", xts[k][:], [bxts[k]], [128, D], F32)
                bk = nextbank([0, 1])
                ptb = PB[bk].bitcast(BF16)

                def trx(e, k=k, ptb=ptb):
                    r = None
                    for c in range(8):
                        r = e.transpose(out=ptb[:, c * 128:(c + 1) * 128], in_=xns[k][:, c * 128:(c + 1) * 128], identity=identb[:])
                    return r
                op("pe", trx, reads=[bxns[k], bC], writes=[bPB[bk]])
                for c in range(8):
                    if c % 2 == 0:
                        op("act", lambda e, c=c, ptb=ptb, i=i: e.activation(out=hT[:, c, i * 128:(i + 1) * 128], in_=ptb[:, c * 128:(c + 1) * 128],
                                                                           func=AF.Identity, scale=modcol(1, c, b), bias=modcol(0, c, b)),
                           reads=[bPB[bk], bmodT], writes=[bhT[i]])
                    else:
                        op("dve", lambda e, c=c, ptb=ptb, i=i: e.tensor_scalar(out=hT[:, c, i * 128:(i + 1) * 128], in0=ptb[:, c * 128:(c + 1) * 128],
                                                                              scalar1=modcol(1, c, b), scalar2=modcol(0, c, b),
                                                                              op0=ALU.mult, op1=ALU.add),
                           reads=[bPB[bk], bmodT], writes=[bhT[i]])
            SC.barrier()
            A.pop()
            if b == 0:
                dump("hT", hT.rearrange("p a b -> p (a b)"), bhT, [128, 8 * SEQ], BF16)

            A.push()
            wda = [A.alloc([128, 8, 384], BF16) for _ in range(2)]
            bwda = [Buf("wda0"), Buf("wda1")]
            qTs = [A.alloc([128, SEQ], BF16) for _ in range(2)]
            kTs = [A.alloc([128, SEQ], BF16) for _ in range(2)]
            vss = [A.alloc([128, NT, 129], BF16) for _ in range(2)]
            bq = [[Buf(f"q{s}{c}") for c in range(4)] for s in range(2)]
            bkk = [[Buf(f"k{s}{c}") for c in range(4)] for s in range(2)]
            bv = [[Buf(f"v{s}{c}") for c in range(4)] for s in range(2)]
            Es = [A.alloc([128, NT, 256], BF16) for _ in range(2)]
            bE = [[Buf(f"E{s}{j}") for j in range(NT)] for s in range(2)]
            osb = [A.alloc([128, 2, 128], F32) for _ in range(2)]
            yat = [A.alloc([128, 2, 128], BF16) for _ in range(2)]
            rzs = [A.alloc([128, 4], F32) for _ in range(2)]
            sss = [A.alloc([128, 2], F32) for _ in range(2)]
            bo = [Buf("o0"), Buf("o1")]
            byat = [Buf("yat0"), Buf("yat1")]
            brz = [Buf("rz0"), Buf("rz1")]
            bss = [Buf("ss0"), Buf("ss1")]
            for s in range(2):
                op("pool", lambda e, s=s: e.memset(vss[s][:, :, 128:129], 1.0), writes=[bv[s][c] for c in range(4)])
            ecnt = 0
            ccnt = 0
            for h in range(8):
                s = h % 2
                dma("sp", wda[s][:].rearrange("p a b -> p (a b)"), wda_b[h * 128:(h + 1) * 128, :], [bDR], [bwda[s]], f"wda{s}")
                for c in range(4):
                    for which in range(2):
                        bk = nextbank([0, 1])

                        def mmg(e, s=s, c=c, which=which, bk=bk):
                            r = None
                            for kc in range(8):
                                r = e.matmul(PB[bk][:, :], lhsT=wda[s][:, kc, which * 128:(which + 1) * 128],
                                             rhs=hT[:, kc, c * 512:(c + 1) * 512], start=(kc == 0), stop=(kc == 7))
                            return r
                        op("pe", mmg, reads=[bwda[s]] + hbufs(4 * c, 4 * c + 4), writes=[bPB[bk]])
                        if which == 0:
                            op("act", lambda e, s=s, c=c, bk=bk: e.activation(out=qTs[s][:, c * 512:(c + 1) * 512], in_=PB[bk][:, :],
                                                                             func=AF.Copy, scale=0.125),
                               reads=[bPB[bk]], writes=[bq[s][c]])
                        else:
                            op("dve", lambda e, s=s, c=c, bk=bk: e.tensor_copy(out=kTs[s][:, c * 512:(c + 1) * 512], in_=PB[bk][:, :]),
                               reads=[bPB[bk]], writes=[bkk[s][c]])
                    bk = nextbank([0, 1])

                    def mmv(e, s=s, c=c, bk=bk):
                        r = None
                        for t in range(4):
                            i = 4 * c + t
                            for kc in range(8):
                                r = e.matmul(PB[bk][:, t * 128:(t + 1) * 128], lhsT=hT[:, kc, i * 128:(i + 1) * 128],
                                             rhs=wda[s][:, kc, 256:384], start=(kc == 0), stop=(kc == 7))
                        return r
                    op("pe", mmv, reads=[bwda[s]] + hbufs(4 * c, 4 * c + 4), writes=[bPB[bk]])
                    op("act", lambda e, s=s, c=c, bk=bk: e.activation(out=vss[s][:, 4 * c:4 * c + 4, 0:128],
                                                                     in_=PB[bk][:, :].rearrange("p (a b) -> p a b", a=4), func=AF.Copy),
                       reads=[bPB[bk]], writes=[bv[s][c]])
                def att_scores(c, m, esl):
                    E = Es[esl]
                    for j in range(2 * c + 2):
                        q0 = 128 if j == 2 * c + 1 else 0
                        sb_ = nextbank([2, 3])
                        op("pe", lambda e: e.matmul(
                            PB[sb_][:, q0:256], lhsT=kTs[s][m * 64:(m + 1) * 64, j * 128:(j + 1) * 128],
                            rhs=qTs[s][m * 64:(m + 1) * 64, c * 256 + q0:(c + 1) * 256], start=True, stop=True),
                           reads=[bkk[s][j // 4], bq[s][c // 2]], writes=[bPB[sb_]])
                        op("act", lambda e: e.activation(out=E[:, j, q0:256], in_=PB[sb_][:, q0:256], func=AF.Exp),
                           reads=[bPB[sb_]], writes=[bE[esl][j]])
                        if j >= 2 * c:
                            d0 = (j - 2 * c) * 128
                            op("pool", lambda e: e.tensor_tensor(out=E[:, j, d0:d0 + 128], in0=E[:, j, d0:d0 + 128],
                                                                 in1=maskb[:], op=ALU.mult),
                               reads=[bE[esl][j], bC], writes=[bE[esl][j]])

                def att_pv(c, m, esl, cs):
                    E = Es[esl]
                    pob = [4 + cs * 2, 5 + cs * 2]
                    pv = PB[pob[m]][:, 0:258].rearrange("p (a b) -> p a b", a=2)

                    def pvg(e):
                        for ii in range(2):
                            i = 2 * c + ii
                            for j in range(i + 1):
                                e.matmul(pv[:, ii, :], lhsT=E[:, j, ii * 128:(ii + 1) * 128], rhs=vss[s][:, j, :],
                                         start=(j == 0), stop=(j == i))
                    op("pe", pvg, reads=[bE[esl][j] for j in range(2 * c + 2)] + [bv[s][j] for j in range(c // 2 + 1)],
                       writes=[bPB[pob[m]]])

                def att_combine(c, cs):
                    pob = [4 + cs * 2, 5 + cs * 2]
                    k = cs
                    p0 = PB[pob[0]][:, 0:258].rearrange("p (a b) -> p a b", a=2)
                    p1 = PB[pob[1]][:, 0:258].rearrange("p (a b) -> p a b", a=2)

                    def st1():
                        op("dve", lambda e: e.reciprocal(out=rzs[k][:, 0:2], in_=p0[:, :, 128]), reads=[bPB[pob[0]]], writes=[brz[k]])
                        op("dve", lambda e: e.reciprocal(out=rzs[k][:, 2:4], in_=p1[:, :, 128]), reads=[bPB[pob[1]]], writes=[brz[k]])
                        op("dve", lambda e: e.tensor_scalar(out=rzs[k][:, 2:4], in0=rzs[k][:, 2:4], scalar1=neglam[:, 0:1], scalar2=None, op0=ALU.mult),
                           reads=[brz[k], bC], writes=[brz[k]])
                        for ii in range(2):
                            op("dve", lambda e, ii=ii: e.tensor_scalar(out=osb[k][:, ii, :], in0=p0[:, ii, 0:128], scalar1=rzs[k][:, ii:ii + 1],
                                                                      scalar2=None, op0=ALU.mult),
                               reads=[bPB[pob[0]], brz[k]], writes=[bo[k]])
                            op("dve", lambda e, ii=ii: e.scalar_tensor_tensor(out=osb[k][:, ii, :], in0=p1[:, ii, 0:128],
                                                                             scalar=rzs[k][:, 2 + ii:3 + ii], in1=osb[k][:, ii, :],
                                                                             op0=ALU.mult, op1=ALU.add),
                               reads=[bPB[pob[1]], brz[k], bo[k]], writes=[bo[k]])

                    def st2():
                        for ii in range(2):
                            op("act", lambda e, ii=ii: e.activation(out=junk[:, 0:128], in_=osb[k][:, ii, :], func=AF.Square,
                                                                    accum_out=sss[k][:, ii:ii + 1]),
                               reads=[bo[k]], writes=[bss[k], bjunk])

                    def st3():
                        op("dve", lambda e: e.tensor_scalar(out=sss[k][:], in0=sss[k][:], scalar1=1.0 / 128.0, scalar2=LN_EPS,
                                                            op0=ALU.mult, op1=ALU.add),
                           reads=[bss[k]], writes=[bss[k]])
                        op("pool", lambda e: e.tensor_tensor(out=sss[k][:], in0=sss[k][:], in1=cm05[:, 0:2], op=ALU.pow),
                           reads=[bss[k], bC], writes=[bss[k]])

                    def st4():
                        for ii in range(2):
                            op("dve", lambda e, ii=ii: e.scalar_tensor_tensor(out=yat[k][:, ii, :], in0=osb[k][:, ii, :], scalar=sss[k][:, ii:ii + 1],
                                                                             in1=gda_bc[:], op0=ALU.mult, op1=ALU.mult),
                               reads=[bo[k], bss[k], bC], writes=[byat[k]])

                    def st5():
                        bk = nextbank([0, 1])
                        ptb = PB[bk].bitcast(BF16)

                        def trya(e):
                            for ii in range(2):
                                e.transpose(out=ptb[:, ii * 128:(ii + 1) * 128], in_=yat[k][:, ii, :], identity=identb[:])
                        op("pe", trya, reads=[byat[k], bC], writes=[bPB[bk]])
                        op("act", lambda e: e.activation(out=yaT[:, h, c * 256:(c + 1) * 256], in_=ptb[:, 0:256], func=AF.Copy),
                           reads=[bPB[bk]], writes=[byaT[c]])
                    cq.extend([st1, st2, st3, st4, st5])

                cq = []

                def cq_pop(n):
                    for _ in range(n):
                        if cq:
                            cq.pop(0)()

                prev = None
                for c in range(8):
                    cs = ccnt % 2
                    ccnt += 1
                    for m in range(2):
                        esl = ecnt % 2
                        ecnt += 1
                        att_scores(c, m, esl)
                        cq_pop(3)
                        if prev is not None:
                            att_pv(*prev)
                            if prev[1] == 1:
                                att_combine(prev[0], prev[3])
                        prev = (c, m, esl, cs)
                att_pv(*prev)
                att_combine(prev[0], prev[3])
                cq_pop(100)
            SC.barrier()
            A.pop()
            if b == 0:
                dump("yaT", yaT.rearrange("p a b -> p (a b)"), byaT, [128, 8 * SEQ], BF16)

            A.push()
            LNS = math.log(128.0 ** -0.5)
            gch = [A.alloc([4, 512], F32) for _ in range(2)]
            negM = A.alloc([4, SEQ], F32)
            ech = [A.alloc([4, 512], F32) for _ in range(2)]
            nfc = [A.alloc([4, 512], F32) for _ in range(2)]
            mch = [A.alloc([4, 512], F32) for _ in range(2)]
            tfc = A.alloc([4, 512], F32)
            gtok = A.alloc([128, NT * 4], F32)
            etok = A.alloc([128, NT * 4], F32)
            bgc = [Buf("gch0"), Buf("gch1")]
            bnegM = Buf("negM")
            bec = [Buf("ech0"), Buf("ech1")]
            bnf = [Buf("nf0"), Buf("nf1")]
            bmc = [Buf("mc0"), Buf("mc1")]
            btf = Buf("tf")
            bgtok = Buf("gtok")
            betok = Buf("etok")
            for c in range(4):
                k = c % 2
                sl = slice(c * 512, (c + 1) * 512)
                bki = nextbank([0, 1])
                bkf = nextbank([0, 1])
                for (bk_, c0) in ((bki, 0), (bkf, 4)):
                    def mmif(e, bk_=bk_, c0=c0, c=c):
                        r = None
                        for kc in range(8):
                            r = e.matmul(PB[bk_][0:4, :], lhsT=wif[:, kc, c0:c0 + 4], rhs=hT[:, kc, c * 512:(c + 1) * 512],
                                         start=(kc == 0), stop=(kc == 7))
                        return r
                    op("pe", mmif, reads=[bC] + hbufs(4 * c, 4 * c + 4), writes=[bPB[bk_]])
                op("act", lambda e, bkf=bkf: e.activation(out=tfc[:], in_=PB[bkf][0:4, :], func=AF.Exp, scale=-1.0, bias=nbf[:, 0:1]),
                   reads=[bPB[bkf], bC], writes=[btf])
                op("act", lambda e: e.activation(out=tfc[:], in_=tfc[:], func=AF.Ln, bias=1.0), reads=[btf], writes=[btf])
                init_nf = 0.0 if c == 0 else nfc[1 - k][:, 511:512]
                op("dve", lambda e, k=k, init_nf=init_nf: e.tensor_tensor_scan(out=nfc[k][:], data0=ones4[:], data1=tfc[:], initial=init_nf,
                                                                              op0=ALU.mult, op1=ALU.add),
                   reads=[btf, bC, bnf[1 - k]], writes=[bnf[k]])
                op("dve", lambda e, k=k, bki=bki, sl=sl: e.scalar_tensor_tensor(out=gch[k][:], in0=PB[bki][0:4, :], scalar=bifT[:, 0:1], in1=nfc[k][:],
                                                                               op0=ALU.add, op1=ALU.add),
                   reads=[bPB[bki], bC, bnf[k]], writes=[bgc[k]])
                init_m = 0.0 if c == 0 else mch[1 - k][:, 511:512]
                op("dve", lambda e, k=k, sl=sl, init_m=init_m: e.tensor_tensor_scan(out=mch[k][:], data0=gch[k][:], data1=gch[k][:], initial=init_m,
                                                                                   op0=ALU.max, op1=ALU.max),
                   reads=[bgc[k], bmc[1 - k]], writes=[bmc[k]])
                op("dve", lambda e, k=k, sl=sl: e.tensor_scalar(out=negM[:, sl], in0=mch[k][:], scalar1=-1.0, scalar2=None, op0=ALU.mult),
                   reads=[bmc[k]], writes=[bnegM])
                op("dve", lambda e, k=k: e.tensor_tensor(out=ech[k][:], in0=nfc[k][:], in1=mch[k][:], op=ALU.subtract),
                   reads=[bnf[k], bmc[k]], writes=[bec[k]])
                op("act", lambda e, k=k: e.activation(out=ech[k][:], in_=ech[k][:], func=AF.Exp), reads=[bec[k]], writes=[bec[k]])

                def trgt(e, k=k, c=c):
                    r = None
                    for t in range(4):
                        i = 4 * c + t
                        r = e.transpose(out=PB[7][:, i * 4:(i + 1) * 4], in_=gch[k][0:4, t * 128:(t + 1) * 128], identity=identf[0:4, 0:4])
                        r = e.transpose(out=PB[7][:, 64 + i * 4:64 + (i + 1) * 4], in_=ech[k][0:4, t * 128:(t + 1) * 128], identity=identf[0:4, 0:4])
                    return r
                op("pe", trgt, reads=[bgc[k], bec[k], bC], writes=[bPB[7]])
            bk = 7
            op("dve", lambda e, bk=bk: e.tensor_scalar(out=gtok[:], in0=PB[bk][:, 0:64], scalar1=LNS, scalar2=None, op0=ALU.add),
               reads=[bPB[bk]], writes=[bgtok])
            op("dve", lambda e, bk=bk: e.tensor_copy(out=etok[:], in_=PB[bk][:, 64:128]), reads=[bPB[bk]], writes=[betok])

            wml = A.alloc([128, 8, 768], BF16)
            bwml = Buf("wml")
            negMbc = [A.alloc([128, 256], F32) for _ in range(2)]
            bnb = [Buf("nb0"), Buf("nb1")]
            zp = A.alloc([128, SEQ + 3], F32)
            bzp = [Buf(f"zp{c}") for c in range(4)]
            cv = A.alloc([128, SEQ], F32)
            bcv = Buf("cv")
            qTm = A.alloc([128, SEQ], BF16)
            kTm = A.alloc([128, SEQ], BF16)
            bqm = Buf("qTm")
            bkm = Buf("kTm")
            vm = A.alloc([128, NT, 257], BF16)
            bvm = [Buf(f"vm{i}") for i in range(8)]
            Ps = [A.alloc([128, NT, 256], BF16) for _ in range(2)]
            bP = [[Buf(f"P{s}{j}") for j in range(NT)] for s in range(2)]
            Wt = [A.alloc([128, 256], F32) for _ in range(2)]
            bWt = [Buf("Wt0"), Buf("Wt1")]
            gml_bc = A.alloc([128, 256], F32)
            bgml = Buf("gml")
            hn = [A.alloc([128, 256], F32) for _ in range(4)]
            bhn = [Buf("hn%d" % i_) for i_ in range(4)]
            og = [A.alloc([128, 256], BF16) for _ in range(4)]
            bog = [Buf("og%d" % i_) for i_ in range(4)]
            ymt = [A.alloc([128, 256], BF16) for _ in range(4)]
            bymt = [Buf("ymt%d" % i_) for i_ in range(4)]
            dens = [A.alloc([128, 2], F32) for _ in range(4)]
            bden = [Buf("den%d" % i_) for i_ in range(4)]
            op("pool", lambda e: e.memset(zp[:, 0:3], 0.0), writes=bzp)
            op("pool", lambda e: e.memset(vm[:, :, 256:257], 1.0), writes=bvm)
            pcnt = 0
            tcnt = 0
            wcnt = 0
            for h in range(4):
                dma("sp", wml[:].rearrange("p a b -> p (a b)"), wml_b[h * 128:(h + 1) * 128, :], [bDR], [bwml], "wml")
                dma("sp", gml_bc[:], mlg_d[0:1, h * 256:(h + 1) * 256].partition_broadcast(128), [], [bgml], "gml")
                for which in range(2):
                    ch = which * 4 + h
                    for c in range(4):
                        bk = nextbank([0, 1])

                        def mmq(e, c=c, which=which, bk=bk):
                            r = None
                            for kc in range(8):
                                r = e.matmul(PB[bk][:, :], lhsT=wml[:, kc, which * 128:(which + 1) * 128],
                                             rhs=hT[:, kc, c * 512:(c + 1) * 512], start=(kc == 0), stop=(kc == 7))
                            return r
                        op("pe", mmq, reads=[bwml] + hbufs(4 * c, 4 * c + 4), writes=[bPB[bk]])
                        op("act", lambda e, c=c, bk=bk: e.activation(out=zp[:, 3 + c * 512:3 + (c + 1) * 512], in_=PB[bk][:, :], func=AF.Copy),
                           reads=[bPB[bk]], writes=[bzp[c]])
                    op("dve", lambda e, ch=ch: e.tensor_scalar(out=cv[:], in0=zp[:, 3:SEQ + 3], scalar1=cwT[:, ch * 4 + 3:ch * 4 + 4],
                                                               scalar2=cbT[:, ch:ch + 1], op0=ALU.mult, op1=ALU.add),
                       reads=bzp + [bC], writes=[bcv])
                    for j in range(3):
                        op("dve", lambda e, ch=ch, j=j: e.scalar_tensor_tensor(out=cv[:], in0=zp[:, j:j + SEQ], scalar=cwT[:, ch * 4 + j:ch * 4 + j + 1],
                                                                              in1=cv[:], op0=ALU.mult, op1=ALU.add),
                           reads=bzp + [bC, bcv], writes=[bcv])
                    dst, bdst = (qTm, bqm) if which == 0 else (kTm, bkm)
                    op("act", lambda e, dst=dst: e.activation(out=dst[:], in_=cv[:], func=AF.Silu), reads=[bcv], writes=[bdst])
                for g2 in range(8):
                    bk = nextbank([0, 1])

                    def mmv2(e, g2=g2, bk=bk):
                        r = None
                        for t in range(2):
                            i = 2 * g2 + t
                            for kc in range(8):
                                r = e.matmul(PB[bk][:, t * 256:(t + 1) * 256], lhsT=hT[:, kc, i * 128:(i + 1) * 128],
                                             rhs=wml[:, kc, 256:512], start=(kc == 0), stop=(kc == 7))
                        return r
                    op("pe", mmv2, reads=[bwml] + hbufs(2 * g2, 2 * g2 + 2), writes=[bPB[bk]])
                    op("dve", lambda e, g2=g2, bk=bk: e.tensor_copy(out=vm[:, 2 * g2:2 * g2 + 2, 0:256],
                                                                   in_=PB[bk][:, :].rearrange("p (a b) -> p a b", a=2)),
                       reads=[bPB[bk]], writes=[bvm[g2]])
                mq = []
                def ml_build(c, psl):
                    nonlocal wcnt
                    P = Ps[psl]
                    bk = nextbank([0, 1])
                    op("pe", lambda e, h=h, c=c, bk=bk: e.matmul(PB[bk][:, 0:256], lhsT=sel4[0:4, h * 128:(h + 1) * 128],
                                                                rhs=negM[0:4, c * 256:(c + 1) * 256], start=True, stop=True),
                       reads=[bC, bnegM], writes=[bPB[bk]])
                    op("act", lambda e, psl=psl, bk=bk: e.activation(out=negMbc[psl][:, :], in_=PB[bk][:, 0:256], func=AF.Copy),
                       reads=[bPB[bk]], writes=[bnb[psl]])
                    for j in range(2 * c + 2):
                        q0 = 128 if j == 2 * c + 1 else 0
                        sb_ = nextbank([2, 3])
                        ws = wcnt % 2
                        wcnt += 1
                        op("pe", lambda e, j=j, c=c, q0=q0, sb_=sb_: e.matmul(
                            PB[sb_][:, q0:256], lhsT=kTm[:, j * 128:(j + 1) * 128], rhs=qTm[:, c * 256 + q0:(c + 1) * 256], start=True, stop=True),
                           reads=[bkm, bqm], writes=[bPB[sb_]])
                        op("act", lambda e, j=j, c=c, q0=q0, ws=ws, h=h, psl=psl: e.activation(out=Wt[ws][:, q0:256], in_=negMbc[psl][:, q0:256],
                                                                                     func=AF.Exp, bias=gtok[:, j * 4 + h:j * 4 + h + 1]),
                           reads=[bnb[psl], bgtok], writes=[bWt[ws]])
                        op("dve", lambda e, P=P, j=j, q0=q0, ws=ws, sb_=sb_: e.tensor_tensor(out=P[:, j, q0:256], in0=PB[sb_][:, q0:256],
                                                                                            in1=Wt[ws][:, q0:256], op=ALU.mult),
                           reads=[bPB[sb_], bWt[ws]], writes=[bP[psl][j]])
                        if mq:
                            mq.pop(0)()
                        if j >= 2 * c:
                            d0 = (j - 2 * c) * 128
                            op("pool", lambda e, P=P, j=j, d0=d0: e.tensor_tensor(out=P[:, j, d0:d0 + 128], in0=P[:, j, d0:d0 + 128],
                                                                                 in1=maskb[:], op=ALU.mult),
                               reads=[bP[psl][j], bC], writes=[bP[psl][j]])

                def ml_pv(c, psl):
                    nonlocal tcnt
                    P = Ps[psl]
                    for ii in range(2):
                        i = 2 * c + ii
                        k = tcnt % 4
                        tcnt += 1
                        pb_ = nextbank([4, 5, 6, 7])

                        def pvm(e, P=P, ii=ii, i=i, pb_=pb_):
                            r = None
                            for j in range(i + 1):
                                r = e.matmul(PB[pb_][:, 0:257], lhsT=P[:, j, ii * 128:(ii + 1) * 128], rhs=vm[:, j, :],
                                             start=(j == 0), stop=(j == i))
                            return r
                        op("pe", pvm, reads=[bP[psl][j] for j in range(i + 1)] + [bvm[j] for j in range(i // 2 + 1)], writes=[bPB[pb_]])
                        bk = nextbank([0, 1])

                        def mmo(e, i=i, bk=bk):
                            r = None
                            for kc in range(8):
                                r = e.matmul(PB[bk][:, 0:256], lhsT=hT[:, kc, i * 128:(i + 1) * 128], rhs=wml[:, kc, 512:768],
                                             start=(kc == 0), stop=(kc == 7))
                            return r
                        op("pe", mmo, reads=[bwml, bhT[i]], writes=[bPB[bk]])
                        op("act", lambda e, k=k, bk=bk: e.activation(out=og[k][:], in_=PB[bk][:, 0:256], func=AF.Sigmoid),
                           reads=[bPB[bk]], writes=[bog[k]])
                        op("dve", lambda e, k=k, pb_=pb_: e.tensor_copy(out=dens[k][:, 0:1], in_=PB[pb_][:, 256:257]),
                           reads=[bPB[pb_]], writes=[bden[k]])
                        op("dve", lambda e, k=k: e.scalar_tensor_tensor(out=dens[k][:, 0:1], in0=dens[k][:, 0:1], scalar=-1.0, in1=dens[k][:, 0:1],
                                                                        op0=ALU.mult, op1=ALU.max),
                           reads=[bden[k]], writes=[bden[k]])
                        op("dve", lambda e, k=k, i=i, h=h: e.tensor_tensor(out=dens[k][:, 0:1], in0=dens[k][:, 0:1], in1=etok[:, i * 4 + h:i * 4 + h + 1],
                                                                          op=ALU.max),
                           reads=[bden[k], betok], writes=[bden[k]])
                        op("dve", lambda e, k=k: e.reciprocal(out=dens[k][:, 0:1], in_=dens[k][:, 0:1]), reads=[bden[k]], writes=[bden[k]])
                        op("dve", lambda e, k=k, pb_=pb_: e.tensor_scalar(out=hn[k][:], in0=PB[pb_][:, 0:256], scalar1=dens[k][:, 0:1], scalar2=None,
                                                                         op0=ALU.mult),
                           reads=[bPB[pb_], bden[k]], writes=[bhn[k]])
                        def sB(k=k):
                            op("act", lambda e: e.activation(out=junk[:, 0:256], in_=hn[k][:], func=AF.Square, accum_out=dens[k][:, 1:2]),
                               reads=[bhn[k]], writes=[bden[k], bjunk])

                        def sC(k=k):
                            op("dve", lambda e: e.tensor_scalar(out=dens[k][:, 1:2], in0=dens[k][:, 1:2], scalar1=1.0 / 256.0, scalar2=LN_EPS,
                                                                op0=ALU.mult, op1=ALU.add),
                               reads=[bden[k]], writes=[bden[k]])
                            op("pool", lambda e: e.tensor_tensor(out=dens[k][:, 1:2], in0=dens[k][:, 1:2], in1=cm05[:, 0:1], op=ALU.pow),
                               reads=[bden[k], bC], writes=[bden[k]])

                        def sD(k=k):
                            op("dve", lambda e: e.scalar_tensor_tensor(out=hn[k][:], in0=hn[k][:], scalar=dens[k][:, 1:2],
                                                                       in1=gml_bc[:], op0=ALU.mult, op1=ALU.mult),
                               reads=[bhn[k], bden[k], bgml], writes=[bhn[k]])
                            op("pool", lambda e: e.tensor_tensor(out=ymt[k][:], in0=hn[k][:], in1=og[k][:], op=ALU.mult),
                               reads=[bhn[k], bog[k]], writes=[bymt[k]])

                        def sE(k=k, i=i):
                            bk = nextbank([0, 1])
                            ptb = PB[bk].bitcast(BF16)

                            def trym(e):
                                for ee in range(2):
                                    e.transpose(out=ptb[:, ee * 128:(ee + 1) * 128], in_=ymt[k][:, ee * 128:(ee + 1) * 128], identity=identb[:])
                            op("pe", trym, reads=[bymt[k], bC], writes=[bPB[bk]])
                            op("act", lambda e: e.activation(out=ymT[:, 2 * h:2 * h + 2, i * 128:(i + 1) * 128],
                                                             in_=ptb[:, 0:256].rearrange("p (a b) -> p a b", a=2), func=AF.Copy),
                               reads=[bPB[bk]], writes=[bymT[i]])
                        mq.extend([sB, sC, sD, sE])

                prevc = None
                for c in range(8):
                    psl = pcnt % 2
                    pcnt += 1
                    ml_build(c, psl)
                    while len(mq) > 4:
                        mq.pop(0)()
                    if prevc is not None:
                        ml_pv(*prevc)
                    prevc = (c, psl)
                ml_pv(*prevc)
                while mq:
                    mq.pop(0)()
            SC.barrier()
            A.pop()
            if b == 0:
                dump("ymT", ymT.rearrange("p a b -> p (a b)"), bymT, [128, 8 * SEQ], BF16)

            A.push()
            yT = A.alloc([128, 8, SEQ], BF16)
            byT = [Buf(f"yT{c}") for c in range(4)]
            wmg = [A.alloc([128, 8, 512], BF16) for _ in range(2)]
            bwmg = [Buf("wmg0"), Buf("wmg1")]
            sgA = [A.alloc([128, 512], F32) for _ in range(2)]
            sgB = [A.alloc([128, 512], F32) for _ in range(2)]
            t1 = [A.alloc([128, 512], F32) for _ in range(2)]
            t2 = [A.alloc([128, 512], F32) for _ in range(2)]
            bsgA = [Buf("sgA0"), Buf("sgA1")]
            bsgB = [Buf("sgB0"), Buf("sgB1")]
            bt1 = [Buf("t10"), Buf("t11")]
            bt2 = [Buf("t20"), Buf("t21")]
            dcnt = 0
            allb = [0, 1, 2, 3, 4, 5, 6, 7]
            for fc in range(8):
                s = fc % 2
                dma("sp", wmg[s][:].rearrange("p a b -> p (a b)"), wmg_b[fc * 128:(fc + 1) * 128, :], [bDR], [bwmg[s]], f"wmg{s}")
                for c in range(4):
                    k = dcnt % 2
                    dcnt += 1
                    banks = []
                    for which, src, srcb in ((0, hT, hbufs(4 * c, 4 * c + 4)), (1, hT, hbufs(4 * c, 4 * c + 4)),
                                             (2, yaT, byaT[2 * c:2 * c + 2]), (3, ymT, bymT[4 * c:4 * c + 4])):
                        bk = nextbank(allb)
                        banks.append(bk)

                        def mmd(e, s=s, c=c, which=which, src=src, bk=bk):
                            r = None
                            for kc in range(8):
                                r = e.matmul(PB[bk][:, :], lhsT=wmg[s][:, kc, which * 128:(which + 1) * 128],
                                             rhs=src[:, kc, c * 512:(c + 1) * 512], start=(kc == 0), stop=(kc == 7))
                            return r
                        op("pe", mmd, reads=[bwmg[s]] + list(srcb), writes=[bPB[bk]])
                    op("act", lambda e, k=k, bk=banks[0]: e.activation(out=sgA[k][:], in_=PB[bk][:, :], func=AF.Sigmoid),
                       reads=[bPB[banks[0]]], writes=[bsgA[k]])
                    op("act", lambda e, k=k, bk=banks[1]: e.activation(out=sgB[k][:], in_=PB[bk][:, :], func=AF.Sigmoid),
                       reads=[bPB[banks[1]]], writes=[bsgB[k]])
                    op("dve", lambda e, k=k, bk=banks[2]: e.tensor_tensor(out=t1[k][:], in0=PB[bk][:, :], in1=sgA[k][:], op=ALU.mult),
                       reads=[bPB[banks[2]], bsgA[k]], writes=[bt1[k]])
                    op("dve", lambda e, k=k, bk=banks[3]: e.tensor_tensor(out=t2[k][:], in0=PB[bk][:, :], in1=sgB[k][:], op=ALU.mult),
                       reads=[bPB[banks[3]], bsgB[k]], writes=[bt2[k]])
                    op("pool", lambda e, k=k, fc=fc, c=c: e.tensor_tensor(out=yT[:, fc, c * 512:(c + 1) * 512], in0=t1[k][:], in1=t2[k][:], op=ALU.add),
                       reads=[bt1[k], bt2[k]], writes=[byT[c]])
            SC.barrier()
            if b == 0:
                dump("yT", yT.rearrange("p a b -> p (a b)"), byT, [128, 8 * SEQ], BF16)
            wo = hT_raw[:, 0:4096].bitcast(BF16).rearrange("p (a b) -> p a b", a=8)
            ttile = hT_raw[:, 4096:5120]
            l1g = hT_raw[:, 5120:6144]
            l1b = hT_raw[:, 6144:7168]
            gt1 = hT_raw[:, 7168:8192]
            xts = [A.alloc([128, D], F32) for _ in range(2)]
            bwo = Buf("wo")
            btt = Buf("tt")
            bl1 = Buf("l1")
            dma("sp", wo.rearrange("p a b -> p (a b)"), wout_b[:, :], [bDR], [bwo], "wo")
            dma("sp", l1g, ln1g_d[0:1, :].partition_broadcast(128), [], [bl1], "l1")
            dma("sp", l1b, ln1b_d[0:1, :].partition_broadcast(128), [], [bl1], "l1")
            dma("sp", gt1, gt_d[b:b + 1, 0:1024].partition_broadcast(128), [bGT], [bl1], "l1")
            for i in range(NT):
                k = i % 2
                r0 = b * SEQ + i * 128
                dma("sp", xts[k][:], x_d[r0:r0 + 128, :], [], [bxts[k]], f"xt{k}")
                for half in range(2):
                    bk = nextbank(allb)

                    def mmo2(e, i=i, half=half, bk=bk):
                        r = None
                        for kc in range(8):
                            r = e.matmul(PB[bk][:, :], lhsT=yT[:, kc, i * 128:(i + 1) * 128], rhs=wo[:, kc, half * 512:(half + 1) * 512],
                                         start=(kc == 0), stop=(kc == 7))
                        return r
                    op("pe", mmo2, reads=[byT[i // 4], bwo], writes=[bPB[bk]])
                    op("dve", lambda e, half=half, bk=bk: e.tensor_tensor(out=ttile[:, half * 512:(half + 1) * 512], in0=PB[bk][:, :],
                                                                         in1=gt1[:, half * 512:(half + 1) * 512], op=ALU.mult),
                       reads=[bPB[bk], bl1], writes=[btt])
                op("dve", lambda e, k=k: e.scalar_tensor_tensor(out=xts[k][:], in0=xts[k][:], scalar=ALPHA, in1=ttile, op0=ALU.mult, op1=ALU.add),
                   reads=[bxts[k], btt], writes=[bxts[k]])
                mv, rs, bm, br = ln_stats(xts[k], bxts[k])
                op("dve", lambda e, k=k, mv=mv, rs=rs: e.tensor_scalar(out=xts[k][:], in0=xts[k][:], scalar1=mv[:, 0:1], scalar2=rs[:],
                                                                      op0=ALU.subtract, op1=ALU.mult),
                   reads=[bxts[k], bm, br], writes=[bxts[k]])
                op("pool", lambda e, k=k: e.tensor_tensor(out=xts[k][:], in0=xts[k][:], in1=l1g, op=ALU.mult), reads=[bxts[k], bl1], writes=[bxts[k]])
                op("pool", lambda e, k=k: e.tensor_tensor(out=xts[k][:], in0=xts[k][:], in1=l1b, op=ALU.add), reads=[bxts[k], bl1], writes=[bxts[k]])
                dst = x1_d if do_peer else out_d
                dma("sp", dst[r0:r0 + 128, :], xts[k][:], [bxts[k]], [bXO[k]], f"xo{k}")
            SC.barrier()
            A.pop()
        A.pop()
        SC.barrier()

        if do_peer:
            A.push()
            RB = [0, 1, 2, 3, 4, 5]
            RBX = [4, 5]
            keysT = A.alloc([128, 256], F32)
            l2g = A.alloc([128, D], F32)
            l2b = A.alloc([128, D], F32)
            gt2 = A.alloc([128, D], F32)
            bl2 = Buf("l2")
            bgt2 = Buf("gt2")
            xps = [A.alloc([128, D], F32) for _ in range(2)]
            bxps = [Buf("xp0"), Buf("xp1")]
            xn2 = A.alloc([128, D], F32)
            bxn2 = Buf("xn2")
            h2T = A.alloc([128, 8, 128], F32)
            h2Tb = A.alloc([128, 8, 128], BF16)
            bh2 = Buf("h2T")
            bh2b = Buf("h2Tb")
            wq = [A.alloc([128, 8, 128], F32) for _ in range(2)]
            bwq = [Buf("wq0"), Buf("wq1")]
            qTh = [A.alloc([128, 128], F32) for _ in range(2)]
            bqTh = [Buf("qTh0"), Buf("qTh1")]
            s_sb = A.alloc([128, 8, 2, 128], F32)
            bs = Buf("s_sb")
            m16 = A.alloc([128, 8, 2, 16], F32)
            bm16 = Buf("m16")
            wk1 = A.alloc([128, 128], F32)
            bwk1 = Buf("wk1")
            cand = A.alloc([128, 8, 16, 16], F32)
            bcand = Buf("cand")
            wk2 = A.alloc([128, 256], F32)
            bwk2 = Buf("wk2")
            c16 = A.alloc([128, 8, 16], F32)
            bc16 = Buf("c16")
            negthr = A.alloc([128, 8], F32)
            zz = A.alloc([128, 8], F32)
            biasE = A.alloc([128, 8], F32)
            bsm = Buf("small")
            Sp = [A.alloc([128, 2048], F32) for _ in range(3)]
            Ep = [A.alloc([128, 2048], BF16) for _ in range(3)]
            Mk = [A.alloc([128, 2048], BF16) for _ in range(2)]
            bSp = [Buf("Sp0"), Buf("Sp1"), Buf("Sp2")]
            bEp = [Buf("Ep0"), Buf("Ep1"), Buf("Ep2")]
            bMk = [Buf("Mk0"), Buf("Mk1"), Buf("Mk2")]
            gstate = [0, 0, 0]
            acc = A.alloc([128, NEXP], BF16)
            bacc = [Buf(f"acc{i}") for i in range(8)]
            ub = [A.alloc([128, 8, 512], BF16) for _ in range(3)]
            bub = [Buf("ub0"), Buf("ub1"), Buf("ub2")]
            vb = [A.alloc([128, 4, D], BF16) for _ in range(3)]
            bvb = [Buf("vb0"), Buf("vb1"), Buf("vb2")]
            gel = [A.alloc([128, 512], F32) for _ in range(2)]
            bgel = [Buf("gel0"), Buf("gel1"), Buf("gel2")]
            Wc = [A.alloc([128, 512], BF16) for _ in range(2)]
            bWc = [Buf("Wc0"), Buf("Wc1"), Buf("Wc2")]
            WT = [A.alloc([128, 4, 128], BF16) for _ in range(2)]
            bWT = [Buf("WT0"), Buf("WT1"), Buf("WT2")]
            tt2 = A.alloc([128, D], F32)
            btt2 = Buf("tt2")
            bPO = [Buf("po0"), Buf("po1")]
            dma("sp", keysT[:], keysT_d[:, :], [], [bl2], "l2")
            dma("sp", l2g[:], ln2g_d[0:1, :].partition_broadcast(128), [], [bl2], "l2")
            dma("sp", l2b[:], ln2b_d[0:1, :].partition_broadcast(128), [], [bl2], "l2")
            wpq3 = wpq_d.rearrange("r (a b) -> r a b", a=8)
            v_b3 = v_b.rearrange("(g p) n -> p g n", p=128)
            gcnt = 0
            ecnt2 = 0
            import os as _os
            _PT = int(_os.environ.get("PEER_TILES", NSEQ * NT))
            _PS = int(_os.environ.get("PEER_STAGE", 9))
            for ti in range(min(_PT, NSEQ * NT)):
                b = ti // NT
                k = ti % 2
                r0 = ti * 128
                if ti % NT == 0:
                    dma("sp", gt2[:], gt_d[b:b + 1, 1024:2048].partition_broadcast(128), [bGT], [bgt2], "gt2")
                dma("sp", xps[k][:], x1_d[r0:r0 + 128, :], [bXO[0], bXO[1]], [bxps[k]], f"xp{k}")
                mv, rs, bm, br = ln_stats(xps[k], bxps[k])
                op("dve", lambda e, k=k, mv=mv, rs=rs: e.tensor_scalar(out=xn2[:], in0=xps[k][:], scalar1=mv[:, 0:1], scalar2=rs[:],
                                                                      op0=ALU.subtract, op1=ALU.mult),
                   reads=[bxps[k], bm, br], writes=[bxn2])
                for hf in range(2):
                    bk = nextbank(RB)

                    def trp(e, hf=hf, bk=bk):
                        for cc in range(4):
                            c = hf * 4 + cc
                            e.transpose(out=PB[bk][:, cc * 128:(cc + 1) * 128], in_=xn2[:, c * 128:(c + 1) * 128], identity=identf[:])
                    op("pe", trp, reads=[bxn2, bC], writes=[bPB[bk]])
                    for cc in range(4):
                        c = hf * 4 + cc
                        if cc % 2 == 0:
                            op("act", lambda e, c=c, cc=cc, bk=bk, b=b: e.activation(out=h2T[:, c, :], in_=PB[bk][:, cc * 128:(cc + 1) * 128],
                                                                                    func=AF.Identity, scale=modcol(4, c, b), bias=modcol(3, c, b)),
                               reads=[bPB[bk], bmodT], writes=[bh2])
                        else:
                            op("dve", lambda e, c=c, cc=cc, bk=bk, b=b: e.tensor_scalar(out=h2T[:, c, :], in0=PB[bk][:, cc * 128:(cc + 1) * 128],
                                                                                       scalar1=modcol(4, c, b), scalar2=modcol(3, c, b),
                                                                                       op0=ALU.mult, op1=ALU.add),
                               reads=[bPB[bk], bmodT], writes=[bh2])
                op("pool", lambda e: e.tensor_copy(out=h2Tb[:].rearrange("p a b -> p (a b)"), in_=h2T[:].rearrange("p a b -> p (a b)")),
                   reads=[bh2], writes=[bh2b])
                sbk = None
                for h in range(8 if _PS >= 2 else 0):
                    ws = h % 2
                    dma("sp", wq[ws][:], wpq3[h * 128:(h + 1) * 128, :, :], [], [bwq[ws]], f"wq{ws}")
                    bk = nextbank(RB)

                    def mmq(e, ws=ws, bk=bk):
                        for kc in range(8):
                            e.matmul(PB[bk][:, 0:128], lhsT=wq[ws][:, kc, :], rhs=h2T[:, kc, :], start=(kc == 0), stop=(kc == 7))
                    op("pe", mmq, reads=[bwq[ws], bh2], writes=[bPB[bk]])
                    op("act", lambda e, ws=ws, bk=bk: e.activation(out=qTh[ws][:], in_=PB[bk][:, 0:128], func=AF.Copy),
                       reads=[bPB[bk]], writes=[bqTh[ws]])
                    if h % 2 == 0:
                        sbk = nextbank(RB)

                    def mms(e, ws=ws, sbk=sbk, h=h):
                        o0 = (h % 2) * 256
                        e.matmul(PB[sbk][:, o0:o0 + 256], lhsT=qTh[ws][:, :], rhs=keysT[:, :], start=True, stop=True)
                    op("pe", mms, reads=[bqTh[ws], bl2], writes=[bPB[sbk]])
                    if h % 2 == 1:
                        op("dve", lambda e, h=h, sbk=sbk: e.tensor_copy(out=s_sb[:, h - 1:h + 1, :, :].rearrange("p a b c -> p (a b c)"), in_=PB[sbk][:, :]),
                           reads=[bPB[sbk]], writes=[bs])
                if _PS < 3:
                    dma("sp", out_d[r0:r0 + 128, :], xps[k][:], [bxps[k], bs, bh2b], [bPO[k]], f"po{k}")
                    continue
                for h in range(8):
                    for half in range(2):
                        op("dve", lambda e, h=h, half=half: e.max(out=m16[:, h, half, 0:8], in_=s_sb[:, h, half, :]), reads=[bs], writes=[bm16])
                        op("dve", lambda e, h=h, half=half: e.match_replace(out=wk1[:], in_to_replace=m16[:, h, half, 0:8], in_values=s_sb[:, h, half, :],
                                                                           imm_value=-1e30),
                           reads=[bs, bm16], writes=[bwk1])
                        op("dve", lambda e, h=h, half=half: e.max(out=m16[:, h, half, 8:16], in_=wk1[:]), reads=[bwk1], writes=[bm16])
                op("dve", lambda e: e.tensor_tensor(out=cand[:], in0=m16[:, :, 0, :].unsqueeze(3).to_broadcast([128, 8, 16, 16]),
                                                    in1=m16[:, :, 1, :].unsqueeze(2).to_broadcast([128, 8, 16, 16]), op=ALU.add),
                   reads=[bm16], writes=[bcand])
                for h in range(8):
                    ch2 = cand[:, h, :, :].rearrange("p a b -> p (a b)")
                    op("dve", lambda e, h=h, ch2=ch2: e.max(out=c16[:, h, 0:8], in_=ch2), reads=[bcand], writes=[bc16])
                    op("dve", lambda e, h=h, ch2=ch2: e.match_replace(out=wk2[:], in_to_replace=c16[:, h, 0:8], in_values=ch2, imm_value=-1e30),
                       reads=[bcand, bc16], writes=[bwk2])
                    op("dve", lambda e, h=h: e.max(out=c16[:, h, 8:16], in_=wk2[:]), reads=[bwk2], writes=[bc16])
                op("dve", lambda e: e.tensor_scalar(out=negthr[:], in0=c16[:, :, 15], scalar1=-1.0, scalar2=None, op0=ALU.mult),
                   reads=[bc16], writes=[bsm])
                for h in range(8):
                    op("act", lambda e, h=h: e.activation(out=junk[:, 0:16], in_=c16[:, h, :], func=AF.Exp, bias=negthr[:, h:h + 1],
                                                          accum_out=zz[:, h:h + 1]),
                       reads=[bc16, bsm], writes=[bsm, bjunk])
                op("act", lambda e: e.activation(out=zz[:], in_=zz[:], func=AF.Ln), reads=[bsm], writes=[bsm])
                op("dve", lambda e: e.tensor_tensor(out=biasE[:], in0=negthr[:], in1=zz[:], op=ALU.subtract), reads=[bsm], writes=[bsm])
                if _PS < 4:
                    dma("sp", out_d[r0:r0 + 128, :], xps[k][:], [bxps[k], bsm], [bPO[k]], f"po{k}")
                    continue
                def g_finish(pc, h, g):
                    gm = g % 2
                    op("dve", lambda e: e.scalar_tensor_tensor(out=Mk[gm][:], in0=Sp[g][:], scalar=c16[:, h, 15:16], in1=Ep[g][:],
                                                               op0=ALU.is_ge, op1=ALU.mult),
                       reads=[bSp[g], bEp[g], bc16], writes=[bMk[gm]])

                    def accmm(e):
                        for q in range(4):
                            e.matmul(PB[q][:, :], lhsT=identb[:, :], rhs=Mk[gm][:, q * 512:(q + 1) * 512],
                                     start=(h == 0), stop=(h == 7))
                    op("pe", accmm, reads=[bMk[gm], bC], writes=[bPB[0], bPB[1], bPB[2], bPB[3]])
                    if h == 7:
                        for q in range(4):
                            op("act", lambda e, q=q: e.activation(out=acc[:, pc * 2048 + q * 512:pc * 2048 + (q + 1) * 512], in_=PB[q][:, :], func=AF.Copy),
                               reads=[bPB[q]], writes=[bacc[pc]])

                def g_start(pc, h):
                    g = gstate[0] % 3
                    gstate[0] += 1
                    op("dve", lambda e: e.tensor_tensor(
                        out=Sp[g][:].rearrange("p (a b) -> p a b", a=16),
                        in0=s_sb[:, h, 0, pc * 16:(pc + 1) * 16].unsqueeze(2).to_broadcast([128, 16, 128]),
                        in1=s_sb[:, h, 1, :].unsqueeze(1).to_broadcast([128, 16, 128]), op=ALU.add),
                       reads=[bs], writes=[bSp[g]])
                    op("act", lambda e: e.activation(out=Ep[g][:], in_=Sp[g][:], func=AF.Exp, bias=biasE[:, h:h + 1]),
                       reads=[bSp[g], bsm], writes=[bEp[g]])
                    return g

                def x_s0(ec):
                    u = gstate[1] % 3
                    gstate[1] += 1
                    dma("sp", ub[u][:].rearrange("p a b -> p (a b)"), ut_b[ec * 128:(ec + 1) * 128, :], [bDR], [bub[u]], f"ub{u}")
                    return {"ec": ec, "u": u}

                def x_s1a(st):
                    ec, u = st["ec"], st["u"]
                    v = gstate[2] % 3
                    w = gstate[2] % 2
                    gstate[2] += 1
                    st["v"], st["w"] = v, w
                    dma("sp", vb[v][:], v_b3[:, ec * 4:(ec + 1) * 4, :], [bDR], [bvb[v]], f"vb{v}")

                    def mma(e):
                        for kc in range(8):
                            e.matmul(PB[4][:, :], lhsT=h2Tb[:, kc, :], rhs=ub[u][:, kc, :], start=(kc == 0), stop=(kc == 7))
                    op("pe", mma, reads=[bh2b, bub[u]], writes=[bPB[4]])

                def x_s1b(st):
                    w = st["w"]
                    op("act", lambda e: e.activation(out=gel[w][:], in_=PB[4][:, :], func=AF.Gelu), reads=[bPB[4]], writes=[bgel[w]])

                def x_s2(st):
                    ec, w = st["ec"], st["w"]
                    op("dve", lambda e: e.tensor_tensor(out=Wc[w][:], in0=gel[w][:], in1=acc[:, ec * 512:(ec + 1) * 512], op=ALU.mult),
                       reads=[bgel[w], bacc[ec // 4]], writes=[bWc[w]])
                    ptb = PB[5].bitcast(BF16)

                    def trw(e):
                        for a in range(4):
                            e.transpose(out=ptb[:, a * 128:(a + 1) * 128], in_=Wc[w][:, a * 128:(a + 1) * 128], identity=identb[:])
                    op("pe", trw, reads=[bWc[w], bC], writes=[bPB[5]])
                    op("act", lambda e: e.activation(out=WT[w][:].rearrange("p a b -> p (a b)"), in_=ptb[:, 0:512], func=AF.Copy),
                       reads=[bPB[5]], writes=[bWT[w]])

                def x_s3(st):
                    ec, v, w = st["ec"], st["v"], st["w"]

                    def mmv3(e):
                        for a in range(4):
                            for half in range(2):
                                e.matmul(PB[6 + half][:, :], lhsT=WT[w][:, a, :], rhs=vb[v][:, a, half * 512:(half + 1) * 512],
                                         start=(ec == 0 and a == 0), stop=(ec == 31 and a == 3))
                    op("pe", mmv3, reads=[bWT[w], bvb[v]], writes=[bPB[6], bPB[7]])

                xq = []
                R = {"P0": None, "A": None, "P1g": None, "P2new": None, "P2": None}

                def phase_a():
                    if R["P0"] is not None:
                        x_s1a(R["P0"])
                        R["A"] = R["P0"]
                        R["P0"] = None
                    if xq:
                        R["P0"] = x_s0(xq.pop(0))
                    if R["P1g"] is not None:
                        x_s2(R["P1g"])
                        R["P2new"] = R["P1g"]
                        R["P1g"] = None

                def phase_b():
                    if R["P2"] is not None:
                        x_s3(R["P2"])
                        R["P2"] = None
                    R["P2"] = R["P2new"]
                    R["P2new"] = None

                def phase_c():
                    if R["A"] is not None:
                        x_s1b(R["A"])
                        R["P1g"] = R["A"]
                        R["A"] = None

                pend = None
                for pc in range(8):
                    for hp in range(4):
                        phase_a()
                        for h in (2 * hp, 2 * hp + 1):
                            g = g_start(pc, h)
                            if pend is not None:
                                g_finish(*pend)
                            pend = (pc, h, g)
                            if h % 2 == 0:
                                phase_b()
                        phase_c()
                    g_finish(*pend)
                    pend = None
                    xq.extend(range(4 * pc, 4 * pc + 4))
                while xq or any(v_ is not None for v_ in R.values()):
                    phase_a()
                    phase_b()
                    phase_c()
                for half in range(2):
                    op("dve", lambda e, half=half: e.tensor_tensor(out=tt2[:, half * 512:(half + 1) * 512], in0=PB[6 + half][:, :],
                                                                  in1=gt2[:, half * 512:(half + 1) * 512], op=ALU.mult),
                       reads=[bPB[6 + half], bgt2], writes=[btt2])
                op("dve", lambda e, k=k: e.scalar_tensor_tensor(out=xps[k][:], in0=xps[k][:], scalar=ALPHA, in1=tt2[:], op0=ALU.mult, op1=ALU.add),
                   reads=[bxps[k], btt2], writes=[bxps[k]])
                mv, rs, bm, br = ln_stats(xps[k], bxps[k])
                op("dve", lambda e, k=k, mv=mv, rs=rs: e.tensor_scalar(out=xps[k][:], in0=xps[k][:], scalar1=mv[:, 0:1], scalar2=rs[:],
                                                                      op0=ALU.subtract, op1=ALU.mult),
                   reads=[bxps[k], bm, br], writes=[bxps[k]])
                op("pool", lambda e, k=k: e.tensor_tensor(out=xps[k][:], in0=xps[k][:], in1=l2g[:], op=ALU.mult), reads=[bxps[k], bl2], writes=[bxps[k]])
                op("pool", lambda e, k=k: e.tensor_tensor(out=xps[k][:], in0=xps[k][:], in1=l2b[:], op=ALU.add), reads=[bxps[k], bl2], writes=[bxps[k]])
                dma("sp", out_d[r0:r0 + 128, :], xps[k][:], [bxps[k]], [bPO[k]], f"po{k}")
            A.pop()
        SC.barrier()
        print("arena peak", A.peak, "ops", SC.nops, "waits", SC.nwait)
        SC.emit_all()
    return nc


def _kmaj(w):
    return np.ascontiguousarray(w.reshape(8, 128, -1).transpose(1, 0, 2))


def prep_shared(inp, do_peer=True):
    f = np.float32
    w_in = np.asarray(inp["w_in"][0], f)
    da_q, da_k, da_v = w_in[:, 0:1024], w_in[:, 1024:2048], w_in[:, 2048:3072]
    ml_q, ml_k = w_in[:, 3072:3584], w_in[:, 3584:4096]
    ml_v, ml_o = w_in[:, 4096:5120], w_in[:, 5120:6144]
    ml_if = w_in[:, 6144:6152]
    g_attn, g_ml = w_in[:, 6152:7176], w_in[:, 7176:8200]
    wba = np.asarray(inp["w_br_attn"][0], f)
    wbm = np.asarray(inp["w_br_mlstm"][0], f)
    sh = {}
    sh["w_ada"] = _kmaj(np.asarray(inp["w_ada"][0], f)).reshape(128, 8 * 6144)
    sh["b_ada"] = np.asarray(inp["b_ada"], f).reshape(1, 6144)
    sh["w_da"] = np.stack([np.concatenate([_kmaj(da_q[:, h * 128:(h + 1) * 128]), _kmaj(da_k[:, h * 128:(h + 1) * 128]),
                                           _kmaj(da_v[:, h * 128:(h + 1) * 128])], axis=2) for h in range(8)]).reshape(1024, 3072)
    sh["w_ml"] = np.stack([np.concatenate([_kmaj(ml_q[:, h * 128:(h + 1) * 128]), _kmaj(ml_k[:, h * 128:(h + 1) * 128]),
                                           _kmaj(ml_v[:, h * 256:(h + 1) * 256]), _kmaj(ml_o[:, h * 256:(h + 1) * 256])], axis=2)
                           for h in range(4)]).reshape(512, 6144)
    sh["w_if"] = _kmaj(ml_if).reshape(128, 64)
    sh["w_mg"] = np.stack([np.concatenate([_kmaj(g_attn[:, c * 128:(c + 1) * 128]), _kmaj(g_ml[:, c * 128:(c + 1) * 128]),
                                           _kmaj(wba[:, c * 128:(c + 1) * 128]), _kmaj(wbm[:, c * 128:(c + 1) * 128])], axis=2)
                           for c in range(8)]).reshape(1024, 4096)
    sh["w_out"] = _kmaj(np.asarray(inp["w_out"][0], f)).reshape(128, 8192)
    wq = np.asarray(inp["peer_wq"][0], f)
    sh["w_pq"] = np.stack([_kmaj(wq[:, h * 128:(h + 1) * 128]) for h in range(8)]).reshape(1024, 1024)
    if do_peer:
        u = np.asarray(inp["peer_u"][0], f)
        sh["UT"] = np.stack([_kmaj(np.ascontiguousarray(u[ec * 512:(ec + 1) * 512, :].T)) for ec in range(32)]).reshape(4096, 4096)
        sh["V"] = np.ascontiguousarray(np.asarray(inp["peer_v"][0], f))
    else:
        sh["UT"] = np.zeros((4096, 4096), f)
        sh["V"] = np.zeros((NEXP, D), f)
    sh["b_ifT"] = np.ascontiguousarray(np.asarray(inp["b_if"][0], f).T)
    sh["conv_wT"] = np.ascontiguousarray(np.asarray(inp["conv_w"][0], f).reshape(4, 8, 128).transpose(2, 1, 0)).reshape(128, 32)
    sh["conv_bT"] = np.ascontiguousarray(np.asarray(inp["conv_b"][0], f).reshape(8, 128).T)
    sh["da_lambda"] = np.asarray(inp["da_lambda"][0], f).reshape(1, 256)
    sh["subln_g"] = np.asarray(inp["da_subln_g"][0], f).reshape(1, 128)
    sh["ml_norm_g"] = np.asarray(inp["ml_norm_g"][0], f).reshape(1, D)
    for nm in ("ln1_g", "ln1_b", "ln2_g", "ln2_b"):
        sh[nm] = np.asarray(inp[nm][0], f).reshape(1, D)
    kt = np.ascontiguousarray(np.asarray(inp["peer_keys"][0], f).transpose(0, 2, 1))
    kz = np.zeros((128, 256), f)
    kz[0:64, 0:128] = kt[0]
    kz[64:128, 128:256] = kt[1]
    sh["keysT"] = kz
    sh["ident"] = np.eye(128, dtype=f)
    kk = np.arange(128)
    sh["cmask"] = (kk[None, :] >= kk[:, None]).astype(f)
    sel = np.zeros((4, 4, 128), f)
    for h in range(4):
        sel[h, h, :] = 1.0
    sh["sel4"] = sel.reshape(4, 512)
    return sh


def core_inputs(inp, sh, b0, nseq):
    f = np.float32
    m = dict(sh)
    m["x"] = np.ascontiguousarray(np.asarray(inp["x"][b0:b0 + nseq], f).reshape(nseq * SEQ, D))
    c = np.asarray(inp["c"][b0:b0 + nseq], f)
    m["cT"] = _kmaj(np.ascontiguousarray(c.T)).reshape(128, 8 * nseq)
    return m


_NC_CACHE = {}


def kernel(**inputs):
    if "full" not in _NC_CACHE:
        _NC_CACHE["full"] = build(NSEQ_FULL, True)
    nc = _NC_CACHE["full"]
    sh = prep_shared(inputs, True)
    in_maps = [core_inputs(inputs, sh, i * NSEQ_FULL, NSEQ_FULL) for i in range(NCORES)]
    res = run_bass_kernel_spmd(nc, in_maps, core_ids=list(range(NCORES)))
    out = np.concatenate([np.asarray(r["out"]).reshape(NSEQ_FULL, SEQ, D) for r in res.results], axis=0)
    return out.astype(np.float32)
```

```python
import contextlib
import math
import numpy as np
import concourse.bass as bass
import concourse.mybir as mybir
from concourse.bass_utils import run_bass_kernel_spmd

F32 = mybir.dt.float32
BF16 = mybir.dt.bfloat16
AF = mybir.ActivationFunctionType
ALU = mybir.AluOpType

D = 1024
SEQ = 2048
NT = SEQ // 128
NCORES = 8
BATCH = 32
NSEQ_FULL = BATCH // NCORES
ALPHA = 2.0 ** 0.25
LN_EPS = 1e-5
LAMBDA_INIT = 0.8 - 0.6 * math.exp(0.0)
NEXP = 16384


class Buf:
    __slots__ = ("name", "last_w", "readers")

    def __init__(self, name):
        self.name = name
        self.last_w = None
        self.readers = []


class _Rec:
    def __init__(self):
        self.calls = []

    def __getattr__(self, name):
        def f(*a, **kw):
            self.calls.append((name, a, kw))
            return None
        return f


class Sched:
    ENGS = ("pe", "act", "dve", "pool", "sp")

    def __init__(self, nc):
        self.nc = nc
        self.ops = {e: [] for e in self.ENGS}
        self.cnt = {e: 0 for e in self.ENGS}
        self.seen = {e: {} for e in self.ENGS}
        self.sems = {}
        self.dma_cnt = {}
        self.nops = 0
        self.nwait = 0

    def op(self, eng, emit, reads=(), writes=(), dma=None, n=1):
        deps = {}
        for b in reads:
            if b.last_w is not None:
                k, v, e = b.last_w
                if not (e == eng and k[0] == "eng" and False):
                    deps[k] = max(deps.get(k, 0), v)
        for b in writes:
            if b.last_w is not None:
                k, v, e = b.last_w
                if not (e == eng and k[0] == "eng" and dma is None):
                    deps[k] = max(deps.get(k, 0), v)
            for (k, v, e) in b.readers:
                if e == eng and k[0] == "eng" and dma is None:
                    continue
                deps[k] = max(deps.get(k, 0), v)
        waits = []
        seen = self.seen[eng]
        for k, v in deps.items():
            if seen.get(k, 0) >= v:
                continue
            seen[k] = v
            waits.append((k, v))
        if dma is None:
            self.cnt[eng] += 1
            key = ("eng", eng)
            tok = (key, self.cnt[eng], eng)
        else:
            key = ("dma", dma)
            self.dma_cnt[key] = self.dma_cnt.get(key, 0) + 16 * n
            tok = (key, self.dma_cnt[key], eng)
        for b in reads:
            b.readers.append(tok)
        for b in writes:
            b.last_w = tok
            b.readers = []
        rec = _Rec()
        emit(rec)
        assert len(rec.calls) >= 1
        if dma is not None:
            assert len(rec.calls) == n, (len(rec.calls), n)
        self.ops[eng].append((waits, rec.calls, key, dma is not None, n))
        self.nops += 1
        self.nwait += len(waits)
        return tok

    def barrier(self):
        cur = {("eng", e): self.cnt[e] for e in self.ENGS if self.cnt[e] > 0}
        cur.update(self.dma_cnt)
        for eng in self.ENGS:
            waits = []
            seen = self.seen[eng]
            for k, v in cur.items():
                if k == ("eng", eng) or seen.get(k, 0) >= v:
                    continue
                seen[k] = v
                waits.append((k, v))
            if waits:
                self.cnt[eng] += 1
                self.ops[eng].append((waits, [("nop", (), {})], ("eng", eng), False, 1))

    def emit_all(self):
        nc = self.nc
        keys = [("eng", e) for e in self.ENGS]
        for e in self.ENGS:
            for rec in self.ops[e]:
                if rec[2] not in keys:
                    keys.append(rec[2])
        for k in keys:
            self.sems[k] = nc.alloc_semaphore("s_" + "_".join(str(x) for x in k))
        with nc.Block() as block:
            def mk(ename):
                def body(eng):
                    for waits, calls, key, is_dma, n in self.ops[ename]:
                        for (k, v) in waits:
                            eng.wait_ge(self.sems[k], v)
                        s = self.sems[key]
                        r = None
                        for (name, a, kw) in calls:
                            r = getattr(eng, name)(*a, **kw)
                            if is_dma:
                                r.then_inc(s, 16)
                        if not is_dma:
                            r.then_inc(s, 1)
                return body
            block.tensor(mk("pe"))
            block.scalar(mk("act"))
            block.vector(mk("dve"))
            block.gpsimd(mk("pool"))
            block.sync(mk("sp"))


class Arena:
    def __init__(self, t, nbytes):
        self.t = t
        self.nbytes = nbytes
        self.off = 0
        self.stack = []
        self.peak = 0

    def push(self):
        self.stack.append(self.off)

    def pop(self):
        self.off = self.stack.pop()

    def alloc(self, shape, dt):
        esz = 4 if dt == F32 else 2
        n = 1
        for s in shape[1:]:
            n *= s
        nb = (n * esz + 63) // 64 * 64
        assert self.off + nb <= self.nbytes, ("arena overflow", self.off, nb, self.nbytes)
        a = self.t[0:shape[0], self.off // 4:(self.off + nb) // 4]
        self.off += nb
        self.peak = max(self.peak, self.off)
        if dt != F32:
            a = a.bitcast(dt)
        a = a[:, 0:n]
        if len(shape) == 3:
            a = a.rearrange("p (a b) -> p a b", a=shape[1])
        elif len(shape) == 4:
            a = a.rearrange("p (a b c) -> p a b c", a=shape[1], b=shape[2])
        return a


def build(NSEQ=NSEQ_FULL, do_peer=True, dbg=False):
    nc = bass.Bass("TRN2", target_bir_lowering=False)
    SC = Sched(nc)
    op = SC.op
    NTOK = NSEQ * SEQ

    def din(name, shape, dt=F32):
        return nc.dram_tensor(name, list(shape), dt, kind="ExternalInput").ap()

    def dscr(name, shape, dt):
        return nc.dram_tensor(name, list(shape), dt, kind="Internal").ap()

    x_d = din("x", [NTOK, D])
    cT_d = din("cT", [128, 8 * NSEQ])
    wada_d = din("w_ada", [128, 8 * 6144])
    bada_d = din("b_ada", [1, 6144])
    wda_d = din("w_da", [8 * 128, 3072])
    wml_d = din("w_ml", [4 * 128, 6144])
    wif_d = din("w_if", [128, 64])
    wmg_d = din("w_mg", [8 * 128, 4096])
    wout_d = din("w_out", [128, 8192])
    wpq_d = din("w_pq", [8 * 128, 1024])
    ut_d = din("UT", [32 * 128, 4096])
    v_d = din("V", [NEXP, D])
    bif_d = din("b_ifT", [4, 2])
    cw_d = din("conv_wT", [128, 32])
    cb_d = din("conv_bT", [128, 8])
    lam_d = din("da_lambda", [1, 256])
    subg_d = din("subln_g", [1, 128])
    mlg_d = din("ml_norm_g", [1, D])
    ln1g_d = din("ln1_g", [1, D])
    ln1b_d = din("ln1_b", [1, D])
    ln2g_d = din("ln2_g", [1, D])
    ln2b_d = din("ln2_b", [1, D])
    keysT_d = din("keysT", [128, 256])
    ident_d = din("ident", [128, 128])
    cmask_d = din("cmask", [128, 128])
    sel4_d = din("sel4", [4, 512])
    out_d = nc.dram_tensor("out", [NTOK, D], F32, kind="ExternalOutput").ap()

    wda_b = dscr("w_da_b", [8 * 128, 3072], BF16)
    wml_b = dscr("w_ml_b", [4 * 128, 6144], BF16)
    wif_b = dscr("w_if_b", [128, 64], BF16)
    wmg_b = dscr("w_mg_b", [8 * 128, 4096], BF16)
    wout_b = dscr("w_out_b", [128, 8192], BF16)
    ut_b = dscr("UT_b", [32 * 128, 4096], BF16)
    v_b = dscr("V_b", [NEXP, D], BF16)
    x1_d = dscr("x1_s", [NTOK, D], F32)
    gt_d = dscr("gt_s", [NSEQ, 2048], F32)

    es = contextlib.ExitStack()
    with es:
        ARENA_BYTES = 206 * 1024
        arena_t = es.enter_context(nc.sbuf_tensor("arena", [128, ARENA_BYTES // 4], F32))
        A = Arena(arena_t, ARENA_BYTES)
        psum_t = es.enter_context(nc.psum_tensor("psum", [128, 8, 512], F32))
        PB = [psum_t[:, i, :] for i in range(8)]
        bPB = [Buf(f"pb{i}") for i in range(8)]
        bank_rr = [0]

        def nextbank(banks):
            i = banks[bank_rr[0] % len(banks)]
            bank_rr[0] += 1
            return i

        def dma(eng, out, in_, reads, writes, key):
            return op(eng, lambda e: e.dma_start(out=out, in_=in_), reads=reads, writes=writes, dma=key)

        dbg_outs = {}

        def dump(name, ap, bufs, shape, dt):
            if not dbg:
                return
            t = nc.dram_tensor("d_" + name, list(shape), dt, kind="ExternalOutput").ap()
            dbg_outs[name] = t
            dma("sp", t, ap, list(bufs), [Buf("dbg")], "dbg_" + name)

        bOUT = Buf("out")
        bDR = Buf("dram_scratch")
        bGT = Buf("gt_scratch")
        bXO = [Buf("xo0"), Buf("xo1")]

        identf = A.alloc([128, 128], F32)
        identb = A.alloc([128, 128], BF16)
        maskb = A.alloc([128, 128], BF16)
        sel4 = A.alloc([4, 512], F32)
        cm05 = A.alloc([128, 8], F32)
        ones4 = A.alloc([4, 512], F32)
        modT = A.alloc([128, 48 * NSEQ], F32)
        neglam = A.alloc([128, 1], F32)
        gda_bc = A.alloc([128, 128], F32)
        bifT = A.alloc([4, 2], F32)
        nbf = A.alloc([4, 1], F32)
        cwT = A.alloc([128, 32], F32)
        cbT = A.alloc([128, 8], F32)
        wif = A.alloc([128, 8, 8], BF16)
        junk = A.alloc([128, 256], F32)
        bC = Buf("consts")
        bjunk = Buf("junk")
        NST = 4
        stt = [A.alloc([128, 12], F32) for _ in range(NST)]
        mvt = [A.alloc([128, 2], F32) for _ in range(NST)]
        rst = [A.alloc([128, 1], F32) for _ in range(NST)]
        bst = [Buf(f"st{i}") for i in range(NST)]
        bmv = [Buf(f"mv{i}") for i in range(NST)]
        brs = [Buf(f"rs{i}") for i in range(NST)]
        stk = [0]

        def ln_stats(src, bsrc):
            k = stk[0] % NST
            stk[0] += 1
            st, mv, rs = stt[k], mvt[k], rst[k]
            op("dve", lambda e: e.bn_stats(out=st[:, 0:6], in_=src[:, 0:512]), reads=[bsrc], writes=[bst[k]])
            op("dve", lambda e: e.bn_stats(out=st[:, 6:12], in_=src[:, 512:1024]), reads=[bsrc], writes=[bst[k]])
            op("dve", lambda e: e.bn_aggr(out=mv[:], in_=st[:]), reads=[bst[k]], writes=[bmv[k]])
            op("dve", lambda e: e.tensor_scalar(out=rs[:], in0=mv[:, 1:2], scalar1=LN_EPS, scalar2=None, op0=ALU.add),
               reads=[bmv[k]], writes=[brs[k]])
            op("pool", lambda e: e.tensor_tensor(out=rs[:], in0=rs[:], in1=cm05[:, 0:1], op=ALU.pow),
               reads=[brs[k], bC], writes=[brs[k]])
            return mv, rs, bmv[k], brs[k]

        A.push()
        tmpf = A.alloc([128, 128], F32)
        tmpm = A.alloc([128, 128], F32)
        lamt = A.alloc([128, 256], F32)
        lamp = A.alloc([128, 128], F32)
        lams = A.alloc([128, 2], F32)
        wif_f = A.alloc([128, 64], F32)
        btmp = Buf("tmp0")
        dma("sp", identf[:], ident_d[:, :], [], [bC], "c0")
        dma("sp", tmpm[:], cmask_d[:, :], [], [btmp], "c1")
        dma("sp", sel4[:], sel4_d[:, :], [], [bC], "c0")
        dma("sp", bifT[:], bif_d[:, :], [], [bC], "c0")
        dma("sp", cwT[:], cw_d[:, :], [], [bC], "c0")
        dma("sp", cbT[:], cb_d[:, :], [], [bC], "c0")
        dma("sp", gda_bc[:], subg_d[0:1, :].partition_broadcast(128), [], [bC], "c0")
        dma("sp", lamt[:], lam_d[0:1, :].partition_broadcast(128), [], [btmp], "c1")
        dma("sp", wif_f[:], wif_d[:, :], [], [btmp], "c1")
        op("pool", lambda e: e.tensor_copy(out=identb[:], in_=identf[:]), reads=[bC], writes=[bC])
        op("pool", lambda e: e.tensor_copy(out=maskb[:], in_=tmpm[:]), reads=[btmp], writes=[bC])
        op("pool", lambda e: e.memset(cm05[:], -0.5), writes=[bC])
        op("pool", lambda e: e.memset(ones4[:], 1.0), writes=[bC])
        op("pool", lambda e: e.tensor_copy(out=wif[:].rearrange("p a b -> p (a b)"), in_=wif_f[:]), reads=[btmp], writes=[bC])
        op("dve", lambda e: e.tensor_scalar(out=gda_bc[:], in0=gda_bc[:], scalar1=1.0 - LAMBDA_INIT, scalar2=None, op0=ALU.mult),
           reads=[bC], writes=[bC])
        op("dve", lambda e: e.tensor_scalar(out=nbf[:], in0=bifT[:, 1:2], scalar1=-1.0, scalar2=None, op0=ALU.mult),
           reads=[bC], writes=[bC])
        lt3 = lamt[:].rearrange("p (a b) -> p a b", a=4)
        op("dve", lambda e: e.tensor_tensor(out=lamp[:].rearrange("p (a b) -> p a b", a=2),
                                            in0=lt3[:, 0:4:2, :], in1=lt3[:, 1:4:2, :], op=ALU.mult),
           reads=[btmp], writes=[btmp])
        op("dve", lambda e: e.tensor_reduce(out=lams[:], in_=lamp[:].rearrange("p (a b) -> p a b", a=2),
                                            axis=mybir.AxisListType.X, op=ALU.add),
           reads=[btmp], writes=[btmp])
        op("act", lambda e: e.activation(out=lams[:], in_=lams[:], func=AF.Exp), reads=[btmp], writes=[btmp])
        op("dve", lambda e: e.tensor_tensor(out=neglam[:], in0=lams[:, 1:2], in1=lams[:, 0:1], op=ALU.subtract),
           reads=[btmp], writes=[bC])
        op("dve", lambda e: e.tensor_scalar(out=neglam[:], in0=neglam[:], scalar1=-LAMBDA_INIT, scalar2=None, op0=ALU.add),
           reads=[bC], writes=[bC])
        SC.barrier()
        A.pop()

        A.push()
        NCV = 3
        CW = 4096
        cvf = [A.alloc([128, CW], F32) for _ in range(NCV)]
        cvb = [A.alloc([128, CW], BF16) for _ in range(NCV)]
        bcvf = [Buf(f"cvf{i}") for i in range(NCV)]
        bcvb = [Buf(f"cvb{i}") for i in range(NCV)]
        kcv = [0]

        def convert(src, dst, R, C):
            for r0 in range(0, R, 128):
                for c0 in range(0, C, CW):
                    cw = min(CW, C - c0)
                    k = kcv[0] % NCV
                    kk = kcv[0]
                    kcv[0] += 1
                    dma("sp", cvf[k][:, 0:cw], src[r0:r0 + 128, c0:c0 + cw], [], [bcvf[k]], f"cvf{k}")
                    if kk % 2 == 0:
                        op("dve", lambda e, k=k, cw=cw: e.tensor_copy(out=cvb[k][:, 0:cw], in_=cvf[k][:, 0:cw]),
                           reads=[bcvf[k]], writes=[bcvb[k]])
                    else:
                        op("act", lambda e, k=k, cw=cw: e.activation(out=cvb[k][:, 0:cw], in_=cvf[k][:, 0:cw], func=AF.Copy),
                           reads=[bcvf[k]], writes=[bcvb[k]])
                    dma("pool", dst[r0:r0 + 128, c0:c0 + cw], cvb[k][:, 0:cw], [bcvb[k]], [bDR], f"cvb{k}")

        convert(wda_d, wda_b, 1024, 3072)
        convert(wml_d, wml_b, 512, 6144)
        convert(wmg_d, wmg_b, 1024, 4096)
        convert(wout_d, wout_b, 128, 8192)
        import os as _os0
        if do_peer and not int(_os0.environ.get("NOCONV", 0)):
            convert(ut_d, ut_b, 4096, 4096)
            convert(v_d, v_b, NEXP, D)
        SC.barrier()
        A.pop()

        A.push()
        siluT = A.alloc([128, 8, NSEQ], F32)
        modall = A.alloc([NSEQ, 6144], F32)
        badab = A.alloc([NSEQ, 6144], F32)
        wad = [A.alloc([128, 8, 512], F32) for _ in range(2)]
        bwad = [Buf("wad0"), Buf("wad1")]
        bsil = Buf("silu")
        bmod = Buf("modall")
        bmodT = Buf("modT")
        dma("sp", siluT[:].rearrange("p a b -> p (a b)"), cT_d[:, :], [], [bsil], "c1")
        dma("sp", badab[:], bada_d[0:1, :].partition_broadcast(NSEQ), [], [bmod], "c0")
        op("act", lambda e: e.activation(out=siluT[:].rearrange("p a b -> p (a b)"),
                                         in_=siluT[:].rearrange("p a b -> p (a b)"), func=AF.Silu),
           reads=[bsil], writes=[bsil])
        wada3 = wada_d.rearrange("p (a b) -> p a b", a=8)
        for pc in range(12):
            k = pc % 2
            dma("sp", wad[k][:], wada3[:, :, pc * 512:(pc + 1) * 512], [], [bwad[k]], f"wad{k}")
            bk = nextbank([0, 1])

            def mmg(e, k=k, bk=bk):
                r = None
                for kc in range(8):
                    r = e.matmul(PB[bk][0:NSEQ, :], lhsT=siluT[:, kc, :], rhs=wad[k][:, kc, :],
                                 start=(kc == 0), stop=(kc == 7))
                return r
            op("pe", mmg, reads=[bsil, bwad[k]], writes=[bPB[bk]])
            op("dve", lambda e, bk=bk, pc=pc: e.tensor_tensor(out=modall[:, pc * 512:(pc + 1) * 512], in0=PB[bk][0:NSEQ, :],
                                                              in1=badab[:, pc * 512:(pc + 1) * 512], op=ALU.add),
               reads=[bPB[bk], bmod], writes=[bmod])
        dma("sp", gt_d[:, 0:1024], modall[:, 2048:3072], [bmod], [bGT], "gtd")
        dma("sp", gt_d[:, 1024:2048], modall[:, 5120:6144], [bmod], [bGT], "gtd")
        bk = nextbank([0, 1])

        def trg(e, bk=bk):
            r = None
            for c in range(48):
                r = e.transpose(out=PB[bk][:, c * NSEQ:(c + 1) * NSEQ], in_=modall[0:NSEQ, c * 128:(c + 1) * 128],
                                identity=identf[0:NSEQ, 0:NSEQ])
            return r
        op("pe", trg, reads=[bmod, bC], writes=[bPB[bk]])
        op("dve", lambda e, bk=bk: e.tensor_copy(out=modT[:], in_=PB[bk][:, 0:48 * NSEQ]), reads=[bPB[bk]], writes=[bmodT])
        for c0 in (8, 32):
            op("dve", lambda e, c0=c0: e.tensor_scalar(out=modT[:, c0 * NSEQ:(c0 + 8) * NSEQ], in0=modT[:, c0 * NSEQ:(c0 + 8) * NSEQ],
                                                       scalar1=1.0, scalar2=None, op0=ALU.add),
               reads=[bmodT], writes=[bmodT])
        SC.barrier()
        A.pop()

        dump("modT", modT[:], [bmodT], [128, 48 * NSEQ], F32)

        def modcol(which, c, b):
            j = (which * 8 + c) * NSEQ + b
            return modT[:, j:j + 1]

        A.push()
        hT_raw = A.alloc([128, 8192], F32)
        hT = hT_raw.bitcast(BF16).rearrange("p (a b) -> p a b", a=8)
        yaT = A.alloc([128, 8, SEQ], BF16)
        ymT = A.alloc([128, 8, SEQ], BF16)
        bxts = [Buf("xt0"), Buf("xt1")]
        bhT = [Buf(f"hT{i}") for i in range(NT)]
        byaT = [Buf(f"yaT{i}") for i in range(8)]
        bymT = [Buf(f"ymT{i}") for i in range(NT)]

        def hbufs(t0, t1):
            return bhT[t0:t1]

        for b in range(NSEQ):
            A.push()
            xts = [A.alloc([128, D], F32) for _ in range(2)]
            xns = [A.alloc([128, D], BF16) for _ in range(2)]
            bxns = [Buf("xn0"), Buf("xn1")]
            for i in range(NT):
                k = i % 2
                r0 = b * SEQ + i * 128
                dma("sp", xts[k][:], x_d[r0:r0 + 128, :], [], [bxts[k]], f"xt{k}")
                mv, rs, bm, br = ln_stats(xts[k], bxts[k])
                op("dve", lambda e, k=k, mv=mv, rs=rs: e.tensor_scalar(out=xns[k][:], in0=xts[k][:], scalar1=mv[:, 0:1], scalar2=rs[:],
                                                                      op0=ALU.subtract, op1=ALU.mult),
                   reads=[bxts[k], bm, br], writes=[bxns[k]])
                if b == 0 and i == 0:
                    dump("mv0", mv[:], [bm], [128, 2], F32)
                    dump("rs0", rs[:], [br], [128, 1], F32)
                    dump("xn0", xns[k][:], [bxns[k]], [128, D], BF16)
                    dump("xt0", xts[k][:], [bxts[k]], [128, D], F32)
                bk = nextbank([0, 1])
                ptb = PB[bk].bitcast(BF16)

                def trx(e, k=k, ptb=ptb):
                    r = None
                    for c in range(8):
                        r = e.transpose(out=ptb[:, c * 128:(c + 1) * 128], in_=xns[k][:, c * 128:(c + 1) * 128], identity=identb[:])
                    return r
                op("pe", trx, reads=[bxns[k], bC], writes=[bPB[bk]])
                for c in range(8):
                    if c % 2 == 0:
                        op("act", lambda e, c=c, ptb=ptb, i=i: e.activation(out=hT[:, c, i * 128:(i + 1) * 128], in_=ptb[:, c * 128:(c + 1) * 128],
                                                                           func=AF.Identity, scale=modcol(1, c, b), bias=modcol(0, c, b)),
                           reads=[bPB[bk], bmodT], writes=[bhT[i]])
                    else:
                        op("dve", lambda e, c=c, ptb=ptb, i=i: e.tensor_scalar(out=hT[:, c, i * 128:(i + 1) * 128], in0=ptb[:, c * 128:(c + 1) * 128],
                                                                              scalar1=modcol(1, c, b), scalar2=modcol(0, c, b),
                                                                              op0=ALU.mult, op1=ALU.add),
                           reads=[bPB[bk], bmodT], writes=[bhT[i]])
            SC.barrier()
            A.pop()
            if b == 0:
                dump("hT", hT.rearrange("p a b -> p (a b)"), bhT, [128, 8 * SEQ], BF16)

            A.push()
            wda = [A.alloc([128, 8, 384], BF16) for _ in range(2)]
            bwda = [Buf("wda0"), Buf("wda1")]
            qTs = [A.alloc([128, SEQ], BF16) for _ in range(2)]
            kTs = [A.alloc([128, SEQ], BF16) for _ in range(2)]
            vss = [A.alloc([128, NT, 129], BF16) for _ in range(2)]
            bq = [[Buf(f"q{s}{c}") for c in range(4)] for s in range(2)]
            bkk = [[Buf(f"k{s}{c}") for c in range(4)] for s in range(2)]
            bv = [[Buf(f"v{s}{c}") for c in range(4)] for s in range(2)]
            Es = [A.alloc([128, NT, 256], BF16) for _ in range(2)]
            bE = [[Buf(f"E{s}{j}") for j in range(NT)] for s in range(2)]
            osb = [A.alloc([128, 2, 128], F32) for _ in range(2)]
            yat = [A.alloc([128, 2, 128], BF16) for _ in range(2)]
            rzs = [A.alloc([128, 4], F32) for _ in range(2)]
            sss = [A.alloc([128, 2], F32) for _ in range(2)]
            bo = [Buf("o0"), Buf("o1")]
            byat = [Buf("yat0"), Buf("yat1")]
            brz = [Buf("rz0"), Buf("rz1")]
            bss = [Buf("ss0"), Buf("ss1")]
            for s in range(2):
                op("pool", lambda e, s=s: e.memset(vss[s][:, :, 128:129], 1.0), writes=[bv[s][c] for c in range(4)])
            ecnt = 0
            ccnt = 0
            for h in range(8):
                s = h % 2
                dma("sp", wda[s][:].rearrange("p a b -> p (a b)"), wda_b[h * 128:(h + 1) * 128, :], [bDR], [bwda[s]], f"wda{s}")
                for c in range(4):
                    for which in range(2):
                        bk = nextbank([0, 1])

                        def mmg(e, s=s, c=c, which=which, bk=bk):
                            r = None
                            for kc in range(8):
                                r = e.matmul(PB[bk][:, :], lhsT=wda[s][:, kc, which * 128:(which + 1) * 128],
                                             rhs=hT[:, kc, c * 512:(c + 1) * 512], start=(kc == 0), stop=(kc == 7))
                            return r
                        op("pe", mmg, reads=[bwda[s]] + hbufs(4 * c, 4 * c + 4), writes=[bPB[bk]])
                        if which == 0:
                            op("act", lambda e, s=s, c=c, bk=bk: e.activation(out=qTs[s][:, c * 512:(c + 1) * 512], in_=PB[bk][:, :],
                                                                             func=AF.Copy, scale=0.125),
                               reads=[bPB[bk]], writes=[bq[s][c]])
                        else:
                            op("dve", lambda e, s=s, c=c, bk=bk: e.tensor_copy(out=kTs[s][:, c * 512:(c + 1) * 512], in_=PB[bk][:, :]),
                               reads=[bPB[bk]], writes=[bkk[s][c]])
                    bk = nextbank([0, 1])

                    def mmv(e, s=s, c=c, bk=bk):
                        r = None
                        for t in range(4):
                            i = 4 * c + t
                            for kc in range(8):
                                r = e.matmul(PB[bk][:, t * 128:(t + 1) * 128], lhsT=hT[:, kc, i * 128:(i + 1) * 128],
                                             rhs=wda[s][:, kc, 256:384], start=(kc == 0), stop=(kc == 7))
                        return r
                    op("pe", mmv, reads=[bwda[s]] + hbufs(4 * c, 4 * c + 4), writes=[bPB[bk]])
                    op("act", lambda e, s=s, c=c, bk=bk: e.activation(out=vss[s][:, 4 * c:4 * c + 4, 0:128],
                                                                     in_=PB[bk][:, :].rearrange("p (a b) -> p a b", a=4), func=AF.Copy),
                       reads=[bPB[bk]], writes=[bv[s][c]])
                def att_scores(c, m, esl):
                    E = Es[esl]
                    for j in range(2 * c + 2):
                        q0 = 128 if j == 2 * c + 1 else 0
                        sb_ = nextbank([2, 3])
                        op("pe", lambda e: e.matmul(
                            PB[sb_][:, q0:256], lhsT=kTs[s][m * 64:(m + 1) * 64, j * 128:(j + 1) * 128],
                            rhs=qTs[s][m * 64:(m + 1) * 64, c * 256 + q0:(c + 1) * 256], start=True, stop=True),
                           reads=[bkk[s][j // 4], bq[s][c // 2]], writes=[bPB[sb_]])
                        op("act", lambda e: e.activation(out=E[:, j, q0:256], in_=PB[sb_][:, q0:256], func=AF.Exp),
                           reads=[bPB[sb_]], writes=[bE[esl][j]])
                        if j >= 2 * c:
                            d0 = (j - 2 * c) * 128
                            op("pool", lambda e: e.tensor_tensor(out=E[:, j, d0:d0 + 128], in0=E[:, j, d0:d0 + 128],
                                                                 in1=maskb[:], op=ALU.mult),
                               reads=[bE[esl][j], bC], writes=[bE[esl][j]])

                def att_pv(c, m, esl, cs):
                    E = Es[esl]
                    pob = [4 + cs * 2, 5 + cs * 2]
                    pv = PB[pob[m]][:, 0:258].rearrange("p (a b) -> p a b", a=2)

                    def pvg(e):
                        for ii in range(2):
                            i = 2 * c + ii
                            for j in range(i + 1):
                                e.matmul(pv[:, ii, :], lhsT=E[:, j, ii * 128:(ii + 1) * 128], rhs=vss[s][:, j, :],
                                         start=(j == 0), stop=(j == i))
                    op("pe", pvg, reads=[bE[esl][j] for j in range(2 * c + 2)] + [bv[s][j] for j in range(c // 2 + 1)],
                       writes=[bPB[pob[m]]])

                def att_combine(c, cs):
                    pob = [4 + cs * 2, 5 + cs * 2]
                    k = cs
                    p0 = PB[pob[0]][:, 0:258].rearrange("p (a b) -> p a b", a=2)
                    p1 = PB[pob[1]][:, 0:258].rearrange("p (a b) -> p a b", a=2)

                    def st1():
                        op("dve", lambda e: e.reciprocal(out=rzs[k][:, 0:2], in_=p0[:, :, 128]), reads=[bPB[pob[0]]], writes=[brz[k]])
                        op("dve", lambda e: e.reciprocal(out=rzs[k][:, 2:4], in_=p1[:, :, 128]), reads=[bPB[pob[1]]], writes=[brz[k]])
                        op("dve", lambda e: e.tensor_scalar(out=rzs[k][:, 2:4], in0=rzs[k][:, 2:4], scalar1=neglam[:, 0:1], scalar2=None, op0=ALU.mult),
                           reads=[brz[k], bC], writes=[brz[k]])
                        for ii in range(2):
                            op("dve", lambda e, ii=ii: e.tensor_scalar(out=osb[k][:, ii, :], in0=p0[:, ii, 0:128], scalar1=rzs[k][:, ii:ii + 1],
                                                                      scalar2=None, op0=ALU.mult),
                               reads=[bPB[pob[0]], brz[k]], writes=[bo[k]])
                            op("dve", lambda e, ii=ii: e.scalar_tensor_tensor(out=osb[k][:, ii, :], in0=p1[:, ii, 0:128],
                                                                             scalar=rzs[k][:, 2 + ii:3 + ii], in1=osb[k][:, ii, :],
                                                                             op0=ALU.mult, op1=ALU.add),
                               reads=[bPB[pob[1]], brz[k], bo[k]], writes=[bo[k]])

                    def st2():
                        for ii in range(2):
                            op("act", lambda e, ii=ii: e.activation(out=junk[:, 0:128], in_=osb[k][:, ii, :], func=AF.Square,
                                                                    accum_out=sss[k][:, ii:ii + 1]),
                               reads=[bo[k]], writes=[bss[k], bjunk])

                    def st3():
                        op("dve", lambda e: e.tensor_scalar(out=sss[k][:], in0=sss[k][:], scalar1=1.0 / 128.0, scalar2=LN_EPS,
                                                            op0=ALU.mult, op1=ALU.add),
                           reads=[bss[k]], writes=[bss[k]])
                        op("pool", lambda e: e.tensor_tensor(out=sss[k][:], in0=sss[k][:], in1=cm05[:, 0:2], op=ALU.pow),
                           reads=[bss[k], bC], writes=[bss[k]])

                    def st4():
                        for ii in range(2):
                            op("dve", lambda e, ii=ii: e.scalar_tensor_tensor(out=yat[k][:, ii, :], in0=osb[k][:, ii, :], scalar=sss[k][:, ii:ii + 1],
                                                                             in1=gda_bc[:], op0=ALU.mult, op1=ALU.mult),
                               reads=[bo[k], bss[k], bC], writes=[byat[k]])

                    def st5():
                        bk = nextbank([0, 1])
                        ptb = PB[bk].bitcast(BF16)

                        def trya(e):
                            for ii in range(2):
                                e.transpose(out=ptb[:, ii * 128:(ii + 1) * 128], in_=yat[k][:, ii, :], identity=identb[:])
                        op("pe", trya, reads=[byat[k], bC], writes=[bPB[bk]])
                        op("act", lambda e: e.activation(out=yaT[:, h, c * 256:(c + 1) * 256], in_=ptb[:, 0:256], func=AF.Copy),
                           reads=[bPB[bk]], writes=[byaT[c]])
                    cq.extend([st1, st2, st3, st4, st5])

                cq = []

                def cq_pop(n):
                    for _ in range(n):
                        if cq:
                            cq.pop(0)()

                prev = None
                for c in range(8):
                    cs = ccnt % 2
                    ccnt += 1
                    for m in range(2):
                        esl = ecnt % 2
                        ecnt += 1
                        att_scores(c, m, esl)
                        cq_pop(3)
                        if prev is not None:
                            att_pv(*prev)
                            if prev[1] == 1:
                                att_combine(prev[0], prev[3])
                        prev = (c, m, esl, cs)
                att_pv(*prev)
                att_combine(prev[0], prev[3])
                cq_pop(100)
            SC.barrier()
            A.pop()
            if b == 0:
                dump("yaT", yaT.rearrange("p a b -> p (a b)"), byaT, [128, 8 * SEQ], BF16)

            A.push()
            LNS = math.log(128.0 ** -0.5)
            gch = [A.alloc([4, 512], F32) for _ in range(2)]
            negM = A.alloc([4, SEQ], F32)
            ech = [A.alloc([4, 512], F32) for _ in range(2)]
            nfc = [A.alloc([4, 512], F32) for _ in range(2)]
            mch = [A.alloc([4, 512], F32) for _ in range(2)]
            tfc = A.alloc([4, 512], F32)
            gtok = A.alloc([128, NT * 4], F32)
            etok = A.alloc([128, NT * 4], F32)
            bgc = [Buf("gch0"), Buf("gch1")]
            bnegM = Buf("negM")
            bec = [Buf("ech0"), Buf("ech1")]
            bnf = [Buf("nf0"), Buf("nf1")]
            bmc = [Buf("mc0"), Buf("mc1")]
            btf = Buf("tf")
            bgtok = Buf("gtok")
            betok = Buf("etok")
            for c in range(4):
                k = c % 2
                sl = slice(c * 512, (c + 1) * 512)
                bki = nextbank([0, 1])
                bkf = nextbank([0, 1])
                for (bk_, c0) in ((bki, 0), (bkf, 4)):
                    def mmif(e, bk_=bk_, c0=c0, c=c):
                        r = None
                        for kc in range(8):
                            r = e.matmul(PB[bk_][0:4, :], lhsT=wif[:, kc, c0:c0 + 4], rhs=hT[:, kc, c * 512:(c + 1) * 512],
                                         start=(kc == 0), stop=(kc == 7))
                        return r
                    op("pe", mmif, reads=[bC] + hbufs(4 * c, 4 * c + 4), writes=[bPB[bk_]])
                op("act", lambda e, bkf=bkf: e.activation(out=tfc[:], in_=PB[bkf][0:4, :], func=AF.Exp, scale=-1.0, bias=nbf[:, 0:1]),
                   reads=[bPB[bkf], bC], writes=[btf])
                op("act", lambda e: e.activation(out=tfc[:], in_=tfc[:], func=AF.Ln, bias=1.0), reads=[btf], writes=[btf])
                init_nf = 0.0 if c == 0 else nfc[1 - k][:, 511:512]
                op("dve", lambda e, k=k, init_nf=init_nf: e.tensor_tensor_scan(out=nfc[k][:], data0=ones4[:], data1=tfc[:], initial=init_nf,
                                                                              op0=ALU.mult, op1=ALU.add),
                   reads=[btf, bC, bnf[1 - k]], writes=[bnf[k]])
                op("dve", lambda e, k=k, bki=bki, sl=sl: e.scalar_tensor_tensor(out=gch[k][:], in0=PB[bki][0:4, :], scalar=bifT[:, 0:1], in1=nfc[k][:],
                                                                               op0=ALU.add, op1=ALU.add),
                   reads=[bPB[bki], bC, bnf[k]], writes=[bgc[k]])
                init_m = 0.0 if c == 0 else mch[1 - k][:, 511:512]
                op("dve", lambda e, k=k, sl=sl, init_m=init_m: e.tensor_tensor_scan(out=mch[k][:], data0=gch[k][:], data1=gch[k][:], initial=init_m,
                                                                                   op0=ALU.max, op1=ALU.max),
                   reads=[bgc[k], bmc[1 - k]], writes=[bmc[k]])
                op("dve", lambda e, k=k, sl=sl: e.tensor_scalar(out=negM[:, sl], in0=mch[k][:], scalar1=-1.0, scalar2=None, op0=ALU.mult),
                   reads=[bmc[k]], writes=[bnegM])
                op("dve", lambda e, k=k: e.tensor_tensor(out=ech[k][:], in0=nfc[k][:], in1=mch[k][:], op=ALU.subtract),
                   reads=[bnf[k], bmc[k]], writes=[bec[k]])
                op("act", lambda e, k=k: e.activation(out=ech[k][:], in_=ech[k][:], func=AF.Exp), reads=[bec[k]], writes=[bec[k]])

                def trgt(e, k=k, c=c):
                    r = None
                    for t in range(4):
                        i = 4 * c + t
                        r = e.transpose(out=PB[7][:, i * 4:(i + 1) * 4], in_=gch[k][0:4, t * 128:(t + 1) * 128], identity=identf[0:4, 0:4])
                        r = e.transpose(out=PB[7][:, 64 + i * 4:64 + (i + 1) * 4], in_=ech[k][0:4, t * 128:(t + 1) * 128], identity=identf[0:4, 0:4])
                    return r
                op("pe", trgt, reads=[bgc[k], bec[k], bC], writes=[bPB[7]])
            bk = 7
            op("dve", lambda e, bk=bk: e.tensor_scalar(out=gtok[:], in0=PB[bk][:, 0:64], scalar1=LNS, scalar2=None, op0=ALU.add),
               reads=[bPB[bk]], writes=[bgtok])
            op("dve", lambda e, bk=bk: e.tensor_copy(out=etok[:], in_=PB[bk][:, 64:128]), reads=[bPB[bk]], writes=[betok])

            wml = A.alloc([128, 8, 768], BF16)
            bwml = Buf("wml")
            negMbc = [A.alloc([128, 256], F32) for _ in range(2)]
            bnb = [Buf("nb0"), Buf("nb1")]
            zp = A.alloc([128, SEQ + 3], F32)
            bzp = [Buf(f"zp{c}") for c in range(4)]
            cv = A.alloc([128, SEQ], F32)
            bcv = Buf("cv")
            qTm = A.alloc([128, SEQ], BF16)
            kTm = A.alloc([128, SEQ], BF16)
            bqm = Buf("qTm")
            bkm = Buf("kTm")
            vm = A.alloc([128, NT, 257], BF16)
            bvm = [Buf(f"vm{i}") for i in range(8)]
            Ps = [A.alloc([128, NT, 256], BF16) for _ in range(2)]
            bP = [[Buf(f"P{s}{j}") for j in range(NT)] for s in range(2)]
            Wt = [A.alloc([128, 256], F32) for _ in range(2)]
            bWt = [Buf("Wt0"), Buf("Wt1")]
            gml_bc = A.alloc([128, 256], F32)
            bgml = Buf("gml")
            hn = [A.alloc([128, 256], F32) for _ in range(4)]
            bhn = [Buf("hn%d" % i_) for i_ in range(4)]
            og = [A.alloc([128, 256], BF16) for _ in range(4)]
            bog = [Buf("og%d" % i_) for i_ in range(4)]
            ymt = [A.alloc([128, 256], BF16) for _ in range(4)]
            bymt = [Buf("ymt%d" % i_) for i_ in range(4)]
            dens = [A.alloc([128, 2], F32) for _ in range(4)]
            bden = [Buf("den%d" % i_) for i_ in range(4)]
            op("pool", lambda e: e.memset(zp[:, 0:3], 0.0), writes=bzp)
            op("pool", lambda e: e.memset(vm[:, :, 256:257], 1.0), writes=bvm)
            pcnt = 0
            tcnt = 0
            wcnt = 0
            for h in range(4):
                dma("sp", wml[:].rearrange("p a b -> p (a b)"), wml_b[h * 128:(h + 1) * 128, :], [bDR], [bwml], "wml")
                dma("sp", gml_bc[:], mlg_d[0:1, h * 256:(h + 1) * 256].partition_broadcast(128), [], [bgml], "gml")
                for which in range(2):
                    ch = which * 4 + h
                    for c in range(4):
                        bk = nextbank([0, 1])

                        def mmq(e, c=c, which=which, bk=bk):
                            r = None
                            for kc in range(8):
                                r = e.matmul(PB[bk][:, :], lhsT=wml[:, kc, which * 128:(which + 1) * 128],
                                             rhs=hT[:, kc, c * 512:(c + 1) * 512], start=(kc == 0), stop=(kc == 7))
                            return r
                        op("pe", mmq, reads=[bwml] + hbufs(4 * c, 4 * c + 4), writes=[bPB[bk]])
                        op("act", lambda e, c=c, bk=bk: e.activation(out=zp[:, 3 + c * 512:3 + (c + 1) * 512], in_=PB[bk][:, :], func=AF.Copy),
                           reads=[bPB[bk]], writes=[bzp[c]])
                    op("dve", lambda e, ch=ch: e.tensor_scalar(out=cv[:], in0=zp[:, 3:SEQ + 3], scalar1=cwT[:, ch * 4 + 3:ch * 4 + 4],
                                                               scalar2=cbT[:, ch:ch + 1], op0=ALU.mult, op1=ALU.add),
                       reads=bzp + [bC], writes=[bcv])
                    for j in range(3):
                        op("dve", lambda e, ch=ch, j=j: e.scalar_tensor_tensor(out=cv[:], in0=zp[:, j:j + SEQ], scalar=cwT[:, ch * 4 + j:ch * 4 + j + 1],
                                                                              in1=cv[:], op0=ALU.mult, op1=ALU.add),
                           reads=bzp + [bC, bcv], writes=[bcv])
                    dst, bdst = (qTm, bqm) if which == 0 else (kTm, bkm)
                    op("act", lambda e, dst=dst: e.activation(out=dst[:], in_=cv[:], func=AF.Silu), reads=[bcv], writes=[bdst])
                for g2 in range(8):
                    bk = nextbank([0, 1])

                    def mmv2(e, g2=g2, bk=bk):
                        r = None
                        for t in range(2):
                            i = 2 * g2 + t
                            for kc in range(8):
                                r = e.matmul(PB[bk][:, t * 256:(t + 1) * 256], lhsT=hT[:, kc, i * 128:(i + 1) * 128],
                                             rhs=wml[:, kc, 256:512], start=(kc == 0), stop=(kc == 7))
                        return r
                    op("pe", mmv2, reads=[bwml] + hbufs(2 * g2, 2 * g2 + 2), writes=[bPB[bk]])
                    op("dve", lambda e, g2=g2, bk=bk: e.tensor_copy(out=vm[:, 2 * g2:2 * g2 + 2, 0:256],
                                                                   in_=PB[bk][:, :].rearrange("p (a b) -> p a b", a=2)),
                       reads=[bPB[bk]], writes=[bvm[g2]])
                mq = []
                def ml_build(c, psl):
                    nonlocal wcnt
                    P = Ps[psl]
                    bk = nextbank([0, 1])
                    op("pe", lambda e, h=h, c=c, bk=bk: e.matmul(PB[bk][:, 0:256], lhsT=sel4[0:4, h * 128:(h + 1) * 128],
                                                                rhs=negM[0:4, c * 256:(c + 1) * 256], start=True, stop=True),
                       reads=[bC, bnegM], writes=[bPB[bk]])
                    op("act", lambda e, psl=psl, bk=bk: e.activation(out=negMbc[psl][:, :], in_=PB[bk][:, 0:256], func=AF.Copy),
                       reads=[bPB[bk]], writes=[bnb[psl]])
                    for j in range(2 * c + 2):
                        q0 = 128 if j == 2 * c + 1 else 0
                        sb_ = nextbank([2, 3])
                        ws = wcnt % 2
                        wcnt += 1
                        op("pe", lambda e, j=j, c=c, q0=q0, sb_=sb_: e.matmul(
                            PB[sb_][:, q0:256], lhsT=kTm[:, j * 128:(j + 1) * 128], rhs=qTm[:, c * 256 + q0:(c + 1) * 256], start=True, stop=True),
                           reads=[bkm, bqm], writes=[bPB[sb_]])
                        op("act", lambda e, j=j, c=c, q0=q0, ws=ws, h=h, psl=psl: e.activation(out=Wt[ws][:, q0:256], in_=negMbc[psl][:, q0:256],
                                                                                     func=AF.Exp, bias=gtok[:, j * 4 + h:j * 4 + h + 1]),
                           reads=[bnb[psl], bgtok], writes=[bWt[ws]])
                        op("dve", lambda e, P=P, j=j, q0=q0, ws=ws, sb_=sb_: e.tensor_tensor(out=P[:, j, q0:256], in0=PB[sb_][:, q0:256],
                                                                                            in1=Wt[ws][:, q0:256], op=ALU.mult),
                           reads=[bPB[sb_], bWt[ws]], writes=[bP[psl][j]])
                        if mq:
                            mq.pop(0)()
                        if j >= 2 * c:
                            d0 = (j - 2 * c) * 128
                            op("pool", lambda e, P=P, j=j, d0=d0: e.tensor_tensor(out=P[:, j, d0:d0 + 128], in0=P[:, j, d0:d0 + 128],
                                                                                 in1=maskb[:], op=ALU.mult),
                               reads=[bP[psl][j], bC], writes=[bP[psl][j]])

                def ml_pv(c, psl):
                    nonlocal tcnt
                    P = Ps[psl]
                    for ii in range(2):
                        i = 2 * c + ii
                        k = tcnt % 4
                        tcnt += 1
                        pb_ = nextbank([4, 5, 6, 7])

                        def pvm(e, P=P, ii=ii, i=i, pb_=pb_):
                            r = None
                            for j in range(i + 1):
                                r = e.matmul(PB[pb_][:, 0:257], lhsT=P[:, j, ii * 128:(ii + 1) * 128], rhs=vm[:, j, :],
                                             start=(j == 0), stop=(j == i))
                            return r
                        op("pe", pvm, reads=[bP[psl][j] for j in range(i + 1)] + [bvm[j] for j in range(i // 2 + 1)], writes=[bPB[pb_]])
                        bk = nextbank([0, 1])

                        def mmo(e, i=i, bk=bk):
                            r = None
                            for kc in range(8):
                                r = e.matmul(PB[bk][:, 0:256], lhsT=hT[:, kc, i * 128:(i + 1) * 128], rhs=wml[:, kc, 512:768],
                                             start=(kc == 0), stop=(kc == 7))
                            return r
                        op("pe", mmo, reads=[bwml, bhT[i]], writes=[bPB[bk]])
                        op("act", lambda e, k=k, bk=bk: e.activation(out=og[k][:], in_=PB[bk][:, 0:256], func=AF.Sigmoid),
                           reads=[bPB[bk]], writes=[bog[k]])
                        op("dve", lambda e, k=k, pb_=pb_: e.tensor_copy(out=dens[k][:, 0:1], in_=PB[pb_][:, 256:257]),
                           reads=[bPB[pb_]], writes=[bden[k]])
                        op("dve", lambda e, k=k: e.scalar_tensor_tensor(out=dens[k][:, 0:1], in0=dens[k][:, 0:1], scalar=-1.0, in1=dens[k][:, 0:1],
                                                                        op0=ALU.mult, op1=ALU.max),
                           reads=[bden[k]], writes=[bden[k]])
                        op("dve", lambda e, k=k, i=i, h=h: e.tensor_tensor(out=dens[k][:, 0:1], in0=dens[k][:, 0:1], in1=etok[:, i * 4 + h:i * 4 + h + 1],
                                                                          op=ALU.max),
                           reads=[bden[k], betok], writes=[bden[k]])
                        op("dve", lambda e, k=k: e.reciprocal(out=dens[k][:, 0:1], in_=dens[k][:, 0:1]), reads=[bden[k]], writes=[bden[k]])
                        op("dve", lambda e, k=k, pb_=pb_: e.tensor_scalar(out=hn[k][:], in0=PB[pb_][:, 0:256], scalar1=dens[k][:, 0:1], scalar2=None,
                                                                         op0=ALU.mult),
                           reads=[bPB[pb_], bden[k]], writes=[bhn[k]])
                        def sB(k=k):
                            op("act", lambda e: e.activation(out=junk[:, 0:256], in_=hn[k][:], func=AF.Square, accum_out=dens[k][:, 1:2]),
                               reads=[bhn[k]], writes=[bden[k], bjunk])

                        def sC(k=k):
                            op("dve", lambda e: e.tensor_scalar(out=dens[k][:, 1:2], in0=dens[k][:, 1:2], scalar1=1.0 / 256.0, scalar2=LN_EPS,
                                                                op0=ALU.mult, op1=ALU.add),
                               reads=[bden[k]], writes=[bden[k]])
                            op("pool", lambda e: e.tensor_tensor(out=dens[k][:, 1:2], in0=dens[k][:, 1:2], in1=cm05[:, 0:1], op=ALU.pow),
                               reads=[bden[k], bC], writes=[bden[k]])

                        def sD(k=k):
                            op("dve", lambda e: e.scalar_tensor_tensor(out=hn[k][:], in0=hn[k][:], scalar=dens[k][:, 1:2],
                                                                       in1=gml_bc[:], op0=ALU.mult, op1=ALU.mult),
                               reads=[bhn[k], bden[k], bgml], writes=[bhn[k]])
                            op("pool", lambda e: e.tensor_tensor(out=ymt[k][:], in0=hn[k][:], in1=og[k][:], op=ALU.mult),
                               reads=[bhn[k], bog[k]], writes=[bymt[k]])

                        def sE(k=k, i=i):
                            bk = nextbank([0, 1])
                            ptb = PB[bk].bitcast(BF16)

                            def trym(e):
                                for ee in range(2):
                                    e.transpose(out=ptb[:, ee * 128:(ee + 1) * 128], in_=ymt[k][:, ee * 128:(ee + 1) * 128], identity=identb[:])
                            op("pe", trym, reads=[bymt[k], bC], writes=[bPB[bk]])
                            op("act", lambda e: e.activation(out=ymT[:, 2 * h:2 * h + 2, i * 128:(i + 1) * 128],
                                                             in_=ptb[:, 0:256].rearrange("p (a b) -> p a b", a=2), func=AF.Copy),
                               reads=[bPB[bk]], writes=[bymT[i]])
                        mq.extend([sB, sC, sD, sE])

                prevc = None
                for c in range(8):
                    psl = pcnt % 2
                    pcnt += 1
                    ml_build(c, psl)
                    while len(mq) > 4:
                        mq.pop(0)()
                    if prevc is not None:
                        ml_pv(*prevc)
                    prevc = (c, psl)
                ml_pv(*prevc)
                while mq:
                    mq.pop(0)()
            SC.barrier()
            A.pop()
            if b == 0:
                dump("ymT", ymT.rearrange("p a b -> p (a b)"), bymT, [128, 8 * SEQ], BF16)

            A.push()
            yT = A.alloc([128, 8, SEQ], BF16)
            byT = [Buf(f"yT{c}") for c in range(4)]
            wmg = [A.alloc([128, 8, 512], BF16) for _ in range(2)]
            bwmg = [Buf("wmg0"), Buf("wmg1")]
            sgA = [A.alloc([128, 512], F32) for _ in range(2)]
            sgB = [A.alloc([128, 512], F32) for _ in range(2)]
            t1 = [A.alloc([128, 512], F32) for _ in range(2)]
            t2 = [A.alloc([128, 512], F32) for _ in range(2)]
            bsgA = [Buf("sgA0"), Buf("sgA1")]
            bsgB = [Buf("sgB0"), Buf("sgB1")]
            bt1 = [Buf("t10"), Buf("t11")]
            bt2 = [Buf("t20"), Buf("t21")]
            dcnt = 0
            allb = [0, 1, 2, 3, 4, 5, 6, 7]
            for fc in range(8):
                s = fc % 2
                dma("sp", wmg[s][:].rearrange("p a b -> p (a b)"), wmg_b[fc * 128:(fc + 1) * 128, :], [bDR], [bwmg[s]], f"wmg{s}")
                for c in range(4):
                    k = dcnt % 2
                    dcnt += 1
                    banks = []
                    for which, src, srcb in ((0, hT, hbufs(4 * c, 4 * c + 4)), (1, hT, hbufs(4 * c, 4 * c + 4)),
                                             (2, yaT, byaT[2 * c:2 * c + 2]), (3, ymT, bymT[4 * c:4 * c + 4])):
                        bk = nextbank(allb)
                        banks.append(bk)

                        def mmd(e, s=s, c=c, which=which, src=src, bk=bk):
                            r = None
                            for kc in range(8):
                                r = e.matmul(PB[bk][:, :], lhsT=wmg[s][:, kc, which * 128:(which + 1) * 128],
                                             rhs=src[:, kc, c * 512:(c + 1) * 512], start=(kc == 0), stop=(kc == 7))
                            return r
                        op("pe", mmd, reads=[bwmg[s]] + list(srcb), writes=[bPB[bk]])
                    op("act", lambda e, k=k, bk=banks[0]: e.activation(out=sgA[k][:], in_=PB[bk][:, :], func=AF.Sigmoid),
                       reads=[bPB[banks[0]]], writes=[bsgA[k]])
                    op("act", lambda e, k=k, bk=banks[1]: e.activation(out=sgB[k][:], in_=PB[bk][:, :], func=AF.Sigmoid),
                       reads=[bPB[banks[1]]], writes=[bsgB[k]])
                    op("dve", lambda e, k=k, bk=banks[2]: e.tensor_tensor(out=t1[k][:], in0=PB[bk][:, :], in1=sgA[k][:], op=ALU.mult),
                       reads=[bPB[banks[2]], bsgA[k]], writes=[bt1[k]])
                    op("dve", lambda e, k=k, bk=banks[3]: e.tensor_tensor(out=t2[k][:], in0=PB[bk][:, :], in1=sgB[k][:], op=ALU.mult),
                       reads=[bPB[banks[3]], bsgB[k]], writes=[bt2[k]])
                    op("pool", lambda e, k=k, fc=fc, c=c: e.tensor_tensor(out=yT[:, fc, c * 512:(c + 1) * 512], in0=t1[k][:], in1=t2[k][:], op=ALU.add),
                       reads=[bt1[k], bt2[k]], writes=[byT[c]])
            SC.barrier()
            if b == 0:
                dump("yT", yT.rearrange("p a b -> p (a b)"), byT, [128, 8 * SEQ], BF16)
            wo = hT_raw[:, 0:4096].bitcast(BF16).rearrange("p (a b) -> p a b", a=8)
            ttile = hT_raw[:, 4096:5120]
            l1g = hT_raw[:, 5120:6144]
            l1b = hT_raw[:, 6144:7168]
            gt1 = hT_raw[:, 7168:8192]
            xts = [A.alloc([128, D], F32) for _ in range(2)]
            bwo = Buf("wo")
            btt = Buf("tt")
            bl1 = Buf("l1")
            dma("sp", wo.rearrange("p a b -> p (a b)"), wout_b[:, :], [bDR], [bwo], "wo")
            dma("sp", l1g, ln1g_d[0:1, :].partition_broadcast(128), [], [bl1], "l1")
            dma("sp", l1b, ln1b_d[0:1, :].partition_broadcast(128), [], [bl1], "l1")
            dma("sp", gt1, gt_d[b:b + 1, 0:1024].partition_broadcast(128), [bGT], [bl1], "l1")
            for i in range(NT):
                k = i % 2
                r0 = b * SEQ + i * 128
                dma("sp", xts[k][:], x_d[r0:r0 + 128, :], [], [bxts[k]], f"xt{k}")
                for half in range(2):
                    bk = nextbank(allb)

                    def mmo2(e, i=i, half=half, bk=bk):
                        r = None
                        for kc in range(8):
                            r = e.matmul(PB[bk][:, :], lhsT=yT[:, kc, i * 128:(i + 1) * 128], rhs=wo[:, kc, half * 512:(half + 1) * 512],
                                         start=(kc == 0), stop=(kc == 7))
                        return r
                    op("pe", mmo2, reads=[byT[i // 4], bwo], writes=[bPB[bk]])
                    op("dve", lambda e, half=half, bk=bk: e.tensor_tensor(out=ttile[:, half * 512:(half + 1) * 512], in0=PB[bk][:, :],
                                                                         in1=gt1[:, half * 512:(half + 1) * 512], op=ALU.mult),
                       reads=[bPB[bk], bl1], writes=[btt])
                op("dve", lambda e, k=k: e.scalar_tensor_tensor(out=xts[k][:], in0=xts[k][:], scalar=ALPHA, in1=ttile, op0=ALU.mult, op1=ALU.add),
                   reads=[bxts[k], btt], writes=[bxts[k]])
                mv, rs, bm, br = ln_stats(xts[k], bxts[k])
                op("dve", lambda e, k=k, mv=mv, rs=rs: e.tensor_scalar(out=xts[k][:], in0=xts[k][:], scalar1=mv[:, 0:1], scalar2=rs[:],
                                                                      op0=ALU.subtract, op1=ALU.mult),
                   reads=[bxts[k], bm, br], writes=[bxts[k]])
                op("pool", lambda e, k=k: e.tensor_tensor(out=xts[k][:], in0=xts[k][:], in1=l1g, op=ALU.mult), reads=[bxts[k], bl1], writes=[bxts[k]])
                op("pool", lambda e, k=k: e.tensor_tensor(out=xts[k][:], in0=xts[k][:], in1=l1b, op=ALU.add), reads=[bxts[k], bl1], writes=[bxts[k]])
                dst = x1_d if do_peer else out_d
                dma("sp", dst[r0:r0 + 128, :], xts[k][:], [bxts[k]], [bXO[k]], f"xo{k}")
            SC.barrier()
            A.pop()
        A.pop()
        SC.barrier()

        if do_peer:
            A.push()
            RB = [0, 1, 2, 3, 4, 5]
            RBX = [4, 5]
            keysT = A.alloc([128, 256], F32)
            l2g = A.alloc([128, D], F32)
            l2b = A.alloc([128, D], F32)
            gt2 = A.alloc([128, D], F32)
            bl2 = Buf("l2")
            bgt2 = Buf("gt2")
            xps = [A.alloc([128, D], F32) for _ in range(2)]
            bxps = [Buf("xp0"), Buf("xp1")]
            xn2 = A.alloc([128, D], F32)
            bxn2 = Buf("xn2")
            h2T = A.alloc([128, 8, 128], F32)
            h2Tb = A.alloc([128, 8, 128], BF16)
            bh2 = Buf("h2T")
            bh2b = Buf("h2Tb")
            wq = [A.alloc([128, 8, 128], F32) for _ in range(2)]
            bwq = [Buf("wq0"), Buf("wq1")]
            qTh = [A.alloc([128, 128], F32) for _ in range(2)]
            bqTh = [Buf("qTh0"), Buf("qTh1")]
            s_sb = A.alloc([128, 8, 2, 128], F32)
            bs = Buf("s_sb")
            m16 = A.alloc([128, 8, 2, 16], F32)
            bm16 = Buf("m16")
            wk1 = A.alloc([128, 128], F32)
            bwk1 = Buf("wk1")
            cand = A.alloc([128, 8, 16, 16], F32)
            bcand = Buf("cand")
            wk2 = A.alloc([128, 256], F32)
            bwk2 = Buf("wk2")
            c16 = A.alloc([128, 8, 16], F32)
            bc16 = Buf("c16")
            negthr = A.alloc([128, 8], F32)
            zz = A.alloc([128, 8], F32)
            biasE = A.alloc([128, 8], F32)
            bsm = Buf("small")
            Sp = [A.alloc([128, 2048], F32) for _ in range(3)]
            Ep = [A.alloc([128, 2048], BF16) for _ in range(3)]
            Mk = [A.alloc([128, 2048], BF16) for _ in range(2)]
            bSp = [Buf("Sp0"), Buf("Sp1"), Buf("Sp2")]
            bEp = [Buf("Ep0"), Buf("Ep1"), Buf("Ep2")]
            bMk = [Buf("Mk0"), Buf("Mk1"), Buf("Mk2")]
            gstate = [0, 0, 0]
            acc = A.alloc([128, NEXP], BF16)
            bacc = [Buf(f"acc{i}") for i in range(8)]
            ub = [A.alloc([128, 8, 512], BF16) for _ in range(3)]
            bub = [Buf("ub0"), Buf("ub1"), Buf("ub2")]
            vb = [A.alloc([128, 4, D], BF16) for _ in range(3)]
            bvb = [Buf("vb0"), Buf("vb1"), Buf("vb2")]
            gel = [A.alloc([128, 512], F32) for _ in range(2)]
            bgel = [Buf("gel0"), Buf("gel1"), Buf("gel2")]
            Wc = [A.alloc([128, 512], BF16) for _ in range(2)]
            bWc = [Buf("Wc0"), Buf("Wc1"), Buf("Wc2")]
            WT = [A.alloc([128, 4, 128], BF16) for _ in range(2)]
            bWT = [Buf("WT0"), Buf("WT1"), Buf("WT2")]
            tt2 = A.alloc([128, D], F32)
            btt2 = Buf("tt2")
            bPO = [Buf("po0"), Buf("po1")]
            dma("sp", keysT[:], keysT_d[:, :], [], [bl2], "l2")
            dma("sp", l2g[:], ln2g_d[0:1, :].partition_broadcast(128), [], [bl2], "l2")
            dma("sp", l2b[:], ln2b_d[0:1, :].partition_broadcast(128), [], [bl2], "l2")
            wpq3 = wpq_d.rearrange("r (a b) -> r a b", a=8)
            v_b3 = v_b.rearrange("(g p) n -> p g n", p=128)
            gcnt = 0
            ecnt2 = 0
            import os as _os
            _PT = int(_os.environ.get("PEER_TILES", NSEQ * NT))
            _PS = int(_os.environ.get("PEER_STAGE", 9))
            for ti in range(min(_PT, NSEQ * NT)):
                b = ti // NT
                k = ti % 2
                r0 = ti * 128
                if ti % NT == 0:
                    dma("sp", gt2[:], gt_d[b:b + 1, 1024:2048].partition_broadcast(128), [bGT], [bgt2], "gt2")
                dma("sp", xps[k][:], x1_d[r0:r0 + 128, :], [bXO[0], bXO[1]], [bxps[k]], f"xp{k}")
                mv, rs, bm, br = ln_stats(xps[k], bxps[k])
                op("dve", lambda e, k=k, mv=mv, rs=rs: e.tensor_scalar(out=xn2[:], in0=xps[k][:], scalar1=mv[:, 0:1], scalar2=rs[:],
                                                                      op0=ALU.subtract, op1=ALU.mult),
                   reads=[bxps[k], bm, br], writes=[bxn2])
                for hf in range(2):
                    bk = nextbank(RB)

                    def trp(e, hf=hf, bk=bk):
                        for cc in range(4):
                            c = hf * 4 + cc
                            e.transpose(out=PB[bk][:, cc * 128:(cc + 1) * 128], in_=xn2[:, c * 128:(c + 1) * 128], identity=identf[:])
                    op("pe", trp, reads=[bxn2, bC], writes=[bPB[bk]])
                    for cc in range(4):
                        c = hf * 4 + cc
                        if cc % 2 == 0:
                            op("act", lambda e, c=c, cc=cc, bk=bk, b=b: e.activation(out=h2T[:, c, :], in_=PB[bk][:, cc * 128:(cc + 1) * 128],
                                                                                    func=AF.Identity, scale=modcol(4, c, b), bias=modcol(3, c, b)),
                               reads=[bPB[bk], bmodT], writes=[bh2])
                        else:
                            op("dve", lambda e, c=c, cc=cc, bk=bk, b=b: e.tensor_scalar(out=h2T[:, c, :], in0=PB[bk][:, cc * 128:(cc + 1) * 128],
                                                                                       scalar1=modcol(4, c, b), scalar2=modcol(3, c, b),
                                                                                       op0=ALU.mult, op1=ALU.add),
                               reads=[bPB[bk], bmodT], writes=[bh2])
                op("pool", lambda e: e.tensor_copy(out=h2Tb[:].rearrange("p a b -> p (a b)"), in_=h2T[:].rearrange("p a b -> p (a b)")),
                   reads=[bh2], writes=[bh2b])
                sbk = None
                for h in range(8 if _PS >= 2 else 0):
                    ws = h % 2
                    dma("sp", wq[ws][:], wpq3[h * 128:(h + 1) * 128, :, :], [], [bwq[ws]], f"wq{ws}")
                    bk = nextbank(RB)

                    def mmq(e, ws=ws, bk=bk):
                        for kc in range(8):
                            e.matmul(PB[bk][:, 0:128], lhsT=wq[ws][:, kc, :], rhs=h2T[:, kc, :], start=(kc == 0), stop=(kc == 7))
                    op("pe", mmq, reads=[bwq[ws], bh2], writes=[bPB[bk]])
                    op("act", lambda e, ws=ws, bk=bk: e.activation(out=qTh[ws][:], in_=PB[bk][:, 0:128], func=AF.Copy),
                       reads=[bPB[bk]], writes=[bqTh[ws]])
                    if h % 2 == 0:
                        sbk = nextbank(RB)

                    def mms(e, ws=ws, sbk=sbk, h=h):
                        o0 = (h % 2) * 256
                        e.matmul(PB[sbk][:, o0:o0 + 256], lhsT=qTh[ws][:, :], rhs=keysT[:, :], start=True, stop=True)
                    op("pe", mms, reads=[bqTh[ws], bl2], writes=[bPB[sbk]])
                    if h % 2 == 1:
                        op("dve", lambda e, h=h, sbk=sbk: e.tensor_copy(out=s_sb[:, h - 1:h + 1, :, :].rearrange("p a b c -> p (a b c)"), in_=PB[sbk][:, :]),
                           reads=[bPB[sbk]], writes=[bs])
                if _PS < 3:
                    dma("sp", out_d[r0:r0 + 128, :], xps[k][:], [bxps[k], bs, bh2b], [bPO[k]], f"po{k}")
                    continue
                for h in range(8):
                    for half in range(2):
                        op("dve", lambda e, h=h, half=half: e.max(out=m16[:, h, half, 0:8], in_=s_sb[:, h, half, :]), reads=[bs], writes=[bm16])
                        op("dve", lambda e, h=h, half=half: e.match_replace(out=wk1[:], in_to_replace=m16[:, h, half, 0:8], in_values=s_sb[:, h, half, :],
                                                                           imm_value=-1e30),
                           reads=[bs, bm16], writes=[bwk1])
                        op("dve", lambda e, h=h, half=half: e.max(out=m16[:, h, half, 8:16], in_=wk1[:]), reads=[bwk1], writes=[bm16])
                op("dve", lambda e: e.tensor_tensor(out=cand[:], in0=m16[:, :, 0, :].unsqueeze(3).to_broadcast([128, 8, 16, 16]),
                                                    in1=m16[:, :, 1, :].unsqueeze(2).to_broadcast([128, 8, 16, 16]), op=ALU.add),
                   reads=[bm16], writes=[bcand])
                for h in range(8):
                    ch2 = cand[:, h, :, :].rearrange("p a b -> p (a b)")
                    op("dve", lambda e, h=h, ch2=ch2: e.max(out=c16[:, h, 0:8], in_=ch2), reads=[bcand], writes=[bc16])
                    op("dve", lambda e, h=h, ch2=ch2: e.match_replace(out=wk2[:], in_to_replace=c16[:, h, 0:8], in_values=ch2, imm_value=-1e30),
                       reads=[bcand, bc16], writes=[bwk2])
                    op("dve", lambda e, h=h: e.max(out=c16[:, h, 8:16], in_=wk2[:]), reads=[bwk2], writes=[bc16])
                op("dve", lambda e: e.tensor_scalar(out=negthr[:], in0=c16[:, :, 15], scalar1=-1.0, scalar2=None, op0=ALU.mult),
                   reads=[bc16], writes=[bsm])
                for h in range(8):
                    op("act", lambda e, h=h: e.activation(out=junk[:, 0:16], in_=c16[:, h, :], func=AF.Exp, bias=negthr[:, h:h + 1],
                                                          accum_out=zz[:, h:h + 1]),
                       reads=[bc16, bsm], writes=[bsm, bjunk])
                op("act", lambda e: e.activation(out=zz[:], in_=zz[:], func=AF.Ln), reads=[bsm], writes=[bsm])
                op("dve", lambda e: e.tensor_tensor(out=biasE[:], in0=negthr[:], in1=zz[:], op=ALU.subtract), reads=[bsm], writes=[bsm])
                if _PS < 4:
                    dma("sp", out_d[r0:r0 + 128, :], xps[k][:], [bxps[k], bsm], [bPO[k]], f"po{k}")
                    continue
                def g_finish(pc, h, g):
                    gm = g % 2
                    op("dve", lambda e: e.scalar_tensor_tensor(out=Mk[gm][:], in0=Sp[g][:], scalar=c16[:, h, 15:16], in1=Ep[g][:],
                                                               op0=ALU.is_ge, op1=ALU.mult),
                       reads=[bSp[g], bEp[g], bc16], writes=[bMk[gm]])

                    def accmm(e):
                        for q in range(4):
                            e.matmul(PB[q][:, :], lhsT=identb[:, :], rhs=Mk[gm][:, q * 512:(q + 1) * 512],
                                     start=(h == 0), stop=(h == 7))
                    op("pe", accmm, reads=[bMk[gm], bC], writes=[bPB[0], bPB[1], bPB[2], bPB[3]])
                    if h == 7:
                        for q in range(4):
                            op("act", lambda e, q=q: e.activation(out=acc[:, pc * 2048 + q * 512:pc * 2048 + (q + 1) * 512], in_=PB[q][:, :], func=AF.Copy),
                               reads=[bPB[q]], writes=[bacc[pc]])

                def g_start(pc, h):
                    g = gstate[0] % 3
                    gstate[0] += 1
                    op("dve", lambda e: e.tensor_tensor(
                        out=Sp[g][:].rearrange("p (a b) -> p a b", a=16),
                        in0=s_sb[:, h, 0, pc * 16:(pc + 1) * 16].unsqueeze(2).to_broadcast([128, 16, 128]),
                        in1=s_sb[:, h, 1, :].unsqueeze(1).to_broadcast([128, 16, 128]), op=ALU.add),
                       reads=[bs], writes=[bSp[g]])
                    op("act", lambda e: e.activation(out=Ep[g][:], in_=Sp[g][:], func=AF.Exp, bias=biasE[:, h:h + 1]),
                       reads=[bSp[g], bsm], writes=[bEp[g]])
                    return g

                def x_s0(ec):
                    u = gstate[1] % 3
                    gstate[1] += 1
                    dma("sp", ub[u][:].rearrange("p a b -> p (a b)"), ut_b[ec * 128:(ec + 1) * 128, :], [bDR], [bub[u]], f"ub{u}")
                    return {"ec": ec, "u": u}

                def x_s1a(st):
                    ec, u = st["ec"], st["u"]
                    v = gstate[2] % 3
                    w = gstate[2] % 2
                    gstate[2] += 1
                    st["v"], st["w"] = v, w
                    dma("sp", vb[v][:], v_b3[:, ec * 4:(ec + 1) * 4, :], [bDR], [bvb[v]], f"vb{v}")

                    def mma(e):
                        for kc in range(8):
                            e.matmul(PB[4][:, :], lhsT=h2Tb[:, kc, :], rhs=ub[u][:, kc, :], start=(kc == 0), stop=(kc == 7))
                    op("pe", mma, reads=[bh2b, bub[u]], writes=[bPB[4]])

                def x_s1b(st):
                    w = st["w"]
                    op("act", lambda e: e.activation(out=gel[w][:], in_=PB[4][:, :], func=AF.Gelu), reads=[bPB[4]], writes=[bgel[w]])

                def x_s2(st):
                    ec, w = st["ec"], st["w"]
                    op("dve", lambda e: e.tensor_tensor(out=Wc[w][:], in0=gel[w][:], in1=acc[:, ec * 512:(ec + 1) * 512], op=ALU.mult),
                       reads=[bgel[w], bacc[ec // 4]], writes=[bWc[w]])
                    ptb = PB[5].bitcast(BF16)

                    def trw(e):
                        for a in range(4):
                            e.transpose(out=ptb[:, a * 128:(a + 1) * 128], in_=Wc[w][:, a * 128:(a + 1) * 128], identity=identb[:])
                    op("pe", trw, reads=[bWc[w], bC], writes=[bPB[5]])
                    op("act", lambda e: e.activation(out=WT[w][:].rearrange("p a b -> p (a b)"), in_=ptb[:, 0:512], func=AF.Copy),
                       reads=[bPB[5]], writes=[bWT[w]])

                def x_s3(st):
                    ec, v, w = st["ec"], st["v"], st["w"]

                    def mmv3(e):
                        for a in range(4):
                            for half in range(2):
                                e.matmul(PB[6 + half][:, :], lhsT=WT[w][:, a, :], rhs=vb[v][:, a, half * 512:(half + 1) * 512],
                                         start=(ec == 0 and a == 0), stop=(ec == 31 and a == 3))
                    op("pe", mmv3, reads=[bWT[w], bvb[v]], writes=[bPB[6], bPB[7]])

                xq = []
                R = {"P0": None, "A": None, "P1g": None, "P2new": None, "P2": None}

                def phase_a():
                    if R["P0"] is not None:
                        x_s1a(R["P0"])
                        R["A"] = R["P0"]
                        R["P0"] = None
                    if xq:
                        R["P0"] = x_s0(xq.pop(0))
                    if R["P1g"] is not None:
                        x_s2(R["P1g"])
                        R["P2new"] = R["P1g"]
                        R["P1g"] = None

                def phase_b():
                    if R["P2"] is not None:
                        x_s3(R["P2"])
                        R["P2"] = None
                    R["P2"] = R["P2new"]
                    R["P2new"] = None

                def phase_c():
                    if R["A"] is not None:
                        x_s1b(R["A"])
                        R["P1g"] = R["A"]
                        R["A"] = None

                pendq = []

                def fin(item):
                    g_finish(*item)
                    if item[1] == 7:
                        xq.extend(range(4 * item[0], 4 * item[0] + 4))

                for pc in range(8):
                    for hp in range(4):
                        phase_a()
                        for h in (2 * hp, 2 * hp + 1):
                            g = g_start(pc, h)
                            pendq.append((pc, h, g))
                            if len(pendq) > 2:
                                fin(pendq.pop(0))
                            if h % 2 == 0:
                                phase_b()
                                phase_c()
                while pendq:
                    fin(pendq.pop(0))
                while xq or any(v_ is not None for v_ in R.values()):
                    phase_a()
                    phase_b()
                    phase_c()
                for half in range(2):
                    op("dve", lambda e, half=half: e.tensor_tensor(out=tt2[:, half * 512:(half + 1) * 512], in0=PB[6 + half][:, :],
                                                                  in1=gt2[:, half * 512:(half + 1) * 512], op=ALU.mult),
                       reads=[bPB[6 + half], bgt2], writes=[btt2])
                op("dve", lambda e, k=k: e.scalar_tensor_tensor(out=xps[k][:], in0=xps[k][:], scalar=ALPHA, in1=tt2[:], op0=ALU.mult, op1=ALU.add),
                   reads=[bxps[k], btt2], writes=[bxps[k]])
                mv, rs, bm, br = ln_stats(xps[k], bxps[k])
                op("dve", lambda e, k=k, mv=mv, rs=rs: e.tensor_scalar(out=xps[k][:], in0=xps[k][:], scalar1=mv[:, 0:1], scalar2=rs[:],
                                                                      op0=ALU.subtract, op1=ALU.mult),
                   reads=[bxps[k], bm, br], writes=[bxps[k]])
                op("pool", lambda e, k=k: e.tensor_tensor(out=xps[k][:], in0=xps[k][:], in1=l2g[:], op=ALU.mult), reads=[bxps[k], bl2], writes=[bxps[k]])
                op("pool", lambda e, k=k: e.tensor_tensor(out=xps[k][:], in0=xps[k][:], in1=l2b[:], op=ALU.add), reads=[bxps[k], bl2], writes=[bxps[k]])
                dma("sp", out_d[r0:r0 + 128, :], xps[k][:], [bxps[k]], [bPO[k]], f"po{k}")
            A.pop()
        SC.barrier()
        print("arena peak", A.peak, "ops", SC.nops, "waits", SC.nwait)
        SC.emit_all()
    return nc


def _kmaj(w):
    return np.ascontiguousarray(w.reshape(8, 128, -1).transpose(1, 0, 2))


def prep_shared(inp, do_peer=True):
    f = np.float32
    w_in = np.asarray(inp["w_in"][0], f)
    da_q, da_k, da_v = w_in[:, 0:1024], w_in[:, 1024:2048], w_in[:, 2048:3072]
    ml_q, ml_k = w_in[:, 3072:3584], w_in[:, 3584:4096]
    ml_v, ml_o = w_in[:, 4096:5120], w_in[:, 5120:6144]
    ml_if = w_in[:, 6144:6152]
    g_attn, g_ml = w_in[:, 6152:7176], w_in[:, 7176:8200]
    wba = np.asarray(inp["w_br_attn"][0], f)
    wbm = np.asarray(inp["w_br_mlstm"][0], f)
    sh = {}
    sh["w_ada"] = _kmaj(np.asarray(inp["w_ada"][0], f)).reshape(128, 8 * 6144)
    sh["b_ada"] = np.asarray(inp["b_ada"], f).reshape(1, 6144)
    sh["w_da"] = np.stack([np.concatenate([_kmaj(da_q[:, h * 128:(h + 1) * 128]), _kmaj(da_k[:, h * 128:(h + 1) * 128]),
                                           _kmaj(da_v[:, h * 128:(h + 1) * 128])], axis=2) for h in range(8)]).reshape(1024, 3072)
    sh["w_ml"] = np.stack([np.concatenate([_kmaj(ml_q[:, h * 128:(h + 1) * 128]), _kmaj(ml_k[:, h * 128:(h + 1) * 128]),
                                           _kmaj(ml_v[:, h * 256:(h + 1) * 256]), _kmaj(ml_o[:, h * 256:(h + 1) * 256])], axis=2)
                           for h in range(4)]).reshape(512, 6144)
    sh["w_if"] = _kmaj(ml_if).reshape(128, 64)
    sh["w_mg"] = np.stack([np.concatenate([_kmaj(g_attn[:, c * 128:(c + 1) * 128]), _kmaj(g_ml[:, c * 128:(c + 1) * 128]),
                                           _kmaj(wba[:, c * 128:(c + 1) * 128]), _kmaj(wbm[:, c * 128:(c + 1) * 128])], axis=2)
                           for c in range(8)]).reshape(1024, 4096)
    sh["w_out"] = _kmaj(np.asarray(inp["w_out"][0], f)).reshape(128, 8192)
    wq = np.asarray(inp["peer_wq"][0], f)
    sh["w_pq"] = np.stack([_kmaj(wq[:, h * 128:(h + 1) * 128]) for h in range(8)]).reshape(1024, 1024)
    if do_peer:
        u = np.asarray(inp["peer_u"][0], f)
        sh["UT"] = np.stack([_kmaj(np.ascontiguousarray(u[ec * 512:(ec + 1) * 512, :].T)) for ec in range(32)]).reshape(4096, 4096)
        sh["V"] = np.ascontiguousarray(np.asarray(inp["peer_v"][0], f))
    else:
        sh["UT"] = np.zeros((4096, 4096), f)
        sh["V"] = np.zeros((NEXP, D), f)
    sh["b_ifT"] = np.ascontiguousarray(np.asarray(inp["b_if"][0], f).T)
    sh["conv_wT"] = np.ascontiguousarray(np.asarray(inp["conv_w"][0], f).reshape(4, 8, 128).transpose(2, 1, 0)).reshape(128, 32)
    sh["conv_bT"] = np.ascontiguousarray(np.asarray(inp["conv_b"][0], f).reshape(8, 128).T)
    sh["da_lambda"] = np.asarray(inp["da_lambda"][0], f).reshape(1, 256)
    sh["subln_g"] = np.asarray(inp["da_subln_g"][0], f).reshape(1, 128)
    sh["ml_norm_g"] = np.asarray(inp["ml_norm_g"][0], f).reshape(1, D)
    for nm in ("ln1_g", "ln1_b", "ln2_g", "ln2_b"):
        sh[nm] = np.asarray(inp[nm][0], f).reshape(1, D)
    kt = np.ascontiguousarray(np.asarray(inp["peer_keys"][0], f).transpose(0, 2, 1))
    kz = np.zeros((128, 256), f)
    kz[0:64, 0:128] = kt[0]
    kz[64:128, 128:256] = kt[1]
    sh["keysT"] = kz
    sh["ident"] = np.eye(128, dtype=f)
    kk = np.arange(128)
    sh["cmask"] = (kk[None, :] >= kk[:, None]).astype(f)
    sel = np.zeros((4, 4, 128), f)
    for h in range(4):
        sel[h, h, :] = 1.0
    sh["sel4"] = sel.reshape(4, 512)
    return sh


def core_inputs(inp, sh, b0, nseq):
    f = np.float32
    m = dict(sh)
    m["x"] = np.ascontiguousarray(np.asarray(inp["x"][b0:b0 + nseq], f).reshape(nseq * SEQ, D))
    c = np.asarray(inp["c"][b0:b0 + nseq], f)
    m["cT"] = _kmaj(np.ascontiguousarray(c.T)).reshape(128, 8 * nseq)
    return m


_NC_CACHE = {}


def kernel(**inputs):
    if "full" not in _NC_CACHE:
        _NC_CACHE["full"] = build(NSEQ_FULL, True)
    nc = _NC_CACHE["full"]
    sh = prep_shared(inputs, True)
    in_maps = [core_inputs(inputs, sh, i * NSEQ_FULL, NSEQ_FULL) for i in range(NCORES)]
    res = run_bass_kernel_spmd(nc, in_maps, core_ids=list(range(NCORES)))
    out = np.concatenate([np.asarray(r["out"]).reshape(NSEQ_FULL, SEQ, D) for r in res.results], axis=0)
    return out.astype(np.float32)
```

```python
import contextlib
import math
import numpy as np
import concourse.bass as bass
import concourse.mybir as mybir
from concourse.bass_utils import run_bass_kernel_spmd

F32 = mybir.dt.float32
BF16 = mybir.dt.bfloat16
AF = mybir.ActivationFunctionType
ALU = mybir.AluOpType

D = 1024
SEQ = 2048
NT = SEQ // 128
NCORES = 8
BATCH = 32
NSEQ_FULL = BATCH // NCORES
ALPHA = 2.0 ** 0.25
LN_EPS = 1e-5
LAMBDA_INIT = 0.8 - 0.6 * math.exp(0.0)
NEXP = 16384


class Buf:
    __slots__ = ("name", "last_w", "readers")

    def __init__(self, name):
        self.name = name
        self.last_w = None
        self.readers = []


class _Rec:
    def __init__(self):
        self.calls = []

    def __getattr__(self, name):
        def f(*a, **kw):
            self.calls.append((name, a, kw))
            return None
        return f


class Sched:
    ENGS = ("pe", "act", "dve", "pool", "sp")

    def __init__(self, nc):
        self.nc = nc
        self.ops = {e: [] for e in self.ENGS}
        self.cnt = {e: 0 for e in self.ENGS}
        self.seen = {e: {} for e in self.ENGS}
        self.sems = {}
        self.dma_cnt = {}
        self.nops = 0
        self.nwait = 0

    def op(self, eng, emit, reads=(), writes=(), dma=None, n=1):
        deps = {}
        for b in reads:
            if b.last_w is not None:
                k, v, e = b.last_w
                if not (e == eng and k[0] == "eng" and False):
                    deps[k] = max(deps.get(k, 0), v)
        for b in writes:
            if b.last_w is not None:
                k, v, e = b.last_w
                if not (e == eng and k[0] == "eng" and dma is None):
                    deps[k] = max(deps.get(k, 0), v)
            for (k, v, e) in b.readers:
                if e == eng and k[0] == "eng" and dma is None:
                    continue
                deps[k] = max(deps.get(k, 0), v)
        waits = []
        seen = self.seen[eng]
        for k, v in deps.items():
            if seen.get(k, 0) >= v:
                continue
            seen[k] = v
            waits.append((k, v))
        if dma is None:
            self.cnt[eng] += 1
            key = ("eng", eng)
            tok = (key, self.cnt[eng], eng)
        else:
            key = ("dma", dma)
            self.dma_cnt[key] = self.dma_cnt.get(key, 0) + 16 * n
            tok = (key, self.dma_cnt[key], eng)
        for b in reads:
            b.readers.append(tok)
        for b in writes:
            b.last_w = tok
            b.readers = []
        rec = _Rec()
        emit(rec)
        assert len(rec.calls) >= 1
        if dma is not None:
            assert len(rec.calls) == n, (len(rec.calls), n)
        self.ops[eng].append((waits, rec.calls, key, dma is not None, n))
        self.nops += 1
        self.nwait += len(waits)
        return tok

    def barrier(self):
        cur = {("eng", e): self.cnt[e] for e in self.ENGS if self.cnt[e] > 0}
        cur.update(self.dma_cnt)
        for eng in self.ENGS:
            waits = []
            seen = self.seen[eng]
            for k, v in cur.items():
                if k == ("eng", eng) or seen.get(k, 0) >= v:
                    continue
                seen[k] = v
                waits.append((k, v))
            if waits:
                self.cnt[eng] += 1
                self.ops[eng].append((waits, [("nop", (), {})], ("eng", eng), False, 1))

    def emit_all(self):
        nc = self.nc
        keys = [("eng", e) for e in self.ENGS]
        for e in self.ENGS:
            for rec in self.ops[e]:
                if rec[2] not in keys:
                    keys.append(rec[2])
        for k in keys:
            self.sems[k] = nc.alloc_semaphore("s_" + "_".join(str(x) for x in k))
        with nc.Block() as block:
            def mk(ename):
                def body(eng):
                    for waits, calls, key, is_dma, n in self.ops[ename]:
                        for (k, v) in waits:
                            eng.wait_ge(self.sems[k], v)
                        s = self.sems[key]
                        r = None
                        for (name, a, kw) in calls:
                            r = getattr(eng, name)(*a, **kw)
                            if is_dma:
                                r.then_inc(s, 16)
                        if not is_dma:
                            r.then_inc(s, 1)
                return body
            block.tensor(mk("pe"))
            block.scalar(mk("act"))
            block.vector(mk("dve"))
            block.gpsimd(mk("pool"))
            block.sync(mk("sp"))


class Arena:
    def __init__(self, t, nbytes):
        self.t = t
        self.nbytes = nbytes
        self.off = 0
        self.stack = []
        self.peak = 0

    def push(self):
        self.stack.append(self.off)

    def pop(self):
        self.off = self.stack.pop()

    def alloc(self, shape, dt):
        esz = 4 if dt == F32 else 2
        n = 1
        for s in shape[1:]:
            n *= s
        nb = (n * esz + 63) // 64 * 64
        assert self.off + nb <= self.nbytes, ("arena overflow", self.off, nb, self.nbytes)
        a = self.t[0:shape[0], self.off // 4:(self.off + nb) // 4]
        self.off += nb
        self.peak = max(self.peak, self.off)
        if dt != F32:
            a = a.bitcast(dt)
        a = a[:, 0:n]
        if len(shape) == 3:
            a = a.rearrange("p (a b) -> p a b", a=shape[1])
        elif len(shape) == 4:
            a = a.rearrange("p (a b c) -> p a b c", a=shape[1], b=shape[2])
        return a


def build(NSEQ=NSEQ_FULL, do_peer=True, dbg=False):
    nc = bass.Bass("TRN2", target_bir_lowering=False)
    SC = Sched(nc)
    op = SC.op
    NTOK = NSEQ * SEQ

    def din(name, shape, dt=F32):
        return nc.dram_tensor(name, list(shape), dt, kind="ExternalInput").ap()

    def dscr(name, shape, dt):
        return nc.dram_tensor(name, list(shape), dt, kind="Internal").ap()

    x_d = din("x", [NTOK, D])
    cT_d = din("cT", [128, 8 * NSEQ])
    wada_d = din("w_ada", [128, 8 * 6144])
    bada_d = din("b_ada", [1, 6144])
    wda_d = din("w_da", [8 * 128, 3072])
    wml_d = din("w_ml", [4 * 128, 6144])
    wif_d = din("w_if", [128, 64])
    wmg_d = din("w_mg", [8 * 128, 4096])
    wout_d = din("w_out", [128, 8192])
    wpq_d = din("w_pq", [8 * 128, 1024])
    ut_d = din("UT", [32 * 128, 4096])
    v_d = din("V", [NEXP, D])
    bif_d = din("b_ifT", [4, 2])
    cw_d = din("conv_wT", [128, 32])
    cb_d = din("conv_bT", [128, 8])
    lam_d = din("da_lambda", [1, 256])
    subg_d = din("subln_g", [1, 128])
    mlg_d = din("ml_norm_g", [1, D])
    ln1g_d = din("ln1_g", [1, D])
    ln1b_d = din("ln1_b", [1, D])
    ln2g_d = din("ln2_g", [1, D])
    ln2b_d = din("ln2_b", [1, D])
    keysT_d = din("keysT", [128, 256])
    ident_d = din("ident", [128, 128])
    cmask_d = din("cmask", [128, 128])
    sel4_d = din("sel4", [4, 512])
    out_d = nc.dram_tensor("out", [NTOK, D], F32, kind="ExternalOutput").ap()

    wda_b = dscr("w_da_b", [8 * 128, 3072], BF16)
    wml_b = dscr("w_ml_b", [4 * 128, 6144], BF16)
    wif_b = dscr("w_if_b", [128, 64], BF16)
    wmg_b = dscr("w_mg_b", [8 * 128, 4096], BF16)
    wout_b = dscr("w_out_b", [128, 8192], BF16)
    ut_b = dscr("UT_b", [32 * 128, 4096], BF16)
    v_b = dscr("V_b", [NEXP, D], BF16)
    x1_d = dscr("x1_s", [NTOK, D], F32)
    gt_d = dscr("gt_s", [NSEQ, 2048], F32)

    es = contextlib.ExitStack()
    with es:
        ARENA_BYTES = 206 * 1024
        arena_t = es.enter_context(nc.sbuf_tensor("arena", [128, ARENA_BYTES // 4], F32))
        A = Arena(arena_t, ARENA_BYTES)
        psum_t = es.enter_context(nc.psum_tensor("psum", [128, 8, 512], F32))
        PB = [psum_t[:, i, :] for i in range(8)]
        bPB = [Buf(f"pb{i}") for i in range(8)]
        bank_rr = [0]

        def nextbank(banks):
            i = banks[bank_rr[0] % len(banks)]
            bank_rr[0] += 1
            return i

        def dma(eng, out, in_, reads, writes, key):
            return op(eng, lambda e: e.dma_start(out=out, in_=in_), reads=reads, writes=writes, dma=key)

        dbg_outs = {}

        def dump(name, ap, bufs, shape, dt):
            if not dbg:
                return
            t = nc.dram_tensor("d_" + name, list(shape), dt, kind="ExternalOutput").ap()
            dbg_outs[name] = t
            dma("sp", t, ap, list(bufs), [Buf("dbg")], "dbg_" + name)

        bOUT = Buf("out")
        bDR = Buf("dram_scratch")
        bGT = Buf("gt_scratch")
        bXO = [Buf("xo0"), Buf("xo1")]

        identf = A.alloc([128, 128], F32)
        identb = A.alloc([128, 128], BF16)
        maskb = A.alloc([128, 128], BF16)
        sel4 = A.alloc([4, 512], F32)
        cm05 = A.alloc([128, 8], F32)
        ones4 = A.alloc([4, 512], F32)
        modT = A.alloc([128, 48 * NSEQ], F32)
        neglam = A.alloc([128, 1], F32)
        gda_bc = A.alloc([128, 128], F32)
        bifT = A.alloc([4, 2], F32)
        nbf = A.alloc([4, 1], F32)
        cwT = A.alloc([128, 32], F32)
        cbT = A.alloc([128, 8], F32)
        wif = A.alloc([128, 8, 8], BF16)
        junk = A.alloc([128, 256], F32)
        bC = Buf("consts")
        bjunk = Buf("junk")
        NST = 4
        stt = [A.alloc([128, 12], F32) for _ in range(NST)]
        mvt = [A.alloc([128, 2], F32) for _ in range(NST)]
        rst = [A.alloc([128, 1], F32) for _ in range(NST)]
        bst = [Buf(f"st{i}") for i in range(NST)]
        bmv = [Buf(f"mv{i}") for i in range(NST)]
        brs = [Buf(f"rs{i}") for i in range(NST)]
        stk = [0]

        def ln_stats(src, bsrc):
            k = stk[0] % NST
            stk[0] += 1
            st, mv, rs = stt[k], mvt[k], rst[k]
            op("dve", lambda e: e.bn_stats(out=st[:, 0:6], in_=src[:, 0:512]), reads=[bsrc], writes=[bst[k]])
            op("dve", lambda e: e.bn_stats(out=st[:, 6:12], in_=src[:, 512:1024]), reads=[bsrc], writes=[bst[k]])
            op("dve", lambda e: e.bn_aggr(out=mv[:], in_=st[:]), reads=[bst[k]], writes=[bmv[k]])
            op("dve", lambda e: e.tensor_scalar(out=rs[:], in0=mv[:, 1:2], scalar1=LN_EPS, scalar2=None, op0=ALU.add),
               reads=[bmv[k]], writes=[brs[k]])
            op("pool", lambda e: e.tensor_tensor(out=rs[:], in0=rs[:], in1=cm05[:, 0:1], op=ALU.pow),
               reads=[brs[k], bC], writes=[brs[k]])
            return mv, rs, bmv[k], brs[k]

        A.push()
        tmpf = A.alloc([128, 128], F32)
        tmpm = A.alloc([128, 128], F32)
        lamt = A.alloc([128, 256], F32)
        lamp = A.alloc([128, 128], F32)
        lams = A.alloc([128, 2], F32)
        wif_f = A.alloc([128, 64], F32)
        btmp = Buf("tmp0")
        dma("sp", identf[:], ident_d[:, :], [], [bC], "c0")
        dma("sp", tmpm[:], cmask_d[:, :], [], [btmp], "c1")
        dma("sp", sel4[:], sel4_d[:, :], [], [bC], "c0")
        dma("sp", bifT[:], bif_d[:, :], [], [bC], "c0")
        dma("sp", cwT[:], cw_d[:, :], [], [bC], "c0")
        dma("sp", cbT[:], cb_d[:, :], [], [bC], "c0")
        dma("sp", gda_bc[:], subg_d[0:1, :].partition_broadcast(128), [], [bC], "c0")
        dma("sp", lamt[:], lam_d[0:1, :].partition_broadcast(128), [], [btmp], "c1")
        dma("sp", wif_f[:], wif_d[:, :], [], [btmp], "c1")
        op("pool", lambda e: e.tensor_copy(out=identb[:], in_=identf[:]), reads=[bC], writes=[bC])
        op("pool", lambda e: e.tensor_copy(out=maskb[:], in_=tmpm[:]), reads=[btmp], writes=[bC])
        op("pool", lambda e: e.memset(cm05[:], -0.5), writes=[bC])
        op("pool", lambda e: e.memset(ones4[:], 1.0), writes=[bC])
        op("pool", lambda e: e.tensor_copy(out=wif[:].rearrange("p a b -> p (a b)"), in_=wif_f[:]), reads=[btmp], writes=[bC])
        op("dve", lambda e: e.tensor_scalar(out=gda_bc[:], in0=gda_bc[:], scalar1=1.0 - LAMBDA_INIT, scalar2=None, op0=ALU.mult),
           reads=[bC], writes=[bC])
        op("dve", lambda e: e.tensor_scalar(out=nbf[:], in0=bifT[:, 1:2], scalar1=-1.0, scalar2=None, op0=ALU.mult),
           reads=[bC], writes=[bC])
        lt3 = lamt[:].rearrange("p (a b) -> p a b", a=4)
        op("dve", lambda e: e.tensor_tensor(out=lamp[:].rearrange("p (a b) -> p a b", a=2),
                                            in0=lt3[:, 0:4:2, :], in1=lt3[:, 1:4:2, :], op=ALU.mult),
           reads=[btmp], writes=[btmp])
        op("dve", lambda e: e.tensor_reduce(out=lams[:], in_=lamp[:].rearrange("p (a b) -> p a b", a=2),
                                            axis=mybir.AxisListType.X, op=ALU.add),
           reads=[btmp], writes=[btmp])
        op("act", lambda e: e.activation(out=lams[:], in_=lams[:], func=AF.Exp), reads=[btmp], writes=[btmp])
        op("dve", lambda e: e.tensor_tensor(out=neglam[:], in0=lams[:, 1:2], in1=lams[:, 0:1], op=ALU.subtract),
           reads=[btmp], writes=[bC])
        op("dve", lambda e: e.tensor_scalar(out=neglam[:], in0=neglam[:], scalar1=-LAMBDA_INIT, scalar2=None, op0=ALU.add),
           reads=[bC], writes=[bC])
        SC.barrier()
        A.pop()

        A.push()
        NCV = 3
        CW = 4096
        cvf = [A.alloc([128, CW], F32) for _ in range(NCV)]
        cvb = [A.alloc([128, CW], BF16) for _ in range(NCV)]
        bcvf = [Buf(f"cvf{i}") for i in range(NCV)]
        bcvb = [Buf(f"cvb{i}") for i in range(NCV)]
        kcv = [0]

        def convert(src, dst, R, C):
            for r0 in range(0, R, 128):
                for c0 in range(0, C, CW):
                    cw = min(CW, C - c0)
                    k = kcv[0] % NCV
                    kk = kcv[0]
                    kcv[0] += 1
                    dma("sp", cvf[k][:, 0:cw], src[r0:r0 + 128, c0:c0 + cw], [], [bcvf[k]], f"cvf{k}")
                    if kk % 2 == 0:
                        op("dve", lambda e, k=k, cw=cw: e.tensor_copy(out=cvb[k][:, 0:cw], in_=cvf[k][:, 0:cw]),
                           reads=[bcvf[k]], writes=[bcvb[k]])
                    else:
                        op("act", lambda e, k=k, cw=cw: e.activation(out=cvb[k][:, 0:cw], in_=cvf[k][:, 0:cw], func=AF.Copy),
                           reads=[bcvf[k]], writes=[bcvb[k]])
                    dma("pool", dst[r0:r0 + 128, c0:c0 + cw], cvb[k][:, 0:cw], [bcvb[k]], [bDR], f"cvb{k}")

        convert(wda_d, wda_b, 1024, 3072)
        convert(wml_d, wml_b, 512, 6144)
        convert(wmg_d, wmg_b, 1024, 4096)
        convert(wout_d, wout_b, 128, 8192)
        import os as _os0
        if do_peer and not int(_os0.environ.get("NOCONV", 0)):
            convert(ut_d, ut_b, 4096, 4096)
            convert(v_d, v_b, NEXP, D)
        SC.barrier()
        A.pop()

        A.push()
        siluT = A.alloc([128, 8, NSEQ], F32)
        modall = A.alloc([NSEQ, 6144], F32)
        badab = A.alloc([NSEQ, 6144], F32)
        wad = [A.alloc([128, 8, 512], F32) for _ in range(2)]
        bwad = [Buf("wad0"), Buf("wad1")]
        bsil = Buf("silu")
        bmod = Buf("modall")
        bmodT = Buf("modT")
        dma("sp", siluT[:].rearrange("p a b -> p (a b)"), cT_d[:, :], [], [bsil], "c1")
        dma("sp", badab[:], bada_d[0:1, :].partition_broadcast(NSEQ), [], [bmod], "c0")
        op("act", lambda e: e.activation(out=siluT[:].rearrange("p a b -> p (a b)"),
                                         in_=siluT[:].rearrange("p a b -> p (a b)"), func=AF.Silu),
           reads=[bsil], writes=[bsil])
        wada3 = wada_d.rearrange("p (a b) -> p a b", a=8)
        for pc in range(12):
            k = pc % 2
            dma("sp", wad[k][:], wada3[:, :, pc * 512:(pc + 1) * 512], [], [bwad[k]], f"wad{k}")
            bk = nextbank([0, 1])

            def mmg(e, k=k, bk=bk):
                r = None
                for kc in range(8):
                    r = e.matmul(PB[bk][0:NSEQ, :], lhsT=siluT[:, kc, :], rhs=wad[k][:, kc, :],
                                 start=(kc == 0), stop=(kc == 7))
                return r
            op("pe", mmg, reads=[bsil, bwad[k]], writes=[bPB[bk]])
            op("dve", lambda e, bk=bk, pc=pc: e.tensor_tensor(out=modall[:, pc * 512:(pc + 1) * 512], in0=PB[bk][0:NSEQ, :],
                                                              in1=badab[:, pc * 512:(pc + 1) * 512], op=ALU.add),
               reads=[bPB[bk], bmod], writes=[bmod])
        dma("sp", gt_d[:, 0:1024], modall[:, 2048:3072], [bmod], [bGT], "gtd")
        dma("sp", gt_d[:, 1024:2048], modall[:, 5120:6144], [bmod], [bGT], "gtd")
        bk = nextbank([0, 1])

        def trg(e, bk=bk):
            r = None
            for c in range(48):
                r = e.transpose(out=PB[bk][:, c * NSEQ:(c + 1) * NSEQ], in_=modall[0:NSEQ, c * 128:(c + 1) * 128],
                                identity=identf[0:NSEQ, 0:NSEQ])
            return r
        op("pe", trg, reads=[bmod, bC], writes=[bPB[bk]])
        op("dve", lambda e, bk=bk: e.tensor_copy(out=modT[:], in_=PB[bk][:, 0:48 * NSEQ]), reads=[bPB[bk]], writes=[bmodT])
        for c0 in (8, 32):
            op("dve", lambda e, c0=c0: e.tensor_scalar(out=modT[:, c0 * NSEQ:(c0 + 8) * NSEQ], in0=modT[:, c0 * NSEQ:(c0 + 8) * NSEQ],
                                                       scalar1=1.0, scalar2=None, op0=ALU.add),
               reads=[bmodT], writes=[bmodT])
        SC.barrier()
        A.pop()

        dump("modT", modT[:], [bmodT], [128, 48 * NSEQ], F32)

        def modcol(which, c, b):
            j = (which * 8 + c) * NSEQ + b
            return modT[:, j:j + 1]

        A.push()
        hT_raw = A.alloc([128, 8192], F32)
        hT = hT_raw.bitcast(BF16).rearrange("p (a b) -> p a b", a=8)
        yaT = A.alloc([128, 8, SEQ], BF16)
        ymT = A.alloc([128, 8, SEQ], BF16)
        bxts = [Buf("xt0"), Buf("xt1")]
        bhT = [Buf(f"hT{i}") for i in range(NT)]
        byaT = [Buf(f"yaT{i}") for i in range(8)]
        bymT = [Buf(f"ymT{i}") for i in range(NT)]

        def hbufs(t0, t1):
            return bhT[t0:t1]

        for b in range(NSEQ):
            A.push()
            xts = [A.alloc([128, D], F32) for _ in range(2)]
            xns = [A.alloc([128, D], BF16) for _ in range(2)]
            bxns = [Buf("xn0"), Buf("xn1")]
            for i in range(NT):
                k = i % 2
                r0 = b * SEQ + i * 128
                dma("sp", xts[k][:], x_d[r0:r0 + 128, :], [], [bxts[k]], f"xt{k}")
                mv, rs, bm, br = ln_stats(xts[k], bxts[k])
                op("dve", lambda e, k=k, mv=mv, rs=rs: e.tensor_scalar(out=xns[k][:], in0=xts[k][:], scalar1=mv[:, 0:1], scalar2=rs[:],
                                                                      op0=ALU.subtract, op1=ALU.mult),
                   reads=[bxts[k], bm, br], writes=[bxns[k]])
                if b == 0 and i == 0:
                    dump("mv0", mv[:], [bm], [128, 2], F32)
                    dump("rs0", rs[:], [br], [128, 1], F32)
                    dump("xn0", xns[k][:], [bxns[k]], [128, D], BF16)
                    dump("xt0", xts[k][:], [bxts[k]], [128, D], F32)
                bk = nextbank([0, 1])
                ptb = PB[bk].bitcast(BF16)

                def trx(e, k=k, ptb=ptb):
                    r = None
                    for c in range(8):
                        r = e.transpose(out=ptb[:, c * 128:(c + 1) * 128], in_=xns[k][:, c * 128:(c + 1) * 128], identity=identb[:])
                    return r
                op("pe", trx, reads=[bxns[k], bC], writes=[bPB[bk]])
                for c in range(8):
                    if c % 2 == 0:
                        op("act", lambda e, c=c, ptb=ptb, i=i: e.activation(out=hT[:, c, i * 128:(i + 1) * 128], in_=ptb[:, c * 128:(c + 1) * 128],
                                                                           func=AF.Identity, scale=modcol(1, c, b), bias=modcol(0, c, b)),
                           reads=[bPB[bk], bmodT], writes=[bhT[i]])
                    else:
                        op("dve", lambda e, c=c, ptb=ptb, i=i: e.tensor_scalar(out=hT[:, c, i * 128:(i + 1) * 128], in0=ptb[:, c * 128:(c + 1) * 128],
                                                                              scalar1=modcol(1, c, b), scalar2=modcol(0, c, b),
                                                                              op0=ALU.mult, op1=ALU.add),
                           reads=[bPB[bk], bmodT], writes=[bhT[i]])
            SC.barrier()
            A.pop()
            if b == 0:
                dump("hT", hT.rearrange("p a b -> p (a b)"), bhT, [128, 8 * SEQ], BF16)

            A.push()
            wda = [A.alloc([128, 8, 384], BF16) for _ in range(2)]
            bwda = [Buf("wda0"), Buf("wda1")]
            qTs = [A.alloc([128, SEQ], BF16) for _ in range(2)]
            kTs = [A.alloc([128, SEQ], BF16) for _ in range(2)]
            vss = [A.alloc([128, NT, 129], BF16) for _ in range(2)]
            bq = [[Buf(f"q{s}{c}") for c in range(4)] for s in range(2)]
            bkk = [[Buf(f"k{s}{c}") for c in range(4)] for s in range(2)]
            bv = [[Buf(f"v{s}{c}") for c in range(4)] for s in range(2)]
            Es = [A.alloc([128, NT, 256], BF16) for _ in range(2)]
            bE = [[Buf(f"E{s}{j}") for j in range(NT)] for s in range(2)]
            osb = [A.alloc([128, 2, 128], F32) for _ in range(2)]
            yat = [A.alloc([128, 2, 128], BF16) for _ in range(2)]
            rzs = [A.alloc([128, 4], F32) for _ in range(2)]
            sss = [A.alloc([128, 2], F32) for _ in range(2)]
            bo = [Buf("o0"), Buf("o1")]
            byat = [Buf("yat0"), Buf("yat1")]
            brz = [Buf("rz0"), Buf("rz1")]
            bss = [Buf("ss0"), Buf("ss1")]
            for s in range(2):
                op("pool", lambda e, s=s: e.memset(vss[s][:, :, 128:129], 1.0), writes=[bv[s][c] for c in range(4)])
            ecnt = 0
            ccnt = 0
            for h in range(8):
                s = h % 2
                dma("sp", wda[s][:].rearrange("p a b -> p (a b)"), wda_b[h * 128:(h + 1) * 128, :], [bDR], [bwda[s]], f"wda{s}")
                for c in range(4):
                    for which in range(2):
                        bk = nextbank([0, 1])

                        def mmg(e, s=s, c=c, which=which, bk=bk):
                            r = None
                            for kc in range(8):
                                r = e.matmul(PB[bk][:, :], lhsT=wda[s][:, kc, which * 128:(which + 1) * 128],
                                             rhs=hT[:, kc, c * 512:(c + 1) * 512], start=(kc == 0), stop=(kc == 7))
                            return r
                        op("pe", mmg, reads=[bwda[s]] + hbufs(4 * c, 4 * c + 4), writes=[bPB[bk]])
                        if which == 0:
                            op("act", lambda e, s=s, c=c, bk=bk: e.activation(out=qTs[s][:, c * 512:(c + 1) * 512], in_=PB[bk][:, :],
                                                                             func=AF.Copy, scale=0.125),
                               reads=[bPB[bk]], writes=[bq[s][c]])
                        else:
                            op("dve", lambda e, s=s, c=c, bk=bk: e.tensor_copy(out=kTs[s][:, c * 512:(c + 1) * 512], in_=PB[bk][:, :]),
                               reads=[bPB[bk]], writes=[bkk[s][c]])
                    bk = nextbank([0, 1])

                    def mmv(e, s=s, c=c, bk=bk):
                        r = None
                        for t in range(4):
                            i = 4 * c + t
                            for kc in range(8):
                                r = e.matmul(PB[bk][:, t * 128:(t + 1) * 128], lhsT=hT[:, kc, i * 128:(i + 1) * 128],
                                             rhs=wda[s][:, kc, 256:384], start=(kc == 0), stop=(kc == 7))
                        return r
                    op("pe", mmv, reads=[bwda[s]] + hbufs(4 * c, 4 * c + 4), writes=[bPB[bk]])
                    op("act", lambda e, s=s, c=c, bk=bk: e.activation(out=vss[s][:, 4 * c:4 * c + 4, 0:128],
                                                                     in_=PB[bk][:, :].rearrange("p (a b) -> p a b", a=4), func=AF.Copy),
                       reads=[bPB[bk]], writes=[bv[s][c]])
                def att_scores(c, m, esl):
                    E = Es[esl]
                    for j in range(2 * c + 2):
                        q0 = 128 if j == 2 * c + 1 else 0
                        sb_ = nextbank([2, 3])
                        op("pe", lambda e: e.matmul(
                            PB[sb_][:, q0:256], lhsT=kTs[s][m * 64:(m + 1) * 64, j * 128:(j + 1) * 128],
                            rhs=qTs[s][m * 64:(m + 1) * 64, c * 256 + q0:(c + 1) * 256], start=True, stop=True),
                           reads=[bkk[s][j // 4], bq[s][c // 2]], writes=[bPB[sb_]])
                        op("act", lambda e: e.activation(out=E[:, j, q0:256], in_=PB[sb_][:, q0:256], func=AF.Exp),
                           reads=[bPB[sb_]], writes=[bE[esl][j]])
                        if j >= 2 * c:
                            d0 = (j - 2 * c) * 128
                            op("pool", lambda e: e.tensor_tensor(out=E[:, j, d0:d0 + 128], in0=E[:, j, d0:d0 + 128],
                                                                 in1=maskb[:], op=ALU.mult),
                               reads=[bE[esl][j], bC], writes=[bE[esl][j]])

                def att_pv(c, m, esl, cs):
                    E = Es[esl]
                    pob = [4 + cs * 2, 5 + cs * 2]
                    pv = PB[pob[m]][:, 0:258].rearrange("p (a b) -> p a b", a=2)

                    def pvg(e):
                        for ii in range(2):
                            i = 2 * c + ii
                            for j in range(i + 1):
                                e.matmul(pv[:, ii, :], lhsT=E[:, j, ii * 128:(ii + 1) * 128], rhs=vss[s][:, j, :],
                                         start=(j == 0), stop=(j == i))
                    op("pe", pvg, reads=[bE[esl][j] for j in range(2 * c + 2)] + [bv[s][j] for j in range(c // 2 + 1)],
                       writes=[bPB[pob[m]]])

                def att_combine(c, cs):
                    pob = [4 + cs * 2, 5 + cs * 2]
                    k = cs
                    p0 = PB[pob[0]][:, 0:258].rearrange("p (a b) -> p a b", a=2)
                    p1 = PB[pob[1]][:, 0:258].rearrange("p (a b) -> p a b", a=2)

                    def st1():
                        op("dve", lambda e: e.reciprocal(out=rzs[k][:, 0:2], in_=p0[:, :, 128]), reads=[bPB[pob[0]]], writes=[brz[k]])
                        op("dve", lambda e: e.reciprocal(out=rzs[k][:, 2:4], in_=p1[:, :, 128]), reads=[bPB[pob[1]]], writes=[brz[k]])
                        op("dve", lambda e: e.tensor_scalar(out=rzs[k][:, 2:4], in0=rzs[k][:, 2:4], scalar1=neglam[:, 0:1], scalar2=None, op0=ALU.mult),
                           reads=[brz[k], bC], writes=[brz[k]])
                        for ii in range(2):
                            op("dve", lambda e, ii=ii: e.tensor_scalar(out=osb[k][:, ii, :], in0=p0[:, ii, 0:128], scalar1=rzs[k][:, ii:ii + 1],
                                                                      scalar2=None, op0=ALU.mult),
                               reads=[bPB[pob[0]], brz[k]], writes=[bo[k]])
                            op("dve", lambda e, ii=ii: e.scalar_tensor_tensor(out=osb[k][:, ii, :], in0=p1[:, ii, 0:128],
                                                                             scalar=rzs[k][:, 2 + ii:3 + ii], in1=osb[k][:, ii, :],
                                                                             op0=ALU.mult, op1=ALU.add),
                               reads=[bPB[pob[1]], brz[k], bo[k]], writes=[bo[k]])

                    def st2():
                        for ii in range(2):
                            op("act", lambda e, ii=ii: e.activation(out=junk[:, 0:128], in_=osb[k][:, ii, :], func=AF.Square,
                                                                    accum_out=sss[k][:, ii:ii + 1]),
                               reads=[bo[k]], writes=[bss[k], bjunk])

                    def st3():
                        op("dve", lambda e: e.tensor_scalar(out=sss[k][:], in0=sss[k][:], scalar1=1.0 / 128.0, scalar2=LN_EPS,
                                                            op0=ALU.mult, op1=ALU.add),
                           reads=[bss[k]], writes=[bss[k]])
                        op("pool", lambda e: e.tensor_tensor(out=sss[k][:], in0=sss[k][:], in1=cm05[:, 0:2], op=ALU.pow),
                           reads=[bss[k], bC], writes=[bss[k]])

                    def st4():
                        for ii in range(2):
                            op("dve", lambda e, ii=ii: e.scalar_tensor_tensor(out=yat[k][:, ii, :], in0=osb[k][:, ii, :], scalar=sss[k][:, ii:ii + 1],
                                                                             in1=gda_bc[:], op0=ALU.mult, op1=ALU.mult),
                               reads=[bo[k], bss[k], bC], writes=[byat[k]])

                    def st5():
                        bk = nextbank([0, 1])
                        ptb = PB[bk].bitcast(BF16)

                        def trya(e):
                            for ii in range(2):
                                e.transpose(out=ptb[:, ii * 128:(ii + 1) * 128], in_=yat[k][:, ii, :], identity=identb[:])
                        op("pe", trya, reads=[byat[k], bC], writes=[bPB[bk]])
                        op("act", lambda e: e.activation(out=yaT[:, h, c * 256:(c + 1) * 256], in_=ptb[:, 0:256], func=AF.Copy),
                           reads=[bPB[bk]], writes=[byaT[c]])
                    cq.extend([st1, st2, st3, st4, st5])

                cq = []

                def cq_pop(n):
                    for _ in range(n):
                        if cq:
                            cq.pop(0)()

                prev = None
                for c in range(8):
                    cs = ccnt % 2
                    ccnt += 1
                    for m in range(2):
                        esl = ecnt % 2
                        ecnt += 1
                        att_scores(c, m, esl)
                        cq_pop(3)
                        if prev is not None:
                            att_pv(*prev)
                            if prev[1] == 1:
                                att_combine(prev[0], prev[3])
                        prev = (c, m, esl, cs)
                att_pv(*prev)
                att_combine(prev[0], prev[3])
                cq_pop(100)
            SC.barrier()
            A.pop()
            if b == 0:
                dump("yaT", yaT.rearrange("p a b -> p (a b)"), byaT, [128, 8 * SEQ], BF16)

            A.push()
            LNS = math.log(128.0 ** -0.5)
            gch = [A.alloc([4, 512], F32) for _ in range(2)]
            negM = A.alloc([4, SEQ], F32)
            ech = [A.alloc([4, 512], F32) for _ in range(2)]
            nfc = [A.alloc([4, 512], F32) for _ in range(2)]
            mch = [A.alloc([4, 512], F32) for _ in range(2)]
            tfc = A.alloc([4, 512], F32)
            gtok = A.alloc([128, NT * 4], F32)
            etok = A.alloc([128, NT * 4], F32)
            bgc = [Buf("gch0"), Buf("gch1")]
            bnegM = Buf("negM")
            bec = [Buf("ech0"), Buf("ech1")]
            bnf = [Buf("nf0"), Buf("nf1")]
            bmc = [Buf("mc0"), Buf("mc1")]
            btf = Buf("tf")
            bgtok = Buf("gtok")
            betok = Buf("etok")
            for c in range(4):
                k = c % 2
                sl = slice(c * 512, (c + 1) * 512)
                bki = nextbank([0, 1])
                bkf = nextbank([0, 1])
                for (bk_, c0) in ((bki, 0), (bkf, 4)):
                    def mmif(e, bk_=bk_, c0=c0, c=c):
                        r = None
                        for kc in range(8):
                            r = e.matmul(PB[bk_][0:4, :], lhsT=wif[:, kc, c0:c0 + 4], rhs=hT[:, kc, c * 512:(c + 1) * 512],
                                         start=(kc == 0), stop=(kc == 7))
                        return r
                    op("pe", mmif, reads=[bC] + hbufs(4 * c, 4 * c + 4), writes=[bPB[bk_]])
                op("act", lambda e, bkf=bkf: e.activation(out=tfc[:], in_=PB[bkf][0:4, :], func=AF.Exp, scale=-1.0, bias=nbf[:, 0:1]),
                   reads=[bPB[bkf], bC], writes=[btf])
                op("act", lambda e: e.activation(out=tfc[:], in_=tfc[:], func=AF.Ln, bias=1.0), reads=[btf], writes=[btf])
                init_nf = 0.0 if c == 0 else nfc[1 - k][:, 511:512]
                op("dve", lambda e, k=k, init_nf=init_nf: e.tensor_tensor_scan(out=nfc[k][:], data0=ones4[:], data1=tfc[:], initial=init_nf,
                                                                              op0=ALU.mult, op1=ALU.add),
                   reads=[btf, bC, bnf[1 - k]], writes=[bnf[k]])
                op("dve", lambda e, k=k, bki=bki, sl=sl: e.scalar_tensor_tensor(out=gch[k][:], in0=PB[bki][0:4, :], scalar=bifT[:, 0:1], in1=nfc[k][:],
                                                                               op0=ALU.add, op1=ALU.add),
                   reads=[bPB[bki], bC, bnf[k]], writes=[bgc[k]])
                init_m = 0.0 if c == 0 else mch[1 - k][:, 511:512]
                op("dve", lambda e, k=k, sl=sl, init_m=init_m: e.tensor_tensor_scan(out=mch[k][:], data0=gch[k][:], data1=gch[k][:], initial=init_m,
                                                                                   op0=ALU.max, op1=ALU.max),
                   reads=[bgc[k], bmc[1 - k]], writes=[bmc[k]])
                op("dve", lambda e, k=k, sl=sl: e.tensor_scalar(out=negM[:, sl], in0=mch[k][:], scalar1=-1.0, scalar2=None, op0=ALU.mult),
                   reads=[bmc[k]], writes=[bnegM])
                op("dve", lambda e, k=k: e.tensor_tensor(out=ech[k][:], in0=nfc[k][:], in1=mch[k][:], op=ALU.subtract),
                   reads=[bnf[k], bmc[k]], writes=[bec[k]])
                op("act", lambda e, k=k: e.activation(out=ech[k][:], in_=ech[k][:], func=AF.Exp), reads=[bec[k]], writes=[bec[k]])

                def trgt(e, k=k, c=c):
                    r = None
                    for t in range(4):
                        i = 4 * c + t
                        r = e.transpose(out=PB[7][:, i * 4:(i + 1) * 4], in_=gch[k][0:4, t * 128:(t + 1) * 128], identity=identf[0:4, 0:4])
                        r = e.transpose(out=PB[7][:, 64 + i * 4:64 + (i + 1) * 4], in_=ech[k][0:4, t * 128:(t + 1) * 128], identity=identf[0:4, 0:4])
                    return r
                op("pe", trgt, reads=[bgc[k], bec[k], bC], writes=[bPB[7]])
            bk = 7
            op("dve", lambda e, bk=bk: e.tensor_scalar(out=gtok[:], in0=PB[bk][:, 0:64], scalar1=LNS, scalar2=None, op0=ALU.add),
               reads=[bPB[bk]], writes=[bgtok])
            op("dve", lambda e, bk=bk: e.tensor_copy(out=etok[:], in_=PB[bk][:, 64:128]), reads=[bPB[bk]], writes=[betok])

            wml = A.alloc([128, 8, 768], BF16)
            bwml = Buf("wml")
            negMbc = [A.alloc([128, 256], F32) for _ in range(2)]
            bnb = [Buf("nb0"), Buf("nb1")]
            zp = A.alloc([128, SEQ + 3], F32)
            bzp = [Buf(f"zp{c}") for c in range(4)]
            cv = A.alloc([128, SEQ], F32)
            bcv = Buf("cv")
            qTm = A.alloc([128, SEQ], BF16)
            kTm = A.alloc([128, SEQ], BF16)
            bqm = Buf("qTm")
            bkm = Buf("kTm")
            vm = A.alloc([128, NT, 257], BF16)
            bvm = [Buf(f"vm{i}") for i in range(8)]
            Ps = [A.alloc([128, NT, 256], BF16) for _ in range(2)]
            bP = [[Buf(f"P{s}{j}") for j in range(NT)] for s in range(2)]
            Wt = [A.alloc([128, 256], F32) for _ in range(2)]
            bWt = [Buf("Wt0"), Buf("Wt1")]
            gml_bc = A.alloc([128, 256], F32)
            bgml = Buf("gml")
            hn = [A.alloc([128, 256], F32) for _ in range(4)]
            bhn = [Buf("hn%d" % i_) for i_ in range(4)]
            og = [A.alloc([128, 256], BF16) for _ in range(4)]
            bog = [Buf("og%d" % i_) for i_ in range(4)]
            ymt = [A.alloc([128, 256], BF16) for _ in range(4)]
            bymt = [Buf("ymt%d" % i_) for i_ in range(4)]
            dens = [A.alloc([128, 2], F32) for _ in range(4)]
            bden = [Buf("den%d" % i_) for i_ in range(4)]
            op("pool", lambda e: e.memset(zp[:, 0:3], 0.0), writes=bzp)
            op("pool", lambda e: e.memset(vm[:, :, 256:257], 1.0), writes=bvm)
            pcnt = 0
            tcnt = 0
            wcnt = 0
            for h in range(4):
                dma("sp", wml[:].rearrange("p a b -> p (a b)"), wml_b[h * 128:(h + 1) * 128, :], [bDR], [bwml], "wml")
                dma("sp", gml_bc[:], mlg_d[0:1, h * 256:(h + 1) * 256].partition_broadcast(128), [], [bgml], "gml")
                for which in range(2):
                    ch = which * 4 + h
                    for c in range(4):
                        bk = nextbank([0, 1])

                        def mmq(e, c=c, which=which, bk=bk):
                            r = None
                            for kc in range(8):
                                r = e.matmul(PB[bk][:, :], lhsT=wml[:, kc, which * 128:(which + 1) * 128],
                                             rhs=hT[:, kc, c * 512:(c + 1) * 512], start=(kc == 0), stop=(kc == 7))
                            return r
                        op("pe", mmq, reads=[bwml] + hbufs(4 * c, 4 * c + 4), writes=[bPB[bk]])
                        op("act", lambda e, c=c, bk=bk: e.activation(out=zp[:, 3 + c * 512:3 + (c + 1) * 512], in_=PB[bk][:, :], func=AF.Copy),
                           reads=[bPB[bk]], writes=[bzp[c]])
                    op("dve", lambda e, ch=ch: e.tensor_scalar(out=cv[:], in0=zp[:, 3:SEQ + 3], scalar1=cwT[:, ch * 4 + 3:ch * 4 + 4],
                                                               scalar2=cbT[:, ch:ch + 1], op0=ALU.mult, op1=ALU.add),
                       reads=bzp + [bC], writes=[bcv])
                    for j in range(3):
                        op("dve", lambda e, ch=ch, j=j: e.scalar_tensor_tensor(out=cv[:], in0=zp[:, j:j + SEQ], scalar=cwT[:, ch * 4 + j:ch * 4 + j + 1],
                                                                              in1=cv[:], op0=ALU.mult, op1=ALU.add),
                           reads=bzp + [bC, bcv], writes=[bcv])
                    dst, bdst = (qTm, bqm) if which == 0 else (kTm, bkm)
                    op("act", lambda e, dst=dst: e.activation(out=dst[:], in_=cv[:], func=AF.Silu), reads=[bcv], writes=[bdst])
                for g2 in range(8):
                    bk = nextbank([0, 1])

                    def mmv2(e, g2=g2, bk=bk):
                        r = None
                        for t in range(2):
                            i = 2 * g2 + t
                            for kc in range(8):
                                r = e.matmul(PB[bk][:, t * 256:(t + 1) * 256], lhsT=hT[:, kc, i * 128:(i + 1) * 128],
                                             rhs=wml[:, kc, 256:512], start=(kc == 0), stop=(kc == 7))
                        return r
                    op("pe", mmv2, reads=[bwml] + hbufs(2 * g2, 2 * g2 + 2), writes=[bPB[bk]])
                    op("dve", lambda e, g2=g2, bk=bk: e.tensor_copy(out=vm[:, 2 * g2:2 * g2 + 2, 0:256],
                                                                   in_=PB[bk][:, :].rearrange("p (a b) -> p a b", a=2)),
                       reads=[bPB[bk]], writes=[bvm[g2]])
                mq = []
                def ml_build(c, psl):
                    nonlocal wcnt
                    P = Ps[psl]
                    bk = nextbank([0, 1])
                    op("pe", lambda e, h=h, c=c, bk=bk: e.matmul(PB[bk][:, 0:256], lhsT=sel4[0:4, h * 128:(h + 1) * 128],
                                                                rhs=negM[0:4, c * 256:(c + 1) * 256], start=True, stop=True),
                       reads=[bC, bnegM], writes=[bPB[bk]])
                    op("act", lambda e, psl=psl, bk=bk: e.activation(out=negMbc[psl][:, :], in_=PB[bk][:, 0:256], func=AF.Copy),
                       reads=[bPB[bk]], writes=[bnb[psl]])
                    for j in range(2 * c + 2):
                        q0 = 128 if j == 2 * c + 1 else 0
                        sb_ = nextbank([2, 3])
                        ws = wcnt % 2
                        wcnt += 1
                        op("pe", lambda e, j=j, c=c, q0=q0, sb_=sb_: e.matmul(
                            PB[sb_][:, q0:256], lhsT=kTm[:, j * 128:(j + 1) * 128], rhs=qTm[:, c * 256 + q0:(c + 1) * 256], start=True, stop=True),
                           reads=[bkm, bqm], writes=[bPB[sb_]])
                        op("act", lambda e, j=j, c=c, q0=q0, ws=ws, h=h, psl=psl: e.activation(out=Wt[ws][:, q0:256], in_=negMbc[psl][:, q0:256],
                                                                                     func=AF.Exp, bias=gtok[:, j * 4 + h:j * 4 + h + 1]),
                           reads=[bnb[psl], bgtok], writes=[bWt[ws]])
                        op("dve", lambda e, P=P, j=j, q0=q0, ws=ws, sb_=sb_: e.tensor_tensor(out=P[:, j, q0:256], in0=PB[sb_][:, q0:256],
                                                                                            in1=Wt[ws][:, q0:256], op=ALU.mult),
                           reads=[bPB[sb_], bWt[ws]], writes=[bP[psl][j]])
                        if mq:
                            mq.pop(0)()
                        if j >= 2 * c:
                            d0 = (j - 2 * c) * 128
                            op("pool", lambda e, P=P, j=j, d0=d0: e.tensor_tensor(out=P[:, j, d0:d0 + 128], in0=P[:, j, d0:d0 + 128],
                                                                                 in1=maskb[:], op=ALU.mult),
                               reads=[bP[psl][j], bC], writes=[bP[psl][j]])

                def ml_pv(c, psl):
                    nonlocal tcnt
                    P = Ps[psl]
                    for ii in range(2):
                        i = 2 * c + ii
                        k = tcnt % 4
                        tcnt += 1
                        pb_ = nextbank([4, 5, 6, 7])

                        def pvm(e, P=P, ii=ii, i=i, pb_=pb_):
                            r = None
                            for j in range(i + 1):
                                r = e.matmul(PB[pb_][:, 0:257], lhsT=P[:, j, ii * 128:(ii + 1) * 128], rhs=vm[:, j, :],
                                             start=(j == 0), stop=(j == i))
                            return r
                        op("pe", pvm, reads=[bP[psl][j] for j in range(i + 1)] + [bvm[j] for j in range(i // 2 + 1)], writes=[bPB[pb_]])
                        bk = nextbank([0, 1])

                        def mmo(e, i=i, bk=bk):
                            r = None
                            for kc in range(8):
                                r = e.matmul(PB[bk][:, 0:256], lhsT=hT[:, kc, i * 128:(i + 1) * 128], rhs=wml[:, kc, 512:768],
                                             start=(kc == 0), stop=(kc == 7))
                            return r
                        op("pe", mmo, reads=[bwml, bhT[i]], writes=[bPB[bk]])
                        op("act", lambda e, k=k, bk=bk: e.activation(out=og[k][:], in_=PB[bk][:, 0:256], func=AF.Sigmoid),
                           reads=[bPB[bk]], writes=[bog[k]])
                        op("dve", lambda e, k=k, pb_=pb_: e.tensor_copy(out=dens[k][:, 0:1], in_=PB[pb_][:, 256:257]),
                           reads=[bPB[pb_]], writes=[bden[k]])
                        op("dve", lambda e, k=k: e.scalar_tensor_tensor(out=dens[k][:, 0:1], in0=dens[k][:, 0:1], scalar=-1.0, in1=dens[k][:, 0:1],
                                                                        op0=ALU.mult, op1=ALU.max),
                           reads=[bden[k]], writes=[bden[k]])
                        op("dve", lambda e, k=k, i=i, h=h: e.tensor_tensor(out=dens[k][:, 0:1], in0=dens[k][:, 0:1], in1=etok[:, i * 4 + h:i * 4 + h + 1],
                                                                          op=ALU.max),
                           reads=[bden[k], betok], writes=[bden[k]])
                        op("dve", lambda e, k=k: e.reciprocal(out=dens[k][:, 0:1], in_=dens[k][:, 0:1]), reads=[bden[k]], writes=[bden[k]])
                        op("dve", lambda e, k=k, pb_=pb_: e.tensor_scalar(out=hn[k][:], in0=PB[pb_][:, 0:256], scalar1=dens[k][:, 0:1], scalar2=None,
                                                                         op0=ALU.mult),
                           reads=[bPB[pb_], bden[k]], writes=[bhn[k]])
                        def sB(k=k):
                            op("act", lambda e: e.activation(out=junk[:, 0:256], in_=hn[k][:], func=AF.Square, accum_out=dens[k][:, 1:2]),
                               reads=[bhn[k]], writes=[bden[k], bjunk])

                        def sC(k=k):
                            op("dve", lambda e: e.tensor_scalar(out=dens[k][:, 1:2], in0=dens[k][:, 1:2], scalar1=1.0 / 256.0, scalar2=LN_EPS,
                                                                op0=ALU.mult, op1=ALU.add),
                               reads=[bden[k]], writes=[bden[k]])
                            op("pool", lambda e: e.tensor_tensor(out=dens[k][:, 1:2], in0=dens[k][:, 1:2], in1=cm05[:, 0:1], op=ALU.pow),
                               reads=[bden[k], bC], writes=[bden[k]])

                        def sD(k=k):
                            op("dve", lambda e: e.scalar_tensor_tensor(out=hn[k][:], in0=hn[k][:], scalar=dens[k][:, 1:2],
                                                                       in1=gml_bc[:], op0=ALU.mult, op1=ALU.mult),
                               reads=[bhn[k], bden[k], bgml], writes=[bhn[k]])
                            op("pool", lambda e: e.tensor_tensor(out=ymt[k][:], in0=hn[k][:], in1=og[k][:], op=ALU.mult),
                               reads=[bhn[k], bog[k]], writes=[bymt[k]])

                        def sE(k=k, i=i):
                            bk = nextbank([0, 1])
                            ptb = PB[bk].bitcast(BF16)

                            def trym(e):
                                for ee in range(2):
                                    e.transpose(out=ptb[:, ee * 128:(ee + 1) * 128], in_=ymt[k][:, ee * 128:(ee + 1) * 128], identity=identb[:])
                            op("pe", trym, reads=[bymt[k], bC], writes=[bPB[bk]])
                            op("act", lambda e: e.activation(out=ymT[:, 2 * h:2 * h + 2, i * 128:(i + 1) * 128],
                                                             in_=ptb[:, 0:256].rearrange("p (a b) -> p a b", a=2), func=AF.Copy),
                               reads=[bPB[bk]], writes=[bymT[i]])
                        mq.extend([sB, sC, sD, sE])

                prevc = None
                for c in range(8):
                    psl = pcnt % 2
                    pcnt += 1
                    ml_build(c, psl)
                    while len(mq) > 4:
                        mq.pop(0)()
                    if prevc is not None:
                        ml_pv(*prevc)
                    prevc = (c, psl)
                ml_pv(*prevc)
                while mq:
                    mq.pop(0)()
            SC.barrier()
            A.pop()
            if b == 0:
                dump("ymT", ymT.rearrange("p a b -> p (a b)"), bymT, [128, 8 * SEQ], BF16)

            A.push()
            yT = A.alloc([128, 8, SEQ], BF16)
            byT = [Buf(f"yT{c}") for c in range(4)]
            wmg = [A.alloc([128, 8, 512], BF16) for _ in range(2)]
            bwmg = [Buf("wmg0"), Buf("wmg1")]
            sgA = [A.alloc([128, 512], F32) for _ in range(2)]
            sgB = [A.alloc([128, 512], F32) for _ in range(2)]
            t1 = [A.alloc([128, 512], F32) for _ in range(2)]
            t2 = [A.alloc([128, 512], F32) for _ in range(2)]
            bsgA = [Buf("sgA0"), Buf("sgA1")]
            bsgB = [Buf("sgB0"), Buf("sgB1")]
            bt1 = [Buf("t10"), Buf("t11")]
            bt2 = [Buf("t20"), Buf("t21")]
            dcnt = 0
            allb = [0, 1, 2, 3, 4, 5, 6, 7]
            for fc in range(8):
                s = fc % 2
                dma("sp", wmg[s][:].rearrange("p a b -> p (a b)"), wmg_b[fc * 128:(fc + 1) * 128, :], [bDR], [bwmg[s]], f"wmg{s}")
                for c in range(4):
                    k = dcnt % 2
                    dcnt += 1
                    banks = []
                    for which, src, srcb in ((0, hT, hbufs(4 * c, 4 * c + 4)), (1, hT, hbufs(4 * c, 4 * c + 4)),
                                             (2, yaT, byaT[2 * c:2 * c + 2]), (3, ymT, bymT[4 * c:4 * c + 4])):
                        bk = nextbank(allb)
                        banks.append(bk)

                        def mmd(e, s=s, c=c, which=which, src=src, bk=bk):
                            r = None
                            for kc in range(8):
                                r = e.matmul(PB[bk][:, :], lhsT=wmg[s][:, kc, which * 128:(which + 1) * 128],
                                             rhs=src[:, kc, c * 512:(c + 1) * 512], start=(kc == 0), stop=(kc == 7))
                            return r
                        op("pe", mmd, reads=[bwmg[s]] + list(srcb), writes=[bPB[bk]])
                    op("act", lambda e, k=k, bk=banks[0]: e.activation(out=sgA[k][:], in_=PB[bk][:, :], func=AF.Sigmoid),
                       reads=[bPB[banks[0]]], writes=[bsgA[k]])
                    op("act", lambda e, k=k, bk=banks[1]: e.activation(out=sgB[k][:], in_=PB[bk][:, :], func=AF.Sigmoid),
                       reads=[bPB[banks[1]]], writes=[bsgB[k]])
                    op("dve", lambda e, k=k, bk=banks[2]: e.tensor_tensor(out=t1[k][:], in0=PB[bk][:, :], in1=sgA[k][:], op=ALU.mult),
                       reads=[bPB[banks[2]], bsgA[k]], writes=[bt1[k]])
                    op("dve", lambda e, k=k, bk=banks[3]: e.tensor_tensor(out=t2[k][:], in0=PB[bk][:, :], in1=sgB[k][:], op=ALU.mult),
                       reads=[bPB[banks[3]], bsgB[k]], writes=[bt2[k]])
                    op("pool", lambda e, k=k, fc=fc, c=c: e.tensor_tensor(out=yT[:, fc, c * 512:(c + 1) * 512], in0=t1[k][:], in1=t2[k][:], op=ALU.add),
                       reads=[bt1[k], bt2[k]], writes=[byT[c]])
            SC.barrier()
            if b == 0:
                dump("yT", yT.rearrange("p a b -> p (a b)"), byT, [128, 8 * SEQ], BF16)
            wo = hT_raw[:, 0:4096].bitcast(BF16).rearrange("p (a b) -> p a b", a=8)
            ttile = hT_raw[:, 4096:5120]
            l1g = hT_raw[:, 5120:6144]
            l1b = hT_raw[:, 6144:7168]
            gt1 = hT_raw[:, 7168:8192]
            xts = [A.alloc([128, D], F32) for _ in range(2)]
            bwo = Buf("wo")
            btt = Buf("tt")
            bl1 = Buf("l1")
            dma("sp", wo.rearrange("p a b -> p (a b)"), wout_b[:, :], [bDR], [bwo], "wo")
            dma("sp", l1g, ln1g_d[0:1, :].partition_broadcast(128), [], [bl1], "l1")
            dma("sp", l1b, ln1b_d[0:1, :].partition_broadcast(128), [], [bl1], "l1")
            dma("sp", gt1, gt_d[b:b + 1, 0:1024].partition_broadcast(128), [bGT], [bl1], "l1")
            for i in range(NT):
                k = i % 2
                r0 = b * SEQ + i * 128
                dma("sp", xts[k][:], x_d[r0:r0 + 128, :], [], [bxts[k]], f"xt{k}")
                for half in range(2):
                    bk = nextbank(allb)

                    def mmo2(e, i=i, half=half, bk=bk):
                        r = None
                        for kc in range(8):
                            r = e.matmul(PB[bk][:, :], lhsT=yT[:, kc, i * 128:(i + 1) * 128], rhs=wo[:, kc, half * 512:(half + 1) * 512],
                                         start=(kc == 0), stop=(kc == 7))
                        return r
                    op("pe", mmo2, reads=[byT[i // 4], bwo], writes=[bPB[bk]])
                    op("dve", lambda e, half=half, bk=bk: e.tensor_tensor(out=ttile[:, half * 512:(half + 1) * 512], in0=PB[bk][:, :],
                                                                         in1=gt1[:, half * 512:(half + 1) * 512], op=ALU.mult),
                       reads=[bPB[bk], bl1], writes=[btt])
                op("dve", lambda e, k=k: e.scalar_tensor_tensor(out=xts[k][:], in0=xts[k][:], scalar=ALPHA, in1=ttile, op0=ALU.mult, op1=ALU.add),
                   reads=[bxts[k], btt], writes=[bxts[k]])
                mv, rs, bm, br = ln_stats(xts[k], bxts[k])
                op("dve", lambda e, k=k, mv=mv, rs=rs: e.tensor_scalar(out=xts[k][:], in0=xts[k][:], scalar1=mv[:, 0:1], scalar2=rs[:],
                                                                      op0=ALU.subtract, op1=ALU.mult),
                   reads=[bxts[k], bm, br], writes=[bxts[k]])
                op("pool", lambda e, k=k: e.tensor_tensor(out=xts[k][:], in0=xts[k][:], in1=l1g, op=ALU.mult), reads=[bxts[k], bl1], writes=[bxts[k]])
                op("pool", lambda e, k=k: e.tensor_tensor(out=xts[k][:], in0=xts[k][:], in1=l1b, op=ALU.add), reads=[bxts[k], bl1], writes=[bxts[k]])
                dst = x1_d if do_peer else out_d
                dma("sp", dst[r0:r0 + 128, :], xts[k][:], [bxts[k]], [bXO[k]], f"xo{k}")
            SC.barrier()
            A.pop()
        A.pop()
        SC.barrier()

        if do_peer:
            A.push()
            RB = [0, 1, 2, 3, 4, 5]
            RBX = [4, 5]
            keysT = A.alloc([128, 256], F32)
            l2g = A.alloc([128, D], F32)
            l2b = A.alloc([128, D], F32)
            gt2 = A.alloc([128, D], F32)
            bl2 = Buf("l2")
            bgt2 = Buf("gt2")
            xps = [A.alloc([128, D], F32) for _ in range(2)]
            bxps = [Buf("xp0"), Buf("xp1")]
            xn2 = A.alloc([128, D], F32)
            bxn2 = Buf("xn2")
            h2T = A.alloc([128, 8, 128], F32)
            h2Tb = A.alloc([128, 8, 128], BF16)
            bh2 = Buf("h2T")
            bh2b = Buf("h2Tb")
            wq = [A.alloc([128, 8, 128], F32) for _ in range(2)]
            bwq = [Buf("wq0"), Buf("wq1")]
            qTh = [A.alloc([128, 128], F32) for _ in range(2)]
            bqTh = [Buf("qTh0"), Buf("qTh1")]
            s_sb = A.alloc([128, 8, 2, 128], F32)
            bs = Buf("s_sb")
            m16 = A.alloc([128, 8, 2, 16], F32)
            bm16 = Buf("m16")
            wk1 = A.alloc([128, 128], F32)
            bwk1 = Buf("wk1")
            cand = A.alloc([128, 8, 16, 16], F32)
            bcand = Buf("cand")
            wk2 = A.alloc([128, 256], F32)
            bwk2 = Buf("wk2")
            c16 = A.alloc([128, 8, 16], F32)
            bc16 = Buf("c16")
            negthr = A.alloc([128, 8], F32)
            zz = A.alloc([128, 8], F32)
            biasE = A.alloc([128, 8], F32)
            bsm = Buf("small")
            Sp = [A.alloc([128, 2048], F32) for _ in range(3)]
            Ep = [A.alloc([128, 2048], BF16) for _ in range(3)]
            Mk = [A.alloc([128, 2048], BF16) for _ in range(2)]
            bSp = [Buf("Sp0"), Buf("Sp1"), Buf("Sp2")]
            bEp = [Buf("Ep0"), Buf("Ep1"), Buf("Ep2")]
            bMk = [Buf("Mk0"), Buf("Mk1"), Buf("Mk2")]
            gstate = [0, 0, 0]
            acc = A.alloc([128, NEXP], BF16)
            bacc = [Buf(f"acc{i}") for i in range(8)]
            ub = [A.alloc([128, 8, 512], BF16) for _ in range(3)]
            bub = [Buf("ub0"), Buf("ub1"), Buf("ub2")]
            vb = [A.alloc([128, 4, D], BF16) for _ in range(3)]
            bvb = [Buf("vb0"), Buf("vb1"), Buf("vb2")]
            gel = [A.alloc([128, 512], F32) for _ in range(2)]
            bgel = [Buf("gel0"), Buf("gel1"), Buf("gel2")]
            Wc = [A.alloc([128, 512], BF16) for _ in range(2)]
            bWc = [Buf("Wc0"), Buf("Wc1"), Buf("Wc2")]
            WT = [A.alloc([128, 4, 128], BF16) for _ in range(2)]
            bWT = [Buf("WT0"), Buf("WT1"), Buf("WT2")]
            tt2 = A.alloc([128, D], F32)
            btt2 = Buf("tt2")
            bPO = [Buf("po0"), Buf("po1")]
            dma("sp", keysT[:], keysT_d[:, :], [], [bl2], "l2")
            dma("sp", l2g[:], ln2g_d[0:1, :].partition_broadcast(128), [], [bl2], "l2")
            dma("sp", l2b[:], ln2b_d[0:1, :].partition_broadcast(128), [], [bl2], "l2")
            wpq3 = wpq_d.rearrange("r (a b) -> r a b", a=8)
            v_b3 = v_b.rearrange("(g p) n -> p g n", p=128)
            gcnt = 0
            ecnt2 = 0
            import os as _os
            _PT = int(_os.environ.get("PEER_TILES", NSEQ * NT))
            _PS = int(_os.environ.get("PEER_STAGE", 9))
            for ti in range(min(_PT, NSEQ * NT)):
                b = ti // NT
                k = ti % 2
                r0 = ti * 128
                if ti % NT == 0:
                    dma("sp", gt2[:], gt_d[b:b + 1, 1024:2048].partition_broadcast(128), [bGT], [bgt2], "gt2")
                dma("sp", xps[k][:], x1_d[r0:r0 + 128, :], [bXO[0], bXO[1]], [bxps[k]], f"xp{k}")
                mv, rs, bm, br = ln_stats(xps[k], bxps[k])
                op("dve", lambda e, k=k, mv=mv, rs=rs: e.tensor_scalar(out=xn2[:], in0=xps[k][:], scalar1=mv[:, 0:1], scalar2=rs[:],
                                                                      op0=ALU.subtract, op1=ALU.mult),
                   reads=[bxps[k], bm, br], writes=[bxn2])
                for hf in range(2):
                    bk = nextbank(RB)

                    def trp(e, hf=hf, bk=bk):
                        for cc in range(4):
                            c = hf * 4 + cc
                            e.transpose(out=PB[bk][:, cc * 128:(cc + 1) * 128], in_=xn2[:, c * 128:(c + 1) * 128], identity=identf[:])
                    op("pe", trp, reads=[bxn2, bC], writes=[bPB[bk]])
                    for cc in range(4):
                        c = hf * 4 + cc
                        if cc % 2 == 0:
                            op("act", lambda e, c=c, cc=cc, bk=bk, b=b: e.activation(out=h2T[:, c, :], in_=PB[bk][:, cc * 128:(cc + 1) * 128],
                                                                                    func=AF.Identity, scale=modcol(4, c, b), bias=modcol(3, c, b)),
                               reads=[bPB[bk], bmodT], writes=[bh2])
                        else:
                            op("dve", lambda e, c=c, cc=cc, bk=bk, b=b: e.tensor_scalar(out=h2T[:, c, :], in0=PB[bk][:, cc * 128:(cc + 1) * 128],
                                                                                       scalar1=modcol(4, c, b), scalar2=modcol(3, c, b),
                                                                                       op0=ALU.mult, op1=ALU.add),
                               reads=[bPB[bk], bmodT], writes=[bh2])
                op("pool", lambda e: e.tensor_copy(out=h2Tb[:].rearrange("p a b -> p (a b)"), in_=h2T[:].rearrange("p a b -> p (a b)")),
                   reads=[bh2], writes=[bh2b])
                sbk = None
                for h in range(8 if _PS >= 2 else 0):
                    ws = h % 2
                    dma("sp", wq[ws][:], wpq3[h * 128:(h + 1) * 128, :, :], [], [bwq[ws]], f"wq{ws}")
                    bk = nextbank(RB)

                    def mmq(e, ws=ws, bk=bk):
                        for kc in range(8):
                            e.matmul(PB[bk][:, 0:128], lhsT=wq[ws][:, kc, :], rhs=h2T[:, kc, :], start=(kc == 0), stop=(kc == 7))
                    op("pe", mmq, reads=[bwq[ws], bh2], writes=[bPB[bk]])
                    op("act", lambda e, ws=ws, bk=bk: e.activation(out=qTh[ws][:], in_=PB[bk][:, 0:128], func=AF.Copy),
                       reads=[bPB[bk]], writes=[bqTh[ws]])
                    if h % 2 == 0:
                        sbk = nextbank(RB)

                    def mms(e, ws=ws, sbk=sbk, h=h):
                        o0 = (h % 2) * 256
                        e.matmul(PB[sbk][:, o0:o0 + 256], lhsT=qTh[ws][:, :], rhs=keysT[:, :], start=True, stop=True)
                    op("pe", mms, reads=[bqTh[ws], bl2], writes=[bPB[sbk]])
                    if h % 2 == 1:
                        op("dve", lambda e, h=h, sbk=sbk: e.tensor_copy(out=s_sb[:, h - 1:h + 1, :, :].rearrange("p a b c -> p (a b c)"), in_=PB[sbk][:, :]),
                           reads=[bPB[sbk]], writes=[bs])
                if _PS < 3:
                    dma("sp", out_d[r0:r0 + 128, :], xps[k][:], [bxps[k], bs, bh2b], [bPO[k]], f"po{k}")
                    continue
                wk16 = cand[:].rearrange("p a b c -> p (a b c)").rearrange("p (g n) -> p g n", g=16)
                grp = [(h, half) for h in range(8) for half in range(2)]
                for gi, (h, half) in enumerate(grp):
                    op("dve", lambda e, h=h, half=half: e.max(out=m16[:, h, half, 0:8], in_=s_sb[:, h, half, :]), reads=[bs], writes=[bm16])
                for gi, (h, half) in enumerate(grp):
                    op("dve", lambda e, h=h, half=half, gi=gi: e.match_replace(out=wk16[:, gi, :], in_to_replace=m16[:, h, half, 0:8],
                                                                              in_values=s_sb[:, h, half, :], imm_value=-1e30),
                       reads=[bs, bm16], writes=[bcand])
                for gi, (h, half) in enumerate(grp):
                    op("dve", lambda e, h=h, half=half, gi=gi: e.max(out=m16[:, h, half, 8:16], in_=wk16[:, gi, :]), reads=[bcand], writes=[bm16])
                op("dve", lambda e: e.tensor_tensor(out=cand[:], in0=m16[:, :, 0, :].unsqueeze(3).to_broadcast([128, 8, 16, 16]),
                                                    in1=m16[:, :, 1, :].unsqueeze(2).to_broadcast([128, 8, 16, 16]), op=ALU.add),
                   reads=[bm16], writes=[bcand])
                sp0v = Sp[0][:].rearrange("p (g n) -> p g n", g=8)
                for h in range(8):
                    ch2 = cand[:, h, :, :].rearrange("p a b -> p (a b)")
                    op("dve", lambda e, h=h, ch2=ch2: e.max(out=c16[:, h, 0:8], in_=ch2), reads=[bcand], writes=[bc16])
                for h in range(8):
                    ch2 = cand[:, h, :, :].rearrange("p a b -> p (a b)")
                    op("dve", lambda e, h=h, ch2=ch2: e.match_replace(out=sp0v[:, h, :], in_to_replace=c16[:, h, 0:8], in_values=ch2, imm_value=-1e30),
                       reads=[bcand, bc16], writes=[bSp[0]])
                for h in range(8):
                    op("dve", lambda e, h=h: e.max(out=c16[:, h, 8:16], in_=sp0v[:, h, :]), reads=[bSp[0]], writes=[bc16])
                op("dve", lambda e: e.tensor_scalar(out=negthr[:], in0=c16[:, :, 15], scalar1=-1.0, scalar2=None, op0=ALU.mult),
                   reads=[bc16], writes=[bsm])
                for h in range(8):
                    op("act", lambda e, h=h: e.activation(out=junk[:, 0:16], in_=c16[:, h, :], func=AF.Exp, bias=negthr[:, h:h + 1],
                                                          accum_out=zz[:, h:h + 1]),
                       reads=[bc16, bsm], writes=[bsm, bjunk])
                op("act", lambda e: e.activation(out=zz[:], in_=zz[:], func=AF.Ln), reads=[bsm], writes=[bsm])
                op("dve", lambda e: e.tensor_tensor(out=biasE[:], in0=negthr[:], in1=zz[:], op=ALU.subtract), reads=[bsm], writes=[bsm])
                if _PS < 4:
                    dma("sp", out_d[r0:r0 + 128, :], xps[k][:], [bxps[k], bsm], [bPO[k]], f"po{k}")
                    continue
                def g_finish(pc, h, g):
                    gm = g % 2
                    op("dve", lambda e: e.scalar_tensor_tensor(out=Mk[gm][:], in0=Sp[g][:], scalar=c16[:, h, 15:16], in1=Ep[g][:],
                                                               op0=ALU.is_ge, op1=ALU.mult),
                       reads=[bSp[g], bEp[g], bc16], writes=[bMk[gm]])

                    def accmm(e):
                        for q in range(4):
                            e.matmul(PB[q][:, :], lhsT=identb[:, :], rhs=Mk[gm][:, q * 512:(q + 1) * 512],
                                     start=(h == 0), stop=(h == 7))
                    op("pe", accmm, reads=[bMk[gm], bC], writes=[bPB[0], bPB[1], bPB[2], bPB[3]])
                    if h == 7:
                        for q in range(4):
                            op("act", lambda e, q=q: e.activation(out=acc[:, pc * 2048 + q * 512:pc * 2048 + (q + 1) * 512], in_=PB[q][:, :], func=AF.Copy),
                               reads=[bPB[q]], writes=[bacc[pc]])

                def g_start(pc, h):
                    g = gstate[0] % 3
                    gstate[0] += 1
                    op("dve", lambda e: e.tensor_tensor(
                        out=Sp[g][:].rearrange("p (a b) -> p a b", a=16),
                        in0=s_sb[:, h, 0, pc * 16:(pc + 1) * 16].unsqueeze(2).to_broadcast([128, 16, 128]),
                        in1=s_sb[:, h, 1, :].unsqueeze(1).to_broadcast([128, 16, 128]), op=ALU.add),
                       reads=[bs], writes=[bSp[g]])
                    op("act", lambda e: e.activation(out=Ep[g][:], in_=Sp[g][:], func=AF.Exp, bias=biasE[:, h:h + 1]),
                       reads=[bSp[g], bsm], writes=[bEp[g]])
                    return g

                def x_s0(ec):
                    u = gstate[1] % 3
                    gstate[1] += 1
                    dma("sp", ub[u][:].rearrange("p a b -> p (a b)"), ut_b[ec * 128:(ec + 1) * 128, :], [bDR], [bub[u]], f"ub{u}")
                    return {"ec": ec, "u": u}

                def x_s1a(st):
                    ec, u = st["ec"], st["u"]
                    v = gstate[2] % 3
                    w = gstate[2] % 2
                    gstate[2] += 1
                    st["v"], st["w"] = v, w
                    dma("sp", vb[v][:], v_b3[:, ec * 4:(ec + 1) * 4, :], [bDR], [bvb[v]], f"vb{v}")

                    def mma(e):
                        for kc in range(8):
                            e.matmul(PB[4][:, :], lhsT=h2Tb[:, kc, :], rhs=ub[u][:, kc, :], start=(kc == 0), stop=(kc == 7))
                    op("pe", mma, reads=[bh2b, bub[u]], writes=[bPB[4]])

                def x_s1b(st):
                    w = st["w"]
                    op("act", lambda e: e.activation(out=gel[w][:], in_=PB[4][:, :], func=AF.Gelu), reads=[bPB[4]], writes=[bgel[w]])

                def x_s2(st):
                    ec, w = st["ec"], st["w"]
                    op("dve", lambda e: e.tensor_tensor(out=Wc[w][:], in0=gel[w][:], in1=acc[:, ec * 512:(ec + 1) * 512], op=ALU.mult),
                       reads=[bgel[w], bacc[ec // 4]], writes=[bWc[w]])
                    ptb = PB[5].bitcast(BF16)

                    def trw(e):
                        for a in range(4):
                            e.transpose(out=ptb[:, a * 128:(a + 1) * 128], in_=Wc[w][:, a * 128:(a + 1) * 128], identity=identb[:])
                    op("pe", trw, reads=[bWc[w], bC], writes=[bPB[5]])
                    op("act", lambda e: e.activation(out=WT[w][:].rearrange("p a b -> p (a b)"), in_=ptb[:, 0:512], func=AF.Copy),
                       reads=[bPB[5]], writes=[bWT[w]])

                def x_s3(st):
                    ec, v, w = st["ec"], st["v"], st["w"]

                    def mmv3(e):
                        for a in range(4):
                            for half in range(2):
                                e.matmul(PB[6 + half][:, :], lhsT=WT[w][:, a, :], rhs=vb[v][:, a, half * 512:(half + 1) * 512],
                                         start=(ec == 0 and a == 0), stop=(ec == 31 and a == 3))
                    op("pe", mmv3, reads=[bWT[w], bvb[v]], writes=[bPB[6], bPB[7]])

                xq = []
                R = {"P0": None, "A": None, "P1g": None, "P2new": None, "P2": None}

                def phase_a():
                    if R["P0"] is not None:
                        x_s1a(R["P0"])
                        R["A"] = R["P0"]
                        R["P0"] = None
                    if xq:
                        R["P0"] = x_s0(xq.pop(0))
                    if R["P1g"] is not None:
                        x_s2(R["P1g"])
                        R["P2new"] = R["P1g"]
                        R["P1g"] = None

                def phase_b():
                    if R["P2"] is not None:
                        x_s3(R["P2"])
                        R["P2"] = None
                    R["P2"] = R["P2new"]
                    R["P2new"] = None

                def phase_c():
                    if R["A"] is not None:
                        x_s1b(R["A"])
                        R["P1g"] = R["A"]
                        R["A"] = None

                pendq = []

                def fin(item):
                    g_finish(*item)
                    if item[1] == 7:
                        xq.extend(range(4 * item[0], 4 * item[0] + 4))

                for pc in range(8):
                    for hp in range(4):
                        phase_a()
                        for h in (2 * hp, 2 * hp + 1):
                            g = g_start(pc, h)
                            pendq.append((pc, h, g))
                            if len(pendq) > 2:
                                fin(pendq.pop(0))
                            if h % 2 == 0:
                                phase_b()
                                phase_c()
                while pendq:
                    fin(pendq.pop(0))
                while xq or any(v_ is not None for v_ in R.values()):
                    phase_a()
                    phase_b()
                    phase_c()
                for half in range(2):
                    op("dve", lambda e, half=half: e.tensor_tensor(out=tt2[:, half * 512:(half + 1) * 512], in0=PB[6 + half][:, :],
                                                                  in1=gt2[:, half * 512:(half + 1) * 512], op=ALU.mult),
                       reads=[bPB[6 + half], bgt2], writes=[btt2])
                op("dve", lambda e, k=k: e.scalar_tensor_tensor(out=xps[k][:], in0=xps[k][:], scalar=ALPHA, in1=tt2[:], op0=ALU.mult, op1=ALU.add),
                   reads=[bxps[k], btt2], writes=[bxps[k]])
                mv, rs, bm, br = ln_stats(xps[k], bxps[k])
                op("dve", lambda e, k=k, mv=mv, rs=rs: e.tensor_scalar(out=xps[k][:], in0=xps[k][:], scalar1=mv[:, 0:1], scalar2=rs[:],
                                                                      op0=ALU.subtract, op1=ALU.mult),
                   reads=[bxps[k], bm, br], writes=[bxps[k]])
                op("pool", lambda e, k=k: e.tensor_tensor(out=xps[k][:], in0=xps[k][:], in1=l2g[:], op=ALU.mult), reads=[bxps[k], bl2], writes=[bxps[k]])
                op("pool", lambda e, k=k: e.tensor_tensor(out=xps[k][:], in0=xps[k][:], in1=l2b[:], op=ALU.add), reads=[bxps[k], bl2], writes=[bxps[k]])
                dma("sp", out_d[r0:r0 + 128, :], xps[k][:], [bxps[k]], [bPO[k]], f"po{k}")
            A.pop()
        SC.barrier()
        print("arena peak", A.peak, "ops", SC.nops, "waits", SC.nwait)
        SC.emit_all()
    return nc


def _kmaj(w):
    return np.ascontiguousarray(w.reshape(8, 128, -1).transpose(1, 0, 2))


def prep_shared(inp, do_peer=True):
    f = np.float32
    w_in = np.asarray(inp["w_in"][0], f)
    da_q, da_k, da_v = w_in[:, 0:1024], w_in[:, 1024:2048], w_in[:, 2048:3072]
    ml_q, ml_k = w_in[:, 3072:3584], w_in[:, 3584:4096]
    ml_v, ml_o = w_in[:, 4096:5120], w_in[:, 5120:6144]
    ml_if = w_in[:, 6144:6152]
    g_attn, g_ml = w_in[:, 6152:7176], w_in[:, 7176:8200]
    wba = np.asarray(inp["w_br_attn"][0], f)
    wbm = np.asarray(inp["w_br_mlstm"][0], f)
    sh = {}
    sh["w_ada"] = _kmaj(np.asarray(inp["w_ada"][0], f)).reshape(128, 8 * 6144)
    sh["b_ada"] = np.asarray(inp["b_ada"], f).reshape(1, 6144)
    sh["w_da"] = np.stack([np.concatenate([_kmaj(da_q[:, h * 128:(h + 1) * 128]), _kmaj(da_k[:, h * 128:(h + 1) * 128]),
                                           _kmaj(da_v[:, h * 128:(h + 1) * 128])], axis=2) for h in range(8)]).reshape(1024, 3072)
    sh["w_ml"] = np.stack([np.concatenate([_kmaj(ml_q[:, h * 128:(h + 1) * 128]), _kmaj(ml_k[:, h * 128:(h + 1) * 128]),
                                           _kmaj(ml_v[:, h * 256:(h + 1) * 256]), _kmaj(ml_o[:, h * 256:(h + 1) * 256])], axis=2)
                           for h in range(4)]).reshape(512, 6144)
    sh["w_if"] = _kmaj(ml_if).reshape(128, 64)
    sh["w_mg"] = np.stack([np.concatenate([_kmaj(g_attn[:, c * 128:(c + 1) * 128]), _kmaj(g_ml[:, c * 128:(c + 1) * 128]),
                                           _kmaj(wba[:, c * 128:(c + 1) * 128]), _kmaj(wbm[:, c * 128:(c + 1) * 128])], axis=2)
                           for c in range(8)]).reshape(1024, 4096)
    sh["w_out"] = _kmaj(np.asarray(inp["w_out"][0], f)).reshape(128, 8192)
    wq = np.asarray(inp["peer_wq"][0], f)
    sh["w_pq"] = np.stack([_kmaj(wq[:, h * 128:(h + 1) * 128]) for h in range(8)]).reshape(1024, 1024)
    if do_peer:
        u = np.asarray(inp["peer_u"][0], f)
        sh["UT"] = np.stack([_kmaj(np.ascontiguousarray(u[ec * 512:(ec + 1) * 512, :].T)) for ec in range(32)]).reshape(4096, 4096)
        sh["V"] = np.ascontiguousarray(np.asarray(inp["peer_v"][0], f))
    else:
        sh["UT"] = np.zeros((4096, 4096), f)
        sh["V"] = np.zeros((NEXP, D), f)
    sh["b_ifT"] = np.ascontiguousarray(np.asarray(inp["b_if"][0], f).T)
    sh["conv_wT"] = np.ascontiguousarray(np.asarray(inp["conv_w"][0], f).reshape(4, 8, 128).transpose(2, 1, 0)).reshape(128, 32)
    sh["conv_bT"] = np.ascontiguousarray(np.asarray(inp["conv_b"][0], f).reshape(8, 128).T)
    sh["da_lambda"] = np.asarray(inp["da_lambda"][0], f).reshape(1, 256)
    sh["subln_g"] = np.asarray(inp["da_subln_g"][0], f).reshape(1, 128)
    sh["ml_norm_g"] = np.asarray(inp["ml_norm_g"][0], f).reshape(1, D)
    for nm in ("ln1_g", "ln1_b", "ln2_g", "ln2_b"):
        sh[nm] = np.asarray(inp[nm][0], f).reshape(1, D)
    kt = np.ascontiguousarray(np.asarray(inp["peer_keys"][0], f).transpose(0, 2, 1))
    kz = np.zeros((128, 256), f)
    kz[0:64, 0:128] = kt[0]
    kz[64:128, 128:256] = kt[1]
    sh["keysT"] = kz
    sh["ident"] = np.eye(128, dtype=f)
    kk = np.arange(128)
    sh["cmask"] = (kk[None, :] >= kk[:, None]).astype(f)
    sel = np.zeros((4, 4, 128), f)
    for h in range(4):
        sel[h, h, :] = 1.0
    sh["sel4"] = sel.reshape(4, 512)
    return sh


def core_inputs(inp, sh, b0, nseq):
    f = np.float32
    m = dict(sh)
    m["x"] = np.ascontiguousarray(np.asarray(inp["x"][b0:b0 + nseq], f).reshape(nseq * SEQ, D))
    c = np.asarray(inp["c"][b0:b0 + nseq], f)
    m["cT"] = _kmaj(np.ascontiguousarray(c.T)).reshape(128, 8 * nseq)
    return m


_NC_CACHE = {}


def kernel(**inputs):
    if "full" not in _NC_CACHE:
        _NC_CACHE["full"] = build(NSEQ_FULL, True)
    nc = _NC_CACHE["full"]
    sh = prep_shared(inputs, True)
    in_maps = [core_inputs(inputs, sh, i * NSEQ_FULL, NSEQ_FULL) for i in range(NCORES)]
    res = run_bass_kernel_spmd(nc, in_maps, core_ids=list(range(NCORES)))
    out = np.concatenate([np.asarray(r["out"]).reshape(NSEQ_FULL, SEQ, D) for r in res.results], axis=0)
    return out.astype(np.float32)
```
